# Optimizing a Trainium2 kernel written in Bass

```python
import math
import jax, jax.numpy as jnp
from jax import lax
import numpy as np

D_MODEL = 1024
BATCH = 2
SEQ = 8192
DEPTH = 2

D_MIX = D_MODEL
D_FF = int(math.ceil(8 * D_MODEL / 3 / 256)) * 256
EPS = 1e-6
S5_W = D_MIX // 4
S5_GROUP = 16
S5_G = S5_W // S5_GROUP
S5_P = 64
HG_W = D_MIX // 4
HG_HEADS = 4
HG_DK = HG_W // HG_HEADS
HG_DV = HG_W // HG_HEADS
HG_CHUNK = 64
NSA_W = D_MIX - S5_W - HG_W
NSA_DH = 64
NSA_H = NSA_W // NSA_DH
NSA_G = 2
NSA_R = NSA_H // NSA_G
CMP_LEN = 32
CMP_STRIDE = 16
CMP_RATIO = CMP_LEN // CMP_STRIDE
SLC_LEN = 64
SLC_RATIO = SLC_LEN // CMP_STRIDE
N_SEL = 16
WIN = 512
Q_BLOCK = 128
FORCE_BONUS = 1e4
NEG_INF = -1e30
TINY = 1e-30
REL_BUCKETS = 32
REL_MAX_DIST = 1024
KV_W = NSA_G * NSA_DH
D_IN = S5_W + 4 * HG_W + NSA_W + 6 * KV_W + 3 * NSA_H

kernel_name = 'hymba_s5_hgrn2_nsa_macaron'


def rmsnorm(x, g):
    xf = x.astype(jnp.float32)
    y = xf * lax.rsqrt(jnp.mean(xf * xf, axis=-1, keepdims=True) + EPS)
    return y.astype(x.dtype) * g


def swiglu(h, wg, wu, wd):
    return (jax.nn.silu(h @ wg) * (h @ wu)) @ wd


def masked_softmax(s, mask):
    s = jnp.where(mask, s.astype(jnp.float32), NEG_INF)
    m = jnp.max(s, axis=-1, keepdims=True)
    e = jnp.where(mask, jnp.exp(s - m), 0.0)
    return e / jnp.maximum(jnp.sum(e, axis=-1, keepdims=True), TINY)


def t5_bucket(dist):
    n = jnp.maximum(dist, 0)
    max_exact = REL_BUCKETS // 2
    nf = jnp.maximum(n, max_exact).astype(jnp.float32)
    large = max_exact + (jnp.log(nf / max_exact) / math.log(REL_MAX_DIST / max_exact)
                         * (REL_BUCKETS - max_exact)).astype(jnp.int32)
    large = jnp.minimum(large, REL_BUCKETS - 1)
    return jnp.where(n < max_exact, n, large)


def s5_mixer(u, lam_re, lam_im, log_dt, b_re, b_im, c_re, c_im, d, w_glu):
    B_, L, _ = u.shape
    ug = u.astype(jnp.float32).reshape(B_, L, S5_G, S5_GROUP)
    lr, li = lam_re.astype(jnp.float32), lam_im.astype(jnp.float32)
    dt = jnp.exp(log_dt.astype(jnp.float32))[:, None]
    mag = jnp.exp(lr * dt)
    ab_re, ab_im = mag * jnp.cos(li * dt), mag * jnp.sin(li * dt)
    den = lr * lr + li * li
    nr, ni = ab_re - 1.0, ab_im
    g_re = (nr * lr + ni * li) / den
    g_im = (ni * lr - nr * li) / den
    br, bi = b_re.astype(jnp.float32), b_im.astype(jnp.float32)
    bb_re = g_re[..., None] * br - g_im[..., None] * bi
    bb_im = g_re[..., None] * bi + g_im[..., None] * br
    bu_re = jnp.einsum('blgh,gph->blgp', ug, bb_re)
    bu_im = jnp.einsum('blgh,gph->blgp', ug, bb_im)
    a_re = jnp.broadcast_to(ab_re, bu_re.shape)
    a_im = jnp.broadcast_to(ab_im, bu_im.shape)

    def combine(e1, e2):
        a1r, a1i, b1r, b1i = e1
        a2r, a2i, b2r, b2i = e2
        return (a2r * a1r - a2i * a1i, a2r * a1i + a2i * a1r,
                a2r * b1r - a2i * b1i + b2r, a2r * b1i + a2i * b1r + b2i)

    _, _, xr, xi = lax.associative_scan(combine, (a_re, a_im, bu_re, bu_im), axis=1)
    y = (jnp.einsum('blgp,ghp->blgh', xr, c_re.astype(jnp.float32))
         - jnp.einsum('blgp,ghp->blgh', xi, c_im.astype(jnp.float32))
         + d.astype(jnp.float32) * ug)
    y = jax.nn.gelu(y.reshape(B_, L, S5_W))
    y = y * jax.nn.sigmoid(y @ w_glu.astype(jnp.float32))
    return y.astype(u.dtype)


def hgrn2_mixer(q, f_logit, i_in, g, lb, norm_gain):
    B_, L, _ = q.shape
    nc = L // HG_CHUNK
    qf = jax.nn.silu(q.astype(jnp.float32))
    lbf = lb.astype(jnp.float32)
    fl = f_logit.astype(jnp.float32)
    f = lbf + (1.0 - lbf) * jax.nn.sigmoid(fl)
    lf = jnp.log(jnp.maximum(f, TINY))
    kf = (1.0 - lbf) * jax.nn.sigmoid(-fl)
    vf = i_in.astype(jnp.float32)

    def to_chunks(a, dim):
        return a.reshape(B_, nc, HG_CHUNK, HG_HEADS, dim).transpose(1, 0, 3, 2, 4)

    causal = jnp.tril(jnp.ones((HG_CHUNK, HG_CHUNK), bool))[:, :, None]

    def step(S, inp):
        qc, kc, vc, lfc = inp
        b = jnp.cumsum(lfc, axis=2)
        diff = b[:, :, :, None, :] - b[:, :, None, :, :]
        decay = jnp.where(causal, jnp.exp(jnp.minimum(diff, 0.0)), 0.0)
        att = jnp.einsum('bhtd,bhsd,bhtsd->bhts', qc, kc, decay)
        o = att @ vc + jnp.einsum('bhtd,bhde->bhte', qc * jnp.exp(b), S)
        b_last = b[:, :, -1:, :]
        S = (jnp.exp(b_last[:, :, 0, :, None]) * S
             + jnp.einsum('bhsd,bhse->bhde', kc * jnp.exp(b_last - b), vc))
        return S, o

    S0 = jnp.zeros((B_, HG_HEADS, HG_DK, HG_DV), jnp.float32)
    _, o = lax.scan(step, S0, (to_chunks(qf, HG_DK), to_chunks(kf, HG_DK),
                               to_chunks(vf, HG_DV), to_chunks(lf, HG_DK)))
    o = o.transpose(1, 0, 3, 2, 4).reshape(B_, L, HG_HEADS, HG_DV)
    o = o * lax.rsqrt(jnp.mean(o * o, axis=-1, keepdims=True) + EPS) * norm_gain.astype(jnp.float32)
    o = o.reshape(B_, L, HG_W) * jax.nn.silu(g.astype(jnp.float32))
    return o.astype(q.dtype)


def compress_blocks(k, pos, w1, w2):
    B_, L = k.shape[0], k.shape[1]
    ch = k.reshape(B_, L // CMP_STRIDE, CMP_STRIDE, NSA_G, NSA_DH)
    n = L // CMP_STRIDE - CMP_RATIO + 1
    blocks = jnp.concatenate([ch[:, r:r + n] for r in range(CMP_RATIO)], axis=2)
    blocks = blocks + pos[None, None, :, None, :]
    flat = blocks.transpose(0, 1, 3, 2, 4).reshape(B_, n, NSA_G, CMP_LEN * NSA_DH)
    return jax.nn.gelu(flat @ w1) @ w2


def nsa_mixer(q, kv, gate_logits, pos_k, w1_k, w2_k, pos_v, w1_v, w2_v, rel_bias):
    B_, L, _ = q.shape
    qh = q.reshape(B_, L, NSA_G, NSA_R, NSA_DH)
    kvh = kv.reshape(B_, L, 6, NSA_G, NSA_DH)
    k_cr, v_cr, k_sl, v_sl, k_wn, v_wn = (kvh[:, :, j] for j in range(6))
    gates = jax.nn.sigmoid(gate_logits.astype(jnp.float32)).reshape(B_, L, NSA_G, NSA_R, 3)
    k_cmp = compress_blocks(k_cr, pos_k, w1_k, w2_k)
    v_cmp = compress_blocks(v_cr, pos_v, w1_v, w2_v)
    ncmp = k_cmp.shape[1]
    nsb = L // SLC_LEN
    n_sel = min(N_SEL, nsb)

    def sel_blocks(a):
        return a.reshape(B_, nsb, SLC_LEN, NSA_G, NSA_DH).transpose(0, 3, 1, 2, 4).reshape(
            B_, NSA_G, nsb, SLC_LEN * NSA_DH)

    k_blk, v_blk = sel_blocks(k_sl), sel_blocks(v_sl)
    k_wp = jnp.pad(k_wn, ((0, 0), (WIN, 0), (0, 0), (0, 0)))
    v_wp = jnp.pad(v_wn, ((0, 0), (WIN, 0), (0, 0), (0, 0)))
    tab = rel_bias.astype(jnp.float32)
    tab_g = tab.reshape(REL_BUCKETS, NSA_G, NSA_R).transpose(1, 0, 2)
    cmp_end = jnp.arange(ncmp) * CMP_STRIDE + CMP_LEN - 1
    blk = jnp.arange(nsb)
    tok = jnp.arange(SLC_LEN)
    scale = NSA_DH ** -0.5
    span = SLC_RATIO * (nsb - 1) + 1
    gidx = jnp.arange(NSA_G)[None, :, None, None]

    def dense_bias(dist):
        return jnp.moveaxis(tab[t5_bucket(dist)], -1, 0).reshape(NSA_G, NSA_R, *dist.shape)

    def one_block(qi):
        q0 = qi * Q_BLOCK
        qb = lax.dynamic_slice_in_dim(qh, q0, Q_BLOCK, axis=1)
        gb = lax.dynamic_slice_in_dim(gates, q0, Q_BLOCK, axis=1)
        t = q0 + jnp.arange(Q_BLOCK)
        d_c = t[:, None] - cmp_end[None, :]
        s = jnp.einsum('bqgrd,bcgd->bgrqc', qb, k_cmp).astype(jnp.float32) * scale + dense_bias(d_c)
        p_cmp = masked_softmax(s, d_c >= 0)
        o_cmp = jnp.einsum('bgrqc,bcgd->bqgrd', p_cmp.astype(v_cmp.dtype), v_cmp)
        imp = jnp.sum(p_cmp, axis=2)
        imp = jnp.pad(imp, ((0, 0), (0, 0), (0, 0), (CMP_RATIO - 1, SLC_RATIO * nsb - ncmp)))
        p_slc = sum(imp[..., m + n:m + n + span:SLC_RATIO]
                    for m in range(SLC_RATIO) for n in range(CMP_RATIO))
        blk_ok = (blk[None, :] * SLC_LEN) <= t[:, None]
        cur = (t // SLC_LEN)[:, None]
        forced = (blk[None, :] == 0) | (blk[None, :] == cur) | (blk[None, :] == cur - 1)
        score = jnp.where(blk_ok, p_slc + jnp.where(forced, FORCE_BONUS, 0.0), NEG_INF)
        _, idx = lax.top_k(score, n_sel)
        idx_f = idx.reshape(B_, NSA_G, Q_BLOCK * n_sel, 1)
        k_sel = jnp.take_along_axis(k_blk, idx_f, axis=2).reshape(B_, NSA_G, Q_BLOCK, n_sel * SLC_LEN, NSA_DH)
        v_sel = jnp.take_along_axis(v_blk, idx_f, axis=2).reshape(B_, NSA_G, Q_BLOCK, n_sel * SLC_LEN, NSA_DH)
        pos = (idx[..., None] * SLC_LEN + tok).reshape(B_, NSA_G, Q_BLOCK, n_sel * SLC_LEN)
        d_s = t[None, None, :, None] - pos
        bias_s = tab_g[gidx, t5_bucket(d_s)].transpose(0, 1, 4, 2, 3)
        s = jnp.einsum('bqgrd,bgqkd->bgrqk', qb, k_sel).astype(jnp.float32) * scale + bias_s
        p = masked_softmax(s, (d_s >= 0)[:, :, None])
        o_slc = jnp.einsum('bgrqk,bgqkd->bqgrd', p.astype(v_sel.dtype), v_sel)
        kw = lax.dynamic_slice_in_dim(k_wp, q0, WIN + Q_BLOCK, axis=1)
        vw = lax.dynamic_slice_in_dim(v_wp, q0, WIN + Q_BLOCK, axis=1)
        pos_w = q0 - WIN + jnp.arange(WIN + Q_BLOCK)
        d_w = t[:, None] - pos_w[None, :]
        ok_w = (d_w >= 0) & (d_w < WIN) & (pos_w[None, :] >= 0)
        s = jnp.einsum('bqgrd,bkgd->bgrqk', qb, kw).astype(jnp.float32) * scale + dense_bias(d_w)
        p = masked_softmax(s, ok_w)
        o_win = jnp.einsum('bgrqk,bkgd->bqgrd', p.astype(vw.dtype), vw)
        o = gb[..., 0:1] * o_cmp + gb[..., 1:2] * o_slc + gb[..., 2:3] * o_win
        return o.astype(q.dtype)

    out = lax.map(one_block, jnp.arange(L // Q_BLOCK))
    return out.transpose(1, 0, 2, 3, 4, 5).reshape(B_, L, NSA_W)


def setup_inputs(seed: int = 0) -> dict:
    key = jax.random.key(seed)
    ks = iter(jax.random.split(key, 40))

    def nrm(shape, scale):
        return jax.random.normal(next(ks), shape, jnp.float32) * scale

    def gain(shape):
        return 1.0 + nrm(shape, 0.02)

    n_idx = jnp.arange(S5_P, dtype=jnp.float32)
    return {
        'x': nrm((BATCH, SEQ, D_MODEL), 1.0),
        'ffn1_norm': gain((DEPTH, D_MODEL)),
        'ffn1_w_gate': nrm((DEPTH, D_MODEL, D_FF), D_MODEL ** -0.5),
        'ffn1_w_up': nrm((DEPTH, D_MODEL, D_FF), D_MODEL ** -0.5),
        'ffn1_w_down': nrm((DEPTH, D_FF, D_MODEL), D_FF ** -0.5),
        'mix_norm': gain((DEPTH, D_MODEL)),
        'w_in': nrm((DEPTH, D_MODEL, D_IN), D_MODEL ** -0.5),
        'w_out': nrm((DEPTH, D_MIX, D_MODEL), D_MIX ** -0.5),
        's5_lambda_re': -0.5 + nrm((DEPTH, S5_G, S5_P), 0.01),
        's5_lambda_im': math.pi * n_idx + nrm((DEPTH, S5_G, S5_P), 0.01),
        's5_log_dt': jax.random.uniform(next(ks), (DEPTH, S5_G), jnp.float32,
                                        minval=math.log(0.001), maxval=math.log(0.1)),
        's5_b_re': nrm((DEPTH, S5_G, S5_P, S5_GROUP), (2 * S5_GROUP) ** -0.5),
        's5_b_im': nrm((DEPTH, S5_G, S5_P, S5_GROUP), (2 * S5_GROUP) ** -0.5),
        's5_c_re': nrm((DEPTH, S5_G, S5_GROUP, S5_P), (2 * S5_P) ** -0.5),
        's5_c_im': nrm((DEPTH, S5_G, S5_GROUP, S5_P), (2 * S5_P) ** -0.5),
        's5_d': nrm((DEPTH, S5_G, S5_GROUP), 1.0),
        's5_w_glu': nrm((DEPTH, S5_W, S5_W), S5_W ** -0.5),
        'hgrn_lb_logits': nrm((DEPTH, HG_W), 1.0),
        'hgrn_norm': gain((DEPTH, HG_DV)),
        'nsa_cmp_pos_k': nrm((DEPTH, CMP_LEN, NSA_DH), 0.02),
        'nsa_cmp_w1_k': nrm((DEPTH, CMP_LEN * NSA_DH, NSA_DH), (CMP_LEN * NSA_DH) ** -0.5),
        'nsa_cmp_w2_k': nrm((DEPTH, NSA_DH, NSA_DH), NSA_DH ** -0.5),
        'nsa_cmp_pos_v': nrm((DEPTH, CMP_LEN, NSA_DH), 0.02),
        'nsa_cmp_w1_v': nrm((DEPTH, CMP_LEN * NSA_DH, NSA_DH), (CMP_LEN * NSA_DH) ** -0.5),
        'nsa_cmp_w2_v': nrm((DEPTH, NSA_DH, NSA_DH), NSA_DH ** -0.5),
        'rel_bias': nrm((REL_BUCKETS, NSA_H), 0.5),
        'ffn2_norm': gain((DEPTH, D_MODEL)),
        'ffn2_w_gate': nrm((DEPTH, D_MODEL, D_FF), D_MODEL ** -0.5),
        'ffn2_w_up': nrm((DEPTH, D_MODEL, D_FF), D_MODEL ** -0.5),
        'ffn2_w_down': nrm((DEPTH, D_FF, D_MODEL), D_FF ** -0.5),
        'final_norm': gain((D_MODEL,)),
    }


def reference(x, ffn1_norm, ffn1_w_gate, ffn1_w_up, ffn1_w_down, mix_norm, w_in, w_out,
              s5_lambda_re, s5_lambda_im, s5_log_dt, s5_b_re, s5_b_im, s5_c_re, s5_c_im, s5_d,
              s5_w_glu, hgrn_lb_logits, hgrn_norm, nsa_cmp_pos_k, nsa_cmp_w1_k, nsa_cmp_w2_k,
              nsa_cmp_pos_v, nsa_cmp_w1_v, nsa_cmp_w2_v, rel_bias, ffn2_norm, ffn2_w_gate,
              ffn2_w_up, ffn2_w_down, final_norm):
    gam = jax.nn.softmax(hgrn_lb_logits.astype(jnp.float32), axis=0)
    lower_bounds = jnp.cumsum(gam, axis=0) - gam[0:1]
    offs = np.cumsum([0, S5_W, HG_W, HG_W, HG_W, HG_W, NSA_W, 6 * KV_W, 3 * NSA_H])
    for l in range(DEPTH):
        x = x + 0.5 * swiglu(rmsnorm(x, ffn1_norm[l]), ffn1_w_gate[l], ffn1_w_up[l], ffn1_w_down[l])
        z = rmsnorm(x, mix_norm[l]) @ w_in[l]
        u_s5, q_hg, f_hg, i_hg, g_hg, q_nsa, kv_nsa, gate_nsa = (
            z[..., int(offs[j]):int(offs[j + 1])] for j in range(8))
        y_s5 = s5_mixer(u_s5, s5_lambda_re[l], s5_lambda_im[l], s5_log_dt[l], s5_b_re[l], s5_b_im[l],
                        s5_c_re[l], s5_c_im[l], s5_d[l], s5_w_glu[l])
        y_hg = hgrn2_mixer(q_hg, f_hg, i_hg, g_hg, lower_bounds[l], hgrn_norm[l])
        y_nsa = nsa_mixer(q_nsa, kv_nsa, gate_nsa, nsa_cmp_pos_k[l], nsa_cmp_w1_k[l], nsa_cmp_w2_k[l],
                          nsa_cmp_pos_v[l], nsa_cmp_w1_v[l], nsa_cmp_w2_v[l], rel_bias)
        y = jnp.concatenate([y_s5, y_hg, y_nsa], axis=-1).astype(x.dtype)
        x = x + y @ w_out[l]
        x = x + 0.5 * swiglu(rmsnorm(x, ffn2_norm[l]), ffn2_w_gate[l], ffn2_w_up[l], ffn2_w_down[l])
    return rmsnorm(x, final_norm)
```

```python
import contextlib
import numpy as np
import concourse.bass as bass
import concourse.mybir as mybir
from concourse.bass_utils import run_bass_kernel_spmd

F32 = mybir.dt.float32
BF16 = mybir.dt.bfloat16
AF = mybir.ActivationFunctionType
ALU = mybir.AluOpType
AX = mybir.AxisListType

D_MODEL = 1024
D_FF = 2816
SEQ = 8192
BATCH = 2
D_IN = 2584
NCORES = 8
EPS = 1e-6


class Prog:
    def __init__(self):
        self.nc = bass.Bass("TRN2", target_bir_lowering=False)
        self.es = contextlib.ExitStack()
        nc = self.nc
        self.eng = {'pe': nc.tensor, 'act': nc.scalar, 'dve': nc.vector, 'pool': nc.gpsimd, 'sp': nc.sync}
        self.sems = {}
        self.cnt = {}
        for k in ['pe', 'act', 'dve', 'pool']:
            self.sems[('e', k)] = self.es.enter_context(nc.semaphore('e_' + k))
            self.cnt[('e', k)] = 0
        self.waited = {k: {} for k in self.eng}
        self.state = {}
        self.nps = 0

    def sbuf(self, name, shape, dt):
        return self.es.enter_context(self.nc.sbuf_tensor("s_" + name, list(shape), dt))

    def psum(self, name, shape, dt=F32):
        return self.es.enter_context(self.nc.psum_tensor("p_" + name, list(shape), dt))

    def dram_in(self, name, shape, dt=F32):
        return self.nc.dram_tensor(name, list(shape), dt, kind="ExternalInput").ap()

    def dram_out(self, name, shape, dt=F32):
        return self.nc.dram_tensor(name, list(shape), dt, kind="ExternalOutput").ap()

    @staticmethod
    def _norm(x):
        return x if isinstance(x, tuple) else (x, None)

    def _deps(self, rs, ws, e):
        need = {}

        def add(sv):
            if sv is None:
                return
            s, v = sv
            if s[0] == 'd':
                v = self.cnt[s]
            if need.get(s, 0) < v:
                need[s] = v

        for (n, k) in rs:
            for kk, ent in self.state.get(n, {}).items():
                if k is None or kk is None or kk == k:
                    add(ent['w'])
        for (n, k) in ws:
            for kk, ent in self.state.get(n, {}).items():
                if k is None or kk is None or kk == k:
                    if ent['w'] is not None and ent['w'][0] != ('e', e):
                        add(ent['w'])
                    for s, v in ent['r'].items():
                        if s != ('e', e):
                            add((s, v))
        if e == 'pe':
            need.pop(('e', 'pe'), None)
        return need

    def _record(self, rs, ws, sv):
        s, v = sv
        for (n, k) in rs:
            ent = self.state.setdefault(n, {}).setdefault(k, {'w': None, 'r': {}})
            ent['r'][s] = max(ent['r'].get(s, 0), v)
        for (n, k) in ws:
            d = self.state.setdefault(n, {})
            if k is None:
                d.clear()
            d[k] = {'w': (s, v), 'r': {}}

    def _emit_waits(self, e, need):
        eng = self.eng[e]
        for s, v in need.items():
            if self.waited[e].get(s, 0) < v:
                eng.wait_ge(self.sems[s], v)
                self.waited[e][s] = v

    def op(self, e, fn, r=(), w=()):
        rs = [self._norm(x) for x in r]
        ws = [self._norm(x) for x in w]
        self._emit_waits(e, self._deps(rs, ws, e))
        inst = fn(self.eng[e])
        s = ('e', e)
        self.cnt[s] += 1
        inst.then_inc(self.sems[s], 1)
        self._record(rs, ws, (s, self.cnt[s]))

    def dma(self, out, in_, r=(), w=(), q='sp', sem=None):
        rs = [self._norm(x) for x in r]
        ws = [self._norm(x) for x in w]
        if sem is None:
            sem = ws[0][0]
        s = ('d', sem)
        if s not in self.sems:
            self.sems[s] = self.es.enter_context(self.nc.semaphore('d_' + sem))
            self.cnt[s] = 0
        self._emit_waits(q, self._deps(rs, ws, q))
        self.eng[q].dma_start(out=out, in_=in_).then_inc(self.sems[s], 16)
        self.cnt[s] += 16
        self._record(rs, ws, (s, self.cnt[s]))

    def finish(self):
        sp = self.eng['sp']
        for s, h in self.sems.items():
            if self.cnt[s] > 0 and self.waited['sp'].get(s, 0) < self.cnt[s]:
                sp.wait_ge(h, self.cnt[s])
        self.es.close()
        return self.nc


def mm(ps, lhsT, rhs, start, stop):
    return lambda e: e.matmul(ps, lhsT=lhsT, rhs=rhs, start=start, stop=stop)


NTOK = 2048
TP = 1024
NFC = D_FF // 128
NZC = 21


def build_T(do_wout, n_ffn, do_win, do_final):
    P = Prog()
    nc = P.nc
    xT_d = P.dram_in("xT", [8, 128, NTOK])
    if do_wout:
        yT_d = P.dram_in("yT", [8, 128, NTOK])
        wglu_d = P.dram_in("wglu", [2, 128, 2, 128])
        wout_d = P.dram_in("wout", [8, 128, 8, 128])
    ffn_d = []
    for i in range(n_ffn):
        ffn_d.append(dict(
            g=P.dram_in(f"f{i}_g", [128, 8]),
            wg=P.dram_in(f"f{i}_wg", [NFC, 128, 8, 128]),
            wu=P.dram_in(f"f{i}_wu", [NFC, 128, 8, 128]),
            wd=P.dram_in(f"f{i}_wd", [8, 128, NFC, 128]),
        ))
    if do_win:
        ming_d = P.dram_in("mix_g", [128, 8])
        win_d = P.dram_in("win", [NZC, 128, 8, 128])
        zT_d = P.dram_out("zT", [NZC, 128, NTOK])
    if do_final:
        fing_d = P.dram_in("fin_g", [128, 8])
    xo_d = P.dram_out("xoT", [8, 128, NTOK])

    xT = P.sbuf("xT_s", [128, 8, TP], F32)
    hT = P.sbuf("hT_s", [128, 8, TP], BF16)
    aT = P.sbuf("aT_s", [128, NFC, TP], BF16)
    sq = P.sbuf("sq_s", [128, 2, TP], BF16)
    rstd = P.sbuf("rstd_s", [128, TP], F32)
    ones = P.sbuf("ones_s", [128, 128], BF16)
    gt = P.sbuf("g_s", [128, 8], F32)
    stg = [P.sbuf(f"stg{i}", [128, NFC * 128], F32) for i in range(3)]
    wbf = [P.sbuf(f"wbf{i}", [128, NFC * 128], BF16) for i in range(3)]
    sg = [P.sbuf(f"sg{i}", [128, 512], F32) for i in range(2)]
    ev = [P.sbuf(f"ev{i}", [128, 512], F32) for i in range(2)]
    ps = [P.psum(f"ps{i}", [128, 512]) for i in range(8)]
    if do_wout:
        yf = P.sbuf("yf_s", [128, 8, TP], F32)

    P.op('pool', lambda e: e.memset(ones[:], 1.0), w=["ones"])

    wctr = [0]

    def load_w(src_ap, nk):
        i = wctr[0] % 3
        wctr[0] += 1
        P.dma(stg[i][:, 0:nk * 128], src_ap, w=[f"stg{i}"])
        P.op('pool', lambda e: e.tensor_copy(out=wbf[i][:, 0:nk * 128], in_=stg[i][:, 0:nk * 128]),
             r=[f"stg{i}"], w=[f"wbf{i}"])
        return wbf[i], f"wbf{i}"

    psctr = [0]

    def next_ps():
        i = psctr[0] % 8
        psctr[0] += 1
        return ps[i], f"ps{i}"

    def rmsnorm(g_dram, final=False):
        P.dma(gt[:], g_dram, w=["g"])
        pss = [next_ps(), next_ps()]
        for k in range(8):
            j = k % 2
            P.op('act', lambda e: e.activation(out=sq[:, j, :], in_=xT[:, k, :], func=AF.Square),
                 r=[("xT", k)], w=[("sq", j)])
            for tg in range(2):
                P.op('pe', mm(pss[tg][0][:, :], ones[:], sq[:, j, tg * 512:(tg + 1) * 512], k == 0, k == 7),
                     r=["ones", ("sq", j)], w=[pss[tg][1]])
        for tg in range(2):
            sl = slice(tg * 512, (tg + 1) * 512)
            P.op('act', lambda e: e.activation(out=rstd[:, sl], in_=pss[tg][0][:, :], func=AF.Sqrt,
                                               scale=1.0 / D_MODEL, bias=EPS),
                 r=[pss[tg][1]], w=[("rstd", tg)])
            P.op('dve', lambda e: e.reciprocal(out=rstd[:, sl], in_=rstd[:, sl]),
                 r=[("rstd", tg)], w=[("rstd", tg)])
        for k in range(8):
            if final:
                P.op('dve', lambda e: e.scalar_tensor_tensor(out=xT[:, k, :], in0=xT[:, k, :], scalar=gt[:, k:k + 1],
                                                             in1=rstd[:, :], op0=ALU.mult, op1=ALU.mult),
                     r=[("xT", k), "g", "rstd"], w=[("xT", k)])
            else:
                P.op('dve', lambda e: e.scalar_tensor_tensor(out=hT[:, k, :], in0=xT[:, k, :], scalar=gt[:, k:k + 1],
                                                             in1=rstd[:, :], op0=ALU.mult, op1=ALU.mult),
                     r=[("xT", k), "g", "rstd"], w=[("hT", k)])

    def ffn(fd):
        rmsnorm(fd['g'])
        for fc in range(NFC):
            wg, wgn = load_w(fd['wg'][fc].rearrange("p k j -> p (k j)"), 8)
            wu, wun = load_w(fd['wu'][fc].rearrange("p k j -> p (k j)"), 8)
            for tg in range(2):
                sl = slice(tg * 512, (tg + 1) * 512)
                pg, pgn = next_ps()
                pu, pun = next_ps()
                for k in range(8):
                    P.op('pe', mm(pg[:, :], wg[:, k * 128:(k + 1) * 128], hT[:, k, sl], k == 0, k == 7),
                         r=[wgn, "hT"], w=[pgn])
                for k in range(8):
                    P.op('pe', mm(pu[:, :], wu[:, k * 128:(k + 1) * 128], hT[:, k, sl], k == 0, k == 7),
                         r=[wun, "hT"], w=[pun])
                i = (fc * 2 + tg) % 2
                P.op('act', lambda e: e.activation(out=sg[i][:, :], in_=pg[:, :], func=AF.Silu),
                     r=[pgn], w=[f"sg{i}"])
                P.op('dve', lambda e: e.tensor_tensor(out=aT[:, fc, sl], in0=sg[i][:, :], in1=pu[:, :], op=ALU.mult),
                     r=[f"sg{i}", pun], w=[("aT", fc)])
        for mc in range(8):
            wd, wdn = load_w(fd['wd'][mc].rearrange("p k j -> p (k j)"), NFC)
            for tg in range(2):
                sl = slice(tg * 512, (tg + 1) * 512)
                pd, pdn = next_ps()
                for k in range(NFC):
                    P.op('pe', mm(pd[:, :], wd[:, k * 128:(k + 1) * 128], aT[:, k, sl], k == 0, k == NFC - 1),
                         r=[wdn, "aT"], w=[pdn])
                P.op('dve', lambda e: e.scalar_tensor_tensor(out=xT[:, mc, sl], in0=pd[:, :], scalar=0.5,
                                                             in1=xT[:, mc, sl], op0=ALU.mult, op1=ALU.add),
                     r=[pdn, ("xT", mc)], w=[("xT", mc)])

    for ps_i in range(NTOK // TP):
        tsl = slice(ps_i * TP, (ps_i + 1) * TP)
        for k in range(8):
            P.dma(xT[:, k, :], xT_d[k, :, tsl], w=[("xT", k)], sem="xT")
        if do_wout:
            for k in range(8):
                P.dma(yf[:, k, :], yT_d[k, :, tsl], w=[("yf", k)], sem="yf")
            for k in range(2):
                P.op('pool', lambda e: e.tensor_copy(out=hT[:, k, :], in_=yf[:, k, :]), r=[("yf", k)], w=[("hT", k)])
            for mc in range(2):
                wl, wln = load_w(wglu_d[mc].rearrange("p k j -> p (k j)"), 2)
                for tg in range(2):
                    sl = slice(tg * 512, (tg + 1) * 512)
                    pg, pgn = next_ps()
                    for k in range(2):
                        P.op('pe', mm(pg[:, :], wl[:, k * 128:(k + 1) * 128], hT[:, k, sl], k == 0, k == 1),
                             r=[wln, ("hT", 0), ("hT", 1)], w=[pgn])
                    i = tg
                    P.op('act', lambda e: e.activation(out=sg[i][:, :], in_=pg[:, :], func=AF.Sigmoid),
                         r=[pgn], w=[f"sg{i}"])
                    P.op('dve', lambda e: e.tensor_tensor(out=aT[:, mc, sl], in0=sg[i][:, :], in1=yf[:, mc, sl],
                                                          op=ALU.mult),
                         r=[f"sg{i}", ("yf", mc)], w=[("aT", mc)])
            for k in range(2, 8):
                P.op('pool', lambda e: e.tensor_copy(out=aT[:, k, :], in_=yf[:, k, :]), r=[("yf", k)], w=[("aT", k)])
            for mc in range(8):
                wl, wln = load_w(wout_d[mc].rearrange("p k j -> p (k j)"), 8)
                for tg in range(2):
                    sl = slice(tg * 512, (tg + 1) * 512)
                    pd, pdn = next_ps()
                    for k in range(8):
                        P.op('pe', mm(pd[:, :], wl[:, k * 128:(k + 1) * 128], aT[:, k, sl], k == 0, k == 7),
                             r=[wln, "aT"], w=[pdn])
                    P.op('dve', lambda e: e.tensor_tensor(out=xT[:, mc, sl], in0=pd[:, :], in1=xT[:, mc, sl],
                                                          op=ALU.add),
                         r=[pdn, ("xT", mc)], w=[("xT", mc)])
        for i in range(n_ffn):
            ffn(ffn_d[i])
        if do_win:
            rmsnorm(ming_d)
            for cc in range(NZC):
                wl, wln = load_w(win_d[cc].rearrange("p k j -> p (k j)"), 8)
                for tg in range(2):
                    sl = slice(tg * 512, (tg + 1) * 512)
                    pz, pzn = next_ps()
                    for k in range(8):
                        P.op('pe', mm(pz[:, :], wl[:, k * 128:(k + 1) * 128], hT[:, k, sl], k == 0, k == 7),
                             r=[wln, "hT"], w=[pzn])
                    i = (cc * 2 + tg) % 2
                    P.op('act', lambda e: e.activation(out=ev[i][:, :], in_=pz[:, :], func=AF.Copy),
                         r=[pzn], w=[f"ev{i}"])
                    P.dma(zT_d[cc, :, ps_i * TP + tg * 512: ps_i * TP + (tg + 1) * 512], ev[i][:, :],
                          r=[f"ev{i}"], w=[("zT", (ps_i, cc, tg))], sem=f"zst{i}")
        if do_final:
            rmsnorm(fing_d, final=True)
        for k in range(8):
            P.dma(xo_d[k, :, tsl], xT[:, k, :], r=[("xT", k)], w=[("xo", (ps_i, k))], sem="xo")
    return P.finish()


def _chunkT(w, nk):
    K, M = w.shape
    return np.ascontiguousarray(w.reshape(nk, 128, M // 128, 128).transpose(2, 1, 0, 3))


def _gcol(g):
    return np.ascontiguousarray(g.reshape(8, 128).T)


def _ffn_maps(prefix, g, wg, wu, wd):
    return {f"{prefix}_g": _gcol(g), f"{prefix}_wg": _chunkT(wg, 8), f"{prefix}_wu": _chunkT(wu, 8),
            f"{prefix}_wd": _chunkT(wd, NFC)}


def run_T(xT_full, common, do_wout, n_ffn, do_win, do_final, yT_full=None):
    nc = build_T(do_wout, n_ffn, do_win, do_final)
    maps = []
    for c in range(NCORES):
        m = dict(common)
        m["xT"] = np.ascontiguousarray(xT_full[:, c * NTOK:(c + 1) * NTOK].reshape(8, 128, NTOK))
        if do_wout:
            m["yT"] = np.ascontiguousarray(yT_full[:, c * NTOK:(c + 1) * NTOK].reshape(8, 128, NTOK))
        maps.append(m)
    res = run_bass_kernel_spmd(nc, maps, core_ids=list(range(NCORES)))
    out = {}
    out["xoT"] = np.concatenate([r["xoT"].reshape(1024, NTOK) for r in res.results], axis=1)
    if do_win:
        out["zT"] = np.concatenate([r["zT"].reshape(NZC * 128, NTOK) for r in res.results], axis=1)
    return out


PI = float(np.pi)
NCH = 1024
HB = 2048
GELU_C = 1.5957691216057308


def emit_gelu(P, dst, src_ps, tmp_a, tmp_b, names, src_names, bias=None):
    dn, an, bn = names
    if bias is None:
        P.op('act', lambda e: e.activation(out=tmp_a, in_=src_ps, func=AF.Copy), r=src_names, w=[an])
    else:
        P.op('act', lambda e: e.activation(out=tmp_a, in_=src_ps, func=AF.Identity, bias=bias[0]), r=src_names + [bias[1]], w=[an])
    P.op('pool', lambda e: e.tensor_tensor(out=tmp_b, in0=tmp_a, in1=tmp_a, op=ALU.mult), r=[an], w=[bn])
    P.op('dve', lambda e: e.tensor_scalar(out=tmp_b, in0=tmp_b, scalar1=0.044715, scalar2=1.0, op0=ALU.mult, op1=ALU.add),
         r=[bn], w=[bn])
    P.op('dve', lambda e: e.tensor_tensor(out=tmp_b, in0=tmp_b, in1=tmp_a, op=ALU.mult), r=[bn, an], w=[bn])
    P.op('act', lambda e: e.activation(out=tmp_b, in_=tmp_b, func=AF.Sigmoid, scale=GELU_C), r=[bn], w=[bn])
    P.op('dve', lambda e: e.tensor_tensor(out=dst, in0=tmp_a, in1=tmp_b, op=ALU.mult), r=[an, bn], w=[dn])


def build_M1(layer):
    P = Prog()
    lr2_d = P.dram_in("lr2", [128, 4]); li2_d = P.dram_in("li2", [128, 4]); ldt_d = P.dram_in("ldt", [128, 4])
    b1_d = P.dram_in("bst1", [128, 4, 16]); b2_d = P.dram_in("bst2", [128, 4, 16])
    c1_d = P.dram_in("cst1", [128, 4, 16]); c2_d = P.dram_in("cst2", [128, 4, 16])
    dcol_d = P.dram_in("dcol", [128, 4]); sgn_d = P.dram_in("sgn", [128, 2]); jv_d = P.dram_in("jv", [128, 4, 24])
    i128_d = P.dram_in("i128", [128, 128]); jsw_d = P.dram_in("jsw", [128, 128]); tmask_d = P.dram_in("tmask", [128, 128])
    uc_d = P.dram_in("uc", [4, 128, NCH])
    y5_d = P.dram_out("y5", [4, 128, NCH])
    hq_d = P.dram_in("hq", [64, SEQ]); hf_d = P.dram_in("hf", [64, SEQ])
    hv64_d = P.dram_in("hv64", [4, 64, 32, 64]); hv128_d = P.dram_in("hv128", [4, 128, 16, 64]); hg64_d = P.dram_in("hg64", [4, 64, 32, 64])
    lbl_d = P.dram_in("lbl", [64, 2]); gain_d = P.dram_in("hgain", [64, 64]); cm_d = P.dram_in("cmask", [64, 512])
    yh_d = P.dram_out("yh", [64, 128, 64])

    ps = [P.psum(f"ps{i}", [128, 512]) for i in range(8)]
    i128 = P.sbuf("i128", [128, 128], F32)
    P.dma(i128[:], i128_d[:, :], w=["i128"])

    import os
    PARTS = os.environ.get('M1PARTS', 's5,hg')
    NG5 = 4 if 's5' in PARTS else 0
    def T(name, shape, dt=F32):
        return P.sbuf(name, shape, dt)
    lr2 = T("lr2", [128, 4]); li2 = T("li2", [128, 4]); dt = T("dt", [128, 4]); sgn = T("sgn", [128, 2])
    b1 = T("b1", [128, 4, 16]); b2 = T("b2", [128, 4, 16]); c1 = T("c1", [128, 4, 16]); c2 = T("c2", [128, 4, 16])
    dcol = T("dcol", [128, 4]); jv = T("jv", [128, 4, 24]); jsw = T("jsw", [128, 128]); tmask = T("tmask", [128, 128])
    for t_, d_, n_ in [(lr2, lr2_d, "lr2"), (li2, li2_d, "li2"), (dt, ldt_d, "dt"), (sgn, sgn_d, "sgn"), (dcol, dcol_d, "dcol"),
                       (jsw, jsw_d, "jsw"), (tmask, tmask_d, "tmask")]:
        P.dma(t_[:], d_[:, :], w=[n_])
    for t_, d_, n_ in [(b1, b1_d, "b1"), (b2, b2_d, "b2"), (c1, c1_d, "c1"), (c2, c2_d, "c2"), (jv, jv_d, "jv")]:
        P.dma(t_[:], d_[:, :, :], w=[n_])
    V = lambda e: e

    def dv(fn, r, w):
        P.op('dve', fn, r=r, w=w)

    def ac(fn, r, w):
        P.op('act', fn, r=r, w=w)
    ac(lambda e: e.activation(out=dt[:], in_=dt[:], func=AF.Exp), ["dt"], ["dt"])
    lrdt = T("lrdt", [128, 4]); lidt = T("lidt", [128, 4])
    dv(lambda e: e.tensor_tensor(out=lrdt[:], in0=lr2[:], in1=dt[:], op=ALU.mult), ["lr2", "dt"], ["lrdt"])
    dv(lambda e: e.tensor_tensor(out=lidt[:], in0=li2[:], in1=dt[:], op=ALU.mult), ["li2", "dt"], ["lidt"])
    am = T("am", [128, 4, 24]); aa = T("aa", [128, 4, 24]); pr = T("pr", [128, 4, 24]); pi_ = T("pi", [128, 4, 24])
    tA = T("tA", [128, 4, 24]); tB = T("tB", [128, 4, 24]); tI = T("tI", [128, 4, 24], mybir.dt.int32)
    bc24 = lambda t_: t_[:, :, None].to_broadcast([128, 4, 24])
    dv(lambda e: e.tensor_tensor(out=am[:], in0=jv[:], in1=bc24(lrdt), op=ALU.mult), ["jv", "lrdt"], ["am"])
    ac(lambda e: e.activation(out=am[:], in_=am[:], func=AF.Exp), ["am"], ["am"])
    dv(lambda e: e.tensor_tensor(out=aa[:], in0=jv[:], in1=bc24(lidt), op=ALU.mult), ["jv", "lidt"], ["aa"])

    def sin_of(dst, dn, src, sn, shift):
        dv(lambda e: e.tensor_scalar(out=tA[:], in0=src[:], scalar1=shift, scalar2=None, op0=ALU.add), [sn], ["tA"])
        dv(lambda e: e.tensor_scalar(out=tB[:], in0=tA[:], scalar1=1.0 / (2 * PI), scalar2=64.5, op0=ALU.mult, op1=ALU.add),
           ["tA"], ["tB"])
        dv(lambda e: e.tensor_copy(out=tI[:], in_=tB[:]), ["tB"], ["tI"])
        dv(lambda e: e.tensor_copy(out=tB[:], in_=tI[:]), ["tI"], ["tB"])
        dv(lambda e: e.tensor_scalar(out=tB[:], in0=tB[:], scalar1=-64.0, scalar2=-2 * PI, op0=ALU.add, op1=ALU.mult),
           ["tB"], ["tB"])
        dv(lambda e: e.tensor_tensor(out=tA[:], in0=tA[:], in1=tB[:], op=ALU.add), ["tA", "tB"], ["tA"])
        dv(lambda e: e.tensor_scalar(out=tB[:], in0=tA[:], scalar1=-PI, scalar2=2 * PI, op0=ALU.is_lt, op1=ALU.mult),
           ["tA"], ["tB"])
        dv(lambda e: e.tensor_tensor(out=tA[:], in0=tA[:], in1=tB[:], op=ALU.add), ["tA", "tB"], ["tA"])
        dv(lambda e: e.tensor_scalar(out=tB[:], in0=tA[:], scalar1=PI, scalar2=-2 * PI, op0=ALU.is_gt, op1=ALU.mult),
           ["tA"], ["tB"])
        dv(lambda e: e.tensor_tensor(out=tA[:], in0=tA[:], in1=tB[:], op=ALU.add), ["tA", "tB"], ["tA"])
        dv(lambda e: e.tensor_scalar(out=tA[:], in0=tA[:], scalar1=-PI, scalar2=PI, op0=ALU.max, op1=ALU.min),
           ["tA"], ["tA"])
        ac(lambda e: e.activation(out=dst[:], in_=tA[:], func=AF.Sin), ["tA"], [dn])
    sin_of(pi_, "pi", aa, "aa", 0.0)
    sin_of(pr, "pr", aa, "aa", PI / 2)
    dv(lambda e: e.tensor_tensor(out=pr[:], in0=pr[:], in1=am[:], op=ALU.mult), ["pr", "am"], ["pr"])
    dv(lambda e: e.tensor_tensor(out=pi_[:], in0=pi_[:], in1=am[:], op=ALU.mult), ["pi", "am"], ["pi"])
    den = T("den", [128, 4]); t4a = T("t4a", [128, 4]); t4b = T("t4b", [128, 4]); nr = T("nr", [128, 4])
    gre = T("gre", [128, 4]); gim = T("gim", [128, 4])
    ar1 = pr[:, :, 8]; ai1 = pi_[:, :, 8]
    dv(lambda e: e.tensor_tensor(out=den[:], in0=lr2[:], in1=lr2[:], op=ALU.mult), ["lr2"], ["den"])
    dv(lambda e: e.tensor_tensor(out=t4a[:], in0=li2[:], in1=li2[:], op=ALU.mult), ["li2"], ["t4a"])
    dv(lambda e: e.tensor_tensor(out=den[:], in0=den[:], in1=t4a[:], op=ALU.add), ["den", "t4a"], ["den"])
    dv(lambda e: e.reciprocal(out=den[:], in_=den[:]), ["den"], ["den"])
    dv(lambda e: e.tensor_scalar(out=nr[:], in0=ar1, scalar1=-1.0, scalar2=None, op0=ALU.add), ["pr"], ["nr"])
    dv(lambda e: e.tensor_tensor(out=t4a[:], in0=nr[:], in1=lr2[:], op=ALU.mult), ["nr", "lr2"], ["t4a"])
    dv(lambda e: e.tensor_tensor(out=t4b[:], in0=ai1, in1=li2[:], op=ALU.mult), ["pi", "li2"], ["t4b"])
    dv(lambda e: e.tensor_tensor(out=t4a[:], in0=t4a[:], in1=t4b[:], op=ALU.add), ["t4a", "t4b"], ["t4a"])
    dv(lambda e: e.tensor_tensor(out=gre[:], in0=t4a[:], in1=den[:], op=ALU.mult), ["t4a", "den"], ["gre"])
    dv(lambda e: e.tensor_tensor(out=t4a[:], in0=ai1, in1=lr2[:], op=ALU.mult), ["pi", "lr2"], ["t4a"])
    dv(lambda e: e.tensor_tensor(out=t4b[:], in0=nr[:], in1=li2[:], op=ALU.mult), ["nr", "li2"], ["t4b"])
    dv(lambda e: e.tensor_tensor(out=t4a[:], in0=t4a[:], in1=t4b[:], op=ALU.subtract), ["t4a", "t4b"], ["t4a"])
    dv(lambda e: e.tensor_tensor(out=gim[:], in0=t4a[:], in1=den[:], op=ALU.mult), ["t4a", "den"], ["gim"])
    er = T("er", [128, 4, 8]); ei = T("ei", [128, 4, 8]); t8 = T("t8", [128, 4, 8])
    bc8 = lambda t_: t_[:, :, None].to_broadcast([128, 4, 8])
    dv(lambda e: e.tensor_tensor(out=er[:], in0=pr[:, :, 0:8], in1=bc8(gre), op=ALU.mult), ["pr", "gre"], ["er"])
    dv(lambda e: e.tensor_tensor(out=t8[:], in0=pi_[:, :, 0:8], in1=bc8(gim), op=ALU.mult), ["pi", "gim"], ["t8"])
    dv(lambda e: e.tensor_tensor(out=er[:], in0=er[:], in1=t8[:], op=ALU.subtract), ["er", "t8"], ["er"])
    dv(lambda e: e.tensor_tensor(out=ei[:], in0=pr[:, :, 0:8], in1=bc8(gim), op=ALU.mult), ["pr", "gim"], ["ei"])
    dv(lambda e: e.tensor_tensor(out=t8[:], in0=pi_[:, :, 0:8], in1=bc8(gre), op=ALU.mult), ["pi", "gre"], ["t8"])
    dv(lambda e: e.tensor_tensor(out=ei[:], in0=ei[:], in1=t8[:], op=ALU.add), ["ei", "t8"], ["ei"])
    m2 = T("m2", [128, 4, 8]); n1f = T("n1f", [128, 4, 8]); n2f = T("n2f", [128, 4, 8]); n1h = T("n1h", [128, 4, 8]); n2h = T("n2h", [128, 4, 8])
    dv(lambda e: e.tensor_scalar(out=m2[:], in0=ei[:], scalar1=sgn[:, 1:2], scalar2=None, op0=ALU.mult), ["ei", "sgn"], ["m2"])
    dv(lambda e: e.tensor_scalar(out=n1f[:], in0=pr[:, :, 8:16], scalar1=sgn[:, 0:1], scalar2=None, op0=ALU.mult), ["pr", "sgn"], ["n1f"])
    dv(lambda e: e.tensor_scalar(out=n2f[:], in0=pi_[:, :, 8:16], scalar1=-1.0, scalar2=None, op0=ALU.mult), ["pi"], ["n2f"])
    dv(lambda e: e.tensor_scalar(out=n1h[:], in0=pr[:, :, 16:24], scalar1=sgn[:, 0:1], scalar2=None, op0=ALU.mult), ["pr", "sgn"], ["n1h"])
    dv(lambda e: e.tensor_scalar(out=n2h[:], in0=pi_[:, :, 16:24], scalar1=-1.0, scalar2=None, op0=ALU.mult), ["pi"], ["n2h"])
    bcm = T("bcm", [128, 4, 8, 16]); ccm = T("ccm", [128, 4, 8, 16]); qmm = T("qmm", [128, 4, 8, 16]); t816 = T("t816", [128, 4, 8, 16])
    S4 = [128, 4, 8, 16]

    def outer(dst, dn, st1, s1n, co1, c1n, st2, s2n, co2, c2n):
        dv(lambda e: e.tensor_tensor(out=dst[:], in0=st1[:, :, None, :].to_broadcast(S4), in1=co1[:, :, :, None].to_broadcast(S4),
                                     op=ALU.mult), [s1n, c1n], [dn])
        dv(lambda e: e.tensor_tensor(out=t816[:], in0=st2[:, :, None, :].to_broadcast(S4), in1=co2[:, :, :, None].to_broadcast(S4),
                                     op=ALU.mult), [s2n, c2n], ["t816"])
        dv(lambda e: e.tensor_tensor(out=dst[:], in0=dst[:], in1=t816[:], op=ALU.add), [dn, "t816"], [dn])
    outer(bcm, "bcm", b1, "b1", er, "er", b2, "b2", m2, "m2")
    outer(ccm, "ccm", c1, "c1", n1f, "n1f", c2, "c2", n2f, "n2f")
    outer(qmm, "qmm", c1, "c1", n1h, "n1h", c2, "c2", n2h, "n2h")
    tz = T("tz", [128, 4, 128]); bct = T("bct", [128, 4, 128])
    for g in range(4):
        pg, pgn = ps[g % 2], f"ps{g % 2}"
        P.op('pe', lambda e: e.matmul(pg[:, 0:128], lhsT=bcm[:, g].rearrange("p s h -> p (s h)"),
                                      rhs=qmm[:, g].rearrange("p s h -> p (s h)"), start=True, stop=True),
             r=["bcm", "qmm"], w=[pgn])
        dv(lambda e: e.tensor_tensor(out=tz[:, g, :], in0=pg[:, 0:128], in1=tmask[:], op=ALU.mult), [pgn, "tmask"], [("tz", g)])
        dv(lambda e: e.scalar_tensor_tensor(out=tz[:, g, :], in0=i128[:], scalar=dcol[:, g:g + 1], in1=tz[:, g, :],
                                            op0=ALU.mult, op1=ALU.add), ["i128", "dcol", ("tz", g)], [("tz", g)])
        pt, ptn = ps[2 + g % 2], f"ps{2 + g % 2}"
        P.op('pe', lambda e: e.transpose(out=pt[:, 0:128], in_=bcm[:, g].rearrange("p s h -> p (s h)"), identity=i128[:]),
             r=["bcm", "i128"], w=[ptn])
        ac(lambda e: e.activation(out=bct[:, g, :], in_=pt[:, 0:128], func=AF.Copy), [ptn], [("bct", g)])
    NK = 10
    a8r = T("a8r", [128, NK, 4]); a8i = T("a8i", [128, NK, 4]); rm = T("rm", [128, NK * 4, 128])
    dv(lambda e: e.tensor_copy(out=a8r[:, 0, :], in_=pr[:, :, 15]), ["pr"], ["a8r"])
    dv(lambda e: e.tensor_copy(out=a8i[:, 0, :], in_=pi_[:, :, 15]), ["pi"], ["a8i"])
    for k in range(1, NK):
        dv(lambda e: e.tensor_tensor(out=t4a[:], in0=a8r[:, k - 1, :], in1=a8r[:, k - 1, :], op=ALU.mult), ["a8r"], ["t4a"])
        dv(lambda e: e.tensor_tensor(out=t4b[:], in0=a8i[:, k - 1, :], in1=a8i[:, k - 1, :], op=ALU.mult), ["a8i"], ["t4b"])
        dv(lambda e: e.tensor_tensor(out=a8r[:, k, :], in0=t4a[:], in1=t4b[:], op=ALU.subtract), ["t4a", "t4b"], ["a8r"])
        dv(lambda e: e.tensor_tensor(out=t4a[:], in0=a8r[:, k - 1, :], in1=a8i[:, k - 1, :], op=ALU.mult), ["a8r", "a8i"], ["t4a"])
        dv(lambda e: e.tensor_scalar(out=a8i[:, k, :], in0=t4a[:], scalar1=2.0, scalar2=None, op0=ALU.mult), ["t4a"], ["a8i"])
    a8is = T("a8is", [128, NK, 4])
    dv(lambda e: e.tensor_scalar(out=a8is[:], in0=a8i[:], scalar1=sgn[:, 0:1], scalar2=None, op0=ALU.mult), ["a8i", "sgn"], ["a8is"])
    for k in range(NK):
        for g in range(4):
            i = k * 4 + g
            dv(lambda e: e.tensor_scalar(out=rm[:, i, :], in0=i128[:], scalar1=a8r[:, k, g:g + 1], scalar2=None, op0=ALU.mult),
               ["i128", "a8r"], [("rm", i)])
            dv(lambda e: e.scalar_tensor_tensor(out=rm[:, i, :], in0=jsw[:], scalar=a8is[:, k, g:g + 1], in1=rm[:, i, :],
                                                op0=ALU.mult, op1=ALU.add), ["jsw", "a8is", ("rm", i)], [("rm", i)])
    uc = [T(f"uc{i}", [128, NCH]) for i in range(2)]
    xs = T("xs", [128, NCH + 1]); ga = T("ga", [128, 512]); gb = T("gb", [128, 512]); yo = [T(f"yo{i}", [128, 512]) for i in range(2)]
    P.op('pool', lambda e: e.memset(xs[:, 0:1], 0.0), w=[("xs", "z")])
    for g in range(NG5):
        u = uc[g % 2]; un = f"uc{g % 2}"
        P.dma(u[:], uc_d[g], w=[un])
        for h in range(2):
            pp, ppn = ps[h], f"ps{h}"
            P.op('pe', lambda e: e.matmul(pp[:, :], lhsT=bct[:, g, :], rhs=u[:, h * 512:(h + 1) * 512], start=True, stop=True),
                 r=[("bct", g), un], w=[ppn])
            ac(lambda e: e.activation(out=xs[:, 1 + h * 512:1 + (h + 1) * 512], in_=pp[:, :], func=AF.Copy), [ppn], [("xs", "x")])
        for k in range(NK):
            d = 1 << k
            n = NCH - d
            pieces = [(0, min(512, n))] + ([(512, n)] if n > 512 else [])
            for h, (a, b) in enumerate(pieces):
                pp, ppn = ps[2 + h], f"ps{2 + h}"
                P.op('pe', lambda e: e.matmul(pp[:, 0:b - a], lhsT=rm[:, k * 4 + g, :], rhs=xs[:, 1 + a:1 + b], start=True, stop=True),
                     r=[("rm", k * 4 + g), ("xs", "x")], w=[ppn])
            for h, (a, b) in enumerate(pieces):
                pp, ppn = ps[2 + h], f"ps{2 + h}"
                dv(lambda e: e.tensor_tensor(out=xs[:, 1 + d + a:1 + d + b], in0=pp[:, 0:b - a], in1=xs[:, 1 + d + a:1 + d + b], op=ALU.add),
                   [ppn, ("xs", "x")], [("xs", "x")])
        for h in range(2):
            pp, ppn = ps[4 + h], f"ps{4 + h}"
            P.op('pe', lambda e: e.matmul(pp[:, :], lhsT=tz[:, g, :], rhs=u[:, h * 512:(h + 1) * 512], start=True, stop=False),
                 r=[("tz", g), un], w=[ppn])
            P.op('pe', lambda e: e.matmul(pp[:, :], lhsT=ccm[:, g].rearrange("p s h -> p (s h)"), rhs=xs[:, h * 512:(h + 1) * 512],
                                          start=False, stop=True), r=["ccm", ("xs", "x"), ("xs", "z")], w=[ppn])
            emit_gelu(P, yo[h][:], pp[:, :], ga[:], gb[:], (f"yo{h}", "ga", "gb"), [ppn])
            P.dma(y5_d[g, :, h * 512:(h + 1) * 512], yo[h][:], r=[f"yo{h}"], w=[("y5", (g, h))], sem=f"y5s{h}")

    if 'hg' not in PARTS:
        return P.finish()
    lbl = T("lbl", [64, 2]); lb = T("lb", [64, 1]); oml = T("oml", [64, 1]); gain = T("gain", [64, 64]); cm = T("cm", [64, 512])
    P.dma(lbl[:], lbl_d[:, :], w=["lbl"]); P.dma(gain[:], gain_d[:, :], w=["gain"]); P.dma(cm[:], cm_d[:, :], w=["cm"])
    if layer == 0:
        P.op('pool', lambda e: e.memset(lb[:], 0.0), w=["lb"])
    else:
        dv(lambda e: e.tensor_tensor(out=lb[:], in0=lbl[:, 1:2], in1=lbl[:, 0:1], op=ALU.subtract), ["lbl"], ["lb"])
        ac(lambda e: e.activation(out=lb[:], in_=lb[:], func=AF.Sigmoid), ["lb"], ["lb"])
    dv(lambda e: e.tensor_scalar(out=oml[:], in0=lb[:], scalar1=-1.0, scalar2=1.0, op0=ALU.mult, op1=ALU.add), ["lb"], ["oml"])
    rmask = T("rmask", [64, HB])
    P.op('pool', lambda e: e.memset(rmask[:], 1.0), w=["rmask"])
    P.op('pool', lambda e: e.memset(rmask[:, 0:HB:64], 0.0), w=["rmask"])
    hq = T("hq", [64, HB]); hf = T("hf", [64, HB]); sig = T("sig", [64, HB]); fbuf = T("fbuf", [64, HB]); bb = T("bb", [64, HB])
    eb = T("eb", [64, HB]); enb = T("enb", [64, HB]); qtil = T("qtil", [64, HB], BF16); ktil = T("ktil", [64, HB])
    ktb = T("ktb", [64, HB], BF16); khat = T("khat", [64, HB]); dec = T("dec", [64, 32])
    khT = T("khT", [128, 16, 64], BF16); v64f = T("v64f", [64, 32, 64]); v64 = T("v64", [64, 32, 64], BF16)
    v128f = T("v128f", [128, 16, 64]); v128 = T("v128", [128, 16, 64], BF16)
    g64 = T("g64", [64, 32, 64]); sall = T("sall", [64, 33, 64]); sbf = T("sbf", [64, 32, 64], BF16)
    attm = T("attm", [64, 512], BF16); osb = T("osb", [64, 8, 64]); osq = T("osq", [64, 8, 64]); ss = T("ss", [64, 8])
    yh = [T(f"yh{i}", [64, 8, 64]) for i in range(2)]
    P.op('pool', lambda e: e.memset(sall[:, 0, :], 0.0), w=[("sall", 0)])
    for blk in range(SEQ // HB):
        tsl = slice(blk * HB, (blk + 1) * HB)
        P.dma(hq[:], hq_d[:, tsl], w=["hq"]); P.dma(hf[:], hf_d[:, tsl], w=["hf"])
        P.dma(v64f[:], hv64_d[blk], w=["v64f"])
        P.dma(v128f[:], hv128_d[blk], w=["v128f"])
        P.dma(g64[:], hg64_d[blk], w=["g64"])
        P.op('pool', lambda e: e.tensor_copy(out=v64[:], in_=v64f[:]), r=["v64f"], w=["v64"])
        P.op('pool', lambda e: e.tensor_copy(out=v128[:], in_=v128f[:]), r=["v128f"], w=["v128"])
        ac(lambda e: e.activation(out=sig[:], in_=hf[:], func=AF.Sigmoid), ["hf"], ["sig"])
        dv(lambda e: e.tensor_scalar(out=fbuf[:], in0=sig[:], scalar1=oml[:, 0:1], scalar2=lb[:, 0:1], op0=ALU.mult, op1=ALU.add),
           ["sig", "oml", "lb"], ["fbuf"])
        ac(lambda e: e.activation(out=fbuf[:], in_=fbuf[:], func=AF.Ln), ["fbuf"], ["fbuf"])
        dv(lambda e: e.tensor_tensor_scan(out=bb[:], data0=rmask[:], data1=fbuf[:], initial=0.0, op0=ALU.mult, op1=ALU.add),
           ["rmask", "fbuf"], ["bb"])
        ac(lambda e: e.activation(out=eb[:], in_=bb[:], func=AF.Exp), ["bb"], ["eb"])
        ac(lambda e: e.activation(out=enb[:], in_=bb[:], func=AF.Exp, scale=-1.0), ["bb"], ["enb"])
        ac(lambda e: e.activation(out=sig[:], in_=hf[:], func=AF.Sigmoid, scale=-1.0), ["hf"], ["sig"])
        dv(lambda e: e.scalar_tensor_tensor(out=ktil[:], in0=sig[:], scalar=oml[:, 0:1], in1=enb[:], op0=ALU.mult, op1=ALU.mult),
           ["sig", "oml", "enb"], ["ktil"])
        P.op('pool', lambda e: e.tensor_copy(out=ktb[:], in_=ktil[:]), r=["ktil"], w=["ktb"])
        ac(lambda e: e.activation(out=hq[:], in_=hq[:], func=AF.Silu), ["hq"], ["hq"])
        dv(lambda e: e.tensor_tensor(out=qtil[:], in0=hq[:], in1=eb[:], op=ALU.mult), ["hq", "eb"], ["qtil"])
        dv(lambda e: e.tensor_copy(out=dec[:], in_=eb[:, 63:HB:64]), ["eb"], ["dec"])
        dv(lambda e: e.tensor_tensor(out=khat[:].rearrange("p (c s) -> p c s", s=64), in0=ktil[:].rearrange("p (c s) -> p c s", s=64),
                                     in1=dec[:, :, None].to_broadcast([64, 32, 64]), op=ALU.mult), ["ktil", "dec"], ["khat"])
        ac(lambda e: e.activation(out=g64[:], in_=g64[:], func=AF.Silu), ["g64"], ["g64"])
        for hlf in range(2):
            pp, ppn = ps[6 + hlf], f"ps{6 + hlf}"
            for j in range(8):
                jj = hlf * 8 + j
                P.op('pe', lambda e: e.transpose(out=pp[:, j * 64:(j + 1) * 64], in_=khat[:, jj * 128:(jj + 1) * 128],
                                                 identity=i128[0:64, 0:64]), r=["khat", "i128"], w=[ppn])
            ac(lambda e: e.activation(out=khT[:, hlf * 8:(hlf + 1) * 8, :].rearrange("p a b -> p (a b)"), in_=pp[:, :], func=AF.Copy),
               [ppn], ["khT"])
        for cg in range(4):
            for ci in range(8):
                c = cg * 8 + ci
                po = (c % 2) * 64
                pu, pun = ps[c % 2], f"ps{c % 2}"
                sl_ = slice((ci // 2) * 64, (ci // 2 + 1) * 64)
                P.op('pe', lambda e: e.matmul(pu[0:64, sl_], lhsT=khT[po:po + 64, c // 2, :],
                                              rhs=v128[po:po + 64, c // 2, :], start=True, stop=True),
                     r=["khT", "v128"], w=[pun])
            for ci in range(8):
                c = cg * 8 + ci
                pu, pun = ps[c % 2], f"ps{c % 2}"
                sl_ = slice((ci // 2) * 64, (ci // 2 + 1) * 64)
                dv(lambda e: e.scalar_tensor_tensor(out=sall[:, c + 1, :], in0=sall[:, c, :], scalar=dec[:, c:c + 1],
                                                    in1=pu[0:64, sl_], op0=ALU.mult, op1=ALU.add),
                   [("sall", c), "dec", pun], [("sall", c + 1)])
        P.op('pool', lambda e: e.tensor_copy(out=sbf[:], in_=sall[:, 0:32, :]), r=["sall"], w=["sbf"])
        for cg in range(4):
            pa, pan = ps[2 + cg % 2], f"ps{2 + cg % 2}"
            po_, pon = ps[4 + cg % 2], f"ps{4 + cg % 2}"
            for ci in range(8):
                c = cg * 8 + ci
                cs = slice(c * 64, (c + 1) * 64)
                P.op('pe', lambda e: e.matmul(pa[0:64, ci * 64:(ci + 1) * 64], lhsT=ktb[:, cs], rhs=qtil[:, cs], start=True, stop=True),
                     r=["ktb", "qtil"], w=[pan])
            dv(lambda e: e.tensor_tensor(out=attm[:], in0=pa[0:64, :], in1=cm[:], op=ALU.mult), [pan, "cm"], ["attm"])
            for ci in range(8):
                c = cg * 8 + ci
                cs = slice(c * 64, (c + 1) * 64)
                P.op('pe', lambda e: e.matmul(po_[0:64, ci * 64:(ci + 1) * 64], lhsT=attm[:, ci * 64:(ci + 1) * 64], rhs=v64[:, c, :],
                                              start=True, stop=False), r=["attm", "v64"], w=[pon])
                P.op('pe', lambda e: e.matmul(po_[0:64, ci * 64:(ci + 1) * 64], lhsT=qtil[:, cs], rhs=sbf[:, c, :],
                                              start=False, stop=True), r=["qtil", "sbf"], w=[pon])
            ac(lambda e: e.activation(out=osb[:].rearrange("p a b -> p (a b)"), in_=po_[0:64, :], func=AF.Copy), [pon], ["osb"])
            P.op('pool', lambda e: e.tensor_tensor(out=osq[:], in0=osb[:], in1=osb[:], op=ALU.mult), r=["osb"], w=["osq"])
            dv(lambda e: e.tensor_reduce(out=ss[:], in_=osq[:], axis=AX.X, op=ALU.add), ["osq"], ["ss"])
            ac(lambda e: e.activation(out=ss[:], in_=ss[:], func=AF.Sqrt, scale=1.0 / 64, bias=EPS), ["ss"], ["ss"])
            dv(lambda e: e.reciprocal(out=ss[:], in_=ss[:]), ["ss"], ["ss"])
            y_ = yh[cg % 2]; yn = f"yh{cg % 2}"
            dv(lambda e: e.tensor_tensor(out=y_[:], in0=osb[:], in1=ss[:, :, None].to_broadcast([64, 8, 64]), op=ALU.mult),
               ["osb", "ss"], [yn])
            dv(lambda e: e.tensor_tensor(out=y_[:], in0=y_[:], in1=gain[:, None, :].to_broadcast([64, 8, 64]), op=ALU.mult),
               [yn, "gain"], [yn])
            dv(lambda e: e.tensor_tensor(out=y_[:], in0=y_[:], in1=g64[:, cg * 8:(cg + 1) * 8, :], op=ALU.mult), [yn, "g64"], [yn])
            c0 = blk * 32 + cg * 8
            P.dma(yh_d[:, c0:c0 + 8, :], y_[:], r=[yn], w=[("yhd", c0)], sem=f"yhs{cg % 2}")
        dv(lambda e: e.tensor_copy(out=sall[:, 0, :], in_=sall[:, 32, :]), [("sall", 32), "sbf"], [("sall", 0)])
    return P.finish()


def _m1_consts():
    jv1 = np.array([7, 6, 5, 4, 3, 2, 1, 0, 1, 2, 3, 4, 5, 6, 7, 8, -7, -6, -5, -4, -3, -2, -1, 0], np.float32)
    jv = np.ascontiguousarray(np.broadcast_to(jv1, (128, 4, 24))).astype(np.float32)
    sgn = np.ones((128, 2), np.float32); sgn[64:, 0] = -1; sgn[:64, 1] = -1
    i128 = np.eye(128, dtype=np.float32)
    jsw = np.zeros((128, 128), np.float32)
    for k in range(128):
        jsw[k, (k + 64) % 128] = 1
    s_idx = np.arange(128) // 16
    tmask = (s_idx[None, :] >= s_idx[:, None]).astype(np.float32)
    st = np.arange(64)
    cm = np.tile((st[:, None] <= st[None, :]).astype(np.float32), (1, 8))
    return dict(jv=jv, sgn=sgn, i128=i128, jsw=jsw, tmask=tmask, cmask=cm)


def run_M1(zT, inp, l):
    nc = build_M1(l)
    cst = _m1_consts()
    maps = []
    for c in range(NCORES):
        b = c // 4
        tsl = slice(b * SEQ, (b + 1) * SEQ)
        m = dict(cst)
        gs = [4 * (c % 4) + gi for gi in range(4)]
        st2 = lambda a: np.concatenate([a, a], axis=0)
        m["lr2"] = np.stack([st2(inp['s5_lambda_re'][l, g]) for g in gs], 1).astype(np.float32)
        m["li2"] = np.stack([st2(inp['s5_lambda_im'][l, g]) for g in gs], 1).astype(np.float32)
        m["ldt"] = np.ascontiguousarray(np.broadcast_to(np.array([inp['s5_log_dt'][l, g] for g in gs], np.float32), (128, 4)))
        m["bst1"] = np.stack([np.concatenate([inp['s5_b_re'][l, g], inp['s5_b_im'][l, g]], 0) for g in gs], 1)
        m["bst2"] = np.stack([np.concatenate([inp['s5_b_im'][l, g], inp['s5_b_re'][l, g]], 0) for g in gs], 1)
        m["cst1"] = np.stack([np.concatenate([inp['s5_c_re'][l, g].T, inp['s5_c_im'][l, g].T], 0) for g in gs], 1)
        m["cst2"] = np.stack([np.concatenate([inp['s5_c_im'][l, g].T, inp['s5_c_re'][l, g].T], 0) for g in gs], 1)
        m["dcol"] = np.stack([np.tile(inp['s5_d'][l, g], 8) for g in gs], 1).astype(np.float32)
        m["uc"] = np.stack([zT[g * 16:(g + 1) * 16, tsl].reshape(16, NCH, 8).transpose(2, 0, 1).reshape(128, NCH) for g in gs], 0)
        h = c % 4
        m["hq"] = zT[256 + 64 * h:256 + 64 * (h + 1), tsl]
        m["hf"] = zT[512 + 64 * h:512 + 64 * (h + 1), tsl]
        v = zT[768 + 64 * h:768 + 64 * (h + 1), tsl].T
        gg = zT[1024 + 64 * h:1024 + 64 * (h + 1), tsl].T
        m["hv64"] = v.reshape(4, 32, 64, 64).transpose(0, 2, 1, 3)
        m["hv128"] = v.reshape(4, 16, 128, 64).transpose(0, 2, 1, 3)
        m["hg64"] = gg.reshape(4, 32, 64, 64).transpose(0, 2, 1, 3)
        m["lbl"] = inp['hgrn_lb_logits'][:, 64 * h:64 * (h + 1)].T
        m["hgain"] = np.broadcast_to(inp['hgrn_norm'][l][None, :], (64, 64))
        maps.append({k: np.ascontiguousarray(v_, dtype=np.float32) for k, v_ in m.items()})
    res = run_bass_kernel_spmd(nc, maps, core_ids=list(range(NCORES)))
    y5T = np.zeros((256, BATCH * SEQ), np.float32)
    yhT = np.zeros((256, BATCH * SEQ), np.float32)
    for c in range(NCORES):
        b = c // 4
        tsl = slice(b * SEQ, (b + 1) * SEQ)
        r = res.results[c]
        for gi in range(4):
            g = 4 * (c % 4) + gi
            y5T[g * 16:(g + 1) * 16, tsl] = r["y5"][gi].reshape(8, 16, NCH).transpose(1, 2, 0).reshape(16, SEQ)
        h = c % 4
        yhT[64 * h:64 * (h + 1), tsl] = r["yh"].transpose(1, 0, 2).reshape(SEQ, 64).T
    return y5T, yhT


OFF_F = 8320
NF_F = OFF_F + 8192 + 128
NEGB = -30000.0
TINY = 1e-30
NQT = 32
SCALE = 0.125


def build_M2():
    P = Prog()
    nc = P.nc
    q65_d = P.dram_in("q65", [4, 65, NQT * 128])
    ksl_d = P.dram_in("kslX", [64, SEQ]); kwn_d = P.dram_in("kwnX", [64, SEQ])
    vsl_d = P.dram_in("vslX", [128, 64 * 64]); vwn_d = P.dram_in("vwnX", [128, 64 * 64])
    kcr_d = P.dram_in("kcrX", [32, 64, 512]); vcr_d = P.dram_in("vcrX", [32, 64, 512])
    w1k_d = P.dram_in("w1k", [64, 32 * 64]); w1v_d = P.dram_in("w1v", [64, 32 * 64])
    w2k_d = P.dram_in("w2k", [64, 64]); w2v_d = P.dram_in("w2v", [64, 64])
    posk_d = P.dram_in("poskT", [64, 32]); posv_d = P.dram_in("posvT", [64, 32])
    fs_d = P.dram_in("fs", [4, NF_F]); fw_d = P.dram_in("fw", [4, NF_F])
    eall_d = P.dram_in("eall", [128, SEQ]); mm_d = P.dram_in("mmat", [128, 4 * 128])
    madd_d = P.dram_in("madd", [NQT, 128, 128]); glog_d = P.dram_in("glog", [128, NQT * 12])
    i128_d = P.dram_in("i128", [128, 128])
    parity_dummy = None
    y_d = P.dram_out("ynsa", [128, NQT, 256])

    T = P.sbuf
    ps = [P.psum(f"ps{i}", [128, 512]) for i in range(8)]
    psA = [(ps[0], "ps0"), (ps[1], "ps1")]
    psL, psS, psO1, psO2, psO3, psB = ps[2], ps[3], ps[4], ps[5], ps[6], ps[7]
    stg = [T(f"stg{i}", [128, 2048], F32) for i in range(2)]
    sctr = [0]

    def stage():
        i = sctr[0] % 2
        sctr[0] += 1
        return stg[i], f"stg{i}"

    i128 = T("i128", [128, 128], F32); P.dma(i128[:], i128_d[:, :], w=["i128"])
    ones = T("ones", [128, 128], BF16); P.op('pool', lambda e: e.memset(ones[:], 1.0), w=["ones"])
    qa = T("qa", [65, 4, NQT * 128], BF16)
    ksl = T("ksl", [65, SEQ], BF16); kwn = T("kwn", [64, SEQ], BF16)
    vsl = T("vsl", [128, 64, 65], BF16); vwn = T("vwn", [128, 64, 65], BF16)
    eall = T("eall", [128, SEQ], BF16); mmt = T("mmt", [128, 4, 128], F32)
    kcmp = T("kcmp", [64, 512], BF16); vcmp = T("vcmp", [128, 4, 65], BF16)
    bs = T("bs", [128, 11, 512], F32)
    glog = T("glog", [128, NQT * 12], F32)

    for r in range(4):
        for hh in range(2):
            st, sn = stage()
            P.dma(st[0:65, :], q65_d[r, :, hh * 2048:(hh + 1) * 2048], w=[sn])
            P.op('pool', lambda e: e.tensor_copy(out=qa[0:64, r, hh * 2048:(hh + 1) * 2048], in_=st[0:64, :]), r=[sn], w=["qa"])
            P.op('act', lambda e: e.activation(out=qa[64:65, r, hh * 2048:(hh + 1) * 2048], in_=st[64:65, :], func=AF.Copy, scale=8.0),
                 r=[sn], w=["qa"])
    for (dst, dn, src) in [(ksl, "ksl", ksl_d), (kwn, "kwn", kwn_d)]:
        for pc in range(4):
            st, sn = stage()
            P.dma(st[0:64, :], src[:, pc * 2048:(pc + 1) * 2048], w=[sn])
            P.op('pool', lambda e: e.tensor_copy(out=dst[0:64, pc * 2048:(pc + 1) * 2048], in_=st[0:64, :]), r=[sn], w=[dn])
    P.op('pool', lambda e: e.memset(ksl[64:65, :], 1.0), w=["ksl"])
    for (dst, dn, src) in [(vsl, "vsl", vsl_d), (vwn, "vwn", vwn_d)]:
        for pc in range(2):
            st, sn = stage()
            P.dma(st[:, :], src[:, pc * 2048:(pc + 1) * 2048], w=[sn])
            P.op('pool', lambda e: e.tensor_copy(out=dst[:, pc * 32:(pc + 1) * 32, 0:64], in_=st[:, :].rearrange("p (a b) -> p a b", b=64)),
                 r=[sn], w=[dn])
        P.op('pool', lambda e: e.memset(dst[:, :, 64:65], 1.0), w=[dn])
    for pc in range(4):
        st, sn = stage()
        P.dma(st[:, :], eall_d[:, pc * 2048:(pc + 1) * 2048], w=[sn])
        P.op('pool', lambda e: e.tensor_copy(out=eall[:, pc * 2048:(pc + 1) * 2048], in_=st[:, :]), r=[sn], w=["eall"])
    P.dma(mmt[:].rearrange("p a b -> p (a b)"), mm_d[:, :], w=["mmt"])
    P.dma(glog[:], glog_d[:, :], w=["glog"])
    P.op('act', lambda e: e.activation(out=glog[:], in_=glog[:], func=AF.Sigmoid), r=["glog"], w=["glog"])
    for j in range(11):
        tab = fs_d if j < 9 else fw_d
        dl = 128 * j if j < 9 else 128 * (j - 5)
        src = bass.AP(tab.tensor, OFF_F + dl - 127, [(1, 128), (NF_F, 4), (1, 128)])
        P.dma(bs[:, j, :].rearrange("p (r q) -> p r q", q=128), src, w=[("bs", j)], sem="bs")

    w1 = T("w1", [64, 32 * 64], BF16); w2 = T("w2", [64, 64], BF16); posT = T("posT", [64, 32], BF16)
    w2f = T("w2f", [64, 64], F32); posf = T("posf", [64, 32], F32)
    pb = T("pb", [64, 1], F32); xj = [T(f"xj{i}", [64, 512], BF16) for i in range(2)]
    ga = T("ga", [64, 512], F32); gb = T("gb", [64, 512], F32); gel = T("gel", [64, 512], BF16)
    for which in range(2):
        w1_d, w2_d, pos_d, x_d = [(w1k_d, w2k_d, posk_d, kcr_d), (w1v_d, w2v_d, posv_d, vcr_d)][which]
        st, sn = stage()
        P.dma(st[0:64, :], w1_d[:, :], w=[sn])
        P.op('pool', lambda e: e.tensor_copy(out=w1[:], in_=st[0:64, :]), r=[sn], w=["w1"])
        P.dma(w2f[:], w2_d[:, :], w=["w2f"]); P.dma(posf[:], pos_d[:, :], w=["posf"])
        P.op('pool', lambda e: e.tensor_copy(out=w2[:], in_=w2f[:]), r=["w2f"], w=["w2"])
        P.op('pool', lambda e: e.tensor_copy(out=posT[:], in_=posf[:]), r=["posf"], w=["posT"])
        for j in range(32):
            P.op('pe', lambda e: e.matmul(psL[0:64, 0:1], lhsT=w1[:, j * 64:(j + 1) * 64], rhs=posT[:, j:j + 1], start=(j == 0), stop=(j == 31)),
                 r=["w1", "posT"], w=["psL"])
        P.op('act', lambda e: e.activation(out=pb[:], in_=psL[0:64, 0:1], func=AF.Copy), r=["psL"], w=["pb"])
        for j in range(32):
            st, sn = stage()
            P.dma(st[0:64, 0:512], x_d[j], w=[sn])
            P.op('pool', lambda e: e.tensor_copy(out=xj[j % 2][:], in_=st[0:64, 0:512]), r=[sn], w=[f"xj{j % 2}"])
            P.op('pe', lambda e: e.matmul(psS[0:64, :], lhsT=w1[:, j * 64:(j + 1) * 64], rhs=xj[j % 2][:], start=(j == 0), stop=(j == 31)),
                 r=["w1", f"xj{j % 2}"], w=["psS"])
        emit_gelu(P, gel[:], psS[0:64, :], ga[:], gb[:], ("gel", "ga", "gb"), ["psS"], bias=(pb[:, 0:1], "pb"))
        if which == 0:
            P.op('pe', lambda e: e.matmul(psO1[0:64, :], lhsT=w2[:], rhs=gel[:], start=True, stop=True), r=["w2", "gel"], w=["psO1"])
            P.op('act', lambda e: e.activation(out=kcmp[:], in_=psO1[0:64, :], func=AF.Copy), r=["psO1"], w=["kcmp"])
        else:
            for ci in range(4):
                P.op('pe', lambda e: e.matmul(psO2[:, ci * 64:(ci + 1) * 64], lhsT=gel[:, ci * 128:(ci + 1) * 128], rhs=w2[:], start=True, stop=True),
                     r=["w2", "gel"], w=["psO2"])
            P.op('act', lambda e: e.activation(out=vcmp[:, :, 0:64], in_=psO2[:, 0:256].rearrange("p (a b) -> p a b", b=64), func=AF.Copy),
                 r=["psO2"], w=["vcmp"])
            P.op('pool', lambda e: e.memset(vcmp[:, :, 64:65], 1.0), w=["vcmp"])

    bc = [T(f"bc{i}", [128, 512], F32) for i in range(2)]
    tmpb = [T(f"tmpb{i}", [128, 512], F32) for i in range(2)]
    ebuf = [T(f"ebuf{i}", [128, 512], BF16) for i in range(2)]
    pbuf = [T(f"pbuf{i}", [128, 512], BF16) for i in range(2)]
    ec = T("ec", [128, 4, 512], BF16)
    rlb = T("rlb", [128, 512], F32); pnb = T("pnb", [128, 512], F32); impT = T("impT", [128, 4, 128], F32)
    madd = [T(f"madd{i}", [128, 128], F32) for i in range(2)]
    score = T("score", [128, 128], F32); sc2 = T("sc2", [128, 128], F32); m8a = T("m8a", [128, 8], F32); m8b = T("m8b", [128, 8], F32)
    self_ = T("self", [128, 128], F32); selT = T("selT", [128, 128], BF16)
    osb = [T(f"osb{i}", [128, 260], F32) for i in range(3)]
    lc = T("lc", [128, 3, 4], F32); coef = T("coef", [128, 3, 4], F32)
    yb = [T(f"yb{i}", [128, 256], F32) for i in range(2)]
    rot = dict(a=0, t=0, e=0, p=0, b=0)

    def nxt(k, n=2):
        v = rot[k] % n
        rot[k] += 1
        return v

    def softmax_tile(A_src, An, bias_ap, bias_names, eout, eout_name):
        if bias_ap is None:
            P.op('act', lambda e: e.activation(out=eout, in_=A_src, func=AF.Exp, scale=SCALE), r=[An], w=[eout_name])
        else:
            ti = nxt('t')
            P.op('dve', lambda e: e.scalar_tensor_tensor(out=tmpb[ti][:], in0=A_src, scalar=SCALE, in1=bias_ap, op0=ALU.mult, op1=ALU.add),
                 r=[An] + bias_names, w=[f"tmpb{ti}"])
            P.op('act', lambda e: e.activation(out=eout, in_=tmpb[ti][:], func=AF.Exp), r=[f"tmpb{ti}"], w=[eout_name])

    def pv(psO, psOn, lhs_tile, lhs_name, v_ap, v_name, first, last=False):
        for r in range(4):
            P.op('pe', lambda e: e.matmul(psO[:, r * 65:(r + 1) * 65], lhsT=lhs_tile[:, r * 128:(r + 1) * 128], rhs=v_ap,
                                          start=(first and r == 0), stop=(last and r == 3)), r=[lhs_name, v_name], w=[psOn])

    import os
    NT_RUN = int(os.environ.get("M2_NT", NQT))
    for m in range(NT_RUN):
        qi = 2 * m + 1
        t0 = 128 * qi
        q64 = qa[0:64, :, m * 128:(m + 1) * 128]
        q65 = qa[0:65, :, m * 128:(m + 1) * 128]
        nck = min(4, ((t0 + 96) // 16) // 128 + 1)
        for ci in range(nck):
            bi = nxt('b')
            src = bass.AP(fs_d.tensor, OFF_F + t0 - 2048 * ci - 2063, [(16, 128), (NF_F, 4), (1, 128)])
            P.dma(bc[bi][:].rearrange("p (r q) -> p r q", q=128), src, w=[f"bc{bi}"])
            ai = nxt('a'); A, An = psA[ai]
            P.op('pe', lambda e: e.matmul(A[:, :], lhsT=kcmp[:, ci * 128:(ci + 1) * 128], rhs=q64, start=True, stop=True),
                 r=["kcmp", "qa"], w=[An])
            softmax_tile(A[:, :], An, bc[bi][:], [f"bc{bi}"], ec[:, ci, :], ("ec", ci))
            P.op('pe', lambda e: e.matmul(psL[:, :], lhsT=ones[:], rhs=ec[:, ci, :], start=(ci == 0), stop=(ci == nck - 1)),
                 r=["ones", ("ec", ci)], w=["psL"])
            pv(psO1, "psO1", ec[:, ci, :], ("ec", ci), vcmp[:, ci, :], "vcmp", ci == 0, ci == nck - 1)
        P.op('dve', lambda e: e.tensor_scalar(out=rlb[:], in0=psL[:, :], scalar1=TINY, scalar2=None, op0=ALU.max), r=["psL"], w=["rlb"])
        P.op('dve', lambda e: e.reciprocal(out=rlb[:], in_=rlb[:]), r=["rlb"], w=["rlb"])
        for ci in range(nck):
            P.op('dve', lambda e: e.tensor_tensor(out=pnb[:], in0=ec[:, ci, :], in1=rlb[:], op=ALU.mult), r=[("ec", ci), "rlb"], w=["pnb"])
            P.op('dve', lambda e: e.tensor_reduce(out=impT[:, ci, :], in_=pnb[:].rearrange("p (r q) -> p q r", q=128), axis=AX.X, op=ALU.add),
                 r=["pnb"], w=[("impT", ci)])
        for ci in range(nck):
            P.op('pe', lambda e: e.matmul(psS[:, 0:128], lhsT=impT[:, ci, :], rhs=mmt[:, ci, :], start=(ci == 0), stop=(ci == nck - 1)),
                 r=[("impT", ci), "mmt"], w=["psS"])
        mi = m % 2
        P.dma(madd[mi][:], madd_d[m], w=[f"madd{mi}"])
        P.op('dve', lambda e: e.tensor_tensor(out=score[:], in0=psS[:, 0:128], in1=madd[mi][:], op=ALU.add), r=["psS", f"madd{mi}"], w=["score"])
        P.op('dve', lambda e: e.max(out=m8a[:], in_=score[:]), r=["score"], w=["m8a"])
        P.op('dve', lambda e: e.match_replace(out=sc2[:], in_to_replace=m8a[:], in_values=score[:], imm_value=-3e38), r=["score", "m8a"], w=["sc2"])
        P.op('dve', lambda e: e.max(out=m8b[:], in_=sc2[:]), r=["sc2"], w=["m8b"])
        P.op('dve', lambda e: e.tensor_scalar(out=self_[:], in0=score[:], scalar1=m8b[:, 7:8], scalar2=None, op0=ALU.is_ge),
             r=["score", "m8b"], w=["self"])
        P.op('pe', lambda e: e.transpose(out=psS[:, 128:256], in_=self_[:], identity=i128[:]), r=["self", "i128"], w=["psS"])
        P.op('act', lambda e: e.activation(out=selT[:], in_=psS[:, 128:256], func=AF.Copy), r=["psS"], w=["selT"])
        for kt in range(qi + 1):
            near = (qi - kt) <= 8
            ai = nxt('a'); A, An = psA[ai]
            if near:
                P.op('pe', lambda e: e.matmul(A[:, :], lhsT=ksl[0:64, kt * 128:(kt + 1) * 128], rhs=q64, start=True, stop=True),
                     r=["ksl", "qa"], w=[An])
            else:
                P.op('pe', lambda e: e.matmul(A[:, :], lhsT=ksl[0:65, kt * 128:(kt + 1) * 128], rhs=q65, start=True, stop=True),
                     r=["ksl", "qa"], w=[An])
            bi = nxt('p')
            P.op('pe', lambda e: e.matmul(psB[:, bi * 128:(bi + 1) * 128], lhsT=eall[:, kt * 128:(kt + 1) * 128], rhs=selT[:], start=True, stop=True),
                 r=["eall", "selT"], w=[("psB", bi)])
            ei = nxt('e')
            if near:
                softmax_tile(A[:, :], An, bs[:, qi - kt, :], [("bs", qi - kt)], ebuf[ei][:], f"ebuf{ei}")
            else:
                softmax_tile(A[:, :], An, None, None, ebuf[ei][:], f"ebuf{ei}")
            P.op('dve', lambda e: e.tensor_tensor(out=pbuf[bi][:].rearrange("p (r q) -> p r q", q=128),
                                                  in0=ebuf[ei][:].rearrange("p (r q) -> p r q", q=128),
                                                  in1=psB[:, None, bi * 128:(bi + 1) * 128].to_broadcast([128, 4, 128]), op=ALU.mult),
                 r=[f"ebuf{ei}", ("psB", bi)], w=[f"pbuf{bi}"])
            pv(psO2, "psO2", pbuf[bi], f"pbuf{bi}", vsl[:, kt, :], "vsl", kt == 0, kt == qi)
        kts = list(range(max(0, qi - 5), qi + 1))
        for kt in kts:
            j = qi - kt
            ai = nxt('a'); A, An = psA[ai]
            P.op('pe', lambda e: e.matmul(A[:, :], lhsT=kwn[0:64, kt * 128:(kt + 1) * 128], rhs=q64, start=True, stop=True),
                 r=["kwn", "qa"], w=[An])
            ei = nxt('e')
            jj = j if j < 4 else 5 + j
            softmax_tile(A[:, :], An, bs[:, jj, :], [("bs", jj)], ebuf[ei][:], f"ebuf{ei}")
            pv(psO3, "psO3", ebuf[ei], f"ebuf{ei}", vwn[:, kt, :], "vwn", kt == kts[0], kt == kts[-1])
        for b_, (pso, pson) in enumerate([(psO1, "psO1"), (psO2, "psO2"), (psO3, "psO3")]):
            P.op('act', lambda e: e.activation(out=osb[b_][:], in_=pso[:, 0:260], func=AF.Copy), r=[pson], w=[f"osb{b_}"])
            P.op('dve', lambda e: e.tensor_copy(out=lc[:, b_, :], in_=osb[b_][:].rearrange("p (r d) -> p r d", d=65)[:, :, 64]),
                 r=[f"osb{b_}"], w=["lc"])
        P.op('dve', lambda e: e.tensor_scalar(out=lc[:], in0=lc[:], scalar1=TINY, scalar2=None, op0=ALU.max), r=["lc"], w=["lc"])
        P.op('dve', lambda e: e.reciprocal(out=lc[:], in_=lc[:]), r=["lc"], w=["lc"])
        P.op('dve', lambda e: e.tensor_tensor(out=coef[:], in0=lc[:],
                                              in1=glog[:, m * 12:(m + 1) * 12].rearrange("p (r b) -> p b r", b=3), op=ALU.mult),
             r=["lc", "glog"], w=["coef"])
        y_ = yb[m % 2]; yn = f"yb{m % 2}"
        for r in range(4):
            for b_ in range(3):
                src_ = osb[b_][:, r * 65:r * 65 + 64]
                if b_ == 0:
                    P.op('dve', lambda e: e.tensor_scalar(out=y_[:, r * 64:(r + 1) * 64], in0=src_, scalar1=coef[:, b_, r:r + 1], scalar2=None,
                                                          op0=ALU.mult), r=[f"osb{b_}", "coef"], w=[(yn, r)])
                else:
                    P.op('dve', lambda e: e.scalar_tensor_tensor(out=y_[:, r * 64:(r + 1) * 64], in0=src_, scalar=coef[:, b_, r:r + 1],
                                                                 in1=y_[:, r * 64:(r + 1) * 64], op0=ALU.mult, op1=ALU.add),
                         r=[f"osb{b_}", "coef", (yn, r)], w=[(yn, r)])
        P.dma(y_d[:, m, :], y_[:], r=[yn], w=[("yd", m)], sem=f"ys{m % 2}")
    return P.finish()


def _t5_bucket_np(d):
    import math
    n = np.maximum(d, 0)
    nf = np.maximum(n, 16).astype(np.float32)
    large = 16 + (np.log(nf / np.float32(16)) / np.float32(math.log(64.0)) * np.float32(16)).astype(np.int32)
    large = np.minimum(large, 31)
    return np.where(n < 16, n, large)


def _m2_consts():
    eall = np.zeros((128, SEQ), np.float32)
    col = np.arange(SEQ)
    kt = col // 128
    p = col % 128
    eall[2 * kt + (127 - p) // 64, col] = 1.0
    mm = np.zeros((128, 4, 128), np.float32)
    wts = {-1: 1.0, 0: 2.0, 1: 2.0, 2: 2.0, 3: 1.0}
    for ci in range(4):
        for pp in range(128):
            c = 128 * ci + 127 - pp
            for dlt, wv in wts.items():
                if (c - dlt) % 4 == 0:
                    j = (c - dlt) // 4
                    if 0 <= j < 128:
                        mm[pp, ci, j] = wv
    return dict(eall=eall, mmat=mm.reshape(128, 512), i128=np.eye(128, dtype=np.float32))


def run_M2(zT, inp, l):
    nc = build_M2()
    cst = _m2_consts()
    rel = inp['rel_bias'].astype(np.float32)
    didx = np.arange(NF_F) - OFF_F
    bk = _t5_bucket_np(didx)
    maps = []
    for c in range(NCORES):
        b = c // 4
        g = (c % 4) // 2
        par = c % 2
        tsl = slice(b * SEQ, (b + 1) * SEQ)
        m = dict(cst)
        sh = 128 * (1 - par)
        fs = np.full((4, NF_F), NEGB, np.float32)
        fw = np.full((4, NF_F), NEGB, np.float32)
        for r in range(4):
            base_s = np.where(didx >= 0, rel[bk, g * 4 + r], NEGB).astype(np.float32)
            base_w = np.where((didx >= 0) & (didx < 512), rel[bk, g * 4 + r], NEGB).astype(np.float32)
            fs[r, sh:] = base_s[:NF_F - sh]
            fw[r, sh:] = base_w[:NF_F - sh]
        m["fs"] = fs
        m["fw"] = fw
        tiles = 2 * np.arange(NQT) + par
        tok = (tiles[:, None] * 128 + np.arange(128)[None, :]).reshape(-1)
        q65 = np.zeros((4, 65, NQT * 128), np.float32)
        for r in range(4):
            q65[r, 0:64] = zT[1280 + g * 256 + r * 64:1280 + g * 256 + (r + 1) * 64, tsl][:, tok]
            q65[r, 64] = rel[31, g * 4 + r]
        m["q65"] = q65

        def kv(j):
            return zT[1792 + j * 128 + g * 64:1792 + j * 128 + (g + 1) * 64, tsl]
        rev = lambda a: a.reshape(64, 64, 128)[:, :, ::-1].reshape(64, SEQ)
        m["kslX"] = rev(kv(2))
        m["kwnX"] = rev(kv(4))
        vrev = lambda a: a.T.reshape(64, 128, 64)[:, ::-1, :].transpose(1, 0, 2).reshape(128, 64 * 64)
        m["vslX"] = vrev(kv(3))
        m["vwnX"] = vrev(kv(5))
        cc = (128 * (np.arange(512) // 128) + 127 - (np.arange(512) % 128))
        for nm, j in [("kcrX", 0), ("vcrX", 1)]:
            src = np.concatenate([kv(j), np.zeros((64, 64), np.float32)], axis=1)
            arr = np.zeros((32, 64, 512), np.float32)
            for jj in range(32):
                arr[jj] = src[:, np.minimum(16 * cc + jj, SEQ + 63)]
            m[nm] = arr
        for sfx, key in [("k", "k"), ("v", "v")]:
            w1 = inp[f'nsa_cmp_w1_{key}'][l]
            m[f"w1{sfx}"] = w1.reshape(32, 64, 64).transpose(1, 0, 2).reshape(64, 2048)
            m[f"w2{sfx}"] = inp[f'nsa_cmp_w2_{key}'][l]
            m[f"pos{sfx}T"] = inp[f'nsa_cmp_pos_{key}'][l].T
        t = tiles[:, None] * 128 + np.arange(128)[None, :]
        blk = np.arange(128)
        cur = t // 64
        ok = (blk[None, None, :] * 64) <= t[:, :, None]
        forced = (blk[None, None, :] == 0) | (blk[None, None, :] == cur[:, :, None]) | (blk[None, None, :] == cur[:, :, None] - 1)
        m["madd"] = np.where(ok, np.where(forced, 1e4, 0.0), -1e30).astype(np.float32)
        gl = zT[2560 + g * 12:2560 + (g + 1) * 12, tsl][:, tok]
        m["glog"] = gl.T.reshape(NQT, 128, 12).transpose(1, 0, 2).reshape(128, NQT * 12)
        maps.append({k: np.ascontiguousarray(v_, dtype=np.float32) for k, v_ in m.items()})
    res = run_bass_kernel_spmd(nc, maps, core_ids=list(range(NCORES)))
    yT = np.zeros((512, BATCH * SEQ), np.float32)
    for c in range(NCORES):
        b = c // 4
        g = (c % 4) // 2
        par = c % 2
        tiles = 2 * np.arange(NQT) + par
        tok = b * SEQ + (tiles[:, None] * 128 + np.arange(128)[None, :]).reshape(-1)
        y = res.results[c]["ynsa"]
        y = y.transpose(1, 0, 2).reshape(NQT * 128, 256)
        yT[g * 256:(g + 1) * 256, tok] = y.T
    return yT


def kernel(**inputs):
    inp = {k: np.asarray(v) for k, v in inputs.items()}
    x = inp['x'].astype(np.float32).reshape(BATCH * SEQ, D_MODEL)
    xT = np.ascontiguousarray(x.T)

    def win_map(l):
        win = np.zeros((D_MODEL, NZC * 128), np.float32)
        win[:, :D_IN] = inp['w_in'][l]
        return {"mix_g": _gcol(inp['mix_norm'][l]), "win": _chunkT(win, 8)}

    def wout_map(l):
        return {"wglu": _chunkT(inp['s5_w_glu'][l], 2), "wout": _chunkT(inp['w_out'][l], 8)}

    def mixers(zT, l):
        y5T, yhT = run_M1(zT, inp, l)
        ynT = run_M2(zT, inp, l)
        return np.concatenate([y5T, yhT, ynT], axis=0)

    common = _ffn_maps("f0", inp['ffn1_norm'][0], inp['ffn1_w_gate'][0], inp['ffn1_w_up'][0], inp['ffn1_w_down'][0])
    common.update(win_map(0))
    o = run_T(xT, common, False, 1, True, False)
    yT = mixers(o["zT"], 0)
    common = _ffn_maps("f0", inp['ffn2_norm'][0], inp['ffn2_w_gate'][0], inp['ffn2_w_up'][0], inp['ffn2_w_down'][0])
    common.update(_ffn_maps("f1", inp['ffn1_norm'][1], inp['ffn1_w_gate'][1], inp['ffn1_w_up'][1], inp['ffn1_w_down'][1]))
    common.update(win_map(1))
    common.update(wout_map(0))
    o = run_T(o["xoT"], common, True, 2, True, False, yT_full=yT)
    yT = mixers(o["zT"], 1)
    common = _ffn_maps("f0", inp['ffn2_norm'][1], inp['ffn2_w_gate'][1], inp['ffn2_w_up'][1], inp['ffn2_w_down'][1])
    common.update(wout_map(1))
    common["fin_g"] = _gcol(inp['final_norm'])
    o = run_T(o["xoT"], common, True, 1, False, True, yT_full=yT)
    out = np.ascontiguousarray(o["xoT"].T).reshape(BATCH, SEQ, D_MODEL).astype(np.float32)
    return out
```

```python
import contextlib
import numpy as np
import concourse.bass as bass
import concourse.mybir as mybir
from concourse.bass_utils import run_bass_kernel_spmd

F32 = mybir.dt.float32
BF16 = mybir.dt.bfloat16
AF = mybir.ActivationFunctionType
ALU = mybir.AluOpType
AX = mybir.AxisListType

D_MODEL = 1024
D_FF = 2816
SEQ = 8192
BATCH = 2
D_IN = 2584
NCORES = 8
EPS = 1e-6


class Prog:
    def __init__(self):
        self.nc = bass.Bass("TRN2", target_bir_lowering=False)
        self.es = contextlib.ExitStack()
        nc = self.nc
        self.eng = {'pe': nc.tensor, 'act': nc.scalar, 'dve': nc.vector, 'pool': nc.gpsimd, 'sp': nc.sync}
        self.sems = {}
        self.cnt = {}
        for k in ['pe', 'act', 'dve', 'pool']:
            self.sems[('e', k)] = self.es.enter_context(nc.semaphore('e_' + k))
            self.cnt[('e', k)] = 0
        self.waited = {k: {} for k in self.eng}
        self.state = {}
        self.nps = 0

    def sbuf(self, name, shape, dt):
        return self.es.enter_context(self.nc.sbuf_tensor("s_" + name, list(shape), dt))

    def psum(self, name, shape, dt=F32):
        return self.es.enter_context(self.nc.psum_tensor("p_" + name, list(shape), dt))

    def dram_in(self, name, shape, dt=F32):
        return self.nc.dram_tensor(name, list(shape), dt, kind="ExternalInput").ap()

    def dram_out(self, name, shape, dt=F32):
        return self.nc.dram_tensor(name, list(shape), dt, kind="ExternalOutput").ap()

    @staticmethod
    def _norm(x):
        return x if isinstance(x, tuple) else (x, None)

    def _deps(self, rs, ws, e):
        need = {}

        def add(sv):
            if sv is None:
                return
            s, v = sv
            if s[0] == 'd':
                v = self.cnt[s]
            if need.get(s, 0) < v:
                need[s] = v

        for (n, k) in rs:
            for kk, ent in self.state.get(n, {}).items():
                if k is None or kk is None or kk == k:
                    add(ent['w'])
        for (n, k) in ws:
            for kk, ent in self.state.get(n, {}).items():
                if k is None or kk is None or kk == k:
                    if ent['w'] is not None and ent['w'][0] != ('e', e):
                        add(ent['w'])
                    for s, v in ent['r'].items():
                        if s != ('e', e):
                            add((s, v))
        if e == 'pe':
            need.pop(('e', 'pe'), None)
        return need

    def _record(self, rs, ws, sv):
        s, v = sv
        for (n, k) in rs:
            ent = self.state.setdefault(n, {}).setdefault(k, {'w': None, 'r': {}})
            ent['r'][s] = max(ent['r'].get(s, 0), v)
        for (n, k) in ws:
            d = self.state.setdefault(n, {})
            if k is None:
                d.clear()
            d[k] = {'w': (s, v), 'r': {}}

    def _emit_waits(self, e, need):
        eng = self.eng[e]
        for s, v in need.items():
            if self.waited[e].get(s, 0) < v:
                eng.wait_ge(self.sems[s], v)
                self.waited[e][s] = v

    def op(self, e, fn, r=(), w=()):
        rs = [self._norm(x) for x in r]
        ws = [self._norm(x) for x in w]
        ws = ws + [(n, None) for (n, k) in rs if n.startswith("ps")]
        ws = [((n, None) if n.startswith("ps") else (n, k)) for (n, k) in ws]
        rs = [x for x in rs if not x[0].startswith("ps")]
        self._emit_waits(e, self._deps(rs, ws, e))
        inst = fn(self.eng[e])
        s = ('e', e)
        self.cnt[s] += 1
        inst.then_inc(self.sems[s], 1)
        self._record(rs, ws, (s, self.cnt[s]))

    def dma(self, out, in_, r=(), w=(), q='sp', sem=None):
        rs = [self._norm(x) for x in r]
        ws = [self._norm(x) for x in w]
        if sem is None:
            sem = ws[0][0]
        s = ('d', sem)
        if s not in self.sems:
            self.sems[s] = self.es.enter_context(self.nc.semaphore('d_' + sem))
            self.cnt[s] = 0
        self._emit_waits(q, self._deps(rs, ws, q))
        self.eng[q].dma_start(out=out, in_=in_).then_inc(self.sems[s], 16)
        self.cnt[s] += 16
        self._record(rs, ws, (s, self.cnt[s]))

    def finish(self):
        sp = self.eng['sp']
        for s, h in self.sems.items():
            if self.cnt[s] > 0 and self.waited['sp'].get(s, 0) < self.cnt[s]:
                sp.wait_ge(h, self.cnt[s])
        self.es.close()
        return self.nc


def mm(ps, lhsT, rhs, start, stop):
    return lambda e: e.matmul(ps, lhsT=lhsT, rhs=rhs, start=start, stop=stop)


NTOK = 2048
TP = 1024
NFC = D_FF // 128
NZC = 21


def build_T(do_wout, n_ffn, do_win, do_final):
    P = Prog()
    nc = P.nc
    xT_d = P.dram_in("xT", [8, 128, NTOK])
    if do_wout:
        yT_d = P.dram_in("yT", [8, 128, NTOK])
        wglu_d = P.dram_in("wglu", [2, 128, 2, 128])
        wout_d = P.dram_in("wout", [8, 128, 8, 128])
    ffn_d = []
    for i in range(n_ffn):
        ffn_d.append(dict(
            g=P.dram_in(f"f{i}_g", [128, 8]),
            wg=P.dram_in(f"f{i}_wg", [NFC, 128, 8, 128]),
            wu=P.dram_in(f"f{i}_wu", [NFC, 128, 8, 128]),
            wd=P.dram_in(f"f{i}_wd", [8, 128, NFC, 128]),
        ))
    if do_win:
        ming_d = P.dram_in("mix_g", [128, 8])
        win_d = P.dram_in("win", [NZC, 128, 8, 128])
        zT_d = P.dram_out("zT", [NZC, 128, NTOK])
    if do_final:
        fing_d = P.dram_in("fin_g", [128, 8])
    xo_d = P.dram_out("xoT", [8, 128, NTOK])

    xT = P.sbuf("xT_s", [128, 8, TP], F32)
    hT = P.sbuf("hT_s", [128, 8, TP], BF16)
    aT = P.sbuf("aT_s", [128, NFC, TP], BF16)
    sq = P.sbuf("sq_s", [128, 2, TP], BF16)
    rstd = P.sbuf("rstd_s", [128, TP], F32)
    ones = P.sbuf("ones_s", [128, 128], BF16)
    gt = P.sbuf("g_s", [128, 8], F32)
    stg = [P.sbuf(f"stg{i}", [128, NFC * 128], F32) for i in range(3)]
    wbf = [P.sbuf(f"wbf{i}", [128, NFC * 128], BF16) for i in range(3)]
    sg = [P.sbuf(f"sg{i}", [128, 512], F32) for i in range(2)]
    ev = [P.sbuf(f"ev{i}", [128, 512], F32) for i in range(2)]
    ps = [P.psum(f"ps{i}", [128, 512]) for i in range(8)]
    if do_wout:
        yf = P.sbuf("yf_s", [128, 8, TP], F32)

    P.op('pool', lambda e: e.memset(ones[:], 1.0), w=["ones"])

    wctr = [0]

    def load_w(src_ap, nk):
        i = wctr[0] % 3
        wctr[0] += 1
        P.dma(stg[i][:, 0:nk * 128], src_ap, w=[f"stg{i}"])
        P.op('pool', lambda e: e.tensor_copy(out=wbf[i][:, 0:nk * 128], in_=stg[i][:, 0:nk * 128]),
             r=[f"stg{i}"], w=[f"wbf{i}"])
        return wbf[i], f"wbf{i}"

    psctr = [0]

    def next_ps():
        i = psctr[0] % 8
        psctr[0] += 1
        return ps[i], f"ps{i}"

    def rmsnorm(g_dram, final=False):
        P.dma(gt[:], g_dram, w=["g"])
        pss = [next_ps(), next_ps()]
        for k in range(8):
            j = k % 2
            P.op('act', lambda e: e.activation(out=sq[:, j, :], in_=xT[:, k, :], func=AF.Square),
                 r=[("xT", k)], w=[("sq", j)])
            for tg in range(2):
                P.op('pe', mm(pss[tg][0][:, :], ones[:], sq[:, j, tg * 512:(tg + 1) * 512], k == 0, k == 7),
                     r=["ones", ("sq", j)], w=[pss[tg][1]])
        for tg in range(2):
            sl = slice(tg * 512, (tg + 1) * 512)
            P.op('act', lambda e: e.activation(out=rstd[:, sl], in_=pss[tg][0][:, :], func=AF.Sqrt,
                                               scale=1.0 / D_MODEL, bias=EPS),
                 r=[pss[tg][1]], w=[("rstd", tg)])
            P.op('dve', lambda e: e.reciprocal(out=rstd[:, sl], in_=rstd[:, sl]),
                 r=[("rstd", tg)], w=[("rstd", tg)])
        for k in range(8):
            if final:
                P.op('dve', lambda e: e.scalar_tensor_tensor(out=xT[:, k, :], in0=xT[:, k, :], scalar=gt[:, k:k + 1],
                                                             in1=rstd[:, :], op0=ALU.mult, op1=ALU.mult),
                     r=[("xT", k), "g", "rstd"], w=[("xT", k)])
            else:
                P.op('dve', lambda e: e.scalar_tensor_tensor(out=hT[:, k, :], in0=xT[:, k, :], scalar=gt[:, k:k + 1],
                                                             in1=rstd[:, :], op0=ALU.mult, op1=ALU.mult),
                     r=[("xT", k), "g", "rstd"], w=[("hT", k)])

    def ffn(fd):
        rmsnorm(fd['g'])
        for fc in range(NFC):
            wg, wgn = load_w(fd['wg'][fc].rearrange("p k j -> p (k j)"), 8)
            wu, wun = load_w(fd['wu'][fc].rearrange("p k j -> p (k j)"), 8)
            for tg in range(2):
                sl = slice(tg * 512, (tg + 1) * 512)
                pg, pgn = next_ps()
                pu, pun = next_ps()
                for k in range(8):
                    P.op('pe', mm(pg[:, :], wg[:, k * 128:(k + 1) * 128], hT[:, k, sl], k == 0, k == 7),
                         r=[wgn, "hT"], w=[pgn])
                for k in range(8):
                    P.op('pe', mm(pu[:, :], wu[:, k * 128:(k + 1) * 128], hT[:, k, sl], k == 0, k == 7),
                         r=[wun, "hT"], w=[pun])
                i = (fc * 2 + tg) % 2
                P.op('act', lambda e: e.activation(out=sg[i][:, :], in_=pg[:, :], func=AF.Silu),
                     r=[pgn], w=[f"sg{i}"])
                P.op('dve', lambda e: e.tensor_tensor(out=aT[:, fc, sl], in0=sg[i][:, :], in1=pu[:, :], op=ALU.mult),
                     r=[f"sg{i}", pun], w=[("aT", fc)])
        for mc in range(8):
            wd, wdn = load_w(fd['wd'][mc].rearrange("p k j -> p (k j)"), NFC)
            for tg in range(2):
                sl = slice(tg * 512, (tg + 1) * 512)
                pd, pdn = next_ps()
                for k in range(NFC):
                    P.op('pe', mm(pd[:, :], wd[:, k * 128:(k + 1) * 128], aT[:, k, sl], k == 0, k == NFC - 1),
                         r=[wdn, "aT"], w=[pdn])
                P.op('dve', lambda e: e.scalar_tensor_tensor(out=xT[:, mc, sl], in0=pd[:, :], scalar=0.5,
                                                             in1=xT[:, mc, sl], op0=ALU.mult, op1=ALU.add),
                     r=[pdn, ("xT", mc)], w=[("xT", mc)])

    for ps_i in range(NTOK // TP):
        tsl = slice(ps_i * TP, (ps_i + 1) * TP)
        for k in range(8):
            P.dma(xT[:, k, :], xT_d[k, :, tsl], w=[("xT", k)], sem="xT")
        if do_wout:
            for k in range(8):
                P.dma(yf[:, k, :], yT_d[k, :, tsl], w=[("yf", k)], sem="yf")
            for k in range(2):
                P.op('pool', lambda e: e.tensor_copy(out=hT[:, k, :], in_=yf[:, k, :]), r=[("yf", k)], w=[("hT", k)])
            for mc in range(2):
                wl, wln = load_w(wglu_d[mc].rearrange("p k j -> p (k j)"), 2)
                for tg in range(2):
                    sl = slice(tg * 512, (tg + 1) * 512)
                    pg, pgn = next_ps()
                    for k in range(2):
                        P.op('pe', mm(pg[:, :], wl[:, k * 128:(k + 1) * 128], hT[:, k, sl], k == 0, k == 1),
                             r=[wln, ("hT", 0), ("hT", 1)], w=[pgn])
                    i = tg
                    P.op('act', lambda e: e.activation(out=sg[i][:, :], in_=pg[:, :], func=AF.Sigmoid),
                         r=[pgn], w=[f"sg{i}"])
                    P.op('dve', lambda e: e.tensor_tensor(out=aT[:, mc, sl], in0=sg[i][:, :], in1=yf[:, mc, sl],
                                                          op=ALU.mult),
                         r=[f"sg{i}", ("yf", mc)], w=[("aT", mc)])
            for k in range(2, 8):
                P.op('pool', lambda e: e.tensor_copy(out=aT[:, k, :], in_=yf[:, k, :]), r=[("yf", k)], w=[("aT", k)])
            for mc in range(8):
                wl, wln = load_w(wout_d[mc].rearrange("p k j -> p (k j)"), 8)
                for tg in range(2):
                    sl = slice(tg * 512, (tg + 1) * 512)
                    pd, pdn = next_ps()
                    for k in range(8):
                        P.op('pe', mm(pd[:, :], wl[:, k * 128:(k + 1) * 128], aT[:, k, sl], k == 0, k == 7),
                             r=[wln, "aT"], w=[pdn])
                    P.op('dve', lambda e: e.tensor_tensor(out=xT[:, mc, sl], in0=pd[:, :], in1=xT[:, mc, sl],
                                                          op=ALU.add),
                         r=[pdn, ("xT", mc)], w=[("xT", mc)])
        for i in range(n_ffn):
            ffn(ffn_d[i])
        if do_win:
            rmsnorm(ming_d)
            for cc in range(NZC):
                wl, wln = load_w(win_d[cc].rearrange("p k j -> p (k j)"), 8)
                for tg in range(2):
                    sl = slice(tg * 512, (tg + 1) * 512)
                    pz, pzn = next_ps()
                    for k in range(8):
                        P.op('pe', mm(pz[:, :], wl[:, k * 128:(k + 1) * 128], hT[:, k, sl], k == 0, k == 7),
                             r=[wln, "hT"], w=[pzn])
                    i = (cc * 2 + tg) % 2
                    P.op('act', lambda e: e.activation(out=ev[i][:, :], in_=pz[:, :], func=AF.Copy),
                         r=[pzn], w=[f"ev{i}"])
                    P.dma(zT_d[cc, :, ps_i * TP + tg * 512: ps_i * TP + (tg + 1) * 512], ev[i][:, :],
                          r=[f"ev{i}"], w=[("zT", (ps_i, cc, tg))], sem=f"zst{i}")
        if do_final:
            rmsnorm(fing_d, final=True)
        for k in range(8):
            P.dma(xo_d[k, :, tsl], xT[:, k, :], r=[("xT", k)], w=[("xo", (ps_i, k))], sem="xo")
    return P.finish()


def _chunkT(w, nk):
    K, M = w.shape
    return np.ascontiguousarray(w.reshape(nk, 128, M // 128, 128).transpose(2, 1, 0, 3))


def _gcol(g):
    return np.ascontiguousarray(g.reshape(8, 128).T)


def _ffn_maps(prefix, g, wg, wu, wd):
    return {f"{prefix}_g": _gcol(g), f"{prefix}_wg": _chunkT(wg, 8), f"{prefix}_wu": _chunkT(wu, 8),
            f"{prefix}_wd": _chunkT(wd, NFC)}


def run_T(xT_full, common, do_wout, n_ffn, do_win, do_final, yT_full=None):
    nc = build_T(do_wout, n_ffn, do_win, do_final)
    maps = []
    for c in range(NCORES):
        m = dict(common)
        m["xT"] = np.ascontiguousarray(xT_full[:, c * NTOK:(c + 1) * NTOK].reshape(8, 128, NTOK))
        if do_wout:
            m["yT"] = np.ascontiguousarray(yT_full[:, c * NTOK:(c + 1) * NTOK].reshape(8, 128, NTOK))
        maps.append(m)
    res = run_bass_kernel_spmd(nc, maps, core_ids=list(range(NCORES)))
    out = {}
    out["xoT"] = np.concatenate([r["xoT"].reshape(1024, NTOK) for r in res.results], axis=1)
    if do_win:
        out["zT"] = np.concatenate([r["zT"].reshape(NZC * 128, NTOK) for r in res.results], axis=1)
    return out


PI = float(np.pi)
NCH = 1024
HB = 2048
GELU_C = 1.5957691216057308


def emit_gelu(P, dst, src_ps, tmp_a, tmp_b, names, src_names, bias=None):
    dn, an, bn = names
    if bias is None:
        P.op('act', lambda e: e.activation(out=tmp_a, in_=src_ps, func=AF.Copy), r=src_names, w=[an])
    else:
        P.op('act', lambda e: e.activation(out=tmp_a, in_=src_ps, func=AF.Identity, bias=bias[0]), r=src_names + [bias[1]], w=[an])
    P.op('pool', lambda e: e.tensor_tensor(out=tmp_b, in0=tmp_a, in1=tmp_a, op=ALU.mult), r=[an], w=[bn])
    P.op('dve', lambda e: e.tensor_scalar(out=tmp_b, in0=tmp_b, scalar1=0.044715, scalar2=1.0, op0=ALU.mult, op1=ALU.add),
         r=[bn], w=[bn])
    P.op('dve', lambda e: e.tensor_tensor(out=tmp_b, in0=tmp_b, in1=tmp_a, op=ALU.mult), r=[bn, an], w=[bn])
    P.op('act', lambda e: e.activation(out=tmp_b, in_=tmp_b, func=AF.Sigmoid, scale=GELU_C), r=[bn], w=[bn])
    P.op('dve', lambda e: e.tensor_tensor(out=dst, in0=tmp_a, in1=tmp_b, op=ALU.mult), r=[an, bn], w=[dn])


def build_M1(layer):
    P = Prog()
    lr2_d = P.dram_in("lr2", [128, 4]); li2_d = P.dram_in("li2", [128, 4]); ldt_d = P.dram_in("ldt", [128, 4])
    b1_d = P.dram_in("bst1", [128, 4, 16]); b2_d = P.dram_in("bst2", [128, 4, 16])
    c1_d = P.dram_in("cst1", [128, 4, 16]); c2_d = P.dram_in("cst2", [128, 4, 16])
    dcol_d = P.dram_in("dcol", [128, 4]); sgn_d = P.dram_in("sgn", [128, 2]); jv_d = P.dram_in("jv", [128, 4, 24])
    i128_d = P.dram_in("i128", [128, 128]); jsw_d = P.dram_in("jsw", [128, 128]); tmask_d = P.dram_in("tmask", [128, 128])
    uc_d = P.dram_in("uc", [4, 128, NCH])
    y5_d = P.dram_out("y5", [4, 128, NCH])
    hq_d = P.dram_in("hq", [64, SEQ]); hf_d = P.dram_in("hf", [64, SEQ])
    hv64_d = P.dram_in("hv64", [4, 64, 32, 64]); hv128_d = P.dram_in("hv128", [4, 128, 16, 64]); hg64_d = P.dram_in("hg64", [4, 64, 32, 64])
    lbl_d = P.dram_in("lbl", [64, 2]); gain_d = P.dram_in("hgain", [64, 64]); cm_d = P.dram_in("cmask", [64, 512])
    yh_d = P.dram_out("yh", [64, 128, 64])

    ps = [P.psum(f"ps{i}", [128, 512]) for i in range(8)]
    i128 = P.sbuf("i128", [128, 128], F32)
    P.dma(i128[:], i128_d[:, :], w=["i128"])

    import os
    PARTS = os.environ.get('M1PARTS', 's5,hg')
    NG5 = 4 if 's5' in PARTS else 0
    def T(name, shape, dt=F32):
        return P.sbuf(name, shape, dt)
    lr2 = T("lr2", [128, 4]); li2 = T("li2", [128, 4]); dt = T("dt", [128, 4]); sgn = T("sgn", [128, 2])
    b1 = T("b1", [128, 4, 16]); b2 = T("b2", [128, 4, 16]); c1 = T("c1", [128, 4, 16]); c2 = T("c2", [128, 4, 16])
    dcol = T("dcol", [128, 4]); jv = T("jv", [128, 4, 24]); jsw = T("jsw", [128, 128]); tmask = T("tmask", [128, 128])
    for t_, d_, n_ in [(lr2, lr2_d, "lr2"), (li2, li2_d, "li2"), (dt, ldt_d, "dt"), (sgn, sgn_d, "sgn"), (dcol, dcol_d, "dcol"),
                       (jsw, jsw_d, "jsw"), (tmask, tmask_d, "tmask")]:
        P.dma(t_[:], d_[:, :], w=[n_])
    for t_, d_, n_ in [(b1, b1_d, "b1"), (b2, b2_d, "b2"), (c1, c1_d, "c1"), (c2, c2_d, "c2"), (jv, jv_d, "jv")]:
        P.dma(t_[:], d_[:, :, :], w=[n_])
    V = lambda e: e

    def dv(fn, r, w):
        P.op('dve', fn, r=r, w=w)

    def ac(fn, r, w):
        P.op('act', fn, r=r, w=w)
    ac(lambda e: e.activation(out=dt[:], in_=dt[:], func=AF.Exp), ["dt"], ["dt"])
    lrdt = T("lrdt", [128, 4]); lidt = T("lidt", [128, 4])
    dv(lambda e: e.tensor_tensor(out=lrdt[:], in0=lr2[:], in1=dt[:], op=ALU.mult), ["lr2", "dt"], ["lrdt"])
    dv(lambda e: e.tensor_tensor(out=lidt[:], in0=li2[:], in1=dt[:], op=ALU.mult), ["li2", "dt"], ["lidt"])
    am = T("am", [128, 4, 24]); aa = T("aa", [128, 4, 24]); pr = T("pr", [128, 4, 24]); pi_ = T("pi", [128, 4, 24])
    tA = T("tA", [128, 4, 24]); tB = T("tB", [128, 4, 24]); tI = T("tI", [128, 4, 24], mybir.dt.int32)
    bc24 = lambda t_: t_[:, :, None].to_broadcast([128, 4, 24])
    dv(lambda e: e.tensor_tensor(out=am[:], in0=jv[:], in1=bc24(lrdt), op=ALU.mult), ["jv", "lrdt"], ["am"])
    ac(lambda e: e.activation(out=am[:], in_=am[:], func=AF.Exp), ["am"], ["am"])
    dv(lambda e: e.tensor_tensor(out=aa[:], in0=jv[:], in1=bc24(lidt), op=ALU.mult), ["jv", "lidt"], ["aa"])

    def sin_of(dst, dn, src, sn, shift):
        dv(lambda e: e.tensor_scalar(out=tA[:], in0=src[:], scalar1=shift, scalar2=None, op0=ALU.add), [sn], ["tA"])
        dv(lambda e: e.tensor_scalar(out=tB[:], in0=tA[:], scalar1=1.0 / (2 * PI), scalar2=64.5, op0=ALU.mult, op1=ALU.add),
           ["tA"], ["tB"])
        dv(lambda e: e.tensor_copy(out=tI[:], in_=tB[:]), ["tB"], ["tI"])
        dv(lambda e: e.tensor_copy(out=tB[:], in_=tI[:]), ["tI"], ["tB"])
        dv(lambda e: e.tensor_scalar(out=tB[:], in0=tB[:], scalar1=-64.0, scalar2=-2 * PI, op0=ALU.add, op1=ALU.mult),
           ["tB"], ["tB"])
        dv(lambda e: e.tensor_tensor(out=tA[:], in0=tA[:], in1=tB[:], op=ALU.add), ["tA", "tB"], ["tA"])
        dv(lambda e: e.tensor_scalar(out=tB[:], in0=tA[:], scalar1=-PI, scalar2=2 * PI, op0=ALU.is_lt, op1=ALU.mult),
           ["tA"], ["tB"])
        dv(lambda e: e.tensor_tensor(out=tA[:], in0=tA[:], in1=tB[:], op=ALU.add), ["tA", "tB"], ["tA"])
        dv(lambda e: e.tensor_scalar(out=tB[:], in0=tA[:], scalar1=PI, scalar2=-2 * PI, op0=ALU.is_gt, op1=ALU.mult),
           ["tA"], ["tB"])
        dv(lambda e: e.tensor_tensor(out=tA[:], in0=tA[:], in1=tB[:], op=ALU.add), ["tA", "tB"], ["tA"])
        dv(lambda e: e.tensor_scalar(out=tA[:], in0=tA[:], scalar1=-PI, scalar2=PI, op0=ALU.max, op1=ALU.min),
           ["tA"], ["tA"])
        ac(lambda e: e.activation(out=dst[:], in_=tA[:], func=AF.Sin), ["tA"], [dn])
    sin_of(pi_, "pi", aa, "aa", 0.0)
    sin_of(pr, "pr", aa, "aa", PI / 2)
    dv(lambda e: e.tensor_tensor(out=pr[:], in0=pr[:], in1=am[:], op=ALU.mult), ["pr", "am"], ["pr"])
    dv(lambda e: e.tensor_tensor(out=pi_[:], in0=pi_[:], in1=am[:], op=ALU.mult), ["pi", "am"], ["pi"])
    den = T("den", [128, 4]); t4a = T("t4a", [128, 4]); t4b = T("t4b", [128, 4]); nr = T("nr", [128, 4])
    gre = T("gre", [128, 4]); gim = T("gim", [128, 4])
    ar1 = pr[:, :, 8]; ai1 = pi_[:, :, 8]
    dv(lambda e: e.tensor_tensor(out=den[:], in0=lr2[:], in1=lr2[:], op=ALU.mult), ["lr2"], ["den"])
    dv(lambda e: e.tensor_tensor(out=t4a[:], in0=li2[:], in1=li2[:], op=ALU.mult), ["li2"], ["t4a"])
    dv(lambda e: e.tensor_tensor(out=den[:], in0=den[:], in1=t4a[:], op=ALU.add), ["den", "t4a"], ["den"])
    dv(lambda e: e.reciprocal(out=den[:], in_=den[:]), ["den"], ["den"])
    dv(lambda e: e.tensor_scalar(out=nr[:], in0=ar1, scalar1=-1.0, scalar2=None, op0=ALU.add), ["pr"], ["nr"])
    dv(lambda e: e.tensor_tensor(out=t4a[:], in0=nr[:], in1=lr2[:], op=ALU.mult), ["nr", "lr2"], ["t4a"])
    dv(lambda e: e.tensor_tensor(out=t4b[:], in0=ai1, in1=li2[:], op=ALU.mult), ["pi", "li2"], ["t4b"])
    dv(lambda e: e.tensor_tensor(out=t4a[:], in0=t4a[:], in1=t4b[:], op=ALU.add), ["t4a", "t4b"], ["t4a"])
    dv(lambda e: e.tensor_tensor(out=gre[:], in0=t4a[:], in1=den[:], op=ALU.mult), ["t4a", "den"], ["gre"])
    dv(lambda e: e.tensor_tensor(out=t4a[:], in0=ai1, in1=lr2[:], op=ALU.mult), ["pi", "lr2"], ["t4a"])
    dv(lambda e: e.tensor_tensor(out=t4b[:], in0=nr[:], in1=li2[:], op=ALU.mult), ["nr", "li2"], ["t4b"])
    dv(lambda e: e.tensor_tensor(out=t4a[:], in0=t4a[:], in1=t4b[:], op=ALU.subtract), ["t4a", "t4b"], ["t4a"])
    dv(lambda e: e.tensor_tensor(out=gim[:], in0=t4a[:], in1=den[:], op=ALU.mult), ["t4a", "den"], ["gim"])
    er = T("er", [128, 4, 8]); ei = T("ei", [128, 4, 8]); t8 = T("t8", [128, 4, 8])
    bc8 = lambda t_: t_[:, :, None].to_broadcast([128, 4, 8])
    dv(lambda e: e.tensor_tensor(out=er[:], in0=pr[:, :, 0:8], in1=bc8(gre), op=ALU.mult), ["pr", "gre"], ["er"])
    dv(lambda e: e.tensor_tensor(out=t8[:], in0=pi_[:, :, 0:8], in1=bc8(gim), op=ALU.mult), ["pi", "gim"], ["t8"])
    dv(lambda e: e.tensor_tensor(out=er[:], in0=er[:], in1=t8[:], op=ALU.subtract), ["er", "t8"], ["er"])
    dv(lambda e: e.tensor_tensor(out=ei[:], in0=pr[:, :, 0:8], in1=bc8(gim), op=ALU.mult), ["pr", "gim"], ["ei"])
    dv(lambda e: e.tensor_tensor(out=t8[:], in0=pi_[:, :, 0:8], in1=bc8(gre), op=ALU.mult), ["pi", "gre"], ["t8"])
    dv(lambda e: e.tensor_tensor(out=ei[:], in0=ei[:], in1=t8[:], op=ALU.add), ["ei", "t8"], ["ei"])
    m2 = T("m2", [128, 4, 8]); n1f = T("n1f", [128, 4, 8]); n2f = T("n2f", [128, 4, 8]); n1h = T("n1h", [128, 4, 8]); n2h = T("n2h", [128, 4, 8])
    dv(lambda e: e.tensor_scalar(out=m2[:], in0=ei[:], scalar1=sgn[:, 1:2], scalar2=None, op0=ALU.mult), ["ei", "sgn"], ["m2"])
    dv(lambda e: e.tensor_scalar(out=n1f[:], in0=pr[:, :, 8:16], scalar1=sgn[:, 0:1], scalar2=None, op0=ALU.mult), ["pr", "sgn"], ["n1f"])
    dv(lambda e: e.tensor_scalar(out=n2f[:], in0=pi_[:, :, 8:16], scalar1=-1.0, scalar2=None, op0=ALU.mult), ["pi"], ["n2f"])
    dv(lambda e: e.tensor_scalar(out=n1h[:], in0=pr[:, :, 16:24], scalar1=sgn[:, 0:1], scalar2=None, op0=ALU.mult), ["pr", "sgn"], ["n1h"])
    dv(lambda e: e.tensor_scalar(out=n2h[:], in0=pi_[:, :, 16:24], scalar1=-1.0, scalar2=None, op0=ALU.mult), ["pi"], ["n2h"])
    bcm = T("bcm", [128, 4, 8, 16]); ccm = T("ccm", [128, 4, 8, 16]); qmm = T("qmm", [128, 4, 8, 16]); t816 = T("t816", [128, 4, 8, 16])
    S4 = [128, 4, 8, 16]

    def outer(dst, dn, st1, s1n, co1, c1n, st2, s2n, co2, c2n):
        dv(lambda e: e.tensor_tensor(out=dst[:], in0=st1[:, :, None, :].to_broadcast(S4), in1=co1[:, :, :, None].to_broadcast(S4),
                                     op=ALU.mult), [s1n, c1n], [dn])
        dv(lambda e: e.tensor_tensor(out=t816[:], in0=st2[:, :, None, :].to_broadcast(S4), in1=co2[:, :, :, None].to_broadcast(S4),
                                     op=ALU.mult), [s2n, c2n], ["t816"])
        dv(lambda e: e.tensor_tensor(out=dst[:], in0=dst[:], in1=t816[:], op=ALU.add), [dn, "t816"], [dn])
    outer(bcm, "bcm", b1, "b1", er, "er", b2, "b2", m2, "m2")
    outer(ccm, "ccm", c1, "c1", n1f, "n1f", c2, "c2", n2f, "n2f")
    outer(qmm, "qmm", c1, "c1", n1h, "n1h", c2, "c2", n2h, "n2h")
    tz = T("tz", [128, 4, 128]); bct = T("bct", [128, 4, 128])
    for g in range(4):
        pg, pgn = ps[g % 2], f"ps{g % 2}"
        P.op('pe', lambda e: e.matmul(pg[:, 0:128], lhsT=bcm[:, g].rearrange("p s h -> p (s h)"),
                                      rhs=qmm[:, g].rearrange("p s h -> p (s h)"), start=True, stop=True),
             r=["bcm", "qmm"], w=[pgn])
        dv(lambda e: e.tensor_tensor(out=tz[:, g, :], in0=pg[:, 0:128], in1=tmask[:], op=ALU.mult), [pgn, "tmask"], [("tz", g)])
        dv(lambda e: e.scalar_tensor_tensor(out=tz[:, g, :], in0=i128[:], scalar=dcol[:, g:g + 1], in1=tz[:, g, :],
                                            op0=ALU.mult, op1=ALU.add), ["i128", "dcol", ("tz", g)], [("tz", g)])
        pt, ptn = ps[2 + g % 2], f"ps{2 + g % 2}"
        P.op('pe', lambda e: e.transpose(out=pt[:, 0:128], in_=bcm[:, g].rearrange("p s h -> p (s h)"), identity=i128[:]),
             r=["bcm", "i128"], w=[ptn])
        ac(lambda e: e.activation(out=bct[:, g, :], in_=pt[:, 0:128], func=AF.Copy), [ptn], [("bct", g)])
    NK = 10
    a8r = T("a8r", [128, NK, 4]); a8i = T("a8i", [128, NK, 4]); rm = T("rm", [128, NK * 4, 128])
    dv(lambda e: e.tensor_copy(out=a8r[:, 0, :], in_=pr[:, :, 15]), ["pr"], ["a8r"])
    dv(lambda e: e.tensor_copy(out=a8i[:, 0, :], in_=pi_[:, :, 15]), ["pi"], ["a8i"])
    for k in range(1, NK):
        dv(lambda e: e.tensor_tensor(out=t4a[:], in0=a8r[:, k - 1, :], in1=a8r[:, k - 1, :], op=ALU.mult), ["a8r"], ["t4a"])
        dv(lambda e: e.tensor_tensor(out=t4b[:], in0=a8i[:, k - 1, :], in1=a8i[:, k - 1, :], op=ALU.mult), ["a8i"], ["t4b"])
        dv(lambda e: e.tensor_tensor(out=a8r[:, k, :], in0=t4a[:], in1=t4b[:], op=ALU.subtract), ["t4a", "t4b"], ["a8r"])
        dv(lambda e: e.tensor_tensor(out=t4a[:], in0=a8r[:, k - 1, :], in1=a8i[:, k - 1, :], op=ALU.mult), ["a8r", "a8i"], ["t4a"])
        dv(lambda e: e.tensor_scalar(out=a8i[:, k, :], in0=t4a[:], scalar1=2.0, scalar2=None, op0=ALU.mult), ["t4a"], ["a8i"])
    a8is = T("a8is", [128, NK, 4])
    dv(lambda e: e.tensor_scalar(out=a8is[:], in0=a8i[:], scalar1=sgn[:, 0:1], scalar2=None, op0=ALU.mult), ["a8i", "sgn"], ["a8is"])
    for k in range(NK):
        for g in range(4):
            i = k * 4 + g
            dv(lambda e: e.tensor_scalar(out=rm[:, i, :], in0=i128[:], scalar1=a8r[:, k, g:g + 1], scalar2=None, op0=ALU.mult),
               ["i128", "a8r"], [("rm", i)])
            dv(lambda e: e.scalar_tensor_tensor(out=rm[:, i, :], in0=jsw[:], scalar=a8is[:, k, g:g + 1], in1=rm[:, i, :],
                                                op0=ALU.mult, op1=ALU.add), ["jsw", "a8is", ("rm", i)], [("rm", i)])
    uc = [T(f"uc{i}", [128, NCH]) for i in range(2)]
    xs = T("xs", [128, NCH + 1]); ga = T("ga", [128, 512]); gb = T("gb", [128, 512]); yo = [T(f"yo{i}", [128, 512]) for i in range(2)]
    P.op('pool', lambda e: e.memset(xs[:, 0:1], 0.0), w=[("xs", "z")])
    for g in range(NG5):
        u = uc[g % 2]; un = f"uc{g % 2}"
        P.dma(u[:], uc_d[g], w=[un])
        for h in range(2):
            pp, ppn = ps[h], f"ps{h}"
            P.op('pe', lambda e: e.matmul(pp[:, :], lhsT=bct[:, g, :], rhs=u[:, h * 512:(h + 1) * 512], start=True, stop=True),
                 r=[("bct", g), un], w=[ppn])
            ac(lambda e: e.activation(out=xs[:, 1 + h * 512:1 + (h + 1) * 512], in_=pp[:, :], func=AF.Copy), [ppn], [("xs", "x")])
        for k in range(NK):
            d = 1 << k
            n = NCH - d
            pieces = [(0, min(512, n))] + ([(512, n)] if n > 512 else [])
            for h, (a, b) in enumerate(pieces):
                pp, ppn = ps[2 + h], f"ps{2 + h}"
                P.op('pe', lambda e: e.matmul(pp[:, 0:b - a], lhsT=rm[:, k * 4 + g, :], rhs=xs[:, 1 + a:1 + b], start=True, stop=True),
                     r=[("rm", k * 4 + g), ("xs", "x")], w=[ppn])
            for h, (a, b) in enumerate(pieces):
                pp, ppn = ps[2 + h], f"ps{2 + h}"
                dv(lambda e: e.tensor_tensor(out=xs[:, 1 + d + a:1 + d + b], in0=pp[:, 0:b - a], in1=xs[:, 1 + d + a:1 + d + b], op=ALU.add),
                   [ppn, ("xs", "x")], [("xs", "x")])
        for h in range(2):
            pp, ppn = ps[4 + h], f"ps{4 + h}"
            P.op('pe', lambda e: e.matmul(pp[:, :], lhsT=tz[:, g, :], rhs=u[:, h * 512:(h + 1) * 512], start=True, stop=False),
                 r=[("tz", g), un], w=[ppn])
            P.op('pe', lambda e: e.matmul(pp[:, :], lhsT=ccm[:, g].rearrange("p s h -> p (s h)"), rhs=xs[:, h * 512:(h + 1) * 512],
                                          start=False, stop=True), r=["ccm", ("xs", "x"), ("xs", "z")], w=[ppn])
            emit_gelu(P, yo[h][:], pp[:, :], ga[:], gb[:], (f"yo{h}", "ga", "gb"), [ppn])
            P.dma(y5_d[g, :, h * 512:(h + 1) * 512], yo[h][:], r=[f"yo{h}"], w=[("y5", (g, h))], sem=f"y5s{h}")

    if 'hg' not in PARTS:
        return P.finish()
    lbl = T("lbl", [64, 2]); lb = T("lb", [64, 1]); oml = T("oml", [64, 1]); gain = T("gain", [64, 64]); cm = T("cm", [64, 512])
    P.dma(lbl[:], lbl_d[:, :], w=["lbl"]); P.dma(gain[:], gain_d[:, :], w=["gain"]); P.dma(cm[:], cm_d[:, :], w=["cm"])
    if layer == 0:
        P.op('pool', lambda e: e.memset(lb[:], 0.0), w=["lb"])
    else:
        dv(lambda e: e.tensor_tensor(out=lb[:], in0=lbl[:, 1:2], in1=lbl[:, 0:1], op=ALU.subtract), ["lbl"], ["lb"])
        ac(lambda e: e.activation(out=lb[:], in_=lb[:], func=AF.Sigmoid), ["lb"], ["lb"])
    dv(lambda e: e.tensor_scalar(out=oml[:], in0=lb[:], scalar1=-1.0, scalar2=1.0, op0=ALU.mult, op1=ALU.add), ["lb"], ["oml"])
    rmask = T("rmask", [64, HB])
    P.op('pool', lambda e: e.memset(rmask[:], 1.0), w=["rmask"])
    P.op('pool', lambda e: e.memset(rmask[:, 0:HB:64], 0.0), w=["rmask"])
    hq = T("hq", [64, HB]); hf = T("hf", [64, HB]); sig = T("sig", [64, HB]); fbuf = T("fbuf", [64, HB]); bb = T("bb", [64, HB])
    eb = T("eb", [64, HB]); enb = T("enb", [64, HB]); qtil = T("qtil", [64, HB], BF16); ktil = T("ktil", [64, HB])
    ktb = T("ktb", [64, HB], BF16); khat = T("khat", [64, HB]); dec = T("dec", [64, 32])
    khT = T("khT", [128, 16, 64], BF16); v64f = T("v64f", [64, 32, 64]); v64 = T("v64", [64, 32, 64], BF16)
    v128f = T("v128f", [128, 16, 64]); v128 = T("v128", [128, 16, 64], BF16)
    g64 = T("g64", [64, 32, 64]); sall = T("sall", [64, 33, 64]); sbf = T("sbf", [64, 32, 64], BF16)
    attm = T("attm", [64, 512], BF16); osb = T("osb", [64, 8, 64]); osq = T("osq", [64, 8, 64]); ss = T("ss", [64, 8])
    yh = [T(f"yh{i}", [64, 8, 64]) for i in range(2)]
    P.op('pool', lambda e: e.memset(sall[:, 0, :], 0.0), w=[("sall", 0)])
    for blk in range(SEQ // HB):
        tsl = slice(blk * HB, (blk + 1) * HB)
        P.dma(hq[:], hq_d[:, tsl], w=["hq"]); P.dma(hf[:], hf_d[:, tsl], w=["hf"])
        P.dma(v64f[:], hv64_d[blk], w=["v64f"])
        P.dma(v128f[:], hv128_d[blk], w=["v128f"])
        P.dma(g64[:], hg64_d[blk], w=["g64"])
        P.op('pool', lambda e: e.tensor_copy(out=v64[:], in_=v64f[:]), r=["v64f"], w=["v64"])
        P.op('pool', lambda e: e.tensor_copy(out=v128[:], in_=v128f[:]), r=["v128f"], w=["v128"])
        ac(lambda e: e.activation(out=sig[:], in_=hf[:], func=AF.Sigmoid), ["hf"], ["sig"])
        dv(lambda e: e.tensor_scalar(out=fbuf[:], in0=sig[:], scalar1=oml[:, 0:1], scalar2=lb[:, 0:1], op0=ALU.mult, op1=ALU.add),
           ["sig", "oml", "lb"], ["fbuf"])
        ac(lambda e: e.activation(out=fbuf[:], in_=fbuf[:], func=AF.Ln), ["fbuf"], ["fbuf"])
        dv(lambda e: e.tensor_tensor_scan(out=bb[:], data0=rmask[:], data1=fbuf[:], initial=0.0, op0=ALU.mult, op1=ALU.add),
           ["rmask", "fbuf"], ["bb"])
        ac(lambda e: e.activation(out=eb[:], in_=bb[:], func=AF.Exp), ["bb"], ["eb"])
        ac(lambda e: e.activation(out=enb[:], in_=bb[:], func=AF.Exp, scale=-1.0), ["bb"], ["enb"])
        ac(lambda e: e.activation(out=sig[:], in_=hf[:], func=AF.Sigmoid, scale=-1.0), ["hf"], ["sig"])
        dv(lambda e: e.scalar_tensor_tensor(out=ktil[:], in0=sig[:], scalar=oml[:, 0:1], in1=enb[:], op0=ALU.mult, op1=ALU.mult),
           ["sig", "oml", "enb"], ["ktil"])
        P.op('pool', lambda e: e.tensor_copy(out=ktb[:], in_=ktil[:]), r=["ktil"], w=["ktb"])
        ac(lambda e: e.activation(out=hq[:], in_=hq[:], func=AF.Silu), ["hq"], ["hq"])
        dv(lambda e: e.tensor_tensor(out=qtil[:], in0=hq[:], in1=eb[:], op=ALU.mult), ["hq", "eb"], ["qtil"])
        dv(lambda e: e.tensor_copy(out=dec[:], in_=eb[:, 63:HB:64]), ["eb"], ["dec"])
        dv(lambda e: e.tensor_tensor(out=khat[:].rearrange("p (c s) -> p c s", s=64), in0=ktil[:].rearrange("p (c s) -> p c s", s=64),
                                     in1=dec[:, :, None].to_broadcast([64, 32, 64]), op=ALU.mult), ["ktil", "dec"], ["khat"])
        ac(lambda e: e.activation(out=g64[:], in_=g64[:], func=AF.Silu), ["g64"], ["g64"])
        for hlf in range(2):
            pp, ppn = ps[6 + hlf], f"ps{6 + hlf}"
            for j in range(8):
                jj = hlf * 8 + j
                P.op('pe', lambda e: e.transpose(out=pp[:, j * 64:(j + 1) * 64], in_=khat[:, jj * 128:(jj + 1) * 128],
                                                 identity=i128[0:64, 0:64]), r=["khat", "i128"], w=[ppn])
            ac(lambda e: e.activation(out=khT[:, hlf * 8:(hlf + 1) * 8, :].rearrange("p a b -> p (a b)"), in_=pp[:, :], func=AF.Copy),
               [ppn], ["khT"])
        for cg in range(4):
            for ci in range(8):
                c = cg * 8 + ci
                po = (c % 2) * 64
                pu, pun = ps[c % 2], f"ps{c % 2}"
                sl_ = slice((ci // 2) * 64, (ci // 2 + 1) * 64)
                P.op('pe', lambda e: e.matmul(pu[0:64, sl_], lhsT=khT[po:po + 64, c // 2, :],
                                              rhs=v128[po:po + 64, c // 2, :], start=True, stop=True),
                     r=["khT", "v128"], w=[pun])
            for ci in range(8):
                c = cg * 8 + ci
                pu, pun = ps[c % 2], f"ps{c % 2}"
                sl_ = slice((ci // 2) * 64, (ci // 2 + 1) * 64)
                dv(lambda e: e.scalar_tensor_tensor(out=sall[:, c + 1, :], in0=sall[:, c, :], scalar=dec[:, c:c + 1],
                                                    in1=pu[0:64, sl_], op0=ALU.mult, op1=ALU.add),
                   [("sall", c), "dec", pun], [("sall", c + 1)])
        P.op('pool', lambda e: e.tensor_copy(out=sbf[:], in_=sall[:, 0:32, :]), r=["sall"], w=["sbf"])
        for cg in range(4):
            pa, pan = ps[2 + cg % 2], f"ps{2 + cg % 2}"
            po_, pon = ps[4 + cg % 2], f"ps{4 + cg % 2}"
            for ci in range(8):
                c = cg * 8 + ci
                cs = slice(c * 64, (c + 1) * 64)
                P.op('pe', lambda e: e.matmul(pa[0:64, ci * 64:(ci + 1) * 64], lhsT=ktb[:, cs], rhs=qtil[:, cs], start=True, stop=True),
                     r=["ktb", "qtil"], w=[pan])
            dv(lambda e: e.tensor_tensor(out=attm[:], in0=pa[0:64, :], in1=cm[:], op=ALU.mult), [pan, "cm"], ["attm"])
            for ci in range(8):
                c = cg * 8 + ci
                cs = slice(c * 64, (c + 1) * 64)
                P.op('pe', lambda e: e.matmul(po_[0:64, ci * 64:(ci + 1) * 64], lhsT=attm[:, ci * 64:(ci + 1) * 64], rhs=v64[:, c, :],
                                              start=True, stop=False), r=["attm", "v64"], w=[pon])
                P.op('pe', lambda e: e.matmul(po_[0:64, ci * 64:(ci + 1) * 64], lhsT=qtil[:, cs], rhs=sbf[:, c, :],
                                              start=False, stop=True), r=["qtil", "sbf"], w=[pon])
            ac(lambda e: e.activation(out=osb[:].rearrange("p a b -> p (a b)"), in_=po_[0:64, :], func=AF.Copy), [pon], ["osb"])
            P.op('pool', lambda e: e.tensor_tensor(out=osq[:], in0=osb[:], in1=osb[:], op=ALU.mult), r=["osb"], w=["osq"])
            dv(lambda e: e.tensor_reduce(out=ss[:], in_=osq[:], axis=AX.X, op=ALU.add), ["osq"], ["ss"])
            ac(lambda e: e.activation(out=ss[:], in_=ss[:], func=AF.Sqrt, scale=1.0 / 64, bias=EPS), ["ss"], ["ss"])
            dv(lambda e: e.reciprocal(out=ss[:], in_=ss[:]), ["ss"], ["ss"])
            y_ = yh[cg % 2]; yn = f"yh{cg % 2}"
            dv(lambda e: e.tensor_tensor(out=y_[:], in0=osb[:], in1=ss[:, :, None].to_broadcast([64, 8, 64]), op=ALU.mult),
               ["osb", "ss"], [yn])
            dv(lambda e: e.tensor_tensor(out=y_[:], in0=y_[:], in1=gain[:, None, :].to_broadcast([64, 8, 64]), op=ALU.mult),
               [yn, "gain"], [yn])
            dv(lambda e: e.tensor_tensor(out=y_[:], in0=y_[:], in1=g64[:, cg * 8:(cg + 1) * 8, :], op=ALU.mult), [yn, "g64"], [yn])
            c0 = blk * 32 + cg * 8
            P.dma(yh_d[:, c0:c0 + 8, :], y_[:], r=[yn], w=[("yhd", c0)], sem=f"yhs{cg % 2}")
        dv(lambda e: e.tensor_copy(out=sall[:, 0, :], in_=sall[:, 32, :]), [("sall", 32), "sbf"], [("sall", 0)])
    return P.finish()


def _m1_consts():
    jv1 = np.array([7, 6, 5, 4, 3, 2, 1, 0, 1, 2, 3, 4, 5, 6, 7, 8, -7, -6, -5, -4, -3, -2, -1, 0], np.float32)
    jv = np.ascontiguousarray(np.broadcast_to(jv1, (128, 4, 24))).astype(np.float32)
    sgn = np.ones((128, 2), np.float32); sgn[64:, 0] = -1; sgn[:64, 1] = -1
    i128 = np.eye(128, dtype=np.float32)
    jsw = np.zeros((128, 128), np.float32)
    for k in range(128):
        jsw[k, (k + 64) % 128] = 1
    s_idx = np.arange(128) // 16
    tmask = (s_idx[None, :] >= s_idx[:, None]).astype(np.float32)
    st = np.arange(64)
    cm = np.tile((st[:, None] <= st[None, :]).astype(np.float32), (1, 8))
    return dict(jv=jv, sgn=sgn, i128=i128, jsw=jsw, tmask=tmask, cmask=cm)


def run_M1(zT, inp, l):
    nc = build_M1(l)
    cst = _m1_consts()
    maps = []
    for c in range(NCORES):
        b = c // 4
        tsl = slice(b * SEQ, (b + 1) * SEQ)
        m = dict(cst)
        gs = [4 * (c % 4) + gi for gi in range(4)]
        st2 = lambda a: np.concatenate([a, a], axis=0)
        m["lr2"] = np.stack([st2(inp['s5_lambda_re'][l, g]) for g in gs], 1).astype(np.float32)
        m["li2"] = np.stack([st2(inp['s5_lambda_im'][l, g]) for g in gs], 1).astype(np.float32)
        m["ldt"] = np.ascontiguousarray(np.broadcast_to(np.array([inp['s5_log_dt'][l, g] for g in gs], np.float32), (128, 4)))
        m["bst1"] = np.stack([np.concatenate([inp['s5_b_re'][l, g], inp['s5_b_im'][l, g]], 0) for g in gs], 1)
        m["bst2"] = np.stack([np.concatenate([inp['s5_b_im'][l, g], inp['s5_b_re'][l, g]], 0) for g in gs], 1)
        m["cst1"] = np.stack([np.concatenate([inp['s5_c_re'][l, g].T, inp['s5_c_im'][l, g].T], 0) for g in gs], 1)
        m["cst2"] = np.stack([np.concatenate([inp['s5_c_im'][l, g].T, inp['s5_c_re'][l, g].T], 0) for g in gs], 1)
        m["dcol"] = np.stack([np.tile(inp['s5_d'][l, g], 8) for g in gs], 1).astype(np.float32)
        m["uc"] = np.stack([zT[g * 16:(g + 1) * 16, tsl].reshape(16, NCH, 8).transpose(2, 0, 1).reshape(128, NCH) for g in gs], 0)
        h = c % 4
        m["hq"] = zT[256 + 64 * h:256 + 64 * (h + 1), tsl]
        m["hf"] = zT[512 + 64 * h:512 + 64 * (h + 1), tsl]
        v = zT[768 + 64 * h:768 + 64 * (h + 1), tsl].T
        gg = zT[1024 + 64 * h:1024 + 64 * (h + 1), tsl].T
        m["hv64"] = v.reshape(4, 32, 64, 64).transpose(0, 2, 1, 3)
        m["hv128"] = v.reshape(4, 16, 128, 64).transpose(0, 2, 1, 3)
        m["hg64"] = gg.reshape(4, 32, 64, 64).transpose(0, 2, 1, 3)
        m["lbl"] = inp['hgrn_lb_logits'][:, 64 * h:64 * (h + 1)].T
        m["hgain"] = np.broadcast_to(inp['hgrn_norm'][l][None, :], (64, 64))
        maps.append({k: np.ascontiguousarray(v_, dtype=np.float32) for k, v_ in m.items()})
    res = run_bass_kernel_spmd(nc, maps, core_ids=list(range(NCORES)))
    y5T = np.zeros((256, BATCH * SEQ), np.float32)
    yhT = np.zeros((256, BATCH * SEQ), np.float32)
    for c in range(NCORES):
        b = c // 4
        tsl = slice(b * SEQ, (b + 1) * SEQ)
        r = res.results[c]
        for gi in range(4):
            g = 4 * (c % 4) + gi
            y5T[g * 16:(g + 1) * 16, tsl] = r["y5"][gi].reshape(8, 16, NCH).transpose(1, 2, 0).reshape(16, SEQ)
        h = c % 4
        yhT[64 * h:64 * (h + 1), tsl] = r["yh"].transpose(1, 0, 2).reshape(SEQ, 64).T
    return y5T, yhT


OFF_F = 8320
NF_F = OFF_F + 8192 + 128
NEGB = -30000.0
TINY = 1e-30
NQT = 32
SCALE = 0.125


def build_M2():
    P = Prog()
    nc = P.nc
    q65_d = P.dram_in("q65", [4, 65, NQT * 128])
    ksl_d = P.dram_in("kslX", [64, SEQ]); kwn_d = P.dram_in("kwnX", [64, SEQ])
    vsl_d = P.dram_in("vslX", [128, 64 * 64]); vwn_d = P.dram_in("vwnX", [128, 64 * 64])
    kcr_d = P.dram_in("kcrX", [32, 64, 512]); vcr_d = P.dram_in("vcrX", [32, 64, 512])
    w1k_d = P.dram_in("w1k", [64, 32 * 64]); w1v_d = P.dram_in("w1v", [64, 32 * 64])
    w2k_d = P.dram_in("w2k", [64, 64]); w2v_d = P.dram_in("w2v", [64, 64])
    posk_d = P.dram_in("poskT", [64, 32]); posv_d = P.dram_in("posvT", [64, 32])
    fs_d = P.dram_in("fs", [4, NF_F]); fw_d = P.dram_in("fw", [4, NF_F])
    eall_d = P.dram_in("eall", [128, SEQ]); mm_d = P.dram_in("mmat", [128, 4 * 128])
    madd_d = P.dram_in("madd", [NQT, 128, 128]); glog_d = P.dram_in("glog", [128, NQT * 12])
    i128_d = P.dram_in("i128", [128, 128])
    parity_dummy = None
    y_d = P.dram_out("ynsa", [128, NQT, 256])

    T = P.sbuf
    ps = [P.psum(f"ps{i}", [128, 512]) for i in range(8)]
    psA = [(ps[0], "ps0"), (ps[1], "ps1")]
    psL, psO1, psO2, psO3 = ps[2], ps[4], ps[5], ps[6]
    psBs = [(ps[3], "psB0"), (ps[7], "psB1")]
    psS = ps[4]
    stg = [T(f"stg{i}", [128, 2048], F32) for i in range(2)]
    sctr = [0]

    def stage():
        i = sctr[0] % 2
        sctr[0] += 1
        return stg[i], f"stg{i}"

    i128 = T("i128", [128, 128], F32); P.dma(i128[:], i128_d[:, :], w=["i128"])
    ones = T("ones", [128, 128], BF16); P.op('pool', lambda e: e.memset(ones[:], 1.0), w=["ones"])
    qa = T("qa", [65, 4, NQT * 128], BF16)
    ksl = T("ksl", [65, SEQ], BF16); kwn = T("kwn", [64, SEQ], BF16)
    vsl = T("vsl", [128, 64, 65], BF16); vwn = T("vwn", [128, 64, 65], BF16)
    eall = T("eall", [128, SEQ], BF16); mmt = T("mmt", [128, 4, 128], F32)
    kcmp = T("kcmp", [64, 512], BF16); vcmp = T("vcmp", [128, 4, 65], BF16)
    bs = T("bs", [128, 11, 512], F32)
    glog = T("glog", [128, NQT * 12], F32)

    for r in range(4):
        for hh in range(2):
            st, sn = stage()
            P.dma(st[0:65, :], q65_d[r, :, hh * 2048:(hh + 1) * 2048], w=[sn])
            P.op('pool', lambda e: e.tensor_copy(out=qa[0:64, r, hh * 2048:(hh + 1) * 2048], in_=st[0:64, :]), r=[sn], w=["qa"])
            P.op('act', lambda e: e.activation(out=qa[64:65, r, hh * 2048:(hh + 1) * 2048], in_=st[64:65, :], func=AF.Copy, scale=8.0),
                 r=[sn], w=["qa"])
    for (dst, dn, src) in [(ksl, "ksl", ksl_d), (kwn, "kwn", kwn_d)]:
        for pc in range(4):
            st, sn = stage()
            P.dma(st[0:64, :], src[:, pc * 2048:(pc + 1) * 2048], w=[sn])
            P.op('pool', lambda e: e.tensor_copy(out=dst[0:64, pc * 2048:(pc + 1) * 2048], in_=st[0:64, :]), r=[sn], w=[dn])
    P.op('pool', lambda e: e.memset(ksl[64:65, :], 1.0), w=["ksl"])
    for (dst, dn, src) in [(vsl, "vsl", vsl_d), (vwn, "vwn", vwn_d)]:
        for pc in range(2):
            st, sn = stage()
            P.dma(st[:, :], src[:, pc * 2048:(pc + 1) * 2048], w=[sn])
            P.op('pool', lambda e: e.tensor_copy(out=dst[:, pc * 32:(pc + 1) * 32, 0:64], in_=st[:, :].rearrange("p (a b) -> p a b", b=64)),
                 r=[sn], w=[dn])
        P.op('pool', lambda e: e.memset(dst[:, :, 64:65], 1.0), w=[dn])
    for pc in range(4):
        st, sn = stage()
        P.dma(st[:, :], eall_d[:, pc * 2048:(pc + 1) * 2048], w=[sn])
        P.op('pool', lambda e: e.tensor_copy(out=eall[:, pc * 2048:(pc + 1) * 2048], in_=st[:, :]), r=[sn], w=["eall"])
    P.dma(mmt[:].rearrange("p a b -> p (a b)"), mm_d[:, :], w=["mmt"])
    P.dma(glog[:], glog_d[:, :], w=["glog"])
    P.op('act', lambda e: e.activation(out=glog[:], in_=glog[:], func=AF.Sigmoid), r=["glog"], w=["glog"])
    for j in range(11):
        tab = fs_d if j < 9 else fw_d
        dl = 128 * j if j < 9 else 128 * (j - 5)
        src = bass.AP(tab.tensor, OFF_F + dl - 127, [(1, 128), (NF_F, 4), (1, 128)])
        P.dma(bs[:, j, :].rearrange("p (r q) -> p r q", q=128), src, w=[("bs", j)], sem="bs")

    w1 = T("w1", [64, 32 * 64], BF16); w2 = T("w2", [64, 64], BF16); posT = T("posT", [64, 32], BF16)
    w2f = T("w2f", [64, 64], F32); posf = T("posf", [64, 32], F32)
    pb = T("pb", [64, 1], F32); xj = [T(f"xj{i}", [64, 512], BF16) for i in range(2)]
    ga = T("ga", [64, 512], F32); gb = T("gb", [64, 512], F32); gel = T("gel", [64, 512], BF16)
    for which in range(2):
        w1_d, w2_d, pos_d, x_d = [(w1k_d, w2k_d, posk_d, kcr_d), (w1v_d, w2v_d, posv_d, vcr_d)][which]
        st, sn = stage()
        P.dma(st[0:64, :], w1_d[:, :], w=[sn])
        P.op('pool', lambda e: e.tensor_copy(out=w1[:], in_=st[0:64, :]), r=[sn], w=["w1"])
        P.dma(w2f[:], w2_d[:, :], w=["w2f"]); P.dma(posf[:], pos_d[:, :], w=["posf"])
        P.op('pool', lambda e: e.tensor_copy(out=w2[:], in_=w2f[:]), r=["w2f"], w=["w2"])
        P.op('pool', lambda e: e.tensor_copy(out=posT[:], in_=posf[:]), r=["posf"], w=["posT"])
        for j in range(32):
            P.op('pe', lambda e: e.matmul(psL[0:64, 0:1], lhsT=w1[:, j * 64:(j + 1) * 64], rhs=posT[:, j:j + 1], start=(j == 0), stop=(j == 31)),
                 r=["w1", "posT"], w=["psL"])
        P.op('act', lambda e: e.activation(out=pb[:], in_=psL[0:64, 0:1], func=AF.Copy), r=["psL"], w=["pb"])
        for j in range(32):
            st, sn = stage()
            P.dma(st[0:64, 0:512], x_d[j], w=[sn])
            P.op('pool', lambda e: e.tensor_copy(out=xj[j % 2][:], in_=st[0:64, 0:512]), r=[sn], w=[f"xj{j % 2}"])
            P.op('pe', lambda e: e.matmul(ps[3][0:64, :], lhsT=w1[:, j * 64:(j + 1) * 64], rhs=xj[j % 2][:], start=(j == 0), stop=(j == 31)),
                 r=["w1", f"xj{j % 2}"], w=["psB0"])
        emit_gelu(P, gel[:], ps[3][0:64, :], ga[:], gb[:], ("gel", "ga", "gb"), ["psB0"], bias=(pb[:, 0:1], "pb"))
        if which == 0:
            P.op('pe', lambda e: e.matmul(psO1[0:64, :], lhsT=w2[:], rhs=gel[:], start=True, stop=True), r=["w2", "gel"], w=["psO1"])
            P.op('act', lambda e: e.activation(out=kcmp[:], in_=psO1[0:64, :], func=AF.Copy), r=["psO1"], w=["kcmp"])
        else:
            for ci in range(4):
                P.op('pe', lambda e: e.matmul(psO2[:, ci * 64:(ci + 1) * 64], lhsT=gel[:, ci * 128:(ci + 1) * 128], rhs=w2[:], start=True, stop=True),
                     r=["w2", "gel"], w=["psO2"])
            P.op('act', lambda e: e.activation(out=vcmp[:, :, 0:64], in_=psO2[:, 0:256].rearrange("p (a b) -> p a b", b=64), func=AF.Copy),
                 r=["psO2"], w=["vcmp"])
            P.op('pool', lambda e: e.memset(vcmp[:, :, 64:65], 1.0), w=["vcmp"])

    bc = [T(f"bc{i}", [128, 512], F32) for i in range(2)]
    tmpb = [T(f"tmpb{i}", [128, 512], F32) for i in range(2)]
    ebuf = [T(f"ebuf{i}", [128, 512], BF16) for i in range(2)]
    pbuf = [T(f"pbuf{i}", [128, 512], BF16) for i in range(2)]
    ec = T("ec", [128, 4, 512], BF16)
    rlb = T("rlb", [128, 512], F32); pnb = T("pnb", [128, 512], F32); impT = T("impT", [128, 4, 128], F32)
    madd = [T(f"madd{i}", [128, 128], F32) for i in range(2)]
    score = T("score", [128, 128], F32); sc2 = T("sc2", [128, 128], F32); m8a = T("m8a", [128, 8], F32); m8b = T("m8b", [128, 8], F32)
    self_ = T("self", [128, 128], F32); selT = T("selT", [128, 128], BF16)
    osb = [T(f"osb{i}", [128, 260], F32) for i in range(3)]
    lc = T("lc", [128, 3, 4], F32); coef = T("coef", [128, 3, 4], F32)
    yb = [T(f"yb{i}", [128, 256], F32) for i in range(2)]
    rot = dict(a=0, t=0, e=0, p=0, b=0)

    def nxt(k, n=2):
        v = rot[k] % n
        rot[k] += 1
        return v

    def softmax_tile(A_src, An, bias_ap, bias_names, eout, eout_name):
        if bias_ap is None:
            P.op('act', lambda e: e.activation(out=eout, in_=A_src, func=AF.Exp, scale=SCALE), r=[An], w=[eout_name])
        else:
            ti = nxt('t')
            P.op('dve', lambda e: e.scalar_tensor_tensor(out=tmpb[ti][:], in0=A_src, scalar=SCALE, in1=bias_ap, op0=ALU.mult, op1=ALU.add),
                 r=[An] + bias_names, w=[f"tmpb{ti}"])
            P.op('act', lambda e: e.activation(out=eout, in_=tmpb[ti][:], func=AF.Exp), r=[f"tmpb{ti}"], w=[eout_name])

    def pv(psO, psOn, lhs_tile, lhs_name, v_ap, v_name, first, last=False):
        for r in range(4):
            P.op('pe', lambda e: e.matmul(psO[:, r * 65:(r + 1) * 65], lhsT=lhs_tile[:, r * 128:(r + 1) * 128], rhs=v_ap,
                                          start=(first and r == 0), stop=(last and r == 3)), r=[lhs_name, v_name], w=[psOn])

    import os
    NT_RUN = int(os.environ.get("M2_NT", NQT))
    for m in range(NT_RUN):
        qi = 2 * m + 1
        t0 = 128 * qi
        q64 = qa[0:64, :, m * 128:(m + 1) * 128]
        q65 = qa[0:65, :, m * 128:(m + 1) * 128]
        def run_pairs(pairs):
            prev = None
            for (A_, B_) in pairs:
                ctx = A_()
                if prev is not None:
                    prev[0](prev[1])
                prev = (B_, ctx)
            if prev is not None:
                prev[0](prev[1])

        nck = min(4, ((t0 + 96) // 16) // 128 + 1)

        def cmpA(ci):
            def f():
                bi = nxt('b')
                src = bass.AP(fs_d.tensor, OFF_F + t0 - 2048 * ci - 2063, [(16, 128), (NF_F, 4), (1, 128)])
                P.dma(bc[bi][:].rearrange("p (r q) -> p r q", q=128), src, w=[f"bc{bi}"])
                ai = nxt('a'); A, An = psA[ai]
                P.op('pe', lambda e: e.matmul(A[:, :], lhsT=kcmp[:, ci * 128:(ci + 1) * 128], rhs=q64, start=True, stop=True),
                     r=["kcmp", "qa"], w=[An])
                softmax_tile(A[:, :], An, bc[bi][:], [f"bc{bi}"], ec[:, ci, :], ("ec", ci))
                return ci
            return f

        def cmpB(ci):
            P.op('pe', lambda e: e.matmul(psL[:, :], lhsT=ones[:], rhs=ec[:, ci, :], start=(ci == 0), stop=(ci == nck - 1)),
                 r=["ones", ("ec", ci)], w=["psL"])
            pv(psO1, "psO1", ec[:, ci, :], ("ec", ci), vcmp[:, ci, :], "vcmp", ci == 0, ci == nck - 1)
        run_pairs([(cmpA(ci), cmpB) for ci in range(nck)])
        P.op('dve', lambda e: e.tensor_scalar(out=rlb[:], in0=psL[:, :], scalar1=TINY, scalar2=None, op0=ALU.max), r=["psL"], w=["rlb"])
        P.op('dve', lambda e: e.reciprocal(out=rlb[:], in_=rlb[:]), r=["rlb"], w=["rlb"])
        for ci in range(nck):
            P.op('dve', lambda e: e.tensor_tensor(out=pnb[:], in0=ec[:, ci, :], in1=rlb[:], op=ALU.mult), r=[("ec", ci), "rlb"], w=["pnb"])
            P.op('dve', lambda e: e.tensor_reduce(out=impT[:, ci, :], in_=pnb[:].rearrange("p (r q) -> p q r", q=128), axis=AX.X, op=ALU.add),
                 r=["pnb"], w=[("impT", ci)])
        for ci in range(nck):
            P.op('pe', lambda e: e.matmul(psS[:, 260:388], lhsT=impT[:, ci, :], rhs=mmt[:, ci, :], start=False, stop=(ci == nck - 1)),
                 r=[("impT", ci), "mmt"], w=["psO1"])
        P.op('act', lambda e: e.activation(out=osb[0][:], in_=psO1[:, 0:260], func=AF.Copy), r=["psO1"], w=["osb0"])
        kts = list(range(max(0, qi - 5), qi + 1))

        def winA(kt):
            def f():
                j = qi - kt
                ai = nxt('a'); A, An = psA[ai]
                P.op('pe', lambda e: e.matmul(A[:, :], lhsT=kwn[0:64, kt * 128:(kt + 1) * 128], rhs=q64, start=True, stop=True),
                     r=["kwn", "qa"], w=[An])
                ei = nxt('e')
                jj = j if j < 4 else 5 + j
                softmax_tile(A[:, :], An, bs[:, jj, :], [("bs", jj)], ebuf[ei][:], f"ebuf{ei}")
                return (kt, ei)
            return f

        def winB(ctx):
            kt, ei = ctx
            pv(psO3, "psO3", ebuf[ei], f"ebuf{ei}", vwn[:, kt, :], "vwn", kt == kts[0], kt == kts[-1])
        run_pairs([(winA(kt), winB) for kt in kts])
        mi = m % 2
        P.dma(madd[mi][:], madd_d[m], w=[f"madd{mi}"])
        P.op('dve', lambda e: e.tensor_tensor(out=score[:], in0=psS[:, 260:388], in1=madd[mi][:], op=ALU.add), r=["psO1", f"madd{mi}"], w=["score"])
        P.op('dve', lambda e: e.max(out=m8a[:], in_=score[:]), r=["score"], w=["m8a"])
        P.op('dve', lambda e: e.match_replace(out=sc2[:], in_to_replace=m8a[:], in_values=score[:], imm_value=-3e38), r=["score", "m8a"], w=["sc2"])
        P.op('dve', lambda e: e.max(out=m8b[:], in_=sc2[:]), r=["sc2"], w=["m8b"])
        P.op('dve', lambda e: e.tensor_scalar(out=self_[:], in0=score[:], scalar1=m8b[:, 7:8], scalar2=None, op0=ALU.is_ge),
             r=["score", "m8b"], w=["self"])
        P.op('pe', lambda e: e.transpose(out=psL[:, 0:128], in_=self_[:], identity=i128[:]), r=["self", "i128"], w=["psL"])
        P.op('act', lambda e: e.activation(out=selT[:], in_=psL[:, 0:128], func=AF.Copy), r=["psL"], w=["selT"])
        def selA(kt):
            def f():
                near = (qi - kt) <= 8
                ai = nxt('a'); A, An = psA[ai]
                if near:
                    P.op('pe', lambda e: e.matmul(A[:, :], lhsT=ksl[0:64, kt * 128:(kt + 1) * 128], rhs=q64, start=True, stop=True),
                         r=["ksl", "qa"], w=[An])
                else:
                    P.op('pe', lambda e: e.matmul(A[:, :], lhsT=ksl[0:65, kt * 128:(kt + 1) * 128], rhs=q65, start=True, stop=True),
                         r=["ksl", "qa"], w=[An])
                bi = nxt('p')
                psB, psBn = psBs[bi]
                P.op('pe', lambda e: e.matmul(psB[:, 0:128], lhsT=eall[:, kt * 128:(kt + 1) * 128], rhs=selT[:], start=True, stop=True),
                     r=["eall", "selT"], w=[psBn])
                ei = nxt('e')
                if near:
                    softmax_tile(A[:, :], An, bs[:, qi - kt, :], [("bs", qi - kt)], ebuf[ei][:], f"ebuf{ei}")
                else:
                    softmax_tile(A[:, :], An, None, None, ebuf[ei][:], f"ebuf{ei}")
                P.op('dve', lambda e: e.tensor_tensor(out=pbuf[bi][:].rearrange("p (r q) -> p r q", q=128),
                                                      in0=ebuf[ei][:].rearrange("p (r q) -> p r q", q=128),
                                                      in1=psB[:, None, 0:128].to_broadcast([128, 4, 128]), op=ALU.mult),
                     r=[f"ebuf{ei}", psBn], w=[f"pbuf{bi}"])
                return (kt, bi)
            return f

        def selB(ctx):
            kt, bi = ctx
            pv(psO2, "psO2", pbuf[bi], f"pbuf{bi}", vsl[:, kt, :], "vsl", kt == 0, kt == qi)
        run_pairs([(selA(kt), selB) for kt in range(qi + 1)])
        for b_, (pso, pson) in enumerate([(psO1, "psO1"), (psO2, "psO2"), (psO3, "psO3")]):
            if b_ > 0:
                P.op('act', lambda e: e.activation(out=osb[b_][:], in_=pso[:, 0:260], func=AF.Copy), r=[pson], w=[f"osb{b_}"])
            P.op('dve', lambda e: e.tensor_copy(out=lc[:, b_, :], in_=osb[b_][:].rearrange("p (r d) -> p r d", d=65)[:, :, 64]),
                 r=[f"osb{b_}"], w=["lc"])
        P.op('dve', lambda e: e.tensor_scalar(out=lc[:], in0=lc[:], scalar1=TINY, scalar2=None, op0=ALU.max), r=["lc"], w=["lc"])
        P.op('dve', lambda e: e.reciprocal(out=lc[:], in_=lc[:]), r=["lc"], w=["lc"])
        P.op('dve', lambda e: e.tensor_tensor(out=coef[:], in0=lc[:],
                                              in1=glog[:, m * 12:(m + 1) * 12].rearrange("p (r b) -> p b r", b=3), op=ALU.mult),
             r=["lc", "glog"], w=["coef"])
        y_ = yb[m % 2]; yn = f"yb{m % 2}"
        for r in range(4):
            for b_ in range(3):
                src_ = osb[b_][:, r * 65:r * 65 + 64]
                if b_ == 0:
                    P.op('dve', lambda e: e.tensor_scalar(out=y_[:, r * 64:(r + 1) * 64], in0=src_, scalar1=coef[:, b_, r:r + 1], scalar2=None,
                                                          op0=ALU.mult), r=[f"osb{b_}", "coef"], w=[(yn, r)])
                else:
                    P.op('dve', lambda e: e.scalar_tensor_tensor(out=y_[:, r * 64:(r + 1) * 64], in0=src_, scalar=coef[:, b_, r:r + 1],
                                                                 in1=y_[:, r * 64:(r + 1) * 64], op0=ALU.mult, op1=ALU.add),
                         r=[f"osb{b_}", "coef", (yn, r)], w=[(yn, r)])
        P.dma(y_d[:, m, :], y_[:], r=[yn], w=[("yd", m)], sem=f"ys{m % 2}")
    return P.finish()


def _t5_bucket_np(d):
    import math
    n = np.maximum(d, 0)
    nf = np.maximum(n, 16).astype(np.float32)
    large = 16 + (np.log(nf / np.float32(16)) / np.float32(math.log(64.0)) * np.float32(16)).astype(np.int32)
    large = np.minimum(large, 31)
    return np.where(n < 16, n, large)


def _m2_consts():
    eall = np.zeros((128, SEQ), np.float32)
    col = np.arange(SEQ)
    kt = col // 128
    p = col % 128
    eall[2 * kt + (127 - p) // 64, col] = 1.0
    mm = np.zeros((128, 4, 128), np.float32)
    wts = {-1: 1.0, 0: 2.0, 1: 2.0, 2: 2.0, 3: 1.0}
    for ci in range(4):
        for pp in range(128):
            c = 128 * ci + 127 - pp
            for dlt, wv in wts.items():
                if (c - dlt) % 4 == 0:
                    j = (c - dlt) // 4
                    if 0 <= j < 128:
                        mm[pp, ci, j] = wv
    return dict(eall=eall, mmat=mm.reshape(128, 512), i128=np.eye(128, dtype=np.float32))


def run_M2(zT, inp, l):
    nc = build_M2()
    cst = _m2_consts()
    rel = inp['rel_bias'].astype(np.float32)
    didx = np.arange(NF_F) - OFF_F
    bk = _t5_bucket_np(didx)
    maps = []
    for c in range(NCORES):
        b = c // 4
        g = (c % 4) // 2
        par = c % 2
        tsl = slice(b * SEQ, (b + 1) * SEQ)
        m = dict(cst)
        sh = 128 * (1 - par)
        fs = np.full((4, NF_F), NEGB, np.float32)
        fw = np.full((4, NF_F), NEGB, np.float32)
        for r in range(4):
            base_s = np.where(didx >= 0, rel[bk, g * 4 + r], NEGB).astype(np.float32)
            base_w = np.where((didx >= 0) & (didx < 512), rel[bk, g * 4 + r], NEGB).astype(np.float32)
            fs[r, sh:] = base_s[:NF_F - sh]
            fw[r, sh:] = base_w[:NF_F - sh]
        m["fs"] = fs
        m["fw"] = fw
        tiles = 2 * np.arange(NQT) + par
        tok = (tiles[:, None] * 128 + np.arange(128)[None, :]).reshape(-1)
        q65 = np.zeros((4, 65, NQT * 128), np.float32)
        for r in range(4):
            q65[r, 0:64] = zT[1280 + g * 256 + r * 64:1280 + g * 256 + (r + 1) * 64, tsl][:, tok]
            q65[r, 64] = rel[31, g * 4 + r]
        m["q65"] = q65

        def kv(j):
            return zT[1792 + j * 128 + g * 64:1792 + j * 128 + (g + 1) * 64, tsl]
        rev = lambda a: a.reshape(64, 64, 128)[:, :, ::-1].reshape(64, SEQ)
        m["kslX"] = rev(kv(2))
        m["kwnX"] = rev(kv(4))
        vrev = lambda a: a.T.reshape(64, 128, 64)[:, ::-1, :].transpose(1, 0, 2).reshape(128, 64 * 64)
        m["vslX"] = vrev(kv(3))
        m["vwnX"] = vrev(kv(5))
        cc = (128 * (np.arange(512) // 128) + 127 - (np.arange(512) % 128))
        for nm, j in [("kcrX", 0), ("vcrX", 1)]:
            src = np.concatenate([kv(j), np.zeros((64, 64), np.float32)], axis=1)
            arr = np.zeros((32, 64, 512), np.float32)
            for jj in range(32):
                arr[jj] = src[:, np.minimum(16 * cc + jj, SEQ + 63)]
            m[nm] = arr
        for sfx, key in [("k", "k"), ("v", "v")]:
            w1 = inp[f'nsa_cmp_w1_{key}'][l]
            m[f"w1{sfx}"] = w1.reshape(32, 64, 64).transpose(1, 0, 2).reshape(64, 2048)
            m[f"w2{sfx}"] = inp[f'nsa_cmp_w2_{key}'][l]
            m[f"pos{sfx}T"] = inp[f'nsa_cmp_pos_{key}'][l].T
        t = tiles[:, None] * 128 + np.arange(128)[None, :]
        blk = np.arange(128)
        cur = t // 64
        ok = (blk[None, None, :] * 64) <= t[:, :, None]
        forced = (blk[None, None, :] == 0) | (blk[None, None, :] == cur[:, :, None]) | (blk[None, None, :] == cur[:, :, None] - 1)
        m["madd"] = np.where(ok, np.where(forced, 1e4, 0.0), -1e30).astype(np.float32)
        gl = zT[2560 + g * 12:2560 + (g + 1) * 12, tsl][:, tok]
        m["glog"] = gl.T.reshape(NQT, 128, 12).transpose(1, 0, 2).reshape(128, NQT * 12)
        maps.append({k: np.ascontiguousarray(v_, dtype=np.float32) for k, v_ in m.items()})
    res = run_bass_kernel_spmd(nc, maps, core_ids=list(range(NCORES)))
    yT = np.zeros((512, BATCH * SEQ), np.float32)
    for c in range(NCORES):
        b = c // 4
        g = (c % 4) // 2
        par = c % 2
        tiles = 2 * np.arange(NQT) + par
        tok = b * SEQ + (tiles[:, None] * 128 + np.arange(128)[None, :]).reshape(-1)
        y = res.results[c]["ynsa"]
        y = y.transpose(1, 0, 2).reshape(NQT * 128, 256)
        yT[g * 256:(g + 1) * 256, tok] = y.T
    return yT


def kernel(**inputs):
    inp = {k: np.asarray(v) for k, v in inputs.items()}
    x = inp['x'].astype(np.float32).reshape(BATCH * SEQ, D_MODEL)
    xT = np.ascontiguousarray(x.T)

    def win_map(l):
        win = np.zeros((D_MODEL, NZC * 128), np.float32)
        win[:, :D_IN] = inp['w_in'][l]
        return {"mix_g": _gcol(inp['mix_norm'][l]), "win": _chunkT(win, 8)}

    def wout_map(l):
        return {"wglu": _chunkT(inp['s5_w_glu'][l], 2), "wout": _chunkT(inp['w_out'][l], 8)}

    def mixers(zT, l):
        y5T, yhT = run_M1(zT, inp, l)
        ynT = run_M2(zT, inp, l)
        return np.concatenate([y5T, yhT, ynT], axis=0)

    common = _ffn_maps("f0", inp['ffn1_norm'][0], inp['ffn1_w_gate'][0], inp['ffn1_w_up'][0], inp['ffn1_w_down'][0])
    common.update(win_map(0))
    o = run_T(xT, common, False, 1, True, False)
    yT = mixers(o["zT"], 0)
    common = _ffn_maps("f0", inp['ffn2_norm'][0], inp['ffn2_w_gate'][0], inp['ffn2_w_up'][0], inp['ffn2_w_down'][0])
    common.update(_ffn_maps("f1", inp['ffn1_norm'][1], inp['ffn1_w_gate'][1], inp['ffn1_w_up'][1], inp['ffn1_w_down'][1]))
    common.update(win_map(1))
    common.update(wout_map(0))
    o = run_T(o["xoT"], common, True, 2, True, False, yT_full=yT)
    yT = mixers(o["zT"], 1)
    common = _ffn_maps("f0", inp['ffn2_norm'][1], inp['ffn2_w_gate'][1], inp['ffn2_w_up'][1], inp['ffn2_w_down'][1])
    common.update(wout_map(1))
    common["fin_g"] = _gcol(inp['final_norm'])
    o = run_T(o["xoT"], common, True, 1, False, True, yT_full=yT)
    out = np.ascontiguousarray(o["xoT"].T).reshape(BATCH, SEQ, D_MODEL).astype(np.float32)
    return out
```

```python
import contextlib
import numpy as np
import concourse.bass as bass
import concourse.mybir as mybir
from concourse.bass_utils import run_bass_kernel_spmd

F32 = mybir.dt.float32
BF16 = mybir.dt.bfloat16
AF = mybir.ActivationFunctionType
ALU = mybir.AluOpType
AX = mybir.AxisListType

D_MODEL = 1024
D_FF = 2816
SEQ = 8192
BATCH = 2
D_IN = 2584
NCORES = 8
EPS = 1e-6


class Prog:
    def __init__(self):
        self.nc = bass.Bass("TRN2", target_bir_lowering=False)
        self.es = contextlib.ExitStack()
        nc = self.nc
        self.eng = {'pe': nc.tensor, 'act': nc.scalar, 'dve': nc.vector, 'pool': nc.gpsimd, 'sp': nc.sync}
        self.sems = {}
        self.cnt = {}
        for k in ['pe', 'act', 'dve', 'pool']:
            self.sems[('e', k)] = self.es.enter_context(nc.semaphore('e_' + k))
            self.cnt[('e', k)] = 0
        self.waited = {k: {} for k in self.eng}
        self.state = {}
        self.nps = 0

    def sbuf(self, name, shape, dt):
        return self.es.enter_context(self.nc.sbuf_tensor("s_" + name, list(shape), dt))

    def psum(self, name, shape, dt=F32):
        return self.es.enter_context(self.nc.psum_tensor("p_" + name, list(shape), dt))

    def dram_in(self, name, shape, dt=F32):
        return self.nc.dram_tensor(name, list(shape), dt, kind="ExternalInput").ap()

    def dram_out(self, name, shape, dt=F32):
        return self.nc.dram_tensor(name, list(shape), dt, kind="ExternalOutput").ap()

    @staticmethod
    def _norm(x):
        return x if isinstance(x, tuple) else (x, None)

    def _deps(self, rs, ws, e):
        need = {}

        def add(sv):
            if sv is None:
                return
            s, v = sv
            if s[0] == 'd':
                v = self.cnt[s]
            if need.get(s, 0) < v:
                need[s] = v

        for (n, k) in rs:
            for kk, ent in self.state.get(n, {}).items():
                if k is None or kk is None or kk == k:
                    add(ent['w'])
        for (n, k) in ws:
            for kk, ent in self.state.get(n, {}).items():
                if k is None or kk is None or kk == k:
                    if ent['w'] is not None and ent['w'][0] != ('e', e):
                        add(ent['w'])
                    for s, v in ent['r'].items():
                        if s != ('e', e):
                            add((s, v))
        if e == 'pe':
            need.pop(('e', 'pe'), None)
        return need

    def _record(self, rs, ws, sv):
        s, v = sv
        for (n, k) in rs:
            ent = self.state.setdefault(n, {}).setdefault(k, {'w': None, 'r': {}})
            ent['r'][s] = max(ent['r'].get(s, 0), v)
        for (n, k) in ws:
            d = self.state.setdefault(n, {})
            if k is None:
                d.clear()
            d[k] = {'w': (s, v), 'r': {}}

    def _emit_waits(self, e, need):
        eng = self.eng[e]
        for s, v in need.items():
            if self.waited[e].get(s, 0) < v:
                eng.wait_ge(self.sems[s], v)
                self.waited[e][s] = v

    def op(self, e, fn, r=(), w=()):
        rs = [self._norm(x) for x in r]
        ws = [self._norm(x) for x in w]
        ws = ws + [(n, None) for (n, k) in rs if n.startswith("ps")]
        ws = [((n, None) if n.startswith("ps") else (n, k)) for (n, k) in ws]
        rs = [x for x in rs if not x[0].startswith("ps")]
        self._emit_waits(e, self._deps(rs, ws, e))
        inst = fn(self.eng[e])
        s = ('e', e)
        self.cnt[s] += 1
        inst.then_inc(self.sems[s], 1)
        self._record(rs, ws, (s, self.cnt[s]))

    def dma(self, out, in_, r=(), w=(), q='sp', sem=None):
        rs = [self._norm(x) for x in r]
        ws = [self._norm(x) for x in w]
        if sem is None:
            sem = ws[0][0]
        s = ('d', sem)
        if s not in self.sems:
            self.sems[s] = self.es.enter_context(self.nc.semaphore('d_' + sem))
            self.cnt[s] = 0
        self._emit_waits(q, self._deps(rs, ws, q))
        self.eng[q].dma_start(out=out, in_=in_).then_inc(self.sems[s], 16)
        self.cnt[s] += 16
        self._record(rs, ws, (s, self.cnt[s]))

    def finish(self):
        sp = self.eng['sp']
        for s, h in self.sems.items():
            if self.cnt[s] > 0 and self.waited['sp'].get(s, 0) < self.cnt[s]:
                sp.wait_ge(h, self.cnt[s])
        self.es.close()
        return self.nc


def mm(ps, lhsT, rhs, start, stop):
    return lambda e: e.matmul(ps, lhsT=lhsT, rhs=rhs, start=start, stop=stop)


NTOK = 2048
TP = 1024
NFC = D_FF // 128
NZC = 21


def build_T(do_wout, n_ffn, do_win, do_final):
    P = Prog()
    nc = P.nc
    xT_d = P.dram_in("xT", [8, 128, NTOK])
    if do_wout:
        yT_d = P.dram_in("yT", [8, 128, NTOK])
        wglu_d = P.dram_in("wglu", [2, 128, 2, 128])
        wout_d = P.dram_in("wout", [8, 128, 8, 128])
    ffn_d = []
    for i in range(n_ffn):
        ffn_d.append(dict(
            g=P.dram_in(f"f{i}_g", [128, 8]),
            wg=P.dram_in(f"f{i}_wg", [NFC, 128, 8, 128]),
            wu=P.dram_in(f"f{i}_wu", [NFC, 128, 8, 128]),
            wd=P.dram_in(f"f{i}_wd", [8, 128, NFC, 128]),
        ))
    if do_win:
        ming_d = P.dram_in("mix_g", [128, 8])
        win_d = P.dram_in("win", [NZC, 128, 8, 128])
        zT_d = P.dram_out("zT", [NZC, 128, NTOK])
    if do_final:
        fing_d = P.dram_in("fin_g", [128, 8])
    xo_d = P.dram_out("xoT", [8, 128, NTOK])

    xT = P.sbuf("xT_s", [128, 8, TP], F32)
    hT = P.sbuf("hT_s", [128, 8, TP], BF16)
    aT = P.sbuf("aT_s", [128, NFC, TP], BF16)
    sq = P.sbuf("sq_s", [128, 2, TP], BF16)
    rstd = P.sbuf("rstd_s", [128, TP], F32)
    ones = P.sbuf("ones_s", [128, 128], BF16)
    gt = P.sbuf("g_s", [128, 8], F32)
    stg = [P.sbuf(f"stg{i}", [128, NFC * 128], F32) for i in range(3)]
    wbf = [P.sbuf(f"wbf{i}", [128, NFC * 128], BF16) for i in range(3)]
    sg = [P.sbuf(f"sg{i}", [128, 512], F32) for i in range(2)]
    ev = [P.sbuf(f"ev{i}", [128, 512], F32) for i in range(2)]
    ps = [P.psum(f"ps{i}", [128, 512]) for i in range(8)]
    if do_wout:
        yf = P.sbuf("yf_s", [128, 8, TP], F32)

    P.op('pool', lambda e: e.memset(ones[:], 1.0), w=["ones"])

    wctr = [0]

    def load_w(src_ap, nk):
        i = wctr[0] % 3
        wctr[0] += 1
        P.dma(stg[i][:, 0:nk * 128], src_ap, w=[f"stg{i}"])
        P.op('pool', lambda e: e.tensor_copy(out=wbf[i][:, 0:nk * 128], in_=stg[i][:, 0:nk * 128]),
             r=[f"stg{i}"], w=[f"wbf{i}"])
        return wbf[i], f"wbf{i}"

    psctr = [0]

    def next_ps():
        i = psctr[0] % 8
        psctr[0] += 1
        return ps[i], f"ps{i}"

    def rmsnorm(g_dram, final=False):
        P.dma(gt[:], g_dram, w=["g"])
        pss = [next_ps(), next_ps()]
        for k in range(8):
            j = k % 2
            P.op('act', lambda e: e.activation(out=sq[:, j, :], in_=xT[:, k, :], func=AF.Square),
                 r=[("xT", k)], w=[("sq", j)])
            for tg in range(2):
                P.op('pe', mm(pss[tg][0][:, :], ones[:], sq[:, j, tg * 512:(tg + 1) * 512], k == 0, k == 7),
                     r=["ones", ("sq", j)], w=[pss[tg][1]])
        for tg in range(2):
            sl = slice(tg * 512, (tg + 1) * 512)
            P.op('act', lambda e: e.activation(out=rstd[:, sl], in_=pss[tg][0][:, :], func=AF.Sqrt,
                                               scale=1.0 / D_MODEL, bias=EPS),
                 r=[pss[tg][1]], w=[("rstd", tg)])
            P.op('dve', lambda e: e.reciprocal(out=rstd[:, sl], in_=rstd[:, sl]),
                 r=[("rstd", tg)], w=[("rstd", tg)])
        for k in range(8):
            if final:
                P.op('dve', lambda e: e.scalar_tensor_tensor(out=xT[:, k, :], in0=xT[:, k, :], scalar=gt[:, k:k + 1],
                                                             in1=rstd[:, :], op0=ALU.mult, op1=ALU.mult),
                     r=[("xT", k), "g", "rstd"], w=[("xT", k)])
            else:
                P.op('dve', lambda e: e.scalar_tensor_tensor(out=hT[:, k, :], in0=xT[:, k, :], scalar=gt[:, k:k + 1],
                                                             in1=rstd[:, :], op0=ALU.mult, op1=ALU.mult),
                     r=[("xT", k), "g", "rstd"], w=[("hT", k)])

    def ffn(fd):
        rmsnorm(fd['g'])
        for fc in range(NFC):
            wg, wgn = load_w(fd['wg'][fc].rearrange("p k j -> p (k j)"), 8)
            wu, wun = load_w(fd['wu'][fc].rearrange("p k j -> p (k j)"), 8)
            for tg in range(2):
                sl = slice(tg * 512, (tg + 1) * 512)
                pg, pgn = next_ps()
                pu, pun = next_ps()
                for k in range(8):
                    P.op('pe', mm(pg[:, :], wg[:, k * 128:(k + 1) * 128], hT[:, k, sl], k == 0, k == 7),
                         r=[wgn, "hT"], w=[pgn])
                for k in range(8):
                    P.op('pe', mm(pu[:, :], wu[:, k * 128:(k + 1) * 128], hT[:, k, sl], k == 0, k == 7),
                         r=[wun, "hT"], w=[pun])
                i = (fc * 2 + tg) % 2
                P.op('act', lambda e: e.activation(out=sg[i][:, :], in_=pg[:, :], func=AF.Silu),
                     r=[pgn], w=[f"sg{i}"])
                P.op('dve', lambda e: e.tensor_tensor(out=aT[:, fc, sl], in0=sg[i][:, :], in1=pu[:, :], op=ALU.mult),
                     r=[f"sg{i}", pun], w=[("aT", fc)])
        for mc in range(8):
            wd, wdn = load_w(fd['wd'][mc].rearrange("p k j -> p (k j)"), NFC)
            for tg in range(2):
                sl = slice(tg * 512, (tg + 1) * 512)
                pd, pdn = next_ps()
                for k in range(NFC):
                    P.op('pe', mm(pd[:, :], wd[:, k * 128:(k + 1) * 128], aT[:, k, sl], k == 0, k == NFC - 1),
                         r=[wdn, "aT"], w=[pdn])
                P.op('dve', lambda e: e.scalar_tensor_tensor(out=xT[:, mc, sl], in0=pd[:, :], scalar=0.5,
                                                             in1=xT[:, mc, sl], op0=ALU.mult, op1=ALU.add),
                     r=[pdn, ("xT", mc)], w=[("xT", mc)])

    for ps_i in range(NTOK // TP):
        tsl = slice(ps_i * TP, (ps_i + 1) * TP)
        for k in range(8):
            P.dma(xT[:, k, :], xT_d[k, :, tsl], w=[("xT", k)], sem="xT")
        if do_wout:
            for k in range(8):
                P.dma(yf[:, k, :], yT_d[k, :, tsl], w=[("yf", k)], sem="yf")
            for k in range(2):
                P.op('pool', lambda e: e.tensor_copy(out=hT[:, k, :], in_=yf[:, k, :]), r=[("yf", k)], w=[("hT", k)])
            for mc in range(2):
                wl, wln = load_w(wglu_d[mc].rearrange("p k j -> p (k j)"), 2)
                for tg in range(2):
                    sl = slice(tg * 512, (tg + 1) * 512)
                    pg, pgn = next_ps()
                    for k in range(2):
                        P.op('pe', mm(pg[:, :], wl[:, k * 128:(k + 1) * 128], hT[:, k, sl], k == 0, k == 1),
                             r=[wln, ("hT", 0), ("hT", 1)], w=[pgn])
                    i = tg
                    P.op('act', lambda e: e.activation(out=sg[i][:, :], in_=pg[:, :], func=AF.Sigmoid),
                         r=[pgn], w=[f"sg{i}"])
                    P.op('dve', lambda e: e.tensor_tensor(out=aT[:, mc, sl], in0=sg[i][:, :], in1=yf[:, mc, sl],
                                                          op=ALU.mult),
                         r=[f"sg{i}", ("yf", mc)], w=[("aT", mc)])
            for k in range(2, 8):
                P.op('pool', lambda e: e.tensor_copy(out=aT[:, k, :], in_=yf[:, k, :]), r=[("yf", k)], w=[("aT", k)])
            for mc in range(8):
                wl, wln = load_w(wout_d[mc].rearrange("p k j -> p (k j)"), 8)
                for tg in range(2):
                    sl = slice(tg * 512, (tg + 1) * 512)
                    pd, pdn = next_ps()
                    for k in range(8):
                        P.op('pe', mm(pd[:, :], wl[:, k * 128:(k + 1) * 128], aT[:, k, sl], k == 0, k == 7),
                             r=[wln, "aT"], w=[pdn])
                    P.op('dve', lambda e: e.tensor_tensor(out=xT[:, mc, sl], in0=pd[:, :], in1=xT[:, mc, sl],
                                                          op=ALU.add),
                         r=[pdn, ("xT", mc)], w=[("xT", mc)])
        for i in range(n_ffn):
            ffn(ffn_d[i])
        if do_win:
            rmsnorm(ming_d)
            for cc in range(NZC):
                wl, wln = load_w(win_d[cc].rearrange("p k j -> p (k j)"), 8)
                for tg in range(2):
                    sl = slice(tg * 512, (tg + 1) * 512)
                    pz, pzn = next_ps()
                    for k in range(8):
                        P.op('pe', mm(pz[:, :], wl[:, k * 128:(k + 1) * 128], hT[:, k, sl], k == 0, k == 7),
                             r=[wln, "hT"], w=[pzn])
                    i = (cc * 2 + tg) % 2
                    P.op('act', lambda e: e.activation(out=ev[i][:, :], in_=pz[:, :], func=AF.Copy),
                         r=[pzn], w=[f"ev{i}"])
                    P.dma(zT_d[cc, :, ps_i * TP + tg * 512: ps_i * TP + (tg + 1) * 512], ev[i][:, :],
                          r=[f"ev{i}"], w=[("zT", (ps_i, cc, tg))], sem=f"zst{i}")
        if do_final:
            rmsnorm(fing_d, final=True)
        for k in range(8):
            P.dma(xo_d[k, :, tsl], xT[:, k, :], r=[("xT", k)], w=[("xo", (ps_i, k))], sem="xo")
    return P.finish()


def _chunkT(w, nk):
    K, M = w.shape
    return np.ascontiguousarray(w.reshape(nk, 128, M // 128, 128).transpose(2, 1, 0, 3))


def _gcol(g):
    return np.ascontiguousarray(g.reshape(8, 128).T)


def _ffn_maps(prefix, g, wg, wu, wd):
    return {f"{prefix}_g": _gcol(g), f"{prefix}_wg": _chunkT(wg, 8), f"{prefix}_wu": _chunkT(wu, 8),
            f"{prefix}_wd": _chunkT(wd, NFC)}


def run_T(xT_full, common, do_wout, n_ffn, do_win, do_final, yT_full=None):
    nc = build_T(do_wout, n_ffn, do_win, do_final)
    maps = []
    for c in range(NCORES):
        m = dict(common)
        m["xT"] = np.ascontiguousarray(xT_full[:, c * NTOK:(c + 1) * NTOK].reshape(8, 128, NTOK))
        if do_wout:
            m["yT"] = np.ascontiguousarray(yT_full[:, c * NTOK:(c + 1) * NTOK].reshape(8, 128, NTOK))
        maps.append(m)
    res = run_bass_kernel_spmd(nc, maps, core_ids=list(range(NCORES)))
    out = {}
    out["xoT"] = np.concatenate([r["xoT"].reshape(1024, NTOK) for r in res.results], axis=1)
    if do_win:
        out["zT"] = np.concatenate([r["zT"].reshape(NZC * 128, NTOK) for r in res.results], axis=1)
    return out


PI = float(np.pi)
NCH = 1024
HB = 2048
GELU_C = 1.5957691216057308


def emit_gelu(P, dst, src_ps, tmp_a, tmp_b, names, src_names, bias=None):
    dn, an, bn = names
    if bias is None:
        P.op('act', lambda e: e.activation(out=tmp_a, in_=src_ps, func=AF.Copy), r=src_names, w=[an])
    else:
        P.op('act', lambda e: e.activation(out=tmp_a, in_=src_ps, func=AF.Identity, bias=bias[0]), r=src_names + [bias[1]], w=[an])
    P.op('pool', lambda e: e.tensor_tensor(out=tmp_b, in0=tmp_a, in1=tmp_a, op=ALU.mult), r=[an], w=[bn])
    P.op('dve', lambda e: e.tensor_scalar(out=tmp_b, in0=tmp_b, scalar1=0.044715, scalar2=1.0, op0=ALU.mult, op1=ALU.add),
         r=[bn], w=[bn])
    P.op('dve', lambda e: e.tensor_tensor(out=tmp_b, in0=tmp_b, in1=tmp_a, op=ALU.mult), r=[bn, an], w=[bn])
    P.op('act', lambda e: e.activation(out=tmp_b, in_=tmp_b, func=AF.Sigmoid, scale=GELU_C), r=[bn], w=[bn])
    P.op('dve', lambda e: e.tensor_tensor(out=dst, in0=tmp_a, in1=tmp_b, op=ALU.mult), r=[an, bn], w=[dn])


def build_M1(layer):
    P = Prog()
    lr2_d = P.dram_in("lr2", [128, 4]); li2_d = P.dram_in("li2", [128, 4]); ldt_d = P.dram_in("ldt", [128, 4])
    b1_d = P.dram_in("bst1", [128, 4, 16]); b2_d = P.dram_in("bst2", [128, 4, 16])
    c1_d = P.dram_in("cst1", [128, 4, 16]); c2_d = P.dram_in("cst2", [128, 4, 16])
    dcol_d = P.dram_in("dcol", [128, 4]); sgn_d = P.dram_in("sgn", [128, 2]); jv_d = P.dram_in("jv", [128, 4, 24])
    i128_d = P.dram_in("i128", [128, 128]); jsw_d = P.dram_in("jsw", [128, 128]); tmask_d = P.dram_in("tmask", [128, 128])
    uc_d = P.dram_in("uc", [4, 128, NCH])
    y5_d = P.dram_out("y5", [4, 128, NCH])
    hq_d = P.dram_in("hq", [64, SEQ]); hf_d = P.dram_in("hf", [64, SEQ])
    hv64_d = P.dram_in("hv64", [4, 64, 32, 64]); hv128_d = P.dram_in("hv128", [4, 128, 16, 64]); hg64_d = P.dram_in("hg64", [4, 64, 32, 64])
    lbl_d = P.dram_in("lbl", [64, 2]); gain_d = P.dram_in("hgain", [64, 64]); cm_d = P.dram_in("cmask", [64, 512])
    yh_d = P.dram_out("yh", [64, 128, 64])

    ps = [P.psum(f"ps{i}", [128, 512]) for i in range(8)]
    i128 = P.sbuf("i128", [128, 128], F32)
    P.dma(i128[:], i128_d[:, :], w=["i128"])

    import os
    PARTS = os.environ.get('M1PARTS', 's5,hg')
    NG5 = 4 if 's5' in PARTS else 0
    def T(name, shape, dt=F32):
        return P.sbuf(name, shape, dt)
    lr2 = T("lr2", [128, 4]); li2 = T("li2", [128, 4]); dt = T("dt", [128, 4]); sgn = T("sgn", [128, 2])
    b1 = T("b1", [128, 4, 16]); b2 = T("b2", [128, 4, 16]); c1 = T("c1", [128, 4, 16]); c2 = T("c2", [128, 4, 16])
    dcol = T("dcol", [128, 4]); jv = T("jv", [128, 4, 24]); jsw = T("jsw", [128, 128]); tmask = T("tmask", [128, 128])
    for t_, d_, n_ in [(lr2, lr2_d, "lr2"), (li2, li2_d, "li2"), (dt, ldt_d, "dt"), (sgn, sgn_d, "sgn"), (dcol, dcol_d, "dcol"),
                       (jsw, jsw_d, "jsw"), (tmask, tmask_d, "tmask")]:
        P.dma(t_[:], d_[:, :], w=[n_])
    for t_, d_, n_ in [(b1, b1_d, "b1"), (b2, b2_d, "b2"), (c1, c1_d, "c1"), (c2, c2_d, "c2"), (jv, jv_d, "jv")]:
        P.dma(t_[:], d_[:, :, :], w=[n_])
    V = lambda e: e

    def dv(fn, r, w):
        P.op('dve', fn, r=r, w=w)

    def ac(fn, r, w):
        P.op('act', fn, r=r, w=w)
    ac(lambda e: e.activation(out=dt[:], in_=dt[:], func=AF.Exp), ["dt"], ["dt"])
    lrdt = T("lrdt", [128, 4]); lidt = T("lidt", [128, 4])
    dv(lambda e: e.tensor_tensor(out=lrdt[:], in0=lr2[:], in1=dt[:], op=ALU.mult), ["lr2", "dt"], ["lrdt"])
    dv(lambda e: e.tensor_tensor(out=lidt[:], in0=li2[:], in1=dt[:], op=ALU.mult), ["li2", "dt"], ["lidt"])
    am = T("am", [128, 4, 24]); aa = T("aa", [128, 4, 24]); pr = T("pr", [128, 4, 24]); pi_ = T("pi", [128, 4, 24])
    tA = T("tA", [128, 4, 24]); tB = T("tB", [128, 4, 24]); tI = T("tI", [128, 4, 24], mybir.dt.int32)
    bc24 = lambda t_: t_[:, :, None].to_broadcast([128, 4, 24])
    dv(lambda e: e.tensor_tensor(out=am[:], in0=jv[:], in1=bc24(lrdt), op=ALU.mult), ["jv", "lrdt"], ["am"])
    ac(lambda e: e.activation(out=am[:], in_=am[:], func=AF.Exp), ["am"], ["am"])
    dv(lambda e: e.tensor_tensor(out=aa[:], in0=jv[:], in1=bc24(lidt), op=ALU.mult), ["jv", "lidt"], ["aa"])

    def sin_of(dst, dn, src, sn, shift):
        dv(lambda e: e.tensor_scalar(out=tA[:], in0=src[:], scalar1=shift, scalar2=None, op0=ALU.add), [sn], ["tA"])
        dv(lambda e: e.tensor_scalar(out=tB[:], in0=tA[:], scalar1=1.0 / (2 * PI), scalar2=64.5, op0=ALU.mult, op1=ALU.add),
           ["tA"], ["tB"])
        dv(lambda e: e.tensor_copy(out=tI[:], in_=tB[:]), ["tB"], ["tI"])
        dv(lambda e: e.tensor_copy(out=tB[:], in_=tI[:]), ["tI"], ["tB"])
        dv(lambda e: e.tensor_scalar(out=tB[:], in0=tB[:], scalar1=-64.0, scalar2=-2 * PI, op0=ALU.add, op1=ALU.mult),
           ["tB"], ["tB"])
        dv(lambda e: e.tensor_tensor(out=tA[:], in0=tA[:], in1=tB[:], op=ALU.add), ["tA", "tB"], ["tA"])
        dv(lambda e: e.tensor_scalar(out=tB[:], in0=tA[:], scalar1=-PI, scalar2=2 * PI, op0=ALU.is_lt, op1=ALU.mult),
           ["tA"], ["tB"])
        dv(lambda e: e.tensor_tensor(out=tA[:], in0=tA[:], in1=tB[:], op=ALU.add), ["tA", "tB"], ["tA"])
        dv(lambda e: e.tensor_scalar(out=tB[:], in0=tA[:], scalar1=PI, scalar2=-2 * PI, op0=ALU.is_gt, op1=ALU.mult),
           ["tA"], ["tB"])
        dv(lambda e: e.tensor_tensor(out=tA[:], in0=tA[:], in1=tB[:], op=ALU.add), ["tA", "tB"], ["tA"])
        dv(lambda e: e.tensor_scalar(out=tA[:], in0=tA[:], scalar1=-PI, scalar2=PI, op0=ALU.max, op1=ALU.min),
           ["tA"], ["tA"])
        ac(lambda e: e.activation(out=dst[:], in_=tA[:], func=AF.Sin), ["tA"], [dn])
    sin_of(pi_, "pi", aa, "aa", 0.0)
    sin_of(pr, "pr", aa, "aa", PI / 2)
    dv(lambda e: e.tensor_tensor(out=pr[:], in0=pr[:], in1=am[:], op=ALU.mult), ["pr", "am"], ["pr"])
    dv(lambda e: e.tensor_tensor(out=pi_[:], in0=pi_[:], in1=am[:], op=ALU.mult), ["pi", "am"], ["pi"])
    den = T("den", [128, 4]); t4a = T("t4a", [128, 4]); t4b = T("t4b", [128, 4]); nr = T("nr", [128, 4])
    gre = T("gre", [128, 4]); gim = T("gim", [128, 4])
    ar1 = pr[:, :, 8]; ai1 = pi_[:, :, 8]
    dv(lambda e: e.tensor_tensor(out=den[:], in0=lr2[:], in1=lr2[:], op=ALU.mult), ["lr2"], ["den"])
    dv(lambda e: e.tensor_tensor(out=t4a[:], in0=li2[:], in1=li2[:], op=ALU.mult), ["li2"], ["t4a"])
    dv(lambda e: e.tensor_tensor(out=den[:], in0=den[:], in1=t4a[:], op=ALU.add), ["den", "t4a"], ["den"])
    dv(lambda e: e.reciprocal(out=den[:], in_=den[:]), ["den"], ["den"])
    dv(lambda e: e.tensor_scalar(out=nr[:], in0=ar1, scalar1=-1.0, scalar2=None, op0=ALU.add), ["pr"], ["nr"])
    dv(lambda e: e.tensor_tensor(out=t4a[:], in0=nr[:], in1=lr2[:], op=ALU.mult), ["nr", "lr2"], ["t4a"])
    dv(lambda e: e.tensor_tensor(out=t4b[:], in0=ai1, in1=li2[:], op=ALU.mult), ["pi", "li2"], ["t4b"])
    dv(lambda e: e.tensor_tensor(out=t4a[:], in0=t4a[:], in1=t4b[:], op=ALU.add), ["t4a", "t4b"], ["t4a"])
    dv(lambda e: e.tensor_tensor(out=gre[:], in0=t4a[:], in1=den[:], op=ALU.mult), ["t4a", "den"], ["gre"])
    dv(lambda e: e.tensor_tensor(out=t4a[:], in0=ai1, in1=lr2[:], op=ALU.mult), ["pi", "lr2"], ["t4a"])
    dv(lambda e: e.tensor_tensor(out=t4b[:], in0=nr[:], in1=li2[:], op=ALU.mult), ["nr", "li2"], ["t4b"])
    dv(lambda e: e.tensor_tensor(out=t4a[:], in0=t4a[:], in1=t4b[:], op=ALU.subtract), ["t4a", "t4b"], ["t4a"])
    dv(lambda e: e.tensor_tensor(out=gim[:], in0=t4a[:], in1=den[:], op=ALU.mult), ["t4a", "den"], ["gim"])
    er = T("er", [128, 4, 8]); ei = T("ei", [128, 4, 8]); t8 = T("t8", [128, 4, 8])
    bc8 = lambda t_: t_[:, :, None].to_broadcast([128, 4, 8])
    dv(lambda e: e.tensor_tensor(out=er[:], in0=pr[:, :, 0:8], in1=bc8(gre), op=ALU.mult), ["pr", "gre"], ["er"])
    dv(lambda e: e.tensor_tensor(out=t8[:], in0=pi_[:, :, 0:8], in1=bc8(gim), op=ALU.mult), ["pi", "gim"], ["t8"])
    dv(lambda e: e.tensor_tensor(out=er[:], in0=er[:], in1=t8[:], op=ALU.subtract), ["er", "t8"], ["er"])
    dv(lambda e: e.tensor_tensor(out=ei[:], in0=pr[:, :, 0:8], in1=bc8(gim), op=ALU.mult), ["pr", "gim"], ["ei"])
    dv(lambda e: e.tensor_tensor(out=t8[:], in0=pi_[:, :, 0:8], in1=bc8(gre), op=ALU.mult), ["pi", "gre"], ["t8"])
    dv(lambda e: e.tensor_tensor(out=ei[:], in0=ei[:], in1=t8[:], op=ALU.add), ["ei", "t8"], ["ei"])
    m2 = T("m2", [128, 4, 8]); n1f = T("n1f", [128, 4, 8]); n2f = T("n2f", [128, 4, 8]); n1h = T("n1h", [128, 4, 8]); n2h = T("n2h", [128, 4, 8])
    dv(lambda e: e.tensor_scalar(out=m2[:], in0=ei[:], scalar1=sgn[:, 1:2], scalar2=None, op0=ALU.mult), ["ei", "sgn"], ["m2"])
    dv(lambda e: e.tensor_scalar(out=n1f[:], in0=pr[:, :, 8:16], scalar1=sgn[:, 0:1], scalar2=None, op0=ALU.mult), ["pr", "sgn"], ["n1f"])
    dv(lambda e: e.tensor_scalar(out=n2f[:], in0=pi_[:, :, 8:16], scalar1=-1.0, scalar2=None, op0=ALU.mult), ["pi"], ["n2f"])
    dv(lambda e: e.tensor_scalar(out=n1h[:], in0=pr[:, :, 16:24], scalar1=sgn[:, 0:1], scalar2=None, op0=ALU.mult), ["pr", "sgn"], ["n1h"])
    dv(lambda e: e.tensor_scalar(out=n2h[:], in0=pi_[:, :, 16:24], scalar1=-1.0, scalar2=None, op0=ALU.mult), ["pi"], ["n2h"])
    bcm = T("bcm", [128, 4, 8, 16]); ccm = T("ccm", [128, 4, 8, 16]); qmm = T("qmm", [128, 4, 8, 16]); t816 = T("t816", [128, 4, 8, 16])
    S4 = [128, 4, 8, 16]

    def outer(dst, dn, st1, s1n, co1, c1n, st2, s2n, co2, c2n):
        dv(lambda e: e.tensor_tensor(out=dst[:], in0=st1[:, :, None, :].to_broadcast(S4), in1=co1[:, :, :, None].to_broadcast(S4),
                                     op=ALU.mult), [s1n, c1n], [dn])
        dv(lambda e: e.tensor_tensor(out=t816[:], in0=st2[:, :, None, :].to_broadcast(S4), in1=co2[:, :, :, None].to_broadcast(S4),
                                     op=ALU.mult), [s2n, c2n], ["t816"])
        dv(lambda e: e.tensor_tensor(out=dst[:], in0=dst[:], in1=t816[:], op=ALU.add), [dn, "t816"], [dn])
    outer(bcm, "bcm", b1, "b1", er, "er", b2, "b2", m2, "m2")
    outer(ccm, "ccm", c1, "c1", n1f, "n1f", c2, "c2", n2f, "n2f")
    outer(qmm, "qmm", c1, "c1", n1h, "n1h", c2, "c2", n2h, "n2h")
    tz = T("tz", [128, 4, 128]); bct = T("bct", [128, 4, 128])
    for g in range(4):
        pg, pgn = ps[g % 2], f"ps{g % 2}"
        P.op('pe', lambda e: e.matmul(pg[:, 0:128], lhsT=bcm[:, g].rearrange("p s h -> p (s h)"),
                                      rhs=qmm[:, g].rearrange("p s h -> p (s h)"), start=True, stop=True),
             r=["bcm", "qmm"], w=[pgn])
        dv(lambda e: e.tensor_tensor(out=tz[:, g, :], in0=pg[:, 0:128], in1=tmask[:], op=ALU.mult), [pgn, "tmask"], [("tz", g)])
        dv(lambda e: e.scalar_tensor_tensor(out=tz[:, g, :], in0=i128[:], scalar=dcol[:, g:g + 1], in1=tz[:, g, :],
                                            op0=ALU.mult, op1=ALU.add), ["i128", "dcol", ("tz", g)], [("tz", g)])
        pt, ptn = ps[2 + g % 2], f"ps{2 + g % 2}"
        P.op('pe', lambda e: e.transpose(out=pt[:, 0:128], in_=bcm[:, g].rearrange("p s h -> p (s h)"), identity=i128[:]),
             r=["bcm", "i128"], w=[ptn])
        ac(lambda e: e.activation(out=bct[:, g, :], in_=pt[:, 0:128], func=AF.Copy), [ptn], [("bct", g)])
    NK = 10
    a8r = T("a8r", [128, NK, 4]); a8i = T("a8i", [128, NK, 4]); rm = T("rm", [128, NK * 4, 128])
    dv(lambda e: e.tensor_copy(out=a8r[:, 0, :], in_=pr[:, :, 15]), ["pr"], ["a8r"])
    dv(lambda e: e.tensor_copy(out=a8i[:, 0, :], in_=pi_[:, :, 15]), ["pi"], ["a8i"])
    for k in range(1, NK):
        dv(lambda e: e.tensor_tensor(out=t4a[:], in0=a8r[:, k - 1, :], in1=a8r[:, k - 1, :], op=ALU.mult), ["a8r"], ["t4a"])
        dv(lambda e: e.tensor_tensor(out=t4b[:], in0=a8i[:, k - 1, :], in1=a8i[:, k - 1, :], op=ALU.mult), ["a8i"], ["t4b"])
        dv(lambda e: e.tensor_tensor(out=a8r[:, k, :], in0=t4a[:], in1=t4b[:], op=ALU.subtract), ["t4a", "t4b"], ["a8r"])
        dv(lambda e: e.tensor_tensor(out=t4a[:], in0=a8r[:, k - 1, :], in1=a8i[:, k - 1, :], op=ALU.mult), ["a8r", "a8i"], ["t4a"])
        dv(lambda e: e.tensor_scalar(out=a8i[:, k, :], in0=t4a[:], scalar1=2.0, scalar2=None, op0=ALU.mult), ["t4a"], ["a8i"])
    a8is = T("a8is", [128, NK, 4])
    dv(lambda e: e.tensor_scalar(out=a8is[:], in0=a8i[:], scalar1=sgn[:, 0:1], scalar2=None, op0=ALU.mult), ["a8i", "sgn"], ["a8is"])
    for k in range(NK):
        for g in range(4):
            i = k * 4 + g
            dv(lambda e: e.tensor_scalar(out=rm[:, i, :], in0=i128[:], scalar1=a8r[:, k, g:g + 1], scalar2=None, op0=ALU.mult),
               ["i128", "a8r"], [("rm", i)])
            dv(lambda e: e.scalar_tensor_tensor(out=rm[:, i, :], in0=jsw[:], scalar=a8is[:, k, g:g + 1], in1=rm[:, i, :],
                                                op0=ALU.mult, op1=ALU.add), ["jsw", "a8is", ("rm", i)], [("rm", i)])
    uc = [T(f"uc{i}", [128, NCH]) for i in range(2)]
    xs = T("xs", [128, NCH + 1]); ga = T("ga", [128, 512]); gb = T("gb", [128, 512]); yo = [T(f"yo{i}", [128, 512]) for i in range(2)]
    P.op('pool', lambda e: e.memset(xs[:, 0:1], 0.0), w=[("xs", "z")])
    for g in range(NG5):
        u = uc[g % 2]; un = f"uc{g % 2}"
        P.dma(u[:], uc_d[g], w=[un])
        for h in range(2):
            pp, ppn = ps[h], f"ps{h}"
            P.op('pe', lambda e: e.matmul(pp[:, :], lhsT=bct[:, g, :], rhs=u[:, h * 512:(h + 1) * 512], start=True, stop=True),
                 r=[("bct", g), un], w=[ppn])
            ac(lambda e: e.activation(out=xs[:, 1 + h * 512:1 + (h + 1) * 512], in_=pp[:, :], func=AF.Copy), [ppn], [("xs", "x")])
        for k in range(NK):
            d = 1 << k
            n = NCH - d
            pieces = [(0, min(512, n))] + ([(512, n)] if n > 512 else [])
            for h, (a, b) in enumerate(pieces):
                pp, ppn = ps[2 + h], f"ps{2 + h}"
                P.op('pe', lambda e: e.matmul(pp[:, 0:b - a], lhsT=rm[:, k * 4 + g, :], rhs=xs[:, 1 + a:1 + b], start=True, stop=True),
                     r=[("rm", k * 4 + g), ("xs", "x")], w=[ppn])
            for h, (a, b) in enumerate(pieces):
                pp, ppn = ps[2 + h], f"ps{2 + h}"
                dv(lambda e: e.tensor_tensor(out=xs[:, 1 + d + a:1 + d + b], in0=pp[:, 0:b - a], in1=xs[:, 1 + d + a:1 + d + b], op=ALU.add),
                   [ppn, ("xs", "x")], [("xs", "x")])
        for h in range(2):
            pp, ppn = ps[4 + h], f"ps{4 + h}"
            P.op('pe', lambda e: e.matmul(pp[:, :], lhsT=tz[:, g, :], rhs=u[:, h * 512:(h + 1) * 512], start=True, stop=False),
                 r=[("tz", g), un], w=[ppn])
            P.op('pe', lambda e: e.matmul(pp[:, :], lhsT=ccm[:, g].rearrange("p s h -> p (s h)"), rhs=xs[:, h * 512:(h + 1) * 512],
                                          start=False, stop=True), r=["ccm", ("xs", "x"), ("xs", "z")], w=[ppn])
            emit_gelu(P, yo[h][:], pp[:, :], ga[:], gb[:], (f"yo{h}", "ga", "gb"), [ppn])
            P.dma(y5_d[g, :, h * 512:(h + 1) * 512], yo[h][:], r=[f"yo{h}"], w=[("y5", (g, h))], sem=f"y5s{h}")

    if 'hg' not in PARTS:
        return P.finish()
    lbl = T("lbl", [64, 2]); lb = T("lb", [64, 1]); oml = T("oml", [64, 1]); gain = T("gain", [64, 64]); cm = T("cm", [64, 512])
    P.dma(lbl[:], lbl_d[:, :], w=["lbl"]); P.dma(gain[:], gain_d[:, :], w=["gain"]); P.dma(cm[:], cm_d[:, :], w=["cm"])
    if layer == 0:
        P.op('pool', lambda e: e.memset(lb[:], 0.0), w=["lb"])
    else:
        dv(lambda e: e.tensor_tensor(out=lb[:], in0=lbl[:, 1:2], in1=lbl[:, 0:1], op=ALU.subtract), ["lbl"], ["lb"])
        ac(lambda e: e.activation(out=lb[:], in_=lb[:], func=AF.Sigmoid), ["lb"], ["lb"])
    dv(lambda e: e.tensor_scalar(out=oml[:], in0=lb[:], scalar1=-1.0, scalar2=1.0, op0=ALU.mult, op1=ALU.add), ["lb"], ["oml"])
    rmask = T("rmask", [64, HB])
    P.op('pool', lambda e: e.memset(rmask[:], 1.0), w=["rmask"])
    P.op('pool', lambda e: e.memset(rmask[:, 0:HB:64], 0.0), w=["rmask"])
    hq = T("hq", [64, HB]); hf = T("hf", [64, HB]); sig = T("sig", [64, HB]); fbuf = T("fbuf", [64, HB]); bb = T("bb", [64, HB])
    eb = T("eb", [64, HB]); enb = T("enb", [64, HB]); qtil = T("qtil", [64, HB], BF16); ktil = T("ktil", [64, HB])
    ktb = T("ktb", [64, HB], BF16); khat = T("khat", [64, HB]); dec = T("dec", [64, 32])
    khT = T("khT", [128, 16, 64], BF16); v64f = T("v64f", [64, 32, 64]); v64 = T("v64", [64, 32, 64], BF16)
    v128f = T("v128f", [128, 16, 64]); v128 = T("v128", [128, 16, 64], BF16)
    g64 = T("g64", [64, 32, 64]); sall = T("sall", [64, 33, 64]); sbf = T("sbf", [64, 32, 64], BF16)
    attm = T("attm", [64, 512], BF16); osb = T("osb", [64, 8, 64]); osq = T("osq", [64, 8, 64]); ss = T("ss", [64, 8])
    yh = [T(f"yh{i}", [64, 8, 64]) for i in range(2)]
    P.op('pool', lambda e: e.memset(sall[:, 0, :], 0.0), w=[("sall", 0)])
    for blk in range(SEQ // HB):
        tsl = slice(blk * HB, (blk + 1) * HB)
        P.dma(hq[:], hq_d[:, tsl], w=["hq"]); P.dma(hf[:], hf_d[:, tsl], w=["hf"])
        P.dma(v64f[:], hv64_d[blk], w=["v64f"])
        P.dma(v128f[:], hv128_d[blk], w=["v128f"])
        P.dma(g64[:], hg64_d[blk], w=["g64"])
        P.op('pool', lambda e: e.tensor_copy(out=v64[:], in_=v64f[:]), r=["v64f"], w=["v64"])
        P.op('pool', lambda e: e.tensor_copy(out=v128[:], in_=v128f[:]), r=["v128f"], w=["v128"])
        ac(lambda e: e.activation(out=sig[:], in_=hf[:], func=AF.Sigmoid), ["hf"], ["sig"])
        dv(lambda e: e.tensor_scalar(out=fbuf[:], in0=sig[:], scalar1=oml[:, 0:1], scalar2=lb[:, 0:1], op0=ALU.mult, op1=ALU.add),
           ["sig", "oml", "lb"], ["fbuf"])
        ac(lambda e: e.activation(out=fbuf[:], in_=fbuf[:], func=AF.Ln), ["fbuf"], ["fbuf"])
        dv(lambda e: e.tensor_tensor_scan(out=bb[:], data0=rmask[:], data1=fbuf[:], initial=0.0, op0=ALU.mult, op1=ALU.add),
           ["rmask", "fbuf"], ["bb"])
        ac(lambda e: e.activation(out=eb[:], in_=bb[:], func=AF.Exp), ["bb"], ["eb"])
        ac(lambda e: e.activation(out=enb[:], in_=bb[:], func=AF.Exp, scale=-1.0), ["bb"], ["enb"])
        ac(lambda e: e.activation(out=sig[:], in_=hf[:], func=AF.Sigmoid, scale=-1.0), ["hf"], ["sig"])
        dv(lambda e: e.scalar_tensor_tensor(out=ktil[:], in0=sig[:], scalar=oml[:, 0:1], in1=enb[:], op0=ALU.mult, op1=ALU.mult),
           ["sig", "oml", "enb"], ["ktil"])
        P.op('pool', lambda e: e.tensor_copy(out=ktb[:], in_=ktil[:]), r=["ktil"], w=["ktb"])
        ac(lambda e: e.activation(out=hq[:], in_=hq[:], func=AF.Silu), ["hq"], ["hq"])
        dv(lambda e: e.tensor_tensor(out=qtil[:], in0=hq[:], in1=eb[:], op=ALU.mult), ["hq", "eb"], ["qtil"])
        dv(lambda e: e.tensor_copy(out=dec[:], in_=eb[:, 63:HB:64]), ["eb"], ["dec"])
        dv(lambda e: e.tensor_tensor(out=khat[:].rearrange("p (c s) -> p c s", s=64), in0=ktil[:].rearrange("p (c s) -> p c s", s=64),
                                     in1=dec[:, :, None].to_broadcast([64, 32, 64]), op=ALU.mult), ["ktil", "dec"], ["khat"])
        ac(lambda e: e.activation(out=g64[:], in_=g64[:], func=AF.Silu), ["g64"], ["g64"])
        for hlf in range(2):
            pp, ppn = ps[6 + hlf], f"ps{6 + hlf}"
            for j in range(8):
                jj = hlf * 8 + j
                P.op('pe', lambda e: e.transpose(out=pp[:, j * 64:(j + 1) * 64], in_=khat[:, jj * 128:(jj + 1) * 128],
                                                 identity=i128[0:64, 0:64]), r=["khat", "i128"], w=[ppn])
            ac(lambda e: e.activation(out=khT[:, hlf * 8:(hlf + 1) * 8, :].rearrange("p a b -> p (a b)"), in_=pp[:, :], func=AF.Copy),
               [ppn], ["khT"])
        for cg in range(4):
            for ci in range(8):
                c = cg * 8 + ci
                po = (c % 2) * 64
                pu, pun = ps[c % 2], f"ps{c % 2}"
                sl_ = slice((ci // 2) * 64, (ci // 2 + 1) * 64)
                P.op('pe', lambda e: e.matmul(pu[0:64, sl_], lhsT=khT[po:po + 64, c // 2, :],
                                              rhs=v128[po:po + 64, c // 2, :], start=True, stop=True),
                     r=["khT", "v128"], w=[pun])
            for ci in range(8):
                c = cg * 8 + ci
                pu, pun = ps[c % 2], f"ps{c % 2}"
                sl_ = slice((ci // 2) * 64, (ci // 2 + 1) * 64)
                dv(lambda e: e.scalar_tensor_tensor(out=sall[:, c + 1, :], in0=sall[:, c, :], scalar=dec[:, c:c + 1],
                                                    in1=pu[0:64, sl_], op0=ALU.mult, op1=ALU.add),
                   [("sall", c), "dec", pun], [("sall", c + 1)])
        P.op('pool', lambda e: e.tensor_copy(out=sbf[:], in_=sall[:, 0:32, :]), r=["sall"], w=["sbf"])
        for cg in range(4):
            pa, pan = ps[2 + cg % 2], f"ps{2 + cg % 2}"
            po_, pon = ps[4 + cg % 2], f"ps{4 + cg % 2}"
            for ci in range(8):
                c = cg * 8 + ci
                cs = slice(c * 64, (c + 1) * 64)
                P.op('pe', lambda e: e.matmul(pa[0:64, ci * 64:(ci + 1) * 64], lhsT=ktb[:, cs], rhs=qtil[:, cs], start=True, stop=True),
                     r=["ktb", "qtil"], w=[pan])
            dv(lambda e: e.tensor_tensor(out=attm[:], in0=pa[0:64, :], in1=cm[:], op=ALU.mult), [pan, "cm"], ["attm"])
            for ci in range(8):
                c = cg * 8 + ci
                cs = slice(c * 64, (c + 1) * 64)
                P.op('pe', lambda e: e.matmul(po_[0:64, ci * 64:(ci + 1) * 64], lhsT=attm[:, ci * 64:(ci + 1) * 64], rhs=v64[:, c, :],
                                              start=True, stop=False), r=["attm", "v64"], w=[pon])
                P.op('pe', lambda e: e.matmul(po_[0:64, ci * 64:(ci + 1) * 64], lhsT=qtil[:, cs], rhs=sbf[:, c, :],
                                              start=False, stop=True), r=["qtil", "sbf"], w=[pon])
            ac(lambda e: e.activation(out=osb[:].rearrange("p a b -> p (a b)"), in_=po_[0:64, :], func=AF.Copy), [pon], ["osb"])
            P.op('pool', lambda e: e.tensor_tensor(out=osq[:], in0=osb[:], in1=osb[:], op=ALU.mult), r=["osb"], w=["osq"])
            dv(lambda e: e.tensor_reduce(out=ss[:], in_=osq[:], axis=AX.X, op=ALU.add), ["osq"], ["ss"])
            ac(lambda e: e.activation(out=ss[:], in_=ss[:], func=AF.Sqrt, scale=1.0 / 64, bias=EPS), ["ss"], ["ss"])
            dv(lambda e: e.reciprocal(out=ss[:], in_=ss[:]), ["ss"], ["ss"])
            y_ = yh[cg % 2]; yn = f"yh{cg % 2}"
            dv(lambda e: e.tensor_tensor(out=y_[:], in0=osb[:], in1=ss[:, :, None].to_broadcast([64, 8, 64]), op=ALU.mult),
               ["osb", "ss"], [yn])
            dv(lambda e: e.tensor_tensor(out=y_[:], in0=y_[:], in1=gain[:, None, :].to_broadcast([64, 8, 64]), op=ALU.mult),
               [yn, "gain"], [yn])
            dv(lambda e: e.tensor_tensor(out=y_[:], in0=y_[:], in1=g64[:, cg * 8:(cg + 1) * 8, :], op=ALU.mult), [yn, "g64"], [yn])
            c0 = blk * 32 + cg * 8
            P.dma(yh_d[:, c0:c0 + 8, :], y_[:], r=[yn], w=[("yhd", c0)], sem=f"yhs{cg % 2}")
        dv(lambda e: e.tensor_copy(out=sall[:, 0, :], in_=sall[:, 32, :]), [("sall", 32), "sbf"], [("sall", 0)])
    return P.finish()


def _m1_consts():
    jv1 = np.array([7, 6, 5, 4, 3, 2, 1, 0, 1, 2, 3, 4, 5, 6, 7, 8, -7, -6, -5, -4, -3, -2, -1, 0], np.float32)
    jv = np.ascontiguousarray(np.broadcast_to(jv1, (128, 4, 24))).astype(np.float32)
    sgn = np.ones((128, 2), np.float32); sgn[64:, 0] = -1; sgn[:64, 1] = -1
    i128 = np.eye(128, dtype=np.float32)
    jsw = np.zeros((128, 128), np.float32)
    for k in range(128):
        jsw[k, (k + 64) % 128] = 1
    s_idx = np.arange(128) // 16
    tmask = (s_idx[None, :] >= s_idx[:, None]).astype(np.float32)
    st = np.arange(64)
    cm = np.tile((st[:, None] <= st[None, :]).astype(np.float32), (1, 8))
    return dict(jv=jv, sgn=sgn, i128=i128, jsw=jsw, tmask=tmask, cmask=cm)


def run_M1(zT, inp, l):
    nc = build_M1(l)
    cst = _m1_consts()
    maps = []
    for c in range(NCORES):
        b = c // 4
        tsl = slice(b * SEQ, (b + 1) * SEQ)
        m = dict(cst)
        gs = [4 * (c % 4) + gi for gi in range(4)]
        st2 = lambda a: np.concatenate([a, a], axis=0)
        m["lr2"] = np.stack([st2(inp['s5_lambda_re'][l, g]) for g in gs], 1).astype(np.float32)
        m["li2"] = np.stack([st2(inp['s5_lambda_im'][l, g]) for g in gs], 1).astype(np.float32)
        m["ldt"] = np.ascontiguousarray(np.broadcast_to(np.array([inp['s5_log_dt'][l, g] for g in gs], np.float32), (128, 4)))
        m["bst1"] = np.stack([np.concatenate([inp['s5_b_re'][l, g], inp['s5_b_im'][l, g]], 0) for g in gs], 1)
        m["bst2"] = np.stack([np.concatenate([inp['s5_b_im'][l, g], inp['s5_b_re'][l, g]], 0) for g in gs], 1)
        m["cst1"] = np.stack([np.concatenate([inp['s5_c_re'][l, g].T, inp['s5_c_im'][l, g].T], 0) for g in gs], 1)
        m["cst2"] = np.stack([np.concatenate([inp['s5_c_im'][l, g].T, inp['s5_c_re'][l, g].T], 0) for g in gs], 1)
        m["dcol"] = np.stack([np.tile(inp['s5_d'][l, g], 8) for g in gs], 1).astype(np.float32)
        m["uc"] = np.stack([zT[g * 16:(g + 1) * 16, tsl].reshape(16, NCH, 8).transpose(2, 0, 1).reshape(128, NCH) for g in gs], 0)
        h = c % 4
        m["hq"] = zT[256 + 64 * h:256 + 64 * (h + 1), tsl]
        m["hf"] = zT[512 + 64 * h:512 + 64 * (h + 1), tsl]
        v = zT[768 + 64 * h:768 + 64 * (h + 1), tsl].T
        gg = zT[1024 + 64 * h:1024 + 64 * (h + 1), tsl].T
        m["hv64"] = v.reshape(4, 32, 64, 64).transpose(0, 2, 1, 3)
        m["hv128"] = v.reshape(4, 16, 128, 64).transpose(0, 2, 1, 3)
        m["hg64"] = gg.reshape(4, 32, 64, 64).transpose(0, 2, 1, 3)
        m["lbl"] = inp['hgrn_lb_logits'][:, 64 * h:64 * (h + 1)].T
        m["hgain"] = np.broadcast_to(inp['hgrn_norm'][l][None, :], (64, 64))
        maps.append({k: np.ascontiguousarray(v_, dtype=np.float32) for k, v_ in m.items()})
    res = run_bass_kernel_spmd(nc, maps, core_ids=list(range(NCORES)))
    y5T = np.zeros((256, BATCH * SEQ), np.float32)
    yhT = np.zeros((256, BATCH * SEQ), np.float32)
    for c in range(NCORES):
        b = c // 4
        tsl = slice(b * SEQ, (b + 1) * SEQ)
        r = res.results[c]
        for gi in range(4):
            g = 4 * (c % 4) + gi
            y5T[g * 16:(g + 1) * 16, tsl] = r["y5"][gi].reshape(8, 16, NCH).transpose(1, 2, 0).reshape(16, SEQ)
        h = c % 4
        yhT[64 * h:64 * (h + 1), tsl] = r["yh"].transpose(1, 0, 2).reshape(SEQ, 64).T
    return y5T, yhT


OFF_F = 8320
NF_F = OFF_F + 8192 + 128
NEGB = -30000.0
TINY = 1e-30
NQT = 32
SCALE = 0.125


def build_M2():
    P = Prog()
    nc = P.nc
    q65_d = P.dram_in("q65", [4, 65, NQT * 128])
    ksl_d = P.dram_in("kslX", [64, SEQ]); kwn_d = P.dram_in("kwnX", [64, SEQ])
    vsl_d = P.dram_in("vslX", [128, 64 * 64]); vwn_d = P.dram_in("vwnX", [128, 64 * 64])
    kcr_d = P.dram_in("kcrX", [32, 64, 512]); vcr_d = P.dram_in("vcrX", [32, 64, 512])
    w1k_d = P.dram_in("w1k", [64, 32 * 64]); w1v_d = P.dram_in("w1v", [64, 32 * 64])
    w2k_d = P.dram_in("w2k", [64, 64]); w2v_d = P.dram_in("w2v", [64, 64])
    posk_d = P.dram_in("poskT", [64, 32]); posv_d = P.dram_in("posvT", [64, 32])
    fs_d = P.dram_in("fs", [4, NF_F]); fw_d = P.dram_in("fw", [4, NF_F])
    eall_d = P.dram_in("eall", [128, SEQ]); mm_d = P.dram_in("mmat", [128, 4 * 128])
    madd_d = P.dram_in("madd", [NQT, 128, 128]); glog_d = P.dram_in("glog", [128, NQT * 12])
    i128_d = P.dram_in("i128", [128, 128])
    parity_dummy = None
    y_d = P.dram_out("ynsa", [128, NQT, 256])

    T = P.sbuf
    ps = [P.psum(f"ps{i}", [128, 512]) for i in range(8)]
    psA = [(ps[0], "ps0"), (ps[1], "ps1"), (ps[3], "psB0"), (ps[7], "psB1")]
    psL, psO1, psO2, psO3 = ps[2], ps[4], ps[5], ps[6]
    psS = ps[4]
    stg = [T(f"stg{i}", [128, 2048], F32) for i in range(2)]
    sctr = [0]

    def stage():
        i = sctr[0] % 2
        sctr[0] += 1
        return stg[i], f"stg{i}"

    i128 = T("i128", [128, 128], F32); P.dma(i128[:], i128_d[:, :], w=["i128"])
    ones = T("ones", [128, 128], BF16); P.op('pool', lambda e: e.memset(ones[:], 1.0), w=["ones"])
    qa = T("qa", [65, 4, NQT * 128], BF16)
    ksl = T("ksl", [65, SEQ], BF16); kwn = T("kwn", [64, SEQ], BF16)
    vsl = T("vsl", [128, 64, 65], BF16); vwn = T("vwn", [128, 64, 65], BF16)
    eall = T("eall", [128, SEQ], BF16); mmt = T("mmt", [128, 4, 128], F32)
    kcmp = T("kcmp", [64, 512], BF16); vcmp = T("vcmp", [128, 4, 65], BF16)
    bs = T("bs", [128, 11, 512], F32)
    glog = T("glog", [128, NQT * 12], F32)

    for r in range(4):
        for hh in range(2):
            st, sn = stage()
            P.dma(st[0:65, :], q65_d[r, :, hh * 2048:(hh + 1) * 2048], w=[sn])
            P.op('pool', lambda e: e.tensor_copy(out=qa[0:64, r, hh * 2048:(hh + 1) * 2048], in_=st[0:64, :]), r=[sn], w=["qa"])
            P.op('act', lambda e: e.activation(out=qa[64:65, r, hh * 2048:(hh + 1) * 2048], in_=st[64:65, :], func=AF.Copy, scale=8.0),
                 r=[sn], w=["qa"])
    for (dst, dn, src) in [(ksl, "ksl", ksl_d), (kwn, "kwn", kwn_d)]:
        for pc in range(4):
            st, sn = stage()
            P.dma(st[0:64, :], src[:, pc * 2048:(pc + 1) * 2048], w=[sn])
            P.op('pool', lambda e: e.tensor_copy(out=dst[0:64, pc * 2048:(pc + 1) * 2048], in_=st[0:64, :]), r=[sn], w=[dn])
    P.op('pool', lambda e: e.memset(ksl[64:65, :], 1.0), w=["ksl"])
    for (dst, dn, src) in [(vsl, "vsl", vsl_d), (vwn, "vwn", vwn_d)]:
        for pc in range(2):
            st, sn = stage()
            P.dma(st[:, :], src[:, pc * 2048:(pc + 1) * 2048], w=[sn])
            P.op('pool', lambda e: e.tensor_copy(out=dst[:, pc * 32:(pc + 1) * 32, 0:64], in_=st[:, :].rearrange("p (a b) -> p a b", b=64)),
                 r=[sn], w=[dn])
        P.op('pool', lambda e: e.memset(dst[:, :, 64:65], 1.0), w=[dn])
    for pc in range(4):
        st, sn = stage()
        P.dma(st[:, :], eall_d[:, pc * 2048:(pc + 1) * 2048], w=[sn])
        P.op('pool', lambda e: e.tensor_copy(out=eall[:, pc * 2048:(pc + 1) * 2048], in_=st[:, :]), r=[sn], w=["eall"])
    P.dma(mmt[:].rearrange("p a b -> p (a b)"), mm_d[:, :], w=["mmt"])
    P.dma(glog[:], glog_d[:, :], w=["glog"])
    P.op('act', lambda e: e.activation(out=glog[:], in_=glog[:], func=AF.Sigmoid), r=["glog"], w=["glog"])
    for j in range(11):
        tab = fs_d if j < 9 else fw_d
        dl = 128 * j if j < 9 else 128 * (j - 5)
        src = bass.AP(tab.tensor, OFF_F + dl - 127, [(1, 128), (NF_F, 4), (1, 128)])
        P.dma(bs[:, j, :].rearrange("p (r q) -> p r q", q=128), src, w=[("bs", j)], sem="bs")

    w1 = T("w1", [64, 32 * 64], BF16); w2 = T("w2", [64, 64], BF16); posT = T("posT", [64, 32], BF16)
    w2f = T("w2f", [64, 64], F32); posf = T("posf", [64, 32], F32)
    pb = T("pb", [64, 1], F32); xj = [T(f"xj{i}", [64, 512], BF16) for i in range(2)]
    ga = T("ga", [64, 512], F32); gb = T("gb", [64, 512], F32); gel = T("gel", [64, 512], BF16)
    for which in range(2):
        w1_d, w2_d, pos_d, x_d = [(w1k_d, w2k_d, posk_d, kcr_d), (w1v_d, w2v_d, posv_d, vcr_d)][which]
        st, sn = stage()
        P.dma(st[0:64, :], w1_d[:, :], w=[sn])
        P.op('pool', lambda e: e.tensor_copy(out=w1[:], in_=st[0:64, :]), r=[sn], w=["w1"])
        P.dma(w2f[:], w2_d[:, :], w=["w2f"]); P.dma(posf[:], pos_d[:, :], w=["posf"])
        P.op('pool', lambda e: e.tensor_copy(out=w2[:], in_=w2f[:]), r=["w2f"], w=["w2"])
        P.op('pool', lambda e: e.tensor_copy(out=posT[:], in_=posf[:]), r=["posf"], w=["posT"])
        for j in range(32):
            P.op('pe', lambda e: e.matmul(psL[0:64, 0:1], lhsT=w1[:, j * 64:(j + 1) * 64], rhs=posT[:, j:j + 1], start=(j == 0), stop=(j == 31)),
                 r=["w1", "posT"], w=["psL"])
        P.op('act', lambda e: e.activation(out=pb[:], in_=psL[0:64, 0:1], func=AF.Copy), r=["psL"], w=["pb"])
        for j in range(32):
            st, sn = stage()
            P.dma(st[0:64, 0:512], x_d[j], w=[sn])
            P.op('pool', lambda e: e.tensor_copy(out=xj[j % 2][:], in_=st[0:64, 0:512]), r=[sn], w=[f"xj{j % 2}"])
            P.op('pe', lambda e: e.matmul(ps[3][0:64, :], lhsT=w1[:, j * 64:(j + 1) * 64], rhs=xj[j % 2][:], start=(j == 0), stop=(j == 31)),
                 r=["w1", f"xj{j % 2}"], w=["psB0"])
        emit_gelu(P, gel[:], ps[3][0:64, :], ga[:], gb[:], ("gel", "ga", "gb"), ["psB0"], bias=(pb[:, 0:1], "pb"))
        if which == 0:
            P.op('pe', lambda e: e.matmul(psO1[0:64, :], lhsT=w2[:], rhs=gel[:], start=True, stop=True), r=["w2", "gel"], w=["psO1"])
            P.op('act', lambda e: e.activation(out=kcmp[:], in_=psO1[0:64, :], func=AF.Copy), r=["psO1"], w=["kcmp"])
        else:
            for ci in range(4):
                P.op('pe', lambda e: e.matmul(psO2[:, ci * 64:(ci + 1) * 64], lhsT=gel[:, ci * 128:(ci + 1) * 128], rhs=w2[:], start=True, stop=True),
                     r=["w2", "gel"], w=["psO2"])
            P.op('act', lambda e: e.activation(out=vcmp[:, :, 0:64], in_=psO2[:, 0:256].rearrange("p (a b) -> p a b", b=64), func=AF.Copy),
                 r=["psO2"], w=["vcmp"])
            P.op('pool', lambda e: e.memset(vcmp[:, :, 64:65], 1.0), w=["vcmp"])

    bc = [T(f"bc{i}", [128, 512], F32) for i in range(2)]
    tmpb = [T(f"tmpb{i}", [128, 512], F32) for i in range(2)]
    ebuf = [T(f"ebuf{i}", [128, 512], BF16) for i in range(4)]
    pbuf = [T(f"pbuf{i}", [128, 512], BF16) for i in range(4)]
    maskall = T("maskall", [128, 64, 128], BF16)
    ec = T("ec", [128, 4, 512], BF16)
    rlb = T("rlb", [128, 512], F32); pnb = T("pnb", [128, 512], F32); impT = T("impT", [128, 4, 128], F32)
    madd = [T(f"madd{i}", [128, 128], F32) for i in range(2)]
    score = T("score", [128, 128], F32); sc2 = T("sc2", [128, 128], F32); m8a = T("m8a", [128, 8], F32); m8b = T("m8b", [128, 8], F32)
    self_ = T("self", [128, 128], F32); selT = T("selT", [128, 128], BF16)
    osb = [T(f"osb{i}", [128, 260], F32) for i in range(3)]
    lc = T("lc", [128, 3, 4], F32); coef = T("coef", [128, 3, 4], F32)
    yb = [T(f"yb{i}", [128, 256], F32) for i in range(2)]
    rot = dict(a=0, t=0, e=0, p=0, b=0)

    NROT = dict(a=4, t=2, e=4, p=4, b=2)

    def nxt(k):
        n = NROT[k]
        v = rot[k] % n
        rot[k] += 1
        return v

    def softmax_tile(A_src, An, bias_ap, bias_names, eout, eout_name):
        if bias_ap is None:
            P.op('act', lambda e: e.activation(out=eout, in_=A_src, func=AF.Exp, scale=SCALE), r=[An], w=[eout_name])
        else:
            ti = nxt('t')
            P.op('dve', lambda e: e.scalar_tensor_tensor(out=tmpb[ti][:], in0=A_src, scalar=SCALE, in1=bias_ap, op0=ALU.mult, op1=ALU.add),
                 r=[An] + bias_names, w=[f"tmpb{ti}"])
            P.op('act', lambda e: e.activation(out=eout, in_=tmpb[ti][:], func=AF.Exp), r=[f"tmpb{ti}"], w=[eout_name])

    def pv(psO, psOn, lhs_tile, lhs_name, v_ap, v_name, first, last=False):
        for r in range(4):
            P.op('pe', lambda e: e.matmul(psO[:, r * 65:(r + 1) * 65], lhsT=lhs_tile[:, r * 128:(r + 1) * 128], rhs=v_ap,
                                          start=(first and r == 0), stop=(last and r == 3)), r=[lhs_name, v_name], w=[psOn])

    import os
    NT_RUN = int(os.environ.get("M2_NT", NQT))
    for m in range(NT_RUN):
        qi = 2 * m + 1
        t0 = 128 * qi
        q64 = qa[0:64, :, m * 128:(m + 1) * 128]
        q65 = qa[0:65, :, m * 128:(m + 1) * 128]
        def run_pairs(pairs, depth=2):
            pend = []
            for (A_, B_) in pairs:
                pend.append((B_, A_()))
                if len(pend) > depth:
                    b_, c_ = pend.pop(0)
                    b_(c_)
            for b_, c_ in pend:
                b_(c_)

        nck = min(4, ((t0 + 96) // 16) // 128 + 1)

        def cmpA(ci):
            def f():
                bi = nxt('b')
                src = bass.AP(fs_d.tensor, OFF_F + t0 - 2048 * ci - 2063, [(16, 128), (NF_F, 4), (1, 128)])
                P.dma(bc[bi][:].rearrange("p (r q) -> p r q", q=128), src, w=[f"bc{bi}"])
                ai = nxt('a'); A, An = psA[ai]
                P.op('pe', lambda e: e.matmul(A[:, :], lhsT=kcmp[:, ci * 128:(ci + 1) * 128], rhs=q64, start=True, stop=True),
                     r=["kcmp", "qa"], w=[An])
                softmax_tile(A[:, :], An, bc[bi][:], [f"bc{bi}"], ec[:, ci, :], ("ec", ci))
                return ci
            return f

        def cmpB(ci):
            P.op('pe', lambda e: e.matmul(psL[:, :], lhsT=ones[:], rhs=ec[:, ci, :], start=(ci == 0), stop=(ci == nck - 1)),
                 r=["ones", ("ec", ci)], w=["psL"])
            pv(psO1, "psO1", ec[:, ci, :], ("ec", ci), vcmp[:, ci, :], "vcmp", ci == 0, ci == nck - 1)
        run_pairs([(cmpA(ci), cmpB) for ci in range(nck)])
        P.op('dve', lambda e: e.tensor_scalar(out=rlb[:], in0=psL[:, :], scalar1=TINY, scalar2=None, op0=ALU.max), r=["psL"], w=["rlb"])
        P.op('dve', lambda e: e.reciprocal(out=rlb[:], in_=rlb[:]), r=["rlb"], w=["rlb"])
        for ci in range(nck):
            P.op('dve', lambda e: e.tensor_tensor(out=pnb[:], in0=ec[:, ci, :], in1=rlb[:], op=ALU.mult), r=[("ec", ci), "rlb"], w=["pnb"])
            P.op('dve', lambda e: e.tensor_reduce(out=impT[:, ci, :], in_=pnb[:].rearrange("p (r q) -> p q r", q=128), axis=AX.X, op=ALU.add),
                 r=["pnb"], w=[("impT", ci)])
        for ci in range(nck):
            P.op('pe', lambda e: e.matmul(psS[:, 260:388], lhsT=impT[:, ci, :], rhs=mmt[:, ci, :], start=False, stop=(ci == nck - 1)),
                 r=[("impT", ci), "mmt"], w=["psO1"])
        P.op('act', lambda e: e.activation(out=osb[0][:], in_=psO1[:, 0:260], func=AF.Copy), r=["psO1"], w=["osb0"])
        kts = list(range(max(0, qi - 5), qi + 1))

        def winA(kt):
            def f():
                j = qi - kt
                ai = nxt('a'); A, An = psA[ai]
                P.op('pe', lambda e: e.matmul(A[:, :], lhsT=kwn[0:64, kt * 128:(kt + 1) * 128], rhs=q64, start=True, stop=True),
                     r=["kwn", "qa"], w=[An])
                ei = nxt('e')
                jj = j if j < 4 else 5 + j
                softmax_tile(A[:, :], An, bs[:, jj, :], [("bs", jj)], ebuf[ei][:], f"ebuf{ei}")
                return (kt, ei)
            return f

        def winB(ctx):
            kt, ei = ctx
            pv(psO3, "psO3", ebuf[ei], f"ebuf{ei}", vwn[:, kt, :], "vwn", kt == kts[0], kt == kts[-1])
        run_pairs([(winA(kt), winB) for kt in kts])
        mi = m % 2
        P.dma(madd[mi][:], madd_d[m], w=[f"madd{mi}"])
        P.op('dve', lambda e: e.tensor_tensor(out=score[:], in0=psS[:, 260:388], in1=madd[mi][:], op=ALU.add), r=["psO1", f"madd{mi}"], w=["score"])
        P.op('dve', lambda e: e.max(out=m8a[:], in_=score[:]), r=["score"], w=["m8a"])
        P.op('dve', lambda e: e.match_replace(out=sc2[:], in_to_replace=m8a[:], in_values=score[:], imm_value=-3e38), r=["score", "m8a"], w=["sc2"])
        P.op('dve', lambda e: e.max(out=m8b[:], in_=sc2[:]), r=["sc2"], w=["m8b"])
        P.op('dve', lambda e: e.tensor_scalar(out=self_[:], in0=score[:], scalar1=m8b[:, 7:8], scalar2=None, op0=ALU.is_ge),
             r=["score", "m8b"], w=["self"])
        P.op('pe', lambda e: e.transpose(out=psL[:, 0:128], in_=self_[:], identity=i128[:]), r=["self", "i128"], w=["psL"])
        P.op('act', lambda e: e.activation(out=selT[:], in_=psL[:, 0:128], func=AF.Copy), r=["psL"], w=["selT"])
        for k0 in range(0, qi + 1, 4):
            n4 = min(4, qi + 1 - k0)
            ai = nxt('a'); A, An = psA[ai]
            for u in range(n4):
                P.op('pe', lambda e: e.matmul(A[:, u * 128:(u + 1) * 128], lhsT=eall[:, (k0 + u) * 128:(k0 + u + 1) * 128], rhs=selT[:],
                                              start=(u == 0), stop=(u == n4 - 1)), r=["eall", "selT"], w=[An])
            P.op('act', lambda e: e.activation(out=maskall[:, k0:k0 + n4, :].rearrange("p a b -> p (a b)"), in_=A[:, 0:n4 * 128], func=AF.Copy),
                 r=[An], w=[("maskall", k0)])

        def selA(kt):
            def f():
                near = (qi - kt) <= 8
                ai = nxt('a'); A, An = psA[ai]
                if near:
                    P.op('pe', lambda e: e.matmul(A[:, :], lhsT=ksl[0:64, kt * 128:(kt + 1) * 128], rhs=q64, start=True, stop=True),
                         r=["ksl", "qa"], w=[An])
                else:
                    P.op('pe', lambda e: e.matmul(A[:, :], lhsT=ksl[0:65, kt * 128:(kt + 1) * 128], rhs=q65, start=True, stop=True),
                         r=["ksl", "qa"], w=[An])
                ei = nxt('e')
                if near:
                    softmax_tile(A[:, :], An, bs[:, qi - kt, :], [("bs", qi - kt)], ebuf[ei][:], f"ebuf{ei}")
                else:
                    softmax_tile(A[:, :], An, None, None, ebuf[ei][:], f"ebuf{ei}")
                bi = nxt('p')
                P.op('pool' if (kt % 5) < 2 else 'dve', lambda e: e.tensor_tensor(out=pbuf[bi][:].rearrange("p (r q) -> p r q", q=128),
                                                       in0=ebuf[ei][:].rearrange("p (r q) -> p r q", q=128),
                                                       in1=maskall[:, kt:kt + 1, :].to_broadcast([128, 4, 128]), op=ALU.mult),
                     r=[f"ebuf{ei}", ("maskall", (kt // 4) * 4)], w=[f"pbuf{bi}"])
                return (kt, bi)
            return f

        def selB(ctx):
            kt, bi = ctx
            pv(psO2, "psO2", pbuf[bi], f"pbuf{bi}", vsl[:, kt, :], "vsl", kt == 0, kt == qi)
        run_pairs([(selA(kt), selB) for kt in range(qi + 1)], depth=3)
        for b_, (pso, pson) in enumerate([(psO1, "psO1"), (psO2, "psO2"), (psO3, "psO3")]):
            if b_ > 0:
                P.op('act', lambda e: e.activation(out=osb[b_][:], in_=pso[:, 0:260], func=AF.Copy), r=[pson], w=[f"osb{b_}"])
            P.op('dve', lambda e: e.tensor_copy(out=lc[:, b_, :], in_=osb[b_][:].rearrange("p (r d) -> p r d", d=65)[:, :, 64]),
                 r=[f"osb{b_}"], w=["lc"])
        P.op('dve', lambda e: e.tensor_scalar(out=lc[:], in0=lc[:], scalar1=TINY, scalar2=None, op0=ALU.max), r=["lc"], w=["lc"])
        P.op('dve', lambda e: e.reciprocal(out=lc[:], in_=lc[:]), r=["lc"], w=["lc"])
        P.op('dve', lambda e: e.tensor_tensor(out=coef[:], in0=lc[:],
                                              in1=glog[:, m * 12:(m + 1) * 12].rearrange("p (r b) -> p b r", b=3), op=ALU.mult),
             r=["lc", "glog"], w=["coef"])
        y_ = yb[m % 2]; yn = f"yb{m % 2}"
        for r in range(4):
            for b_ in range(3):
                src_ = osb[b_][:, r * 65:r * 65 + 64]
                if b_ == 0:
                    P.op('dve', lambda e: e.tensor_scalar(out=y_[:, r * 64:(r + 1) * 64], in0=src_, scalar1=coef[:, b_, r:r + 1], scalar2=None,
                                                          op0=ALU.mult), r=[f"osb{b_}", "coef"], w=[(yn, r)])
                else:
                    P.op('dve', lambda e: e.scalar_tensor_tensor(out=y_[:, r * 64:(r + 1) * 64], in0=src_, scalar=coef[:, b_, r:r + 1],
                                                                 in1=y_[:, r * 64:(r + 1) * 64], op0=ALU.mult, op1=ALU.add),
                         r=[f"osb{b_}", "coef", (yn, r)], w=[(yn, r)])
        P.dma(y_d[:, m, :], y_[:], r=[yn], w=[("yd", m)], sem=f"ys{m % 2}")
    return P.finish()


def _t5_bucket_np(d):
    import math
    n = np.maximum(d, 0)
    nf = np.maximum(n, 16).astype(np.float32)
    large = 16 + (np.log(nf / np.float32(16)) / np.float32(math.log(64.0)) * np.float32(16)).astype(np.int32)
    large = np.minimum(large, 31)
    return np.where(n < 16, n, large)


def _m2_consts():
    eall = np.zeros((128, SEQ), np.float32)
    col = np.arange(SEQ)
    kt = col // 128
    p = col % 128
    eall[2 * kt + (127 - p) // 64, col] = 1.0
    mm = np.zeros((128, 4, 128), np.float32)
    wts = {-1: 1.0, 0: 2.0, 1: 2.0, 2: 2.0, 3: 1.0}
    for ci in range(4):
        for pp in range(128):
            c = 128 * ci + 127 - pp
            for dlt, wv in wts.items():
                if (c - dlt) % 4 == 0:
                    j = (c - dlt) // 4
                    if 0 <= j < 128:
                        mm[pp, ci, j] = wv
    return dict(eall=eall, mmat=mm.reshape(128, 512), i128=np.eye(128, dtype=np.float32))


def run_M2(zT, inp, l):
    nc = build_M2()
    cst = _m2_consts()
    rel = inp['rel_bias'].astype(np.float32)
    didx = np.arange(NF_F) - OFF_F
    bk = _t5_bucket_np(didx)
    maps = []
    for c in range(NCORES):
        b = c // 4
        g = (c % 4) // 2
        par = c % 2
        tsl = slice(b * SEQ, (b + 1) * SEQ)
        m = dict(cst)
        sh = 128 * (1 - par)
        fs = np.full((4, NF_F), NEGB, np.float32)
        fw = np.full((4, NF_F), NEGB, np.float32)
        for r in range(4):
            base_s = np.where(didx >= 0, rel[bk, g * 4 + r], NEGB).astype(np.float32)
            base_w = np.where((didx >= 0) & (didx < 512), rel[bk, g * 4 + r], NEGB).astype(np.float32)
            fs[r, sh:] = base_s[:NF_F - sh]
            fw[r, sh:] = base_w[:NF_F - sh]
        m["fs"] = fs
        m["fw"] = fw
        tiles = 2 * np.arange(NQT) + par
        tok = (tiles[:, None] * 128 + np.arange(128)[None, :]).reshape(-1)
        q65 = np.zeros((4, 65, NQT * 128), np.float32)
        for r in range(4):
            q65[r, 0:64] = zT[1280 + g * 256 + r * 64:1280 + g * 256 + (r + 1) * 64, tsl][:, tok]
            q65[r, 64] = rel[31, g * 4 + r]
        m["q65"] = q65

        def kv(j):
            return zT[1792 + j * 128 + g * 64:1792 + j * 128 + (g + 1) * 64, tsl]
        rev = lambda a: a.reshape(64, 64, 128)[:, :, ::-1].reshape(64, SEQ)
        m["kslX"] = rev(kv(2))
        m["kwnX"] = rev(kv(4))
        vrev = lambda a: a.T.reshape(64, 128, 64)[:, ::-1, :].transpose(1, 0, 2).reshape(128, 64 * 64)
        m["vslX"] = vrev(kv(3))
        m["vwnX"] = vrev(kv(5))
        cc = (128 * (np.arange(512) // 128) + 127 - (np.arange(512) % 128))
        for nm, j in [("kcrX", 0), ("vcrX", 1)]:
            src = np.concatenate([kv(j), np.zeros((64, 64), np.float32)], axis=1)
            arr = np.zeros((32, 64, 512), np.float32)
            for jj in range(32):
                arr[jj] = src[:, np.minimum(16 * cc + jj, SEQ + 63)]
            m[nm] = arr
        for sfx, key in [("k", "k"), ("v", "v")]:
            w1 = inp[f'nsa_cmp_w1_{key}'][l]
            m[f"w1{sfx}"] = w1.reshape(32, 64, 64).transpose(1, 0, 2).reshape(64, 2048)
            m[f"w2{sfx}"] = inp[f'nsa_cmp_w2_{key}'][l]
            m[f"pos{sfx}T"] = inp[f'nsa_cmp_pos_{key}'][l].T
        t = tiles[:, None] * 128 + np.arange(128)[None, :]
        blk = np.arange(128)
        cur = t // 64
        ok = (blk[None, None, :] * 64) <= t[:, :, None]
        forced = (blk[None, None, :] == 0) | (blk[None, None, :] == cur[:, :, None]) | (blk[None, None, :] == cur[:, :, None] - 1)
        m["madd"] = np.where(ok, np.where(forced, 1e4, 0.0), -1e30).astype(np.float32)
        gl = zT[2560 + g * 12:2560 + (g + 1) * 12, tsl][:, tok]
        m["glog"] = gl.T.reshape(NQT, 128, 12).transpose(1, 0, 2).reshape(128, NQT * 12)
        maps.append({k: np.ascontiguousarray(v_, dtype=np.float32) for k, v_ in m.items()})
    res = run_bass_kernel_spmd(nc, maps, core_ids=list(range(NCORES)))
    yT = np.zeros((512, BATCH * SEQ), np.float32)
    for c in range(NCORES):
        b = c // 4
        g = (c % 4) // 2
        par = c % 2
        tiles = 2 * np.arange(NQT) + par
        tok = b * SEQ + (tiles[:, None] * 128 + np.arange(128)[None, :]).reshape(-1)
        y = res.results[c]["ynsa"]
        y = y.transpose(1, 0, 2).reshape(NQT * 128, 256)
        yT[g * 256:(g + 1) * 256, tok] = y.T
    return yT


def kernel(**inputs):
    inp = {k: np.asarray(v) for k, v in inputs.items()}
    x = inp['x'].astype(np.float32).reshape(BATCH * SEQ, D_MODEL)
    xT = np.ascontiguousarray(x.T)

    def win_map(l):
        win = np.zeros((D_MODEL, NZC * 128), np.float32)
        win[:, :D_IN] = inp['w_in'][l]
        return {"mix_g": _gcol(inp['mix_norm'][l]), "win": _chunkT(win, 8)}

    def wout_map(l):
        return {"wglu": _chunkT(inp['s5_w_glu'][l], 2), "wout": _chunkT(inp['w_out'][l], 8)}

    def mixers(zT, l):
        y5T, yhT = run_M1(zT, inp, l)
        ynT = run_M2(zT, inp, l)
        return np.concatenate([y5T, yhT, ynT], axis=0)

    common = _ffn_maps("f0", inp['ffn1_norm'][0], inp['ffn1_w_gate'][0], inp['ffn1_w_up'][0], inp['ffn1_w_down'][0])
    common.update(win_map(0))
    o = run_T(xT, common, False, 1, True, False)
    yT = mixers(o["zT"], 0)
    common = _ffn_maps("f0", inp['ffn2_norm'][0], inp['ffn2_w_gate'][0], inp['ffn2_w_up'][0], inp['ffn2_w_down'][0])
    common.update(_ffn_maps("f1", inp['ffn1_norm'][1], inp['ffn1_w_gate'][1], inp['ffn1_w_up'][1], inp['ffn1_w_down'][1]))
    common.update(win_map(1))
    common.update(wout_map(0))
    o = run_T(o["xoT"], common, True, 2, True, False, yT_full=yT)
    yT = mixers(o["zT"], 1)
    common = _ffn_maps("f0", inp['ffn2_norm'][1], inp['ffn2_w_gate'][1], inp['ffn2_w_up'][1], inp['ffn2_w_down'][1])
    common.update(wout_map(1))
    common["fin_g"] = _gcol(inp['final_norm'])
    o = run_T(o["xoT"], common, True, 1, False, True, yT_full=yT)
    out = np.ascontiguousarray(o["xoT"].T).reshape(BATCH, SEQ, D_MODEL).astype(np.float32)
    return out
```

```python
import contextlib
import numpy as np
import concourse.bass as bass
import concourse.mybir as mybir
from concourse.bass_utils import run_bass_kernel_spmd

F32 = mybir.dt.float32
BF16 = mybir.dt.bfloat16
AF = mybir.ActivationFunctionType
ALU = mybir.AluOpType
AX = mybir.AxisListType

D_MODEL = 1024
D_FF = 2816
SEQ = 8192
BATCH = 2
D_IN = 2584
NCORES = 8
EPS = 1e-6


class Prog:
    def __init__(self):
        self.nc = bass.Bass("TRN2", target_bir_lowering=False)
        self.es = contextlib.ExitStack()
        nc = self.nc
        self.eng = {'pe': nc.tensor, 'act': nc.scalar, 'dve': nc.vector, 'pool': nc.gpsimd, 'sp': nc.sync}
        self.sems = {}
        self.cnt = {}
        for k in ['pe', 'act', 'dve', 'pool']:
            self.sems[('e', k)] = self.es.enter_context(nc.semaphore('e_' + k))
            self.cnt[('e', k)] = 0
        self.waited = {k: {} for k in self.eng}
        self.state = {}
        self.nps = 0

    def sbuf(self, name, shape, dt):
        return self.es.enter_context(self.nc.sbuf_tensor("s_" + name, list(shape), dt))

    def psum(self, name, shape, dt=F32):
        return self.es.enter_context(self.nc.psum_tensor("p_" + name, list(shape), dt))

    def dram_in(self, name, shape, dt=F32):
        return self.nc.dram_tensor(name, list(shape), dt, kind="ExternalInput").ap()

    def dram_out(self, name, shape, dt=F32):
        return self.nc.dram_tensor(name, list(shape), dt, kind="ExternalOutput").ap()

    @staticmethod
    def _norm(x):
        return x if isinstance(x, tuple) else (x, None)

    def _deps(self, rs, ws, e):
        need = {}

        def add(sv):
            if sv is None:
                return
            s, v = sv
            if s[0] == 'd':
                v = self.cnt[s]
            if need.get(s, 0) < v:
                need[s] = v

        for (n, k) in rs:
            for kk, ent in self.state.get(n, {}).items():
                if k is None or kk is None or kk == k:
                    add(ent['w'])
        for (n, k) in ws:
            for kk, ent in self.state.get(n, {}).items():
                if k is None or kk is None or kk == k:
                    if ent['w'] is not None and ent['w'][0] != ('e', e):
                        add(ent['w'])
                    for s, v in ent['r'].items():
                        if s != ('e', e):
                            add((s, v))
        if e == 'pe':
            need.pop(('e', 'pe'), None)
        return need

    def _record(self, rs, ws, sv):
        s, v = sv
        for (n, k) in rs:
            ent = self.state.setdefault(n, {}).setdefault(k, {'w': None, 'r': {}})
            ent['r'][s] = max(ent['r'].get(s, 0), v)
        for (n, k) in ws:
            d = self.state.setdefault(n, {})
            if k is None:
                d.clear()
            d[k] = {'w': (s, v), 'r': {}}

    def _emit_waits(self, e, need):
        eng = self.eng[e]
        for s, v in need.items():
            if self.waited[e].get(s, 0) < v:
                eng.wait_ge(self.sems[s], v)
                self.waited[e][s] = v

    def op(self, e, fn, r=(), w=()):
        rs = [self._norm(x) for x in r]
        ws = [self._norm(x) for x in w]
        ws = ws + [(n, None) for (n, k) in rs if n.startswith("ps")]
        ws = [((n, None) if n.startswith("ps") else (n, k)) for (n, k) in ws]
        rs = [x for x in rs if not x[0].startswith("ps")]
        self._emit_waits(e, self._deps(rs, ws, e))
        inst = fn(self.eng[e])
        s = ('e', e)
        self.cnt[s] += 1
        inst.then_inc(self.sems[s], 1)
        self._record(rs, ws, (s, self.cnt[s]))

    def dma(self, out, in_, r=(), w=(), q='sp', sem=None):
        rs = [self._norm(x) for x in r]
        ws = [self._norm(x) for x in w]
        if sem is None:
            sem = ws[0][0]
        s = ('d', sem)
        if s not in self.sems:
            self.sems[s] = self.es.enter_context(self.nc.semaphore('d_' + sem))
            self.cnt[s] = 0
        self._emit_waits(q, self._deps(rs, ws, q))
        self.eng[q].dma_start(out=out, in_=in_).then_inc(self.sems[s], 16)
        self.cnt[s] += 16
        self._record(rs, ws, (s, self.cnt[s]))

    def finish(self):
        sp = self.eng['sp']
        for s, h in self.sems.items():
            if self.cnt[s] > 0 and self.waited['sp'].get(s, 0) < self.cnt[s]:
                sp.wait_ge(h, self.cnt[s])
        self.es.close()
        return self.nc


def mm(ps, lhsT, rhs, start, stop):
    return lambda e: e.matmul(ps, lhsT=lhsT, rhs=rhs, start=start, stop=stop)


NTOK = 2048
TP = 1024
NFC = D_FF // 128
NZC = 21


def build_T(do_wout, n_ffn, do_win, do_final):
    P = Prog()
    nc = P.nc
    xT_d = P.dram_in("xT", [8, 128, NTOK])
    if do_wout:
        yT_d = P.dram_in("yT", [8, 128, NTOK])
        wglu_d = P.dram_in("wglu", [2, 128, 2, 128])
        wout_d = P.dram_in("wout", [8, 128, 8, 128])
    ffn_d = []
    for i in range(n_ffn):
        ffn_d.append(dict(
            g=P.dram_in(f"f{i}_g", [128, 8]),
            wg=P.dram_in(f"f{i}_wg", [NFC, 128, 8, 128]),
            wu=P.dram_in(f"f{i}_wu", [NFC, 128, 8, 128]),
            wd=P.dram_in(f"f{i}_wd", [8, 128, NFC, 128]),
        ))
    if do_win:
        ming_d = P.dram_in("mix_g", [128, 8])
        win_d = P.dram_in("win", [NZC, 128, 8, 128])
        zT_d = P.dram_out("zT", [NZC, 128, NTOK])
    if do_final:
        fing_d = P.dram_in("fin_g", [128, 8])
    xo_d = P.dram_out("xoT", [8, 128, NTOK])

    xT = P.sbuf("xT_s", [128, 8, TP], F32)
    hT = P.sbuf("hT_s", [128, 8, TP], BF16)
    aT = P.sbuf("aT_s", [128, NFC, TP], BF16)
    sq = P.sbuf("sq_s", [128, 2, TP], BF16)
    rstd = P.sbuf("rstd_s", [128, TP], F32)
    ones = P.sbuf("ones_s", [128, 128], BF16)
    gt = P.sbuf("g_s", [128, 8], F32)
    stg = [P.sbuf(f"stg{i}", [128, NFC * 128], F32) for i in range(3)]
    wbf = [P.sbuf(f"wbf{i}", [128, NFC * 128], BF16) for i in range(3)]
    sg = [P.sbuf(f"sg{i}", [128, 512], F32) for i in range(2)]
    ev = [P.sbuf(f"ev{i}", [128, 512], F32) for i in range(2)]
    ps = [P.psum(f"ps{i}", [128, 512]) for i in range(8)]
    if do_wout:
        yf = P.sbuf("yf_s", [128, 8, TP], F32)

    P.op('pool', lambda e: e.memset(ones[:], 1.0), w=["ones"])

    wctr = [0]

    def load_w(src_ap, nk):
        i = wctr[0] % 3
        wctr[0] += 1
        P.dma(stg[i][:, 0:nk * 128], src_ap, w=[f"stg{i}"])
        P.op('pool', lambda e: e.tensor_copy(out=wbf[i][:, 0:nk * 128], in_=stg[i][:, 0:nk * 128]),
             r=[f"stg{i}"], w=[f"wbf{i}"])
        return wbf[i], f"wbf{i}"

    psctr = [0]

    def next_ps():
        i = psctr[0] % 8
        psctr[0] += 1
        return ps[i], f"ps{i}"

    def rmsnorm(g_dram, final=False):
        P.dma(gt[:], g_dram, w=["g"])
        pss = [next_ps(), next_ps()]
        for k in range(8):
            j = k % 2
            P.op('act', lambda e: e.activation(out=sq[:, j, :], in_=xT[:, k, :], func=AF.Square),
                 r=[("xT", k)], w=[("sq", j)])
            for tg in range(2):
                P.op('pe', mm(pss[tg][0][:, :], ones[:], sq[:, j, tg * 512:(tg + 1) * 512], k == 0, k == 7),
                     r=["ones", ("sq", j)], w=[pss[tg][1]])
        for tg in range(2):
            sl = slice(tg * 512, (tg + 1) * 512)
            P.op('act', lambda e: e.activation(out=rstd[:, sl], in_=pss[tg][0][:, :], func=AF.Sqrt,
                                               scale=1.0 / D_MODEL, bias=EPS),
                 r=[pss[tg][1]], w=[("rstd", tg)])
            P.op('dve', lambda e: e.reciprocal(out=rstd[:, sl], in_=rstd[:, sl]),
                 r=[("rstd", tg)], w=[("rstd", tg)])
        for k in range(8):
            if final:
                P.op('dve', lambda e: e.scalar_tensor_tensor(out=xT[:, k, :], in0=xT[:, k, :], scalar=gt[:, k:k + 1],
                                                             in1=rstd[:, :], op0=ALU.mult, op1=ALU.mult),
                     r=[("xT", k), "g", "rstd"], w=[("xT", k)])
            else:
                P.op('dve', lambda e: e.scalar_tensor_tensor(out=hT[:, k, :], in0=xT[:, k, :], scalar=gt[:, k:k + 1],
                                                             in1=rstd[:, :], op0=ALU.mult, op1=ALU.mult),
                     r=[("xT", k), "g", "rstd"], w=[("hT", k)])

    def ffn(fd):
        rmsnorm(fd['g'])
        for fc in range(NFC):
            wg, wgn = load_w(fd['wg'][fc].rearrange("p k j -> p (k j)"), 8)
            wu, wun = load_w(fd['wu'][fc].rearrange("p k j -> p (k j)"), 8)
            pg = [next_ps(), next_ps()]
            pu = [next_ps(), next_ps()]
            for (w_, wn_, pp_) in [(wg, wgn, pg), (wu, wun, pu)]:
                for k in range(8):
                    for tg in range(2):
                        P.op('pe', mm(pp_[tg][0][:, :], w_[:, k * 128:(k + 1) * 128], hT[:, k, tg * 512:(tg + 1) * 512], k == 0, k == 7),
                             r=[wn_, "hT"], w=[pp_[tg][1]])
            for tg in range(2):
                sl = slice(tg * 512, (tg + 1) * 512)
                i = tg
                P.op('act', lambda e: e.activation(out=sg[i][:, :], in_=pg[tg][0][:, :], func=AF.Silu),
                     r=[pg[tg][1]], w=[f"sg{i}"])
                P.op('dve', lambda e: e.tensor_tensor(out=aT[:, fc, sl], in0=sg[i][:, :], in1=pu[tg][0][:, :], op=ALU.mult),
                     r=[f"sg{i}", pu[tg][1]], w=[("aT", fc)])
        for mc in range(8):
            wd, wdn = load_w(fd['wd'][mc].rearrange("p k j -> p (k j)"), NFC)
            pd = [next_ps(), next_ps()]
            for k in range(NFC):
                for tg in range(2):
                    P.op('pe', mm(pd[tg][0][:, :], wd[:, k * 128:(k + 1) * 128], aT[:, k, tg * 512:(tg + 1) * 512], k == 0, k == NFC - 1),
                         r=[wdn, "aT"], w=[pd[tg][1]])
            for tg in range(2):
                sl = slice(tg * 512, (tg + 1) * 512)
                P.op('dve', lambda e: e.scalar_tensor_tensor(out=xT[:, mc, sl], in0=pd[tg][0][:, :], scalar=0.5,
                                                             in1=xT[:, mc, sl], op0=ALU.mult, op1=ALU.add),
                     r=[pd[tg][1], ("xT", mc)], w=[("xT", mc)])

    for ps_i in range(NTOK // TP):
        tsl = slice(ps_i * TP, (ps_i + 1) * TP)
        for k in range(8):
            P.dma(xT[:, k, :], xT_d[k, :, tsl], w=[("xT", k)], sem="xT")
        if do_wout:
            for k in range(8):
                P.dma(yf[:, k, :], yT_d[k, :, tsl], w=[("yf", k)], sem="yf")
            for k in range(2):
                P.op('pool', lambda e: e.tensor_copy(out=hT[:, k, :], in_=yf[:, k, :]), r=[("yf", k)], w=[("hT", k)])
            for mc in range(2):
                wl, wln = load_w(wglu_d[mc].rearrange("p k j -> p (k j)"), 2)
                for tg in range(2):
                    sl = slice(tg * 512, (tg + 1) * 512)
                    pg, pgn = next_ps()
                    for k in range(2):
                        P.op('pe', mm(pg[:, :], wl[:, k * 128:(k + 1) * 128], hT[:, k, sl], k == 0, k == 1),
                             r=[wln, ("hT", 0), ("hT", 1)], w=[pgn])
                    i = tg
                    P.op('act', lambda e: e.activation(out=sg[i][:, :], in_=pg[:, :], func=AF.Sigmoid),
                         r=[pgn], w=[f"sg{i}"])
                    P.op('dve', lambda e: e.tensor_tensor(out=aT[:, mc, sl], in0=sg[i][:, :], in1=yf[:, mc, sl],
                                                          op=ALU.mult),
                         r=[f"sg{i}", ("yf", mc)], w=[("aT", mc)])
            for k in range(2, 8):
                P.op('pool', lambda e: e.tensor_copy(out=aT[:, k, :], in_=yf[:, k, :]), r=[("yf", k)], w=[("aT", k)])
            for mc in range(8):
                wl, wln = load_w(wout_d[mc].rearrange("p k j -> p (k j)"), 8)
                for tg in range(2):
                    sl = slice(tg * 512, (tg + 1) * 512)
                    pd, pdn = next_ps()
                    for k in range(8):
                        P.op('pe', mm(pd[:, :], wl[:, k * 128:(k + 1) * 128], aT[:, k, sl], k == 0, k == 7),
                             r=[wln, "aT"], w=[pdn])
                    P.op('dve', lambda e: e.tensor_tensor(out=xT[:, mc, sl], in0=pd[:, :], in1=xT[:, mc, sl],
                                                          op=ALU.add),
                         r=[pdn, ("xT", mc)], w=[("xT", mc)])
        for i in range(n_ffn):
            ffn(ffn_d[i])
        if do_win:
            rmsnorm(ming_d)
            for cc in range(NZC):
                wl, wln = load_w(win_d[cc].rearrange("p k j -> p (k j)"), 8)
                for tg in range(2):
                    sl = slice(tg * 512, (tg + 1) * 512)
                    pz, pzn = next_ps()
                    for k in range(8):
                        P.op('pe', mm(pz[:, :], wl[:, k * 128:(k + 1) * 128], hT[:, k, sl], k == 0, k == 7),
                             r=[wln, "hT"], w=[pzn])
                    i = (cc * 2 + tg) % 2
                    P.op('act', lambda e: e.activation(out=ev[i][:, :], in_=pz[:, :], func=AF.Copy),
                         r=[pzn], w=[f"ev{i}"])
                    P.dma(zT_d[cc, :, ps_i * TP + tg * 512: ps_i * TP + (tg + 1) * 512], ev[i][:, :],
                          r=[f"ev{i}"], w=[("zT", (ps_i, cc, tg))], sem=f"zst{i}")
        if do_final:
            rmsnorm(fing_d, final=True)
        for k in range(8):
            P.dma(xo_d[k, :, tsl], xT[:, k, :], r=[("xT", k)], w=[("xo", (ps_i, k))], sem="xo")
    return P.finish()


def _chunkT(w, nk):
    K, M = w.shape
    return np.ascontiguousarray(w.reshape(nk, 128, M // 128, 128).transpose(2, 1, 0, 3))


def _gcol(g):
    return np.ascontiguousarray(g.reshape(8, 128).T)


def _ffn_maps(prefix, g, wg, wu, wd):
    return {f"{prefix}_g": _gcol(g), f"{prefix}_wg": _chunkT(wg, 8), f"{prefix}_wu": _chunkT(wu, 8),
            f"{prefix}_wd": _chunkT(wd, NFC)}


def run_T(xT_full, common, do_wout, n_ffn, do_win, do_final, yT_full=None):
    nc = build_T(do_wout, n_ffn, do_win, do_final)
    maps = []
    for c in range(NCORES):
        m = dict(common)
        m["xT"] = np.ascontiguousarray(xT_full[:, c * NTOK:(c + 1) * NTOK].reshape(8, 128, NTOK))
        if do_wout:
            m["yT"] = np.ascontiguousarray(yT_full[:, c * NTOK:(c + 1) * NTOK].reshape(8, 128, NTOK))
        maps.append(m)
    res = run_bass_kernel_spmd(nc, maps, core_ids=list(range(NCORES)))
    out = {}
    out["xoT"] = np.concatenate([r["xoT"].reshape(1024, NTOK) for r in res.results], axis=1)
    if do_win:
        out["zT"] = np.concatenate([r["zT"].reshape(NZC * 128, NTOK) for r in res.results], axis=1)
    return out


PI = float(np.pi)
NCH = 1024
HB = 2048
GELU_C = 1.5957691216057308


def emit_gelu(P, dst, src_ps, tmp_a, tmp_b, names, src_names, bias=None):
    dn, an, bn = names
    if bias is None:
        P.op('act', lambda e: e.activation(out=tmp_a, in_=src_ps, func=AF.Copy), r=src_names, w=[an])
    else:
        P.op('act', lambda e: e.activation(out=tmp_a, in_=src_ps, func=AF.Identity, bias=bias[0]), r=src_names + [bias[1]], w=[an])
    P.op('pool', lambda e: e.tensor_tensor(out=tmp_b, in0=tmp_a, in1=tmp_a, op=ALU.mult), r=[an], w=[bn])
    P.op('dve', lambda e: e.tensor_scalar(out=tmp_b, in0=tmp_b, scalar1=0.044715, scalar2=1.0, op0=ALU.mult, op1=ALU.add),
         r=[bn], w=[bn])
    P.op('dve', lambda e: e.tensor_tensor(out=tmp_b, in0=tmp_b, in1=tmp_a, op=ALU.mult), r=[bn, an], w=[bn])
    P.op('act', lambda e: e.activation(out=tmp_b, in_=tmp_b, func=AF.Sigmoid, scale=GELU_C), r=[bn], w=[bn])
    P.op('dve', lambda e: e.tensor_tensor(out=dst, in0=tmp_a, in1=tmp_b, op=ALU.mult), r=[an, bn], w=[dn])


def build_M1(layer):
    P = Prog()
    lr2_d = P.dram_in("lr2", [128, 4]); li2_d = P.dram_in("li2", [128, 4]); ldt_d = P.dram_in("ldt", [128, 4])
    b1_d = P.dram_in("bst1", [128, 4, 16]); b2_d = P.dram_in("bst2", [128, 4, 16])
    c1_d = P.dram_in("cst1", [128, 4, 16]); c2_d = P.dram_in("cst2", [128, 4, 16])
    dcol_d = P.dram_in("dcol", [128, 4]); sgn_d = P.dram_in("sgn", [128, 2]); jv_d = P.dram_in("jv", [128, 4, 24])
    i128_d = P.dram_in("i128", [128, 128]); jsw_d = P.dram_in("jsw", [128, 128]); tmask_d = P.dram_in("tmask", [128, 128])
    uc_d = P.dram_in("uc", [4, 128, NCH])
    y5_d = P.dram_out("y5", [4, 128, NCH])
    hq_d = P.dram_in("hq", [64, SEQ]); hf_d = P.dram_in("hf", [64, SEQ])
    hv64_d = P.dram_in("hv64", [4, 64, 32, 64]); hv128_d = P.dram_in("hv128", [4, 128, 16, 64]); hg64_d = P.dram_in("hg64", [4, 64, 32, 64])
    lbl_d = P.dram_in("lbl", [64, 2]); gain_d = P.dram_in("hgain", [64, 64]); cm_d = P.dram_in("cmask", [64, 512])
    yh_d = P.dram_out("yh", [64, 128, 64])

    ps = [P.psum(f"ps{i}", [128, 512]) for i in range(8)]
    i128 = P.sbuf("i128", [128, 128], F32)
    P.dma(i128[:], i128_d[:, :], w=["i128"])

    import os
    PARTS = os.environ.get('M1PARTS', 's5,hg')
    NG5 = 4 if 's5' in PARTS else 0
    def T(name, shape, dt=F32):
        return P.sbuf(name, shape, dt)
    lr2 = T("lr2", [128, 4]); li2 = T("li2", [128, 4]); dt = T("dt", [128, 4]); sgn = T("sgn", [128, 2])
    b1 = T("b1", [128, 4, 16]); b2 = T("b2", [128, 4, 16]); c1 = T("c1", [128, 4, 16]); c2 = T("c2", [128, 4, 16])
    dcol = T("dcol", [128, 4]); jv = T("jv", [128, 4, 24]); jsw = T("jsw", [128, 128]); tmask = T("tmask", [128, 128])
    for t_, d_, n_ in [(lr2, lr2_d, "lr2"), (li2, li2_d, "li2"), (dt, ldt_d, "dt"), (sgn, sgn_d, "sgn"), (dcol, dcol_d, "dcol"),
                       (jsw, jsw_d, "jsw"), (tmask, tmask_d, "tmask")]:
        P.dma(t_[:], d_[:, :], w=[n_])
    for t_, d_, n_ in [(b1, b1_d, "b1"), (b2, b2_d, "b2"), (c1, c1_d, "c1"), (c2, c2_d, "c2"), (jv, jv_d, "jv")]:
        P.dma(t_[:], d_[:, :, :], w=[n_])
    V = lambda e: e

    def dv(fn, r, w):
        P.op('dve', fn, r=r, w=w)

    def ac(fn, r, w):
        P.op('act', fn, r=r, w=w)
    ac(lambda e: e.activation(out=dt[:], in_=dt[:], func=AF.Exp), ["dt"], ["dt"])
    lrdt = T("lrdt", [128, 4]); lidt = T("lidt", [128, 4])
    dv(lambda e: e.tensor_tensor(out=lrdt[:], in0=lr2[:], in1=dt[:], op=ALU.mult), ["lr2", "dt"], ["lrdt"])
    dv(lambda e: e.tensor_tensor(out=lidt[:], in0=li2[:], in1=dt[:], op=ALU.mult), ["li2", "dt"], ["lidt"])
    am = T("am", [128, 4, 24]); aa = T("aa", [128, 4, 24]); pr = T("pr", [128, 4, 24]); pi_ = T("pi", [128, 4, 24])
    tA = T("tA", [128, 4, 24]); tB = T("tB", [128, 4, 24]); tI = T("tI", [128, 4, 24], mybir.dt.int32)
    bc24 = lambda t_: t_[:, :, None].to_broadcast([128, 4, 24])
    dv(lambda e: e.tensor_tensor(out=am[:], in0=jv[:], in1=bc24(lrdt), op=ALU.mult), ["jv", "lrdt"], ["am"])
    ac(lambda e: e.activation(out=am[:], in_=am[:], func=AF.Exp), ["am"], ["am"])
    dv(lambda e: e.tensor_tensor(out=aa[:], in0=jv[:], in1=bc24(lidt), op=ALU.mult), ["jv", "lidt"], ["aa"])

    def sin_of(dst, dn, src, sn, shift):
        dv(lambda e: e.tensor_scalar(out=tA[:], in0=src[:], scalar1=shift, scalar2=None, op0=ALU.add), [sn], ["tA"])
        dv(lambda e: e.tensor_scalar(out=tB[:], in0=tA[:], scalar1=1.0 / (2 * PI), scalar2=64.5, op0=ALU.mult, op1=ALU.add),
           ["tA"], ["tB"])
        dv(lambda e: e.tensor_copy(out=tI[:], in_=tB[:]), ["tB"], ["tI"])
        dv(lambda e: e.tensor_copy(out=tB[:], in_=tI[:]), ["tI"], ["tB"])
        dv(lambda e: e.tensor_scalar(out=tB[:], in0=tB[:], scalar1=-64.0, scalar2=-2 * PI, op0=ALU.add, op1=ALU.mult),
           ["tB"], ["tB"])
        dv(lambda e: e.tensor_tensor(out=tA[:], in0=tA[:], in1=tB[:], op=ALU.add), ["tA", "tB"], ["tA"])
        dv(lambda e: e.tensor_scalar(out=tB[:], in0=tA[:], scalar1=-PI, scalar2=2 * PI, op0=ALU.is_lt, op1=ALU.mult),
           ["tA"], ["tB"])
        dv(lambda e: e.tensor_tensor(out=tA[:], in0=tA[:], in1=tB[:], op=ALU.add), ["tA", "tB"], ["tA"])
        dv(lambda e: e.tensor_scalar(out=tB[:], in0=tA[:], scalar1=PI, scalar2=-2 * PI, op0=ALU.is_gt, op1=ALU.mult),
           ["tA"], ["tB"])
        dv(lambda e: e.tensor_tensor(out=tA[:], in0=tA[:], in1=tB[:], op=ALU.add), ["tA", "tB"], ["tA"])
        dv(lambda e: e.tensor_scalar(out=tA[:], in0=tA[:], scalar1=-PI, scalar2=PI, op0=ALU.max, op1=ALU.min),
           ["tA"], ["tA"])
        ac(lambda e: e.activation(out=dst[:], in_=tA[:], func=AF.Sin), ["tA"], [dn])
    sin_of(pi_, "pi", aa, "aa", 0.0)
    sin_of(pr, "pr", aa, "aa", PI / 2)
    dv(lambda e: e.tensor_tensor(out=pr[:], in0=pr[:], in1=am[:], op=ALU.mult), ["pr", "am"], ["pr"])
    dv(lambda e: e.tensor_tensor(out=pi_[:], in0=pi_[:], in1=am[:], op=ALU.mult), ["pi", "am"], ["pi"])
    den = T("den", [128, 4]); t4a = T("t4a", [128, 4]); t4b = T("t4b", [128, 4]); nr = T("nr", [128, 4])
    gre = T("gre", [128, 4]); gim = T("gim", [128, 4])
    ar1 = pr[:, :, 8]; ai1 = pi_[:, :, 8]
    dv(lambda e: e.tensor_tensor(out=den[:], in0=lr2[:], in1=lr2[:], op=ALU.mult), ["lr2"], ["den"])
    dv(lambda e: e.tensor_tensor(out=t4a[:], in0=li2[:], in1=li2[:], op=ALU.mult), ["li2"], ["t4a"])
    dv(lambda e: e.tensor_tensor(out=den[:], in0=den[:], in1=t4a[:], op=ALU.add), ["den", "t4a"], ["den"])
    dv(lambda e: e.reciprocal(out=den[:], in_=den[:]), ["den"], ["den"])
    dv(lambda e: e.tensor_scalar(out=nr[:], in0=ar1, scalar1=-1.0, scalar2=None, op0=ALU.add), ["pr"], ["nr"])
    dv(lambda e: e.tensor_tensor(out=t4a[:], in0=nr[:], in1=lr2[:], op=ALU.mult), ["nr", "lr2"], ["t4a"])
    dv(lambda e: e.tensor_tensor(out=t4b[:], in0=ai1, in1=li2[:], op=ALU.mult), ["pi", "li2"], ["t4b"])
    dv(lambda e: e.tensor_tensor(out=t4a[:], in0=t4a[:], in1=t4b[:], op=ALU.add), ["t4a", "t4b"], ["t4a"])
    dv(lambda e: e.tensor_tensor(out=gre[:], in0=t4a[:], in1=den[:], op=ALU.mult), ["t4a", "den"], ["gre"])
    dv(lambda e: e.tensor_tensor(out=t4a[:], in0=ai1, in1=lr2[:], op=ALU.mult), ["pi", "lr2"], ["t4a"])
    dv(lambda e: e.tensor_tensor(out=t4b[:], in0=nr[:], in1=li2[:], op=ALU.mult), ["nr", "li2"], ["t4b"])
    dv(lambda e: e.tensor_tensor(out=t4a[:], in0=t4a[:], in1=t4b[:], op=ALU.subtract), ["t4a", "t4b"], ["t4a"])
    dv(lambda e: e.tensor_tensor(out=gim[:], in0=t4a[:], in1=den[:], op=ALU.mult), ["t4a", "den"], ["gim"])
    er = T("er", [128, 4, 8]); ei = T("ei", [128, 4, 8]); t8 = T("t8", [128, 4, 8])
    bc8 = lambda t_: t_[:, :, None].to_broadcast([128, 4, 8])
    dv(lambda e: e.tensor_tensor(out=er[:], in0=pr[:, :, 0:8], in1=bc8(gre), op=ALU.mult), ["pr", "gre"], ["er"])
    dv(lambda e: e.tensor_tensor(out=t8[:], in0=pi_[:, :, 0:8], in1=bc8(gim), op=ALU.mult), ["pi", "gim"], ["t8"])
    dv(lambda e: e.tensor_tensor(out=er[:], in0=er[:], in1=t8[:], op=ALU.subtract), ["er", "t8"], ["er"])
    dv(lambda e: e.tensor_tensor(out=ei[:], in0=pr[:, :, 0:8], in1=bc8(gim), op=ALU.mult), ["pr", "gim"], ["ei"])
    dv(lambda e: e.tensor_tensor(out=t8[:], in0=pi_[:, :, 0:8], in1=bc8(gre), op=ALU.mult), ["pi", "gre"], ["t8"])
    dv(lambda e: e.tensor_tensor(out=ei[:], in0=ei[:], in1=t8[:], op=ALU.add), ["ei", "t8"], ["ei"])
    m2 = T("m2", [128, 4, 8]); n1f = T("n1f", [128, 4, 8]); n2f = T("n2f", [128, 4, 8]); n1h = T("n1h", [128, 4, 8]); n2h = T("n2h", [128, 4, 8])
    dv(lambda e: e.tensor_scalar(out=m2[:], in0=ei[:], scalar1=sgn[:, 1:2], scalar2=None, op0=ALU.mult), ["ei", "sgn"], ["m2"])
    dv(lambda e: e.tensor_scalar(out=n1f[:], in0=pr[:, :, 8:16], scalar1=sgn[:, 0:1], scalar2=None, op0=ALU.mult), ["pr", "sgn"], ["n1f"])
    dv(lambda e: e.tensor_scalar(out=n2f[:], in0=pi_[:, :, 8:16], scalar1=-1.0, scalar2=None, op0=ALU.mult), ["pi"], ["n2f"])
    dv(lambda e: e.tensor_scalar(out=n1h[:], in0=pr[:, :, 16:24], scalar1=sgn[:, 0:1], scalar2=None, op0=ALU.mult), ["pr", "sgn"], ["n1h"])
    dv(lambda e: e.tensor_scalar(out=n2h[:], in0=pi_[:, :, 16:24], scalar1=-1.0, scalar2=None, op0=ALU.mult), ["pi"], ["n2h"])
    bcm = T("bcm", [128, 4, 8, 16]); ccm = T("ccm", [128, 4, 8, 16]); qmm = T("qmm", [128, 4, 8, 16]); t816 = T("t816", [128, 4, 8, 16])
    S4 = [128, 4, 8, 16]

    def outer(dst, dn, st1, s1n, co1, c1n, st2, s2n, co2, c2n):
        dv(lambda e: e.tensor_tensor(out=dst[:], in0=st1[:, :, None, :].to_broadcast(S4), in1=co1[:, :, :, None].to_broadcast(S4),
                                     op=ALU.mult), [s1n, c1n], [dn])
        dv(lambda e: e.tensor_tensor(out=t816[:], in0=st2[:, :, None, :].to_broadcast(S4), in1=co2[:, :, :, None].to_broadcast(S4),
                                     op=ALU.mult), [s2n, c2n], ["t816"])
        dv(lambda e: e.tensor_tensor(out=dst[:], in0=dst[:], in1=t816[:], op=ALU.add), [dn, "t816"], [dn])
    outer(bcm, "bcm", b1, "b1", er, "er", b2, "b2", m2, "m2")
    outer(ccm, "ccm", c1, "c1", n1f, "n1f", c2, "c2", n2f, "n2f")
    outer(qmm, "qmm", c1, "c1", n1h, "n1h", c2, "c2", n2h, "n2h")
    tz = T("tz", [128, 4, 128]); bct = T("bct", [128, 4, 128])
    for g in range(4):
        pg, pgn = ps[g % 2], f"ps{g % 2}"
        P.op('pe', lambda e: e.matmul(pg[:, 0:128], lhsT=bcm[:, g].rearrange("p s h -> p (s h)"),
                                      rhs=qmm[:, g].rearrange("p s h -> p (s h)"), start=True, stop=True),
             r=["bcm", "qmm"], w=[pgn])
        dv(lambda e: e.tensor_tensor(out=tz[:, g, :], in0=pg[:, 0:128], in1=tmask[:], op=ALU.mult), [pgn, "tmask"], [("tz", g)])
        dv(lambda e: e.scalar_tensor_tensor(out=tz[:, g, :], in0=i128[:], scalar=dcol[:, g:g + 1], in1=tz[:, g, :],
                                            op0=ALU.mult, op1=ALU.add), ["i128", "dcol", ("tz", g)], [("tz", g)])
        pt, ptn = ps[2 + g % 2], f"ps{2 + g % 2}"
        P.op('pe', lambda e: e.transpose(out=pt[:, 0:128], in_=bcm[:, g].rearrange("p s h -> p (s h)"), identity=i128[:]),
             r=["bcm", "i128"], w=[ptn])
        ac(lambda e: e.activation(out=bct[:, g, :], in_=pt[:, 0:128], func=AF.Copy), [ptn], [("bct", g)])
    NK = 10
    a8r = T("a8r", [128, NK, 4]); a8i = T("a8i", [128, NK, 4]); rm = T("rm", [128, NK * 4, 128])
    dv(lambda e: e.tensor_copy(out=a8r[:, 0, :], in_=pr[:, :, 15]), ["pr"], ["a8r"])
    dv(lambda e: e.tensor_copy(out=a8i[:, 0, :], in_=pi_[:, :, 15]), ["pi"], ["a8i"])
    for k in range(1, NK):
        dv(lambda e: e.tensor_tensor(out=t4a[:], in0=a8r[:, k - 1, :], in1=a8r[:, k - 1, :], op=ALU.mult), ["a8r"], ["t4a"])
        dv(lambda e: e.tensor_tensor(out=t4b[:], in0=a8i[:, k - 1, :], in1=a8i[:, k - 1, :], op=ALU.mult), ["a8i"], ["t4b"])
        dv(lambda e: e.tensor_tensor(out=a8r[:, k, :], in0=t4a[:], in1=t4b[:], op=ALU.subtract), ["t4a", "t4b"], ["a8r"])
        dv(lambda e: e.tensor_tensor(out=t4a[:], in0=a8r[:, k - 1, :], in1=a8i[:, k - 1, :], op=ALU.mult), ["a8r", "a8i"], ["t4a"])
        dv(lambda e: e.tensor_scalar(out=a8i[:, k, :], in0=t4a[:], scalar1=2.0, scalar2=None, op0=ALU.mult), ["t4a"], ["a8i"])
    a8is = T("a8is", [128, NK, 4])
    dv(lambda e: e.tensor_scalar(out=a8is[:], in0=a8i[:], scalar1=sgn[:, 0:1], scalar2=None, op0=ALU.mult), ["a8i", "sgn"], ["a8is"])
    for k in range(NK):
        for g in range(4):
            i = k * 4 + g
            dv(lambda e: e.tensor_scalar(out=rm[:, i, :], in0=i128[:], scalar1=a8r[:, k, g:g + 1], scalar2=None, op0=ALU.mult),
               ["i128", "a8r"], [("rm", i)])
            dv(lambda e: e.scalar_tensor_tensor(out=rm[:, i, :], in0=jsw[:], scalar=a8is[:, k, g:g + 1], in1=rm[:, i, :],
                                                op0=ALU.mult, op1=ALU.add), ["jsw", "a8is", ("rm", i)], [("rm", i)])
    uc = [T(f"uc{i}", [128, NCH]) for i in range(2)]
    xs = T("xs", [128, NCH + 1]); ga = T("ga", [128, 512]); gb = T("gb", [128, 512]); yo = [T(f"yo{i}", [128, 512]) for i in range(2)]
    P.op('pool', lambda e: e.memset(xs[:, 0:1], 0.0), w=[("xs", "z")])
    for g in range(NG5):
        u = uc[g % 2]; un = f"uc{g % 2}"
        P.dma(u[:], uc_d[g], w=[un])
        for h in range(2):
            pp, ppn = ps[h], f"ps{h}"
            P.op('pe', lambda e: e.matmul(pp[:, :], lhsT=bct[:, g, :], rhs=u[:, h * 512:(h + 1) * 512], start=True, stop=True),
                 r=[("bct", g), un], w=[ppn])
            ac(lambda e: e.activation(out=xs[:, 1 + h * 512:1 + (h + 1) * 512], in_=pp[:, :], func=AF.Copy), [ppn], [("xs", "x")])
        for k in range(NK):
            d = 1 << k
            n = NCH - d
            pieces = [(0, min(512, n))] + ([(512, n)] if n > 512 else [])
            for h, (a, b) in enumerate(pieces):
                pp, ppn = ps[2 + h], f"ps{2 + h}"
                P.op('pe', lambda e: e.matmul(pp[:, 0:b - a], lhsT=rm[:, k * 4 + g, :], rhs=xs[:, 1 + a:1 + b], start=True, stop=True),
                     r=[("rm", k * 4 + g), ("xs", "x")], w=[ppn])
            for h, (a, b) in enumerate(pieces):
                pp, ppn = ps[2 + h], f"ps{2 + h}"
                dv(lambda e: e.tensor_tensor(out=xs[:, 1 + d + a:1 + d + b], in0=pp[:, 0:b - a], in1=xs[:, 1 + d + a:1 + d + b], op=ALU.add),
                   [ppn, ("xs", "x")], [("xs", "x")])
        for h in range(2):
            pp, ppn = ps[4 + h], f"ps{4 + h}"
            P.op('pe', lambda e: e.matmul(pp[:, :], lhsT=tz[:, g, :], rhs=u[:, h * 512:(h + 1) * 512], start=True, stop=False),
                 r=[("tz", g), un], w=[ppn])
            P.op('pe', lambda e: e.matmul(pp[:, :], lhsT=ccm[:, g].rearrange("p s h -> p (s h)"), rhs=xs[:, h * 512:(h + 1) * 512],
                                          start=False, stop=True), r=["ccm", ("xs", "x"), ("xs", "z")], w=[ppn])
            emit_gelu(P, yo[h][:], pp[:, :], ga[:], gb[:], (f"yo{h}", "ga", "gb"), [ppn])
            P.dma(y5_d[g, :, h * 512:(h + 1) * 512], yo[h][:], r=[f"yo{h}"], w=[("y5", (g, h))], sem=f"y5s{h}")

    if 'hg' not in PARTS:
        return P.finish()
    lbl = T("lbl", [64, 2]); lb = T("lb", [64, 1]); oml = T("oml", [64, 1]); gain = T("gain", [64, 64]); cm = T("cm", [64, 512])
    P.dma(lbl[:], lbl_d[:, :], w=["lbl"]); P.dma(gain[:], gain_d[:, :], w=["gain"]); P.dma(cm[:], cm_d[:, :], w=["cm"])
    if layer == 0:
        P.op('pool', lambda e: e.memset(lb[:], 0.0), w=["lb"])
    else:
        dv(lambda e: e.tensor_tensor(out=lb[:], in0=lbl[:, 1:2], in1=lbl[:, 0:1], op=ALU.subtract), ["lbl"], ["lb"])
        ac(lambda e: e.activation(out=lb[:], in_=lb[:], func=AF.Sigmoid), ["lb"], ["lb"])
    dv(lambda e: e.tensor_scalar(out=oml[:], in0=lb[:], scalar1=-1.0, scalar2=1.0, op0=ALU.mult, op1=ALU.add), ["lb"], ["oml"])
    rmask = T("rmask", [64, HB])
    P.op('pool', lambda e: e.memset(rmask[:], 1.0), w=["rmask"])
    P.op('pool', lambda e: e.memset(rmask[:, 0:HB:64], 0.0), w=["rmask"])
    hq = T("hq", [64, HB]); hf = T("hf", [64, HB]); sig = T("sig", [64, HB]); fbuf = T("fbuf", [64, HB]); bb = T("bb", [64, HB])
    eb = T("eb", [64, HB]); enb = T("enb", [64, HB]); qtil = T("qtil", [64, HB], BF16); ktil = T("ktil", [64, HB])
    ktb = T("ktb", [64, HB], BF16); khat = T("khat", [64, HB]); dec = T("dec", [64, 32])
    khT = T("khT", [128, 16, 64], BF16); v64f = T("v64f", [64, 32, 64]); v64 = T("v64", [64, 32, 64], BF16)
    v128f = T("v128f", [128, 16, 64]); v128 = T("v128", [128, 16, 64], BF16)
    g64 = T("g64", [64, 32, 64]); sall = T("sall", [64, 33, 64]); sbf = T("sbf", [64, 32, 64], BF16)
    attm = T("attm", [64, 512], BF16); osb = T("osb", [64, 8, 64]); osq = T("osq", [64, 8, 64]); ss = T("ss", [64, 8])
    yh = [T(f"yh{i}", [64, 8, 64]) for i in range(2)]
    P.op('pool', lambda e: e.memset(sall[:, 0, :], 0.0), w=[("sall", 0)])
    for blk in range(SEQ // HB):
        tsl = slice(blk * HB, (blk + 1) * HB)
        P.dma(hq[:], hq_d[:, tsl], w=["hq"]); P.dma(hf[:], hf_d[:, tsl], w=["hf"])
        P.dma(v64f[:], hv64_d[blk], w=["v64f"])
        P.dma(v128f[:], hv128_d[blk], w=["v128f"])
        P.dma(g64[:], hg64_d[blk], w=["g64"])
        P.op('pool', lambda e: e.tensor_copy(out=v64[:], in_=v64f[:]), r=["v64f"], w=["v64"])
        P.op('pool', lambda e: e.tensor_copy(out=v128[:], in_=v128f[:]), r=["v128f"], w=["v128"])
        ac(lambda e: e.activation(out=sig[:], in_=hf[:], func=AF.Sigmoid), ["hf"], ["sig"])
        dv(lambda e: e.tensor_scalar(out=fbuf[:], in0=sig[:], scalar1=oml[:, 0:1], scalar2=lb[:, 0:1], op0=ALU.mult, op1=ALU.add),
           ["sig", "oml", "lb"], ["fbuf"])
        ac(lambda e: e.activation(out=fbuf[:], in_=fbuf[:], func=AF.Ln), ["fbuf"], ["fbuf"])
        dv(lambda e: e.tensor_tensor_scan(out=bb[:], data0=rmask[:], data1=fbuf[:], initial=0.0, op0=ALU.mult, op1=ALU.add),
           ["rmask", "fbuf"], ["bb"])
        ac(lambda e: e.activation(out=eb[:], in_=bb[:], func=AF.Exp), ["bb"], ["eb"])
        ac(lambda e: e.activation(out=enb[:], in_=bb[:], func=AF.Exp, scale=-1.0), ["bb"], ["enb"])
        ac(lambda e: e.activation(out=sig[:], in_=hf[:], func=AF.Sigmoid, scale=-1.0), ["hf"], ["sig"])
        dv(lambda e: e.scalar_tensor_tensor(out=ktil[:], in0=sig[:], scalar=oml[:, 0:1], in1=enb[:], op0=ALU.mult, op1=ALU.mult),
           ["sig", "oml", "enb"], ["ktil"])
        P.op('pool', lambda e: e.tensor_copy(out=ktb[:], in_=ktil[:]), r=["ktil"], w=["ktb"])
        ac(lambda e: e.activation(out=hq[:], in_=hq[:], func=AF.Silu), ["hq"], ["hq"])
        dv(lambda e: e.tensor_tensor(out=qtil[:], in0=hq[:], in1=eb[:], op=ALU.mult), ["hq", "eb"], ["qtil"])
        dv(lambda e: e.tensor_copy(out=dec[:], in_=eb[:, 63:HB:64]), ["eb"], ["dec"])
        dv(lambda e: e.tensor_tensor(out=khat[:].rearrange("p (c s) -> p c s", s=64), in0=ktil[:].rearrange("p (c s) -> p c s", s=64),
                                     in1=dec[:, :, None].to_broadcast([64, 32, 64]), op=ALU.mult), ["ktil", "dec"], ["khat"])
        ac(lambda e: e.activation(out=g64[:], in_=g64[:], func=AF.Silu), ["g64"], ["g64"])
        for hlf in range(2):
            pp, ppn = ps[6 + hlf], f"ps{6 + hlf}"
            for j in range(8):
                jj = hlf * 8 + j
                P.op('pe', lambda e: e.transpose(out=pp[:, j * 64:(j + 1) * 64], in_=khat[:, jj * 128:(jj + 1) * 128],
                                                 identity=i128[0:64, 0:64]), r=["khat", "i128"], w=[ppn])
            ac(lambda e: e.activation(out=khT[:, hlf * 8:(hlf + 1) * 8, :].rearrange("p a b -> p (a b)"), in_=pp[:, :], func=AF.Copy),
               [ppn], ["khT"])
        for cg in range(4):
            for ci in range(8):
                c = cg * 8 + ci
                po = (c % 2) * 64
                pu, pun = ps[c % 2], f"ps{c % 2}"
                sl_ = slice((ci // 2) * 64, (ci // 2 + 1) * 64)
                P.op('pe', lambda e: e.matmul(pu[0:64, sl_], lhsT=khT[po:po + 64, c // 2, :],
                                              rhs=v128[po:po + 64, c // 2, :], start=True, stop=True),
                     r=["khT", "v128"], w=[pun])
            for ci in range(8):
                c = cg * 8 + ci
                pu, pun = ps[c % 2], f"ps{c % 2}"
                sl_ = slice((ci // 2) * 64, (ci // 2 + 1) * 64)
                dv(lambda e: e.scalar_tensor_tensor(out=sall[:, c + 1, :], in0=sall[:, c, :], scalar=dec[:, c:c + 1],
                                                    in1=pu[0:64, sl_], op0=ALU.mult, op1=ALU.add),
                   [("sall", c), "dec", pun], [("sall", c + 1)])
        P.op('pool', lambda e: e.tensor_copy(out=sbf[:], in_=sall[:, 0:32, :]), r=["sall"], w=["sbf"])
        for cg in range(4):
            pa, pan = ps[2 + cg % 2], f"ps{2 + cg % 2}"
            po_, pon = ps[4 + cg % 2], f"ps{4 + cg % 2}"
            for ci in range(8):
                c = cg * 8 + ci
                cs = slice(c * 64, (c + 1) * 64)
                P.op('pe', lambda e: e.matmul(pa[0:64, ci * 64:(ci + 1) * 64], lhsT=ktb[:, cs], rhs=qtil[:, cs], start=True, stop=True),
                     r=["ktb", "qtil"], w=[pan])
            dv(lambda e: e.tensor_tensor(out=attm[:], in0=pa[0:64, :], in1=cm[:], op=ALU.mult), [pan, "cm"], ["attm"])
            for ci in range(8):
                c = cg * 8 + ci
                cs = slice(c * 64, (c + 1) * 64)
                P.op('pe', lambda e: e.matmul(po_[0:64, ci * 64:(ci + 1) * 64], lhsT=attm[:, ci * 64:(ci + 1) * 64], rhs=v64[:, c, :],
                                              start=True, stop=False), r=["attm", "v64"], w=[pon])
                P.op('pe', lambda e: e.matmul(po_[0:64, ci * 64:(ci + 1) * 64], lhsT=qtil[:, cs], rhs=sbf[:, c, :],
                                              start=False, stop=True), r=["qtil", "sbf"], w=[pon])
            ac(lambda e: e.activation(out=osb[:].rearrange("p a b -> p (a b)"), in_=po_[0:64, :], func=AF.Copy), [pon], ["osb"])
            P.op('pool', lambda e: e.tensor_tensor(out=osq[:], in0=osb[:], in1=osb[:], op=ALU.mult), r=["osb"], w=["osq"])
            dv(lambda e: e.tensor_reduce(out=ss[:], in_=osq[:], axis=AX.X, op=ALU.add), ["osq"], ["ss"])
            ac(lambda e: e.activation(out=ss[:], in_=ss[:], func=AF.Sqrt, scale=1.0 / 64, bias=EPS), ["ss"], ["ss"])
            dv(lambda e: e.reciprocal(out=ss[:], in_=ss[:]), ["ss"], ["ss"])
            y_ = yh[cg % 2]; yn = f"yh{cg % 2}"
            dv(lambda e: e.tensor_tensor(out=y_[:], in0=osb[:], in1=ss[:, :, None].to_broadcast([64, 8, 64]), op=ALU.mult),
               ["osb", "ss"], [yn])
            dv(lambda e: e.tensor_tensor(out=y_[:], in0=y_[:], in1=gain[:, None, :].to_broadcast([64, 8, 64]), op=ALU.mult),
               [yn, "gain"], [yn])
            dv(lambda e: e.tensor_tensor(out=y_[:], in0=y_[:], in1=g64[:, cg * 8:(cg + 1) * 8, :], op=ALU.mult), [yn, "g64"], [yn])
            c0 = blk * 32 + cg * 8
            P.dma(yh_d[:, c0:c0 + 8, :], y_[:], r=[yn], w=[("yhd", c0)], sem=f"yhs{cg % 2}")
        dv(lambda e: e.tensor_copy(out=sall[:, 0, :], in_=sall[:, 32, :]), [("sall", 32), "sbf"], [("sall", 0)])
    return P.finish()


def _m1_consts():
    jv1 = np.array([7, 6, 5, 4, 3, 2, 1, 0, 1, 2, 3, 4, 5, 6, 7, 8, -7, -6, -5, -4, -3, -2, -1, 0], np.float32)
    jv = np.ascontiguousarray(np.broadcast_to(jv1, (128, 4, 24))).astype(np.float32)
    sgn = np.ones((128, 2), np.float32); sgn[64:, 0] = -1; sgn[:64, 1] = -1
    i128 = np.eye(128, dtype=np.float32)
    jsw = np.zeros((128, 128), np.float32)
    for k in range(128):
        jsw[k, (k + 64) % 128] = 1
    s_idx = np.arange(128) // 16
    tmask = (s_idx[None, :] >= s_idx[:, None]).astype(np.float32)
    st = np.arange(64)
    cm = np.tile((st[:, None] <= st[None, :]).astype(np.float32), (1, 8))
    return dict(jv=jv, sgn=sgn, i128=i128, jsw=jsw, tmask=tmask, cmask=cm)


def run_M1(zT, inp, l):
    nc = build_M1(l)
    cst = _m1_consts()
    maps = []
    for c in range(NCORES):
        b = c // 4
        tsl = slice(b * SEQ, (b + 1) * SEQ)
        m = dict(cst)
        gs = [4 * (c % 4) + gi for gi in range(4)]
        st2 = lambda a: np.concatenate([a, a], axis=0)
        m["lr2"] = np.stack([st2(inp['s5_lambda_re'][l, g]) for g in gs], 1).astype(np.float32)
        m["li2"] = np.stack([st2(inp['s5_lambda_im'][l, g]) for g in gs], 1).astype(np.float32)
        m["ldt"] = np.ascontiguousarray(np.broadcast_to(np.array([inp['s5_log_dt'][l, g] for g in gs], np.float32), (128, 4)))
        m["bst1"] = np.stack([np.concatenate([inp['s5_b_re'][l, g], inp['s5_b_im'][l, g]], 0) for g in gs], 1)
        m["bst2"] = np.stack([np.concatenate([inp['s5_b_im'][l, g], inp['s5_b_re'][l, g]], 0) for g in gs], 1)
        m["cst1"] = np.stack([np.concatenate([inp['s5_c_re'][l, g].T, inp['s5_c_im'][l, g].T], 0) for g in gs], 1)
        m["cst2"] = np.stack([np.concatenate([inp['s5_c_im'][l, g].T, inp['s5_c_re'][l, g].T], 0) for g in gs], 1)
        m["dcol"] = np.stack([np.tile(inp['s5_d'][l, g], 8) for g in gs], 1).astype(np.float32)
        m["uc"] = np.stack([zT[g * 16:(g + 1) * 16, tsl].reshape(16, NCH, 8).transpose(2, 0, 1).reshape(128, NCH) for g in gs], 0)
        h = c % 4
        m["hq"] = zT[256 + 64 * h:256 + 64 * (h + 1), tsl]
        m["hf"] = zT[512 + 64 * h:512 + 64 * (h + 1), tsl]
        v = zT[768 + 64 * h:768 + 64 * (h + 1), tsl].T
        gg = zT[1024 + 64 * h:1024 + 64 * (h + 1), tsl].T
        m["hv64"] = v.reshape(4, 32, 64, 64).transpose(0, 2, 1, 3)
        m["hv128"] = v.reshape(4, 16, 128, 64).transpose(0, 2, 1, 3)
        m["hg64"] = gg.reshape(4, 32, 64, 64).transpose(0, 2, 1, 3)
        m["lbl"] = inp['hgrn_lb_logits'][:, 64 * h:64 * (h + 1)].T
        m["hgain"] = np.broadcast_to(inp['hgrn_norm'][l][None, :], (64, 64))
        maps.append({k: np.ascontiguousarray(v_, dtype=np.float32) for k, v_ in m.items()})
    res = run_bass_kernel_spmd(nc, maps, core_ids=list(range(NCORES)))
    y5T = np.zeros((256, BATCH * SEQ), np.float32)
    yhT = np.zeros((256, BATCH * SEQ), np.float32)
    for c in range(NCORES):
        b = c // 4
        tsl = slice(b * SEQ, (b + 1) * SEQ)
        r = res.results[c]
        for gi in range(4):
            g = 4 * (c % 4) + gi
            y5T[g * 16:(g + 1) * 16, tsl] = r["y5"][gi].reshape(8, 16, NCH).transpose(1, 2, 0).reshape(16, SEQ)
        h = c % 4
        yhT[64 * h:64 * (h + 1), tsl] = r["yh"].transpose(1, 0, 2).reshape(SEQ, 64).T
    return y5T, yhT


OFF_F = 8320
NF_F = OFF_F + 8192 + 128
NEGB = -30000.0
TINY = 1e-30
NQT = 32
SCALE = 0.125


def build_M2():
    P = Prog()
    nc = P.nc
    q65_d = P.dram_in("q65", [4, 65, NQT * 128])
    ksl_d = P.dram_in("kslX", [64, SEQ]); kwn_d = P.dram_in("kwnX", [64, SEQ])
    vsl_d = P.dram_in("vslX", [128, 64 * 64]); vwn_d = P.dram_in("vwnX", [128, 64 * 64])
    kcr_d = P.dram_in("kcrX", [32, 64, 512]); vcr_d = P.dram_in("vcrX", [32, 64, 512])
    w1k_d = P.dram_in("w1k", [64, 32 * 64]); w1v_d = P.dram_in("w1v", [64, 32 * 64])
    w2k_d = P.dram_in("w2k", [64, 64]); w2v_d = P.dram_in("w2v", [64, 64])
    posk_d = P.dram_in("poskT", [64, 32]); posv_d = P.dram_in("posvT", [64, 32])
    fs_d = P.dram_in("fs", [4, NF_F]); fw_d = P.dram_in("fw", [4, NF_F])
    eall_d = P.dram_in("eall", [128, SEQ]); mm_d = P.dram_in("mmat", [128, 4 * 128])
    madd_d = P.dram_in("madd", [NQT, 128, 128]); glog_d = P.dram_in("glog", [128, NQT * 12])
    i128_d = P.dram_in("i128", [128, 128])
    parity_dummy = None
    y_d = P.dram_out("ynsa", [128, NQT, 256])

    T = P.sbuf
    ps = [P.psum(f"ps{i}", [128, 512]) for i in range(8)]
    psA = [(ps[0], "ps0"), (ps[1], "ps1"), (ps[3], "psB0"), (ps[7], "psB1")]
    psL, psO1, psO2, psO3 = ps[2], ps[4], ps[5], ps[6]
    psS = ps[4]
    stgall = T("stgall", [128, 4096], F32)
    stg = [stgall[:, 0:2048], stgall[:, 2048:4096]]
    sctr = [0]

    def stage():
        i = sctr[0] % 2
        sctr[0] += 1
        return stg[i], f"stg{i}"

    i128 = T("i128", [128, 128], F32); P.dma(i128[:], i128_d[:, :], w=["i128"])
    ones = T("ones", [128, 128], BF16); P.op('pool', lambda e: e.memset(ones[:], 1.0), w=["ones"])
    qa = T("qa", [65, 4, NQT * 128], BF16)
    ksl = T("ksl", [65, SEQ], BF16); kwn = T("kwn", [64, SEQ], BF16)
    vsl = T("vsl", [128, 64, 65], BF16); vwn = T("vwn", [128, 64, 65], BF16)
    eall = T("eall", [128, SEQ], BF16); mmt = T("mmt", [128, 4, 128], F32)
    kcmp = T("kcmp", [64, 512], BF16); vcmp = T("vcmp", [128, 4, 65], BF16)
    bs = T("bs", [128, 11, 512], F32)
    glog = T("glog", [128, NQT * 12], F32)

    for r in range(4):
        for hh in range(2):
            st, sn = stage()
            P.dma(st[0:65, :], q65_d[r, :, hh * 2048:(hh + 1) * 2048], w=[sn])
            P.op('pool', lambda e: e.tensor_copy(out=qa[0:64, r, hh * 2048:(hh + 1) * 2048], in_=st[0:64, :]), r=[sn], w=["qa"])
            P.op('act', lambda e: e.activation(out=qa[64:65, r, hh * 2048:(hh + 1) * 2048], in_=st[64:65, :], func=AF.Copy, scale=8.0),
                 r=[sn], w=["qa"])
    for (dst, dn, src) in [(ksl, "ksl", ksl_d), (kwn, "kwn", kwn_d)]:
        for pc in range(4):
            st, sn = stage()
            P.dma(st[0:64, :], src[:, pc * 2048:(pc + 1) * 2048], w=[sn])
            P.op('pool', lambda e: e.tensor_copy(out=dst[0:64, pc * 2048:(pc + 1) * 2048], in_=st[0:64, :]), r=[sn], w=[dn])
    P.op('pool', lambda e: e.memset(ksl[64:65, :], 1.0), w=["ksl"])
    for (dst, dn, src) in [(vsl, "vsl", vsl_d), (vwn, "vwn", vwn_d)]:
        for pc in range(2):
            st, sn = stage()
            P.dma(st[:, :], src[:, pc * 2048:(pc + 1) * 2048], w=[sn])
            P.op('pool', lambda e: e.tensor_copy(out=dst[:, pc * 32:(pc + 1) * 32, 0:64], in_=st[:, :].rearrange("p (a b) -> p a b", b=64)),
                 r=[sn], w=[dn])
        P.op('pool', lambda e: e.memset(dst[:, :, 64:65], 1.0), w=[dn])
    for pc in range(4):
        st, sn = stage()
        P.dma(st[:, :], eall_d[:, pc * 2048:(pc + 1) * 2048], w=[sn])
        P.op('pool', lambda e: e.tensor_copy(out=eall[:, pc * 2048:(pc + 1) * 2048], in_=st[:, :]), r=[sn], w=["eall"])
    P.dma(mmt[:].rearrange("p a b -> p (a b)"), mm_d[:, :], w=["mmt"])
    P.dma(glog[:], glog_d[:, :], w=["glog"])
    P.op('act', lambda e: e.activation(out=glog[:], in_=glog[:], func=AF.Sigmoid), r=["glog"], w=["glog"])
    for j in range(11):
        tab = fs_d if j < 9 else fw_d
        dl = 128 * j if j < 9 else 128 * (j - 5)
        src = bass.AP(tab.tensor, OFF_F + dl - 127, [(1, 128), (NF_F, 4), (1, 128)])
        P.dma(bs[:, j, :].rearrange("p (r q) -> p r q", q=128), src, w=[("bs", j)], sem="bs")

    w1 = T("w1", [64, 32 * 64], BF16); w2 = T("w2", [64, 64], BF16); posT = T("posT", [64, 32], BF16)
    w2f = T("w2f", [64, 64], F32); posf = T("posf", [64, 32], F32)
    pb = T("pb", [64, 1], F32); xj = [T(f"xj{i}", [64, 512], BF16) for i in range(2)]
    ga = T("ga", [64, 512], F32); gb = T("gb", [64, 512], F32); gel = T("gel", [64, 512], BF16)
    for which in range(2):
        w1_d, w2_d, pos_d, x_d = [(w1k_d, w2k_d, posk_d, kcr_d), (w1v_d, w2v_d, posv_d, vcr_d)][which]
        st, sn = stage()
        P.dma(st[0:64, :], w1_d[:, :], w=[sn])
        P.op('pool', lambda e: e.tensor_copy(out=w1[:], in_=st[0:64, :]), r=[sn], w=["w1"])
        P.dma(w2f[:], w2_d[:, :], w=["w2f"]); P.dma(posf[:], pos_d[:, :], w=["posf"])
        P.op('pool', lambda e: e.tensor_copy(out=w2[:], in_=w2f[:]), r=["w2f"], w=["w2"])
        P.op('pool', lambda e: e.tensor_copy(out=posT[:], in_=posf[:]), r=["posf"], w=["posT"])
        for j in range(32):
            P.op('pe', lambda e: e.matmul(psL[0:64, 0:1], lhsT=w1[:, j * 64:(j + 1) * 64], rhs=posT[:, j:j + 1], start=(j == 0), stop=(j == 31)),
                 r=["w1", "posT"], w=["psL"])
        P.op('act', lambda e: e.activation(out=pb[:], in_=psL[0:64, 0:1], func=AF.Copy), r=["psL"], w=["pb"])
        for j in range(32):
            st, sn = stage()
            P.dma(st[0:64, 0:512], x_d[j], w=[sn])
            P.op('pool', lambda e: e.tensor_copy(out=xj[j % 2][:], in_=st[0:64, 0:512]), r=[sn], w=[f"xj{j % 2}"])
            P.op('pe', lambda e: e.matmul(ps[3][0:64, :], lhsT=w1[:, j * 64:(j + 1) * 64], rhs=xj[j % 2][:], start=(j == 0), stop=(j == 31)),
                 r=["w1", f"xj{j % 2}"], w=["psB0"])
        emit_gelu(P, gel[:], ps[3][0:64, :], ga[:], gb[:], ("gel", "ga", "gb"), ["psB0"], bias=(pb[:, 0:1], "pb"))
        if which == 0:
            P.op('pe', lambda e: e.matmul(psO1[0:64, :], lhsT=w2[:], rhs=gel[:], start=True, stop=True), r=["w2", "gel"], w=["psO1"])
            P.op('act', lambda e: e.activation(out=kcmp[:], in_=psO1[0:64, :], func=AF.Copy), r=["psO1"], w=["kcmp"])
        else:
            for ci in range(4):
                P.op('pe', lambda e: e.matmul(psO2[:, ci * 64:(ci + 1) * 64], lhsT=gel[:, ci * 128:(ci + 1) * 128], rhs=w2[:], start=True, stop=True),
                     r=["w2", "gel"], w=["psO2"])
            P.op('act', lambda e: e.activation(out=vcmp[:, :, 0:64], in_=psO2[:, 0:256].rearrange("p (a b) -> p a b", b=64), func=AF.Copy),
                 r=["psO2"], w=["vcmp"])
            P.op('pool', lambda e: e.memset(vcmp[:, :, 64:65], 1.0), w=["vcmp"])

    bc = [T(f"bc{i}", [128, 512], F32) for i in range(2)]
    tmpb = [T(f"tmpb{i}", [128, 512], F32) for i in range(2)]
    ebuf = [T(f"ebuf{i}", [128, 512], BF16) for i in range(4)]
    pbuf = [T(f"pbuf{i}", [128, 512], BF16) for i in range(4)]
    maskall = T("maskall", [128, 64, 128], BF16)
    ec = T("ec", [128, 4, 512], BF16)
    rlb = T("rlb", [128, 512], F32); pnb = T("pnb", [128, 512], F32); impT = T("impT", [128, 4, 128], F32)
    madd = [T(f"madd{i}", [128, 128], F32) for i in range(2)]
    score = T("score", [128, 128], F32); sc2 = T("sc2", [128, 128], F32); m8a = T("m8a", [128, 8], F32); m8b = T("m8b", [128, 8], F32)
    self_ = T("self", [128, 128], F32); selT = T("selT", [128, 128], BF16)
    osb = [T(f"osb{i}", [128, 260], F32) for i in range(3)]
    lc = T("lc", [128, 3, 4], F32); coef = T("coef", [128, 3, 4], F32)
    yb = [T(f"yb{i}", [128, 256], F32) for i in range(2)]
    rot = dict(a=0, t=0, e=0, p=0, b=0)

    NROT = dict(a=4, t=2, e=4, p=4, b=2)

    def nxt(k):
        n = NROT[k]
        v = rot[k] % n
        rot[k] += 1
        return v

    def softmax_tile(A_src, An, bias_ap, bias_names, eout, eout_name):
        if bias_ap is None:
            P.op('act', lambda e: e.activation(out=eout, in_=A_src, func=AF.Exp, scale=SCALE), r=[An], w=[eout_name])
        else:
            ti = nxt('t')
            P.op('dve', lambda e: e.scalar_tensor_tensor(out=tmpb[ti][:], in0=A_src, scalar=SCALE, in1=bias_ap, op0=ALU.mult, op1=ALU.add),
                 r=[An] + bias_names, w=[f"tmpb{ti}"])
            P.op('act', lambda e: e.activation(out=eout, in_=tmpb[ti][:], func=AF.Exp), r=[f"tmpb{ti}"], w=[eout_name])

    def pv(psO, psOn, lhs_tile, lhs_name, v_ap, v_name, first, last=False):
        for r in range(4):
            P.op('pe', lambda e: e.matmul(psO[:, r * 65:(r + 1) * 65], lhsT=lhs_tile[:, r * 128:(r + 1) * 128], rhs=v_ap,
                                          start=(first and r == 0), stop=(last and r == 3)), r=[lhs_name, v_name], w=[psOn])

    import os
    NT_RUN = int(os.environ.get("M2_NT", NQT))
    scr = T("scr", [128, 1], F32)
    P.op('pool', lambda e: e.memset(scr[:], 0.0), w=["stg0", "stg1", "maskall1", "scr"])
    maskall2 = [maskall, stgall[:, :].bitcast(BF16).rearrange("p (a b) -> p a b", b=128)]
    osb0 = [osb[0], T("osb0b", [128, 260], F32), T("osb0c", [128, 260], F32)]
    osb2 = [osb[2], T("osb2b", [128, 260], F32), T("osb2c", [128, 260], F32)]
    selT3 = [selT, T("selTb", [128, 128], BF16), T("selTc", [128, 128], BF16)]

    def pairs_gen(pairs, depth):
        pend = []
        for (A_, B_) in pairs:
            pend.append((B_, A_()))
            if len(pend) > depth:
                b_, c_ = pend.pop(0)
                b_(c_)
            yield
        for b_, c_ in pend:
            b_(c_)
            yield

    def phaseX(m):
        qi = 2 * m + 1
        t0 = 128 * qi
        q64 = qa[0:64, :, m * 128:(m + 1) * 128]
        nck = min(4, ((t0 + 96) // 16) // 128 + 1)
        selTm = selT3[m % 3]; selTn = f"selT{m % 3}"

        def cmpA(ci):
            def f():
                bi = nxt('b')
                src = bass.AP(fs_d.tensor, OFF_F + t0 - 2048 * ci - 2063, [(16, 128), (NF_F, 4), (1, 128)])
                P.dma(bc[bi][:].rearrange("p (r q) -> p r q", q=128), src, w=[f"bc{bi}"])
                ai = nxt('a'); A, An = psA[ai]
                P.op('pe', lambda e: e.matmul(A[:, :], lhsT=kcmp[:, ci * 128:(ci + 1) * 128], rhs=q64, start=True, stop=True),
                     r=["kcmp", "qa"], w=[An])
                softmax_tile(A[:, :], An, bc[bi][:], [f"bc{bi}"], ec[:, ci, :], ("ec", ci))
                return ci
            return f

        def cmpB(ci):
            P.op('pe', lambda e: e.matmul(psL[:, :], lhsT=ones[:], rhs=ec[:, ci, :], start=(ci == 0), stop=(ci == nck - 1)),
                 r=["ones", ("ec", ci)], w=["psL"])
            pv(psO1, "psO1", ec[:, ci, :], ("ec", ci), vcmp[:, ci, :], "vcmp", ci == 0, ci == nck - 1)
        yield from pairs_gen([(cmpA(ci), cmpB) for ci in range(nck)], 2)
        P.op('dve', lambda e: e.tensor_scalar(out=rlb[:], in0=psL[:, :], scalar1=TINY, scalar2=None, op0=ALU.max), r=["psL"], w=["rlb"])
        P.op('dve', lambda e: e.reciprocal(out=rlb[:], in_=rlb[:]), r=["rlb"], w=["rlb"])
        yield
        for ci in range(nck):
            P.op('dve', lambda e: e.tensor_tensor(out=pnb[:], in0=ec[:, ci, :], in1=rlb[:], op=ALU.mult), r=[("ec", ci), "rlb"], w=["pnb"])
            P.op('dve', lambda e: e.tensor_reduce(out=impT[:, ci, :], in_=pnb[:].rearrange("p (r q) -> p q r", q=128), axis=AX.X, op=ALU.add),
                 r=["pnb"], w=[("impT", ci)])
            yield
        for ci in range(nck):
            P.op('pe', lambda e: e.matmul(psS[:, 260:388], lhsT=impT[:, ci, :], rhs=mmt[:, ci, :], start=False, stop=(ci == nck - 1)),
                 r=[("impT", ci), "mmt"], w=["psO1"])
        P.op('act', lambda e: e.activation(out=osb0[m % 3][:], in_=psO1[:, 0:260], func=AF.Copy), r=["psO1"], w=[f"osb0_{m % 3}"])
        yield
        mi = m % 2
        P.dma(madd[mi][:], madd_d[m], w=[f"madd{mi}"])
        P.op('dve', lambda e: e.tensor_tensor(out=score[:], in0=psS[:, 260:388], in1=madd[mi][:], op=ALU.add), r=["psO1", f"madd{mi}"], w=["score"])
        P.op('dve', lambda e: e.max(out=m8a[:], in_=score[:]), r=["score"], w=["m8a"])
        yield
        P.op('dve', lambda e: e.match_replace(out=sc2[:], in_to_replace=m8a[:], in_values=score[:], imm_value=-3e38), r=["score", "m8a"], w=["sc2"])
        P.op('dve', lambda e: e.max(out=m8b[:], in_=sc2[:]), r=["sc2"], w=["m8b"])
        yield
        P.op('dve', lambda e: e.tensor_scalar(out=self_[:], in0=score[:], scalar1=m8b[:, 7:8], scalar2=None, op0=ALU.is_ge),
             r=["score", "m8b"], w=["self"])
        P.op('pe', lambda e: e.transpose(out=psL[:, 0:128], in_=self_[:], identity=i128[:]), r=["self", "i128"], w=["psL"])
        P.op('act', lambda e: e.activation(out=selTm[:], in_=psL[:, 0:128], func=AF.Copy), r=["psL"], w=[selTn])
        yield
        kts = list(range(max(0, qi - 5), qi + 1))

        def winA(kt):
            def f():
                j = qi - kt
                ai = nxt('a'); A, An = psA[ai]
                P.op('pe', lambda e: e.matmul(A[:, :], lhsT=kwn[0:64, kt * 128:(kt + 1) * 128], rhs=q64, start=True, stop=True),
                     r=["kwn", "qa"], w=[An])
                ei = nxt('e')
                jj = j if j < 4 else 5 + j
                softmax_tile(A[:, :], An, bs[:, jj, :], [("bs", jj)], ebuf[ei][:], f"ebuf{ei}")
                return (kt, ei)
            return f

        def winB(ctx):
            kt, ei = ctx
            pv(psO3, "psO3", ebuf[ei], f"ebuf{ei}", vwn[:, kt, :], "vwn", kt == kts[0], kt == kts[-1])
        yield from pairs_gen([(winA(kt), winB) for kt in kts], 2)
        P.op('act', lambda e: e.activation(out=osb2[m % 3][:], in_=psO3[:, 0:260], func=AF.Copy), r=["psO3"], w=[f"osb2_{m % 3}"])
        yield

    def phaseX2(m):
        qi = 2 * m + 1
        mk = maskall2[m % 2]; mkn = f"maskall{m % 2}"
        selTm = selT3[m % 3]; selTn = f"selT{m % 3}"
        for k0 in range(0, qi + 1, 4):
            n4 = min(4, qi + 1 - k0)
            ai = nxt('a'); A, An = psA[ai]
            for u in range(n4):
                P.op('pe', lambda e: e.matmul(A[:, u * 128:(u + 1) * 128], lhsT=eall[:, (k0 + u) * 128:(k0 + u + 1) * 128], rhs=selTm[:],
                                              start=(u == 0), stop=(u == n4 - 1)), r=["eall", selTn], w=[An])
            P.op('act', lambda e: e.activation(out=mk[:, k0:k0 + n4, :].rearrange("p a b -> p (a b)"), in_=A[:, 0:n4 * 128], func=AF.Copy),
                 r=[An], w=[(mkn, k0)])
            yield

    def phaseY(m):
        qi = 2 * m + 1
        q64 = qa[0:64, :, m * 128:(m + 1) * 128]
        q65 = qa[0:65, :, m * 128:(m + 1) * 128]
        mk = maskall2[m % 2]; mkn = f"maskall{m % 2}"

        def selA(kt):
            def f():
                near = (qi - kt) <= 8
                ai = nxt('a'); A, An = psA[ai]
                if near:
                    P.op('pe', lambda e: e.matmul(A[:, :], lhsT=ksl[0:64, kt * 128:(kt + 1) * 128], rhs=q64, start=True, stop=True),
                         r=["ksl", "qa"], w=[An])
                else:
                    P.op('pe', lambda e: e.matmul(A[:, :], lhsT=ksl[0:65, kt * 128:(kt + 1) * 128], rhs=q65, start=True, stop=True),
                         r=["ksl", "qa"], w=[An])
                ei = nxt('e')
                if near:
                    softmax_tile(A[:, :], An, bs[:, qi - kt, :], [("bs", qi - kt)], ebuf[ei][:], f"ebuf{ei}")
                else:
                    softmax_tile(A[:, :], An, None, None, ebuf[ei][:], f"ebuf{ei}")
                bi = nxt('p')
                P.op('pool' if (kt % 5) < 2 else 'dve', lambda e: e.tensor_tensor(out=pbuf[bi][:].rearrange("p (r q) -> p r q", q=128),
                                                       in0=ebuf[ei][:].rearrange("p (r q) -> p r q", q=128),
                                                       in1=mk[:, kt:kt + 1, :].to_broadcast([128, 4, 128]), op=ALU.mult),
                     r=[f"ebuf{ei}", (mkn, (kt // 4) * 4)], w=[f"pbuf{bi}"])
                return (kt, bi)
            return f

        def selB(ctx):
            kt, bi = ctx
            pv(psO2, "psO2", pbuf[bi], f"pbuf{bi}", vsl[:, kt, :], "vsl", kt == 0, kt == qi)
        yield from pairs_gen([(selA(kt), selB) for kt in range(qi + 1)], 3)
        srcs = [(osb0[m % 3], f"osb0_{m % 3}"), (osb[1], "osb1"), (osb2[m % 3], f"osb2_{m % 3}")]
        P.op('act', lambda e: e.activation(out=osb[1][:], in_=psO2[:, 0:260], func=AF.Copy), r=["psO2"], w=["osb1"])
        for b_, (ot, on) in enumerate(srcs):
            P.op('dve', lambda e: e.tensor_copy(out=lc[:, b_, :], in_=ot[:].rearrange("p (r d) -> p r d", d=65)[:, :, 64]),
                 r=[on], w=["lc"])
        P.op('dve', lambda e: e.tensor_scalar(out=lc[:], in0=lc[:], scalar1=TINY, scalar2=None, op0=ALU.max), r=["lc"], w=["lc"])
        P.op('dve', lambda e: e.reciprocal(out=lc[:], in_=lc[:]), r=["lc"], w=["lc"])
        P.op('dve', lambda e: e.tensor_tensor(out=coef[:], in0=lc[:],
                                              in1=glog[:, m * 12:(m + 1) * 12].rearrange("p (r b) -> p b r", b=3), op=ALU.mult),
             r=["lc", "glog"], w=["coef"])
        yield
        y_ = yb[m % 2]; yn = f"yb{m % 2}"
        for r in range(4):
            for b_, (ot, on) in enumerate(srcs):
                src_ = ot[:, r * 65:r * 65 + 64]
                if b_ == 0:
                    P.op('dve', lambda e: e.tensor_scalar(out=y_[:, r * 64:(r + 1) * 64], in0=src_, scalar1=coef[:, b_, r:r + 1], scalar2=None,
                                                          op0=ALU.mult), r=[on, "coef"], w=[(yn, r)])
                else:
                    P.op('dve', lambda e: e.scalar_tensor_tensor(out=y_[:, r * 64:(r + 1) * 64], in0=src_, scalar=coef[:, b_, r:r + 1],
                                                                 in1=y_[:, r * 64:(r + 1) * 64], op0=ALU.mult, op1=ALU.add),
                         r=[on, "coef", (yn, r)], w=[(yn, r)])
            yield
        P.dma(y_d[:, m, :], y_[:], r=[yn], w=[("yd", m)], sem=f"ys{m % 2}")
        yield

    import itertools
    for g_ in [phaseX(0), phaseX2(0)] + ([phaseX(1)] if NT_RUN > 1 else []):
        for _ in g_:
            pass
    for m in range(NT_RUN):
        gy = phaseY(m)
        gens = []
        if m + 1 < NT_RUN:
            gens.append(phaseX2(m + 1))
        if m + 2 < NT_RUN:
            gens.append(phaseX(m + 2))
        gx = itertools.chain(*gens)
        ny = 2 * m + 2 + 6
        kx = max(1, -(-90 // ny))
        for _ in gy:
            for _ in range(kx):
                next(gx, None)
        for _ in gx:
            pass
    return P.finish()


def _t5_bucket_np(d):
    import math
    n = np.maximum(d, 0)
    nf = np.maximum(n, 16).astype(np.float32)
    large = 16 + (np.log(nf / np.float32(16)) / np.float32(math.log(64.0)) * np.float32(16)).astype(np.int32)
    large = np.minimum(large, 31)
    return np.where(n < 16, n, large)


def _m2_consts():
    eall = np.zeros((128, SEQ), np.float32)
    col = np.arange(SEQ)
    kt = col // 128
    p = col % 128
    eall[2 * kt + (127 - p) // 64, col] = 1.0
    mm = np.zeros((128, 4, 128), np.float32)
    wts = {-1: 1.0, 0: 2.0, 1: 2.0, 2: 2.0, 3: 1.0}
    for ci in range(4):
        for pp in range(128):
            c = 128 * ci + 127 - pp
            for dlt, wv in wts.items():
                if (c - dlt) % 4 == 0:
                    j = (c - dlt) // 4
                    if 0 <= j < 128:
                        mm[pp, ci, j] = wv
    return dict(eall=eall, mmat=mm.reshape(128, 512), i128=np.eye(128, dtype=np.float32))


def run_M2(zT, inp, l):
    nc = build_M2()
    cst = _m2_consts()
    rel = inp['rel_bias'].astype(np.float32)
    didx = np.arange(NF_F) - OFF_F
    bk = _t5_bucket_np(didx)
    maps = []
    for c in range(NCORES):
        b = c // 4
        g = (c % 4) // 2
        par = c % 2
        tsl = slice(b * SEQ, (b + 1) * SEQ)
        m = dict(cst)
        sh = 128 * (1 - par)
        fs = np.full((4, NF_F), NEGB, np.float32)
        fw = np.full((4, NF_F), NEGB, np.float32)
        for r in range(4):
            base_s = np.where(didx >= 0, rel[bk, g * 4 + r], NEGB).astype(np.float32)
            base_w = np.where((didx >= 0) & (didx < 512), rel[bk, g * 4 + r], NEGB).astype(np.float32)
            fs[r, sh:] = base_s[:NF_F - sh]
            fw[r, sh:] = base_w[:NF_F - sh]
        m["fs"] = fs
        m["fw"] = fw
        tiles = 2 * np.arange(NQT) + par
        tok = (tiles[:, None] * 128 + np.arange(128)[None, :]).reshape(-1)
        q65 = np.zeros((4, 65, NQT * 128), np.float32)
        for r in range(4):
            q65[r, 0:64] = zT[1280 + g * 256 + r * 64:1280 + g * 256 + (r + 1) * 64, tsl][:, tok]
            q65[r, 64] = rel[31, g * 4 + r]
        m["q65"] = q65

        def kv(j):
            return zT[1792 + j * 128 + g * 64:1792 + j * 128 + (g + 1) * 64, tsl]
        rev = lambda a: a.reshape(64, 64, 128)[:, :, ::-1].reshape(64, SEQ)
        m["kslX"] = rev(kv(2))
        m["kwnX"] = rev(kv(4))
        vrev = lambda a: a.T.reshape(64, 128, 64)[:, ::-1, :].transpose(1, 0, 2).reshape(128, 64 * 64)
        m["vslX"] = vrev(kv(3))
        m["vwnX"] = vrev(kv(5))
        cc = (128 * (np.arange(512) // 128) + 127 - (np.arange(512) % 128))
        for nm, j in [("kcrX", 0), ("vcrX", 1)]:
            src = np.concatenate([kv(j), np.zeros((64, 64), np.float32)], axis=1)
            arr = np.zeros((32, 64, 512), np.float32)
            for jj in range(32):
                arr[jj] = src[:, np.minimum(16 * cc + jj, SEQ + 63)]
            m[nm] = arr
        for sfx, key in [("k", "k"), ("v", "v")]:
            w1 = inp[f'nsa_cmp_w1_{key}'][l]
            m[f"w1{sfx}"] = w1.reshape(32, 64, 64).transpose(1, 0, 2).reshape(64, 2048)
            m[f"w2{sfx}"] = inp[f'nsa_cmp_w2_{key}'][l]
            m[f"pos{sfx}T"] = inp[f'nsa_cmp_pos_{key}'][l].T
        t = tiles[:, None] * 128 + np.arange(128)[None, :]
        blk = np.arange(128)
        cur = t // 64
        ok = (blk[None, None, :] * 64) <= t[:, :, None]
        forced = (blk[None, None, :] == 0) | (blk[None, None, :] == cur[:, :, None]) | (blk[None, None, :] == cur[:, :, None] - 1)
        m["madd"] = np.where(ok, np.where(forced, 1e4, 0.0), -1e30).astype(np.float32)
        gl = zT[2560 + g * 12:2560 + (g + 1) * 12, tsl][:, tok]
        m["glog"] = gl.T.reshape(NQT, 128, 12).transpose(1, 0, 2).reshape(128, NQT * 12)
        maps.append({k: np.ascontiguousarray(v_, dtype=np.float32) for k, v_ in m.items()})
    res = run_bass_kernel_spmd(nc, maps, core_ids=list(range(NCORES)))
    yT = np.zeros((512, BATCH * SEQ), np.float32)
    for c in range(NCORES):
        b = c // 4
        g = (c % 4) // 2
        par = c % 2
        tiles = 2 * np.arange(NQT) + par
        tok = b * SEQ + (tiles[:, None] * 128 + np.arange(128)[None, :]).reshape(-1)
        y = res.results[c]["ynsa"]
        y = y.transpose(1, 0, 2).reshape(NQT * 128, 256)
        yT[g * 256:(g + 1) * 256, tok] = y.T
    return yT


def kernel(**inputs):
    inp = {k: np.asarray(v) for k, v in inputs.items()}
    x = inp['x'].astype(np.float32).reshape(BATCH * SEQ, D_MODEL)
    xT = np.ascontiguousarray(x.T)

    def win_map(l):
        win = np.zeros((D_MODEL, NZC * 128), np.float32)
        win[:, :D_IN] = inp['w_in'][l]
        return {"mix_g": _gcol(inp['mix_norm'][l]), "win": _chunkT(win, 8)}

    def wout_map(l):
        return {"wglu": _chunkT(inp['s5_w_glu'][l], 2), "wout": _chunkT(inp['w_out'][l], 8)}

    def mixers(zT, l):
        y5T, yhT = run_M1(zT, inp, l)
        ynT = run_M2(zT, inp, l)
        return np.concatenate([y5T, yhT, ynT], axis=0)

    common = _ffn_maps("f0", inp['ffn1_norm'][0], inp['ffn1_w_gate'][0], inp['ffn1_w_up'][0], inp['ffn1_w_down'][0])
    common.update(win_map(0))
    o = run_T(xT, common, False, 1, True, False)
    yT = mixers(o["zT"], 0)
    common = _ffn_maps("f0", inp['ffn2_norm'][0], inp['ffn2_w_gate'][0], inp['ffn2_w_up'][0], inp['ffn2_w_down'][0])
    common.update(_ffn_maps("f1", inp['ffn1_norm'][1], inp['ffn1_w_gate'][1], inp['ffn1_w_up'][1], inp['ffn1_w_down'][1]))
    common.update(win_map(1))
    common.update(wout_map(0))
    o = run_T(o["xoT"], common, True, 2, True, False, yT_full=yT)
    yT = mixers(o["zT"], 1)
    common = _ffn_maps("f0", inp['ffn2_norm'][1], inp['ffn2_w_gate'][1], inp['ffn2_w_up'][1], inp['ffn2_w_down'][1])
    common.update(wout_map(1))
    common["fin_g"] = _gcol(inp['final_norm'])
    o = run_T(o["xoT"], common, True, 1, False, True, yT_full=yT)
    out = np.ascontiguousarray(o["xoT"].T).reshape(BATCH, SEQ, D_MODEL).astype(np.float32)
    return out
```

```python
import contextlib
import numpy as np
import concourse.bass as bass
import concourse.mybir as mybir
from concourse.bass_utils import run_bass_kernel_spmd

F32 = mybir.dt.float32
BF16 = mybir.dt.bfloat16
AF = mybir.ActivationFunctionType
ALU = mybir.AluOpType
AX = mybir.AxisListType

D_MODEL = 1024
D_FF = 2816
SEQ = 8192
BATCH = 2
D_IN = 2584
NCORES = 8
EPS = 1e-6


class Prog:
    def __init__(self):
        self.nc = bass.Bass("TRN2", target_bir_lowering=False)
        self.es = contextlib.ExitStack()
        nc = self.nc
        self.eng = {'pe': nc.tensor, 'act': nc.scalar, 'dve': nc.vector, 'pool': nc.gpsimd, 'sp': nc.sync}
        self.sems = {}
        self.cnt = {}
        for k in ['pe', 'act', 'dve', 'pool']:
            self.sems[('e', k)] = self.es.enter_context(nc.semaphore('e_' + k))
            self.cnt[('e', k)] = 0
        self.waited = {k: {} for k in self.eng}
        self.state = {}
        self.nps = 0

    def sbuf(self, name, shape, dt):
        return self.es.enter_context(self.nc.sbuf_tensor("s_" + name, list(shape), dt))

    def psum(self, name, shape, dt=F32):
        return self.es.enter_context(self.nc.psum_tensor("p_" + name, list(shape), dt))

    def dram_in(self, name, shape, dt=F32):
        return self.nc.dram_tensor(name, list(shape), dt, kind="ExternalInput").ap()

    def dram_out(self, name, shape, dt=F32):
        return self.nc.dram_tensor(name, list(shape), dt, kind="ExternalOutput").ap()

    @staticmethod
    def _norm(x):
        return x if isinstance(x, tuple) else (x, None)

    def _deps(self, rs, ws, e):
        need = {}

        def add(sv):
            if sv is None:
                return
            s, v = sv
            if s[0] == 'd':
                v = self.cnt[s]
            if need.get(s, 0) < v:
                need[s] = v

        for (n, k) in rs:
            for kk, ent in self.state.get(n, {}).items():
                if k is None or kk is None or kk == k:
                    add(ent['w'])
        for (n, k) in ws:
            for kk, ent in self.state.get(n, {}).items():
                if k is None or kk is None or kk == k:
                    if ent['w'] is not None and ent['w'][0] != ('e', e):
                        add(ent['w'])
                    for s, v in ent['r'].items():
                        if s != ('e', e):
                            add((s, v))
        if e == 'pe':
            need.pop(('e', 'pe'), None)
        return need

    def _record(self, rs, ws, sv):
        s, v = sv
        for (n, k) in rs:
            ent = self.state.setdefault(n, {}).setdefault(k, {'w': None, 'r': {}})
            ent['r'][s] = max(ent['r'].get(s, 0), v)
        for (n, k) in ws:
            d = self.state.setdefault(n, {})
            if k is None:
                d.clear()
            d[k] = {'w': (s, v), 'r': {}}

    def _emit_waits(self, e, need):
        eng = self.eng[e]
        for s, v in need.items():
            if self.waited[e].get(s, 0) < v:
                eng.wait_ge(self.sems[s], v)
                self.waited[e][s] = v

    def op(self, e, fn, r=(), w=()):
        rs = [self._norm(x) for x in r]
        ws = [self._norm(x) for x in w]
        ws = ws + [(n, None) for (n, k) in rs if n.startswith("ps")]
        ws = [((n, None) if n.startswith("ps") else (n, k)) for (n, k) in ws]
        rs = [x for x in rs if not x[0].startswith("ps")]
        self._emit_waits(e, self._deps(rs, ws, e))
        inst = fn(self.eng[e])
        s = ('e', e)
        self.cnt[s] += 1
        inst.then_inc(self.sems[s], 1)
        self._record(rs, ws, (s, self.cnt[s]))

    def dma(self, out, in_, r=(), w=(), q='sp', sem=None):
        rs = [self._norm(x) for x in r]
        ws = [self._norm(x) for x in w]
        if sem is None:
            sem = ws[0][0]
        s = ('d', sem)
        if s not in self.sems:
            self.sems[s] = self.es.enter_context(self.nc.semaphore('d_' + sem))
            self.cnt[s] = 0
        self._emit_waits(q, self._deps(rs, ws, q))
        self.eng[q].dma_start(out=out, in_=in_).then_inc(self.sems[s], 16)
        self.cnt[s] += 16
        self._record(rs, ws, (s, self.cnt[s]))

    def finish(self):
        sp = self.eng['sp']
        for s, h in self.sems.items():
            if self.cnt[s] > 0 and self.waited['sp'].get(s, 0) < self.cnt[s]:
                sp.wait_ge(h, self.cnt[s])
        self.es.close()
        return self.nc


def mm(ps, lhsT, rhs, start, stop):
    return lambda e: e.matmul(ps, lhsT=lhsT, rhs=rhs, start=start, stop=stop)


NTOK = 2048
TP = 1024
NFC = D_FF // 128
NZC = 21


def build_T(do_wout, n_ffn, do_win, do_final):
    P = Prog()
    nc = P.nc
    xT_d = P.dram_in("xT", [8, 128, NTOK])
    if do_wout:
        yT_d = P.dram_in("yT", [8, 128, NTOK])
        wglu_d = P.dram_in("wglu", [2, 128, 2, 128])
        wout_d = P.dram_in("wout", [8, 128, 8, 128])
    ffn_d = []
    for i in range(n_ffn):
        ffn_d.append(dict(
            g=P.dram_in(f"f{i}_g", [128, 8]),
            wg=P.dram_in(f"f{i}_wg", [NFC, 128, 8, 128]),
            wu=P.dram_in(f"f{i}_wu", [NFC, 128, 8, 128]),
            wd=P.dram_in(f"f{i}_wd", [8, 128, NFC, 128]),
        ))
    if do_win:
        ming_d = P.dram_in("mix_g", [128, 8])
        win_d = P.dram_in("win", [NZC, 128, 8, 128])
        zT_d = P.dram_out("zT", [NZC, 128, NTOK])
    if do_final:
        fing_d = P.dram_in("fin_g", [128, 8])
    xo_d = P.dram_out("xoT", [8, 128, NTOK])

    xT = P.sbuf("xT_s", [128, 8, TP], F32)
    hT = P.sbuf("hT_s", [128, 8, TP], BF16)
    aT = P.sbuf("aT_s", [128, NFC, TP], BF16)
    sq = P.sbuf("sq_s", [128, 2, TP], BF16)
    rstd = P.sbuf("rstd_s", [128, TP], F32)
    ones = P.sbuf("ones_s", [128, 128], BF16)
    gt = P.sbuf("g_s", [128, 8], F32)
    stg = [P.sbuf(f"stg{i}", [128, NFC * 128], F32) for i in range(3)]
    wbf = [P.sbuf(f"wbf{i}", [128, NFC * 128], BF16) for i in range(3)]
    sg = [P.sbuf(f"sg{i}", [128, 512], F32) for i in range(2)]
    ev = [P.sbuf(f"ev{i}", [128, 512], F32) for i in range(2)]
    ps = [P.psum(f"ps{i}", [128, 512]) for i in range(8)]
    if do_wout:
        yf = P.sbuf("yf_s", [128, 8, TP], F32)

    P.op('pool', lambda e: e.memset(ones[:], 1.0), w=["ones"])

    wctr = [0]

    def load_w(src_ap, nk):
        i = wctr[0] % 3
        wctr[0] += 1
        P.dma(stg[i][:, 0:nk * 128], src_ap, w=[f"stg{i}"])
        P.op('pool', lambda e: e.tensor_copy(out=wbf[i][:, 0:nk * 128], in_=stg[i][:, 0:nk * 128]),
             r=[f"stg{i}"], w=[f"wbf{i}"])
        return wbf[i], f"wbf{i}"

    psctr = [0]

    def next_ps():
        i = psctr[0] % 8
        psctr[0] += 1
        return ps[i], f"ps{i}"

    def rmsnorm(g_dram, final=False):
        P.dma(gt[:], g_dram, w=["g"])
        pss = [next_ps(), next_ps()]
        for k in range(8):
            j = k % 2
            P.op('act', lambda e: e.activation(out=sq[:, j, :], in_=xT[:, k, :], func=AF.Square),
                 r=[("xT", k)], w=[("sq", j)])
            for tg in range(2):
                P.op('pe', mm(pss[tg][0][:, :], ones[:], sq[:, j, tg * 512:(tg + 1) * 512], k == 0, k == 7),
                     r=["ones", ("sq", j)], w=[pss[tg][1]])
        for tg in range(2):
            sl = slice(tg * 512, (tg + 1) * 512)
            P.op('act', lambda e: e.activation(out=rstd[:, sl], in_=pss[tg][0][:, :], func=AF.Sqrt,
                                               scale=1.0 / D_MODEL, bias=EPS),
                 r=[pss[tg][1]], w=[("rstd", tg)])
            P.op('dve', lambda e: e.reciprocal(out=rstd[:, sl], in_=rstd[:, sl]),
                 r=[("rstd", tg)], w=[("rstd", tg)])
        for k in range(8):
            if final:
                P.op('dve', lambda e: e.scalar_tensor_tensor(out=xT[:, k, :], in0=xT[:, k, :], scalar=gt[:, k:k + 1],
                                                             in1=rstd[:, :], op0=ALU.mult, op1=ALU.mult),
                     r=[("xT", k), "g", "rstd"], w=[("xT", k)])
            else:
                P.op('dve', lambda e: e.scalar_tensor_tensor(out=hT[:, k, :], in0=xT[:, k, :], scalar=gt[:, k:k + 1],
                                                             in1=rstd[:, :], op0=ALU.mult, op1=ALU.mult),
                     r=[("xT", k), "g", "rstd"], w=[("hT", k)])

    def ffn(fd):
        rmsnorm(fd['g'])
        for fc in range(NFC):
            wg, wgn = load_w(fd['wg'][fc].rearrange("p k j -> p (k j)"), 8)
            wu, wun = load_w(fd['wu'][fc].rearrange("p k j -> p (k j)"), 8)
            pg = [next_ps(), next_ps()]
            pu = [next_ps(), next_ps()]
            for (w_, wn_, pp_) in [(wg, wgn, pg), (wu, wun, pu)]:
                for k in range(8):
                    for tg in range(2):
                        P.op('pe', mm(pp_[tg][0][:, :], w_[:, k * 128:(k + 1) * 128], hT[:, k, tg * 512:(tg + 1) * 512], k == 0, k == 7),
                             r=[wn_, "hT"], w=[pp_[tg][1]])
            for tg in range(2):
                sl = slice(tg * 512, (tg + 1) * 512)
                i = tg
                P.op('act', lambda e: e.activation(out=sg[i][:, :], in_=pg[tg][0][:, :], func=AF.Silu),
                     r=[pg[tg][1]], w=[f"sg{i}"])
                P.op('dve', lambda e: e.tensor_tensor(out=aT[:, fc, sl], in0=sg[i][:, :], in1=pu[tg][0][:, :], op=ALU.mult),
                     r=[f"sg{i}", pu[tg][1]], w=[("aT", fc)])
        for mc in range(8):
            wd, wdn = load_w(fd['wd'][mc].rearrange("p k j -> p (k j)"), NFC)
            pd = [next_ps(), next_ps()]
            for k in range(NFC):
                for tg in range(2):
                    P.op('pe', mm(pd[tg][0][:, :], wd[:, k * 128:(k + 1) * 128], aT[:, k, tg * 512:(tg + 1) * 512], k == 0, k == NFC - 1),
                         r=[wdn, "aT"], w=[pd[tg][1]])
            for tg in range(2):
                sl = slice(tg * 512, (tg + 1) * 512)
                P.op('dve', lambda e: e.scalar_tensor_tensor(out=xT[:, mc, sl], in0=pd[tg][0][:, :], scalar=0.5,
                                                             in1=xT[:, mc, sl], op0=ALU.mult, op1=ALU.add),
                     r=[pd[tg][1], ("xT", mc)], w=[("xT", mc)])

    for ps_i in range(NTOK // TP):
        tsl = slice(ps_i * TP, (ps_i + 1) * TP)
        for k in range(8):
            P.dma(xT[:, k, :], xT_d[k, :, tsl], w=[("xT", k)], sem="xT")
        if do_wout:
            for k in range(8):
                P.dma(yf[:, k, :], yT_d[k, :, tsl], w=[("yf", k)], sem="yf")
            for k in range(2):
                P.op('pool', lambda e: e.tensor_copy(out=hT[:, k, :], in_=yf[:, k, :]), r=[("yf", k)], w=[("hT", k)])
            for mc in range(2):
                wl, wln = load_w(wglu_d[mc].rearrange("p k j -> p (k j)"), 2)
                for tg in range(2):
                    sl = slice(tg * 512, (tg + 1) * 512)
                    pg, pgn = next_ps()
                    for k in range(2):
                        P.op('pe', mm(pg[:, :], wl[:, k * 128:(k + 1) * 128], hT[:, k, sl], k == 0, k == 1),
                             r=[wln, ("hT", 0), ("hT", 1)], w=[pgn])
                    i = tg
                    P.op('act', lambda e: e.activation(out=sg[i][:, :], in_=pg[:, :], func=AF.Sigmoid),
                         r=[pgn], w=[f"sg{i}"])
                    P.op('dve', lambda e: e.tensor_tensor(out=aT[:, mc, sl], in0=sg[i][:, :], in1=yf[:, mc, sl],
                                                          op=ALU.mult),
                         r=[f"sg{i}", ("yf", mc)], w=[("aT", mc)])
            for k in range(2, 8):
                P.op('pool', lambda e: e.tensor_copy(out=aT[:, k, :], in_=yf[:, k, :]), r=[("yf", k)], w=[("aT", k)])
            for mc in range(8):
                wl, wln = load_w(wout_d[mc].rearrange("p k j -> p (k j)"), 8)
                for tg in range(2):
                    sl = slice(tg * 512, (tg + 1) * 512)
                    pd, pdn = next_ps()
                    for k in range(8):
                        P.op('pe', mm(pd[:, :], wl[:, k * 128:(k + 1) * 128], aT[:, k, sl], k == 0, k == 7),
                             r=[wln, "aT"], w=[pdn])
                    P.op('dve', lambda e: e.tensor_tensor(out=xT[:, mc, sl], in0=pd[:, :], in1=xT[:, mc, sl],
                                                          op=ALU.add),
                         r=[pdn, ("xT", mc)], w=[("xT", mc)])
        for i in range(n_ffn):
            ffn(ffn_d[i])
        if do_win:
            rmsnorm(ming_d)
            for cc in range(NZC):
                wl, wln = load_w(win_d[cc].rearrange("p k j -> p (k j)"), 8)
                for tg in range(2):
                    sl = slice(tg * 512, (tg + 1) * 512)
                    pz, pzn = next_ps()
                    for k in range(8):
                        P.op('pe', mm(pz[:, :], wl[:, k * 128:(k + 1) * 128], hT[:, k, sl], k == 0, k == 7),
                             r=[wln, "hT"], w=[pzn])
                    i = (cc * 2 + tg) % 2
                    P.op('act', lambda e: e.activation(out=ev[i][:, :], in_=pz[:, :], func=AF.Copy),
                         r=[pzn], w=[f"ev{i}"])
                    P.dma(zT_d[cc, :, ps_i * TP + tg * 512: ps_i * TP + (tg + 1) * 512], ev[i][:, :],
                          r=[f"ev{i}"], w=[("zT", (ps_i, cc, tg))], sem=f"zst{i}")
        if do_final:
            rmsnorm(fing_d, final=True)
        for k in range(8):
            P.dma(xo_d[k, :, tsl], xT[:, k, :], r=[("xT", k)], w=[("xo", (ps_i, k))], sem="xo")
    return P.finish()


def _chunkT(w, nk):
    K, M = w.shape
    return np.ascontiguousarray(w.reshape(nk, 128, M // 128, 128).transpose(2, 1, 0, 3))


def _gcol(g):
    return np.ascontiguousarray(g.reshape(8, 128).T)


def _ffn_maps(prefix, g, wg, wu, wd):
    return {f"{prefix}_g": _gcol(g), f"{prefix}_wg": _chunkT(wg, 8), f"{prefix}_wu": _chunkT(wu, 8),
            f"{prefix}_wd": _chunkT(wd, NFC)}


def run_T(xT_full, common, do_wout, n_ffn, do_win, do_final, yT_full=None):
    nc = build_T(do_wout, n_ffn, do_win, do_final)
    maps = []
    for c in range(NCORES):
        m = dict(common)
        m["xT"] = np.ascontiguousarray(xT_full[:, c * NTOK:(c + 1) * NTOK].reshape(8, 128, NTOK))
        if do_wout:
            m["yT"] = np.ascontiguousarray(yT_full[:, c * NTOK:(c + 1) * NTOK].reshape(8, 128, NTOK))
        maps.append(m)
    res = run_bass_kernel_spmd(nc, maps, core_ids=list(range(NCORES)))
    out = {}
    out["xoT"] = np.concatenate([r["xoT"].reshape(1024, NTOK) for r in res.results], axis=1)
    if do_win:
        out["zT"] = np.concatenate([r["zT"].reshape(NZC * 128, NTOK) for r in res.results], axis=1)
    return out


PI = float(np.pi)
NCH = 1024
HB = 2048
GELU_C = 1.5957691216057308


def emit_gelu(P, dst, src_ps, tmp_a, tmp_b, names, src_names, bias=None):
    dn, an, bn = names
    if bias is None:
        P.op('act', lambda e: e.activation(out=tmp_a, in_=src_ps, func=AF.Copy), r=src_names, w=[an])
    else:
        P.op('act', lambda e: e.activation(out=tmp_a, in_=src_ps, func=AF.Identity, bias=bias[0]), r=src_names + [bias[1]], w=[an])
    P.op('pool', lambda e: e.tensor_tensor(out=tmp_b, in0=tmp_a, in1=tmp_a, op=ALU.mult), r=[an], w=[bn])
    P.op('dve', lambda e: e.tensor_scalar(out=tmp_b, in0=tmp_b, scalar1=0.044715, scalar2=1.0, op0=ALU.mult, op1=ALU.add),
         r=[bn], w=[bn])
    P.op('dve', lambda e: e.tensor_tensor(out=tmp_b, in0=tmp_b, in1=tmp_a, op=ALU.mult), r=[bn, an], w=[bn])
    P.op('act', lambda e: e.activation(out=tmp_b, in_=tmp_b, func=AF.Sigmoid, scale=GELU_C), r=[bn], w=[bn])
    P.op('dve', lambda e: e.tensor_tensor(out=dst, in0=tmp_a, in1=tmp_b, op=ALU.mult), r=[an, bn], w=[dn])


def build_M1(layer):
    P = Prog()
    lr2_d = P.dram_in("lr2", [128, 4]); li2_d = P.dram_in("li2", [128, 4]); ldt_d = P.dram_in("ldt", [128, 4])
    b1_d = P.dram_in("bst1", [128, 4, 16]); b2_d = P.dram_in("bst2", [128, 4, 16])
    c1_d = P.dram_in("cst1", [128, 4, 16]); c2_d = P.dram_in("cst2", [128, 4, 16])
    dcol_d = P.dram_in("dcol", [128, 4]); sgn_d = P.dram_in("sgn", [128, 2]); jv_d = P.dram_in("jv", [128, 4, 24])
    i128_d = P.dram_in("i128", [128, 128]); jsw_d = P.dram_in("jsw", [128, 128]); tmask_d = P.dram_in("tmask", [128, 128])
    uc_d = P.dram_in("uc", [4, 128, NCH])
    y5_d = P.dram_out("y5", [4, 128, NCH])
    hq_d = P.dram_in("hq", [64, SEQ]); hf_d = P.dram_in("hf", [64, SEQ])
    hv64_d = P.dram_in("hv64", [4, 64, 32, 64]); hv128_d = P.dram_in("hv128", [4, 128, 16, 64]); hg64_d = P.dram_in("hg64", [4, 64, 32, 64])
    lbl_d = P.dram_in("lbl", [64, 2]); gain_d = P.dram_in("hgain", [64, 64]); cm_d = P.dram_in("cmask", [64, 512])
    yh_d = P.dram_out("yh", [64, 128, 64])

    ps = [P.psum(f"ps{i}", [128, 512]) for i in range(8)]
    i128 = P.sbuf("i128", [128, 128], F32)
    P.dma(i128[:], i128_d[:, :], w=["i128"])

    import os
    PARTS = os.environ.get('M1PARTS', 's5,hg')
    NG5 = 4 if 's5' in PARTS else 0
    def T(name, shape, dt=F32):
        return P.sbuf(name, shape, dt)
    lr2 = T("lr2", [128, 4]); li2 = T("li2", [128, 4]); dt = T("dt", [128, 4]); sgn = T("sgn", [128, 2])
    b1 = T("b1", [128, 4, 16]); b2 = T("b2", [128, 4, 16]); c1 = T("c1", [128, 4, 16]); c2 = T("c2", [128, 4, 16])
    dcol = T("dcol", [128, 4]); jv = T("jv", [128, 4, 24]); jsw = T("jsw", [128, 128]); tmask = T("tmask", [128, 128])
    for t_, d_, n_ in [(lr2, lr2_d, "lr2"), (li2, li2_d, "li2"), (dt, ldt_d, "dt"), (sgn, sgn_d, "sgn"), (dcol, dcol_d, "dcol"),
                       (jsw, jsw_d, "jsw"), (tmask, tmask_d, "tmask")]:
        P.dma(t_[:], d_[:, :], w=[n_])
    for t_, d_, n_ in [(b1, b1_d, "b1"), (b2, b2_d, "b2"), (c1, c1_d, "c1"), (c2, c2_d, "c2"), (jv, jv_d, "jv")]:
        P.dma(t_[:], d_[:, :, :], w=[n_])
    V = lambda e: e

    def dv(fn, r, w):
        P.op('dve', fn, r=r, w=w)

    def ac(fn, r, w):
        P.op('act', fn, r=r, w=w)
    ac(lambda e: e.activation(out=dt[:], in_=dt[:], func=AF.Exp), ["dt"], ["dt"])
    lrdt = T("lrdt", [128, 4]); lidt = T("lidt", [128, 4])
    dv(lambda e: e.tensor_tensor(out=lrdt[:], in0=lr2[:], in1=dt[:], op=ALU.mult), ["lr2", "dt"], ["lrdt"])
    dv(lambda e: e.tensor_tensor(out=lidt[:], in0=li2[:], in1=dt[:], op=ALU.mult), ["li2", "dt"], ["lidt"])
    am = T("am", [128, 4, 24]); aa = T("aa", [128, 4, 24]); pr = T("pr", [128, 4, 24]); pi_ = T("pi", [128, 4, 24])
    tA = T("tA", [128, 4, 24]); tB = T("tB", [128, 4, 24]); tI = T("tI", [128, 4, 24], mybir.dt.int32)
    bc24 = lambda t_: t_[:, :, None].to_broadcast([128, 4, 24])
    dv(lambda e: e.tensor_tensor(out=am[:], in0=jv[:], in1=bc24(lrdt), op=ALU.mult), ["jv", "lrdt"], ["am"])
    ac(lambda e: e.activation(out=am[:], in_=am[:], func=AF.Exp), ["am"], ["am"])
    dv(lambda e: e.tensor_tensor(out=aa[:], in0=jv[:], in1=bc24(lidt), op=ALU.mult), ["jv", "lidt"], ["aa"])

    def sin_of(dst, dn, src, sn, shift):
        dv(lambda e: e.tensor_scalar(out=tA[:], in0=src[:], scalar1=shift, scalar2=None, op0=ALU.add), [sn], ["tA"])
        dv(lambda e: e.tensor_scalar(out=tB[:], in0=tA[:], scalar1=1.0 / (2 * PI), scalar2=64.5, op0=ALU.mult, op1=ALU.add),
           ["tA"], ["tB"])
        dv(lambda e: e.tensor_copy(out=tI[:], in_=tB[:]), ["tB"], ["tI"])
        dv(lambda e: e.tensor_copy(out=tB[:], in_=tI[:]), ["tI"], ["tB"])
        dv(lambda e: e.tensor_scalar(out=tB[:], in0=tB[:], scalar1=-64.0, scalar2=-2 * PI, op0=ALU.add, op1=ALU.mult),
           ["tB"], ["tB"])
        dv(lambda e: e.tensor_tensor(out=tA[:], in0=tA[:], in1=tB[:], op=ALU.add), ["tA", "tB"], ["tA"])
        dv(lambda e: e.tensor_scalar(out=tB[:], in0=tA[:], scalar1=-PI, scalar2=2 * PI, op0=ALU.is_lt, op1=ALU.mult),
           ["tA"], ["tB"])
        dv(lambda e: e.tensor_tensor(out=tA[:], in0=tA[:], in1=tB[:], op=ALU.add), ["tA", "tB"], ["tA"])
        dv(lambda e: e.tensor_scalar(out=tB[:], in0=tA[:], scalar1=PI, scalar2=-2 * PI, op0=ALU.is_gt, op1=ALU.mult),
           ["tA"], ["tB"])
        dv(lambda e: e.tensor_tensor(out=tA[:], in0=tA[:], in1=tB[:], op=ALU.add), ["tA", "tB"], ["tA"])
        dv(lambda e: e.tensor_scalar(out=tA[:], in0=tA[:], scalar1=-PI, scalar2=PI, op0=ALU.max, op1=ALU.min),
           ["tA"], ["tA"])
        ac(lambda e: e.activation(out=dst[:], in_=tA[:], func=AF.Sin), ["tA"], [dn])
    sin_of(pi_, "pi", aa, "aa", 0.0)
    sin_of(pr, "pr", aa, "aa", PI / 2)
    dv(lambda e: e.tensor_tensor(out=pr[:], in0=pr[:], in1=am[:], op=ALU.mult), ["pr", "am"], ["pr"])
    dv(lambda e: e.tensor_tensor(out=pi_[:], in0=pi_[:], in1=am[:], op=ALU.mult), ["pi", "am"], ["pi"])
    den = T("den", [128, 4]); t4a = T("t4a", [128, 4]); t4b = T("t4b", [128, 4]); nr = T("nr", [128, 4])
    gre = T("gre", [128, 4]); gim = T("gim", [128, 4])
    ar1 = pr[:, :, 8]; ai1 = pi_[:, :, 8]
    dv(lambda e: e.tensor_tensor(out=den[:], in0=lr2[:], in1=lr2[:], op=ALU.mult), ["lr2"], ["den"])
    dv(lambda e: e.tensor_tensor(out=t4a[:], in0=li2[:], in1=li2[:], op=ALU.mult), ["li2"], ["t4a"])
    dv(lambda e: e.tensor_tensor(out=den[:], in0=den[:], in1=t4a[:], op=ALU.add), ["den", "t4a"], ["den"])
    dv(lambda e: e.reciprocal(out=den[:], in_=den[:]), ["den"], ["den"])
    dv(lambda e: e.tensor_scalar(out=nr[:], in0=ar1, scalar1=-1.0, scalar2=None, op0=ALU.add), ["pr"], ["nr"])
    dv(lambda e: e.tensor_tensor(out=t4a[:], in0=nr[:], in1=lr2[:], op=ALU.mult), ["nr", "lr2"], ["t4a"])
    dv(lambda e: e.tensor_tensor(out=t4b[:], in0=ai1, in1=li2[:], op=ALU.mult), ["pi", "li2"], ["t4b"])
    dv(lambda e: e.tensor_tensor(out=t4a[:], in0=t4a[:], in1=t4b[:], op=ALU.add), ["t4a", "t4b"], ["t4a"])
    dv(lambda e: e.tensor_tensor(out=gre[:], in0=t4a[:], in1=den[:], op=ALU.mult), ["t4a", "den"], ["gre"])
    dv(lambda e: e.tensor_tensor(out=t4a[:], in0=ai1, in1=lr2[:], op=ALU.mult), ["pi", "lr2"], ["t4a"])
    dv(lambda e: e.tensor_tensor(out=t4b[:], in0=nr[:], in1=li2[:], op=ALU.mult), ["nr", "li2"], ["t4b"])
    dv(lambda e: e.tensor_tensor(out=t4a[:], in0=t4a[:], in1=t4b[:], op=ALU.subtract), ["t4a", "t4b"], ["t4a"])
    dv(lambda e: e.tensor_tensor(out=gim[:], in0=t4a[:], in1=den[:], op=ALU.mult), ["t4a", "den"], ["gim"])
    er = T("er", [128, 4, 8]); ei = T("ei", [128, 4, 8]); t8 = T("t8", [128, 4, 8])
    bc8 = lambda t_: t_[:, :, None].to_broadcast([128, 4, 8])
    dv(lambda e: e.tensor_tensor(out=er[:], in0=pr[:, :, 0:8], in1=bc8(gre), op=ALU.mult), ["pr", "gre"], ["er"])
    dv(lambda e: e.tensor_tensor(out=t8[:], in0=pi_[:, :, 0:8], in1=bc8(gim), op=ALU.mult), ["pi", "gim"], ["t8"])
    dv(lambda e: e.tensor_tensor(out=er[:], in0=er[:], in1=t8[:], op=ALU.subtract), ["er", "t8"], ["er"])
    dv(lambda e: e.tensor_tensor(out=ei[:], in0=pr[:, :, 0:8], in1=bc8(gim), op=ALU.mult), ["pr", "gim"], ["ei"])
    dv(lambda e: e.tensor_tensor(out=t8[:], in0=pi_[:, :, 0:8], in1=bc8(gre), op=ALU.mult), ["pi", "gre"], ["t8"])
    dv(lambda e: e.tensor_tensor(out=ei[:], in0=ei[:], in1=t8[:], op=ALU.add), ["ei", "t8"], ["ei"])
    m2 = T("m2", [128, 4, 8]); n1f = T("n1f", [128, 4, 8]); n2f = T("n2f", [128, 4, 8]); n1h = T("n1h", [128, 4, 8]); n2h = T("n2h", [128, 4, 8])
    dv(lambda e: e.tensor_scalar(out=m2[:], in0=ei[:], scalar1=sgn[:, 1:2], scalar2=None, op0=ALU.mult), ["ei", "sgn"], ["m2"])
    dv(lambda e: e.tensor_scalar(out=n1f[:], in0=pr[:, :, 8:16], scalar1=sgn[:, 0:1], scalar2=None, op0=ALU.mult), ["pr", "sgn"], ["n1f"])
    dv(lambda e: e.tensor_scalar(out=n2f[:], in0=pi_[:, :, 8:16], scalar1=-1.0, scalar2=None, op0=ALU.mult), ["pi"], ["n2f"])
    dv(lambda e: e.tensor_scalar(out=n1h[:], in0=pr[:, :, 16:24], scalar1=sgn[:, 0:1], scalar2=None, op0=ALU.mult), ["pr", "sgn"], ["n1h"])
    dv(lambda e: e.tensor_scalar(out=n2h[:], in0=pi_[:, :, 16:24], scalar1=-1.0, scalar2=None, op0=ALU.mult), ["pi"], ["n2h"])
    bcm = T("bcm", [128, 4, 8, 16]); ccm = T("ccm", [128, 4, 8, 16]); qmm = T("qmm", [128, 4, 8, 16]); t816 = T("t816", [128, 4, 8, 16])
    S4 = [128, 4, 8, 16]

    def outer(dst, dn, st1, s1n, co1, c1n, st2, s2n, co2, c2n):
        dv(lambda e: e.tensor_tensor(out=dst[:], in0=st1[:, :, None, :].to_broadcast(S4), in1=co1[:, :, :, None].to_broadcast(S4),
                                     op=ALU.mult), [s1n, c1n], [dn])
        dv(lambda e: e.tensor_tensor(out=t816[:], in0=st2[:, :, None, :].to_broadcast(S4), in1=co2[:, :, :, None].to_broadcast(S4),
                                     op=ALU.mult), [s2n, c2n], ["t816"])
        dv(lambda e: e.tensor_tensor(out=dst[:], in0=dst[:], in1=t816[:], op=ALU.add), [dn, "t816"], [dn])
    outer(bcm, "bcm", b1, "b1", er, "er", b2, "b2", m2, "m2")
    outer(ccm, "ccm", c1, "c1", n1f, "n1f", c2, "c2", n2f, "n2f")
    outer(qmm, "qmm", c1, "c1", n1h, "n1h", c2, "c2", n2h, "n2h")
    tz = T("tz", [128, 4, 128]); bct = T("bct", [128, 4, 128])
    for g in range(4):
        pg, pgn = ps[g % 2], f"ps{g % 2}"
        P.op('pe', lambda e: e.matmul(pg[:, 0:128], lhsT=bcm[:, g].rearrange("p s h -> p (s h)"),
                                      rhs=qmm[:, g].rearrange("p s h -> p (s h)"), start=True, stop=True),
             r=["bcm", "qmm"], w=[pgn])
        dv(lambda e: e.tensor_tensor(out=tz[:, g, :], in0=pg[:, 0:128], in1=tmask[:], op=ALU.mult), [pgn, "tmask"], [("tz", g)])
        dv(lambda e: e.scalar_tensor_tensor(out=tz[:, g, :], in0=i128[:], scalar=dcol[:, g:g + 1], in1=tz[:, g, :],
                                            op0=ALU.mult, op1=ALU.add), ["i128", "dcol", ("tz", g)], [("tz", g)])
        pt, ptn = ps[2 + g % 2], f"ps{2 + g % 2}"
        P.op('pe', lambda e: e.transpose(out=pt[:, 0:128], in_=bcm[:, g].rearrange("p s h -> p (s h)"), identity=i128[:]),
             r=["bcm", "i128"], w=[ptn])
        ac(lambda e: e.activation(out=bct[:, g, :], in_=pt[:, 0:128], func=AF.Copy), [ptn], [("bct", g)])
    NK = 10
    a8r = T("a8r", [128, NK, 4]); a8i = T("a8i", [128, NK, 4]); rm = T("rm", [128, NK * 4, 128])
    dv(lambda e: e.tensor_copy(out=a8r[:, 0, :], in_=pr[:, :, 15]), ["pr"], ["a8r"])
    dv(lambda e: e.tensor_copy(out=a8i[:, 0, :], in_=pi_[:, :, 15]), ["pi"], ["a8i"])
    for k in range(1, NK):
        dv(lambda e: e.tensor_tensor(out=t4a[:], in0=a8r[:, k - 1, :], in1=a8r[:, k - 1, :], op=ALU.mult), ["a8r"], ["t4a"])
        dv(lambda e: e.tensor_tensor(out=t4b[:], in0=a8i[:, k - 1, :], in1=a8i[:, k - 1, :], op=ALU.mult), ["a8i"], ["t4b"])
        dv(lambda e: e.tensor_tensor(out=a8r[:, k, :], in0=t4a[:], in1=t4b[:], op=ALU.subtract), ["t4a", "t4b"], ["a8r"])
        dv(lambda e: e.tensor_tensor(out=t4a[:], in0=a8r[:, k - 1, :], in1=a8i[:, k - 1, :], op=ALU.mult), ["a8r", "a8i"], ["t4a"])
        dv(lambda e: e.tensor_scalar(out=a8i[:, k, :], in0=t4a[:], scalar1=2.0, scalar2=None, op0=ALU.mult), ["t4a"], ["a8i"])
    a8is = T("a8is", [128, NK, 4])
    dv(lambda e: e.tensor_scalar(out=a8is[:], in0=a8i[:], scalar1=sgn[:, 0:1], scalar2=None, op0=ALU.mult), ["a8i", "sgn"], ["a8is"])
    for k in range(NK):
        for g in range(4):
            i = k * 4 + g
            dv(lambda e: e.tensor_scalar(out=rm[:, i, :], in0=i128[:], scalar1=a8r[:, k, g:g + 1], scalar2=None, op0=ALU.mult),
               ["i128", "a8r"], [("rm", i)])
            dv(lambda e: e.scalar_tensor_tensor(out=rm[:, i, :], in0=jsw[:], scalar=a8is[:, k, g:g + 1], in1=rm[:, i, :],
                                                op0=ALU.mult, op1=ALU.add), ["jsw", "a8is", ("rm", i)], [("rm", i)])
    uc = [T(f"uc{i}", [128, NCH]) for i in range(2)]
    xs = T("xs", [128, NCH + 1]); ga = T("ga", [128, 512]); gb = T("gb", [128, 512]); yo = [T(f"yo{i}", [128, 512]) for i in range(2)]
    P.op('pool', lambda e: e.memset(xs[:, 0:1], 0.0), w=[("xs", "z")])
    def s5_gen():
        for g in range(NG5):
            u = uc[g % 2]; un = f"uc{g % 2}"
            P.dma(u[:], uc_d[g], w=[un])
            for h in range(2):
                pp, ppn = ps[h], f"ps{h}"
                P.op('pe', lambda e: e.matmul(pp[:, :], lhsT=bct[:, g, :], rhs=u[:, h * 512:(h + 1) * 512], start=True, stop=True),
                     r=[("bct", g), un], w=[ppn])
                ac(lambda e: e.activation(out=xs[:, 1 + h * 512:1 + (h + 1) * 512], in_=pp[:, :], func=AF.Copy), [ppn], [("xs", "x")])
            for k in range(NK):
                d = 1 << k
                n = NCH - d
                pieces = [(0, min(512, n))] + ([(512, n)] if n > 512 else [])
                for h, (a, b) in enumerate(pieces):
                    pp, ppn = ps[h], f"ps{h}"
                    P.op('pe', lambda e: e.matmul(pp[:, 0:b - a], lhsT=rm[:, k * 4 + g, :], rhs=xs[:, 1 + a:1 + b], start=True, stop=True),
                         r=[("rm", k * 4 + g), ("xs", "x")], w=[ppn])
                for h, (a, b) in enumerate(pieces):
                    pp, ppn = ps[h], f"ps{h}"
                    dv(lambda e: e.tensor_tensor(out=xs[:, 1 + d + a:1 + d + b], in0=pp[:, 0:b - a], in1=xs[:, 1 + d + a:1 + d + b], op=ALU.add),
                       [ppn, ("xs", "x")], [("xs", "x")])
                yield
            for h in range(2):
                pp, ppn = ps[h], f"ps{h}"
                P.op('pe', lambda e: e.matmul(pp[:, :], lhsT=tz[:, g, :], rhs=u[:, h * 512:(h + 1) * 512], start=True, stop=False),
                     r=[("tz", g), un], w=[ppn])
                P.op('pe', lambda e: e.matmul(pp[:, :], lhsT=ccm[:, g].rearrange("p s h -> p (s h)"), rhs=xs[:, h * 512:(h + 1) * 512],
                                              start=False, stop=True), r=["ccm", ("xs", "x"), ("xs", "z")], w=[ppn])
                emit_gelu(P, yo[h][:], pp[:, :], ga[:], gb[:], (f"yo{h}", "ga", "gb"), [ppn])
                P.dma(y5_d[g, :, h * 512:(h + 1) * 512], yo[h][:], r=[f"yo{h}"], w=[("y5", (g, h))], sem=f"y5s{h}")
                yield


    lbl = T("lbl", [64, 2]); lb = T("lb", [64, 1]); oml = T("oml", [64, 1]); gain = T("gain", [64, 64]); cm = T("cm", [64, 512])
    P.dma(lbl[:], lbl_d[:, :], w=["lbl"]); P.dma(gain[:], gain_d[:, :], w=["gain"]); P.dma(cm[:], cm_d[:, :], w=["cm"])
    if layer == 0:
        P.op('pool', lambda e: e.memset(lb[:], 0.0), w=["lb"])
    else:
        dv(lambda e: e.tensor_tensor(out=lb[:], in0=lbl[:, 1:2], in1=lbl[:, 0:1], op=ALU.subtract), ["lbl"], ["lb"])
        ac(lambda e: e.activation(out=lb[:], in_=lb[:], func=AF.Sigmoid), ["lb"], ["lb"])
    dv(lambda e: e.tensor_scalar(out=oml[:], in0=lb[:], scalar1=-1.0, scalar2=1.0, op0=ALU.mult, op1=ALU.add), ["lb"], ["oml"])
    rmask = T("rmask", [64, HB])
    P.op('pool', lambda e: e.memset(rmask[:], 1.0), w=["rmask"])
    P.op('pool', lambda e: e.memset(rmask[:, 0:HB:64], 0.0), w=["rmask"])
    hq = T("hq", [64, HB]); hf = T("hf", [64, HB]); sig = T("sig", [64, HB]); fbuf = T("fbuf", [64, HB]); bb = T("bb", [64, HB])
    eb = T("eb", [64, HB]); enb = T("enb", [64, HB]); qtil = T("qtil", [64, HB], BF16); ktil = T("ktil", [64, HB])
    ktb = T("ktb", [64, HB], BF16); khat = T("khat", [64, HB]); dec = T("dec", [64, 32])
    khT = T("khT", [128, 16, 64], BF16); v64f = T("v64f", [64, 32, 64]); v64 = T("v64", [64, 32, 64], BF16)
    v128f = T("v128f", [128, 16, 64]); v128 = T("v128", [128, 16, 64], BF16)
    g64 = T("g64", [64, 32, 64]); sall = T("sall", [64, 33, 64]); sbf = T("sbf", [64, 32, 64], BF16)
    attm = T("attm", [64, 512], BF16); osb = T("osb", [64, 8, 64]); osq = T("osq", [64, 8, 64]); ss = T("ss", [64, 8])
    yh = [T(f"yh{i}", [64, 8, 64]) for i in range(2)]
    P.op('pool', lambda e: e.memset(sall[:, 0, :], 0.0), w=[("sall", 0)])
    def hg_gen():
        for blk in range(SEQ // HB):
            tsl = slice(blk * HB, (blk + 1) * HB)
            P.dma(hq[:], hq_d[:, tsl], w=["hq"]); P.dma(hf[:], hf_d[:, tsl], w=["hf"])
            P.dma(v64f[:], hv64_d[blk], w=["v64f"])
            P.dma(v128f[:], hv128_d[blk], w=["v128f"])
            P.dma(g64[:], hg64_d[blk], w=["g64"])
            P.op('pool', lambda e: e.tensor_copy(out=v64[:], in_=v64f[:]), r=["v64f"], w=["v64"])
            P.op('pool', lambda e: e.tensor_copy(out=v128[:], in_=v128f[:]), r=["v128f"], w=["v128"])
            ac(lambda e: e.activation(out=sig[:], in_=hf[:], func=AF.Sigmoid), ["hf"], ["sig"])
            dv(lambda e: e.tensor_scalar(out=fbuf[:], in0=sig[:], scalar1=oml[:, 0:1], scalar2=lb[:, 0:1], op0=ALU.mult, op1=ALU.add),
               ["sig", "oml", "lb"], ["fbuf"])
            ac(lambda e: e.activation(out=fbuf[:], in_=fbuf[:], func=AF.Ln), ["fbuf"], ["fbuf"])
            dv(lambda e: e.tensor_tensor_scan(out=bb[:], data0=rmask[:], data1=fbuf[:], initial=0.0, op0=ALU.mult, op1=ALU.add),
               ["rmask", "fbuf"], ["bb"])
            ac(lambda e: e.activation(out=eb[:], in_=bb[:], func=AF.Exp), ["bb"], ["eb"])
            ac(lambda e: e.activation(out=enb[:], in_=bb[:], func=AF.Exp, scale=-1.0), ["bb"], ["enb"])
            ac(lambda e: e.activation(out=sig[:], in_=hf[:], func=AF.Sigmoid, scale=-1.0), ["hf"], ["sig"])
            dv(lambda e: e.scalar_tensor_tensor(out=ktil[:], in0=sig[:], scalar=oml[:, 0:1], in1=enb[:], op0=ALU.mult, op1=ALU.mult),
               ["sig", "oml", "enb"], ["ktil"])
            P.op('pool', lambda e: e.tensor_copy(out=ktb[:], in_=ktil[:]), r=["ktil"], w=["ktb"])
            ac(lambda e: e.activation(out=hq[:], in_=hq[:], func=AF.Silu), ["hq"], ["hq"])
            dv(lambda e: e.tensor_tensor(out=qtil[:], in0=hq[:], in1=eb[:], op=ALU.mult), ["hq", "eb"], ["qtil"])
            dv(lambda e: e.tensor_copy(out=dec[:], in_=eb[:, 63:HB:64]), ["eb"], ["dec"])
            dv(lambda e: e.tensor_tensor(out=khat[:].rearrange("p (c s) -> p c s", s=64), in0=ktil[:].rearrange("p (c s) -> p c s", s=64),
                                         in1=dec[:, :, None].to_broadcast([64, 32, 64]), op=ALU.mult), ["ktil", "dec"], ["khat"])
            ac(lambda e: e.activation(out=g64[:], in_=g64[:], func=AF.Silu), ["g64"], ["g64"])
            yield
            for hlf in range(2):
                pp, ppn = ps[6 + hlf], f"ps{6 + hlf}"
                for j in range(8):
                    jj = hlf * 8 + j
                    P.op('pe', lambda e: e.transpose(out=pp[:, j * 64:(j + 1) * 64], in_=khat[:, jj * 128:(jj + 1) * 128],
                                                     identity=i128[0:64, 0:64]), r=["khat", "i128"], w=[ppn])
                ac(lambda e: e.activation(out=khT[:, hlf * 8:(hlf + 1) * 8, :].rearrange("p a b -> p (a b)"), in_=pp[:, :], func=AF.Copy),
                   [ppn], ["khT"])
                yield
            for cg in range(4):
                for ci in range(8):
                    c = cg * 8 + ci
                    po = (c % 2) * 64
                    pu, pun = ps[2 + c % 2], f"ps{2 + c % 2}"
                    sl_ = slice((ci // 2) * 64, (ci // 2 + 1) * 64)
                    P.op('pe', lambda e: e.matmul(pu[0:64, sl_], lhsT=khT[po:po + 64, c // 2, :],
                                                  rhs=v128[po:po + 64, c // 2, :], start=True, stop=True),
                         r=["khT", "v128"], w=[pun])
                for ci in range(8):
                    c = cg * 8 + ci
                    pu, pun = ps[2 + c % 2], f"ps{2 + c % 2}"
                    sl_ = slice((ci // 2) * 64, (ci // 2 + 1) * 64)
                    dv(lambda e: e.scalar_tensor_tensor(out=sall[:, c + 1, :], in0=sall[:, c, :], scalar=dec[:, c:c + 1],
                                                        in1=pu[0:64, sl_], op0=ALU.mult, op1=ALU.add),
                       [("sall", c), "dec", pun], [("sall", c + 1)])
                    if ci % 2 == 1:
                        yield
            P.op('pool', lambda e: e.tensor_copy(out=sbf[:], in_=sall[:, 0:32, :]), r=["sall"], w=["sbf"])
            for cg in range(4):
                pa, pan = ps[4], "ps4"
                po_, pon = ps[5], "ps5"
                for ci in range(8):
                    c = cg * 8 + ci
                    cs = slice(c * 64, (c + 1) * 64)
                    P.op('pe', lambda e: e.matmul(pa[0:64, ci * 64:(ci + 1) * 64], lhsT=ktb[:, cs], rhs=qtil[:, cs], start=True, stop=True),
                         r=["ktb", "qtil"], w=[pan])
                dv(lambda e: e.tensor_tensor(out=attm[:], in0=pa[0:64, :], in1=cm[:], op=ALU.mult), [pan, "cm"], ["attm"])
                for ci in range(8):
                    c = cg * 8 + ci
                    cs = slice(c * 64, (c + 1) * 64)
                    P.op('pe', lambda e: e.matmul(po_[0:64, ci * 64:(ci + 1) * 64], lhsT=attm[:, ci * 64:(ci + 1) * 64], rhs=v64[:, c, :],
                                                  start=True, stop=False), r=["attm", "v64"], w=[pon])
                    P.op('pe', lambda e: e.matmul(po_[0:64, ci * 64:(ci + 1) * 64], lhsT=qtil[:, cs], rhs=sbf[:, c, :],
                                                  start=False, stop=True), r=["qtil", "sbf"], w=[pon])
                ac(lambda e: e.activation(out=osb[:].rearrange("p a b -> p (a b)"), in_=po_[0:64, :], func=AF.Copy), [pon], ["osb"])
                P.op('pool', lambda e: e.tensor_tensor(out=osq[:], in0=osb[:], in1=osb[:], op=ALU.mult), r=["osb"], w=["osq"])
                dv(lambda e: e.tensor_reduce(out=ss[:], in_=osq[:], axis=AX.X, op=ALU.add), ["osq"], ["ss"])
                ac(lambda e: e.activation(out=ss[:], in_=ss[:], func=AF.Sqrt, scale=1.0 / 64, bias=EPS), ["ss"], ["ss"])
                dv(lambda e: e.reciprocal(out=ss[:], in_=ss[:]), ["ss"], ["ss"])
                y_ = yh[cg % 2]; yn = f"yh{cg % 2}"
                dv(lambda e: e.tensor_tensor(out=y_[:], in0=osb[:], in1=ss[:, :, None].to_broadcast([64, 8, 64]), op=ALU.mult),
                   ["osb", "ss"], [yn])
                dv(lambda e: e.tensor_tensor(out=y_[:], in0=y_[:], in1=gain[:, None, :].to_broadcast([64, 8, 64]), op=ALU.mult),
                   [yn, "gain"], [yn])
                dv(lambda e: e.tensor_tensor(out=y_[:], in0=y_[:], in1=g64[:, cg * 8:(cg + 1) * 8, :], op=ALU.mult), [yn, "g64"], [yn])
                c0 = blk * 32 + cg * 8
                P.dma(yh_d[:, c0:c0 + 8, :], y_[:], r=[yn], w=[("yhd", c0)], sem=f"yhs{cg % 2}")
                yield
            dv(lambda e: e.tensor_copy(out=sall[:, 0, :], in_=sall[:, 32, :]), [("sall", 32), "sbf"], [("sall", 0)])

    threads = []
    if 's5' in PARTS:
        threads.append(s5_gen())
    if 'hg' in PARTS:
        threads.append(hg_gen())
    while threads:
        for t_ in list(threads):
            try:
                next(t_)
            except StopIteration:
                threads.remove(t_)
    return P.finish()


def _m1_consts():
    jv1 = np.array([7, 6, 5, 4, 3, 2, 1, 0, 1, 2, 3, 4, 5, 6, 7, 8, -7, -6, -5, -4, -3, -2, -1, 0], np.float32)
    jv = np.ascontiguousarray(np.broadcast_to(jv1, (128, 4, 24))).astype(np.float32)
    sgn = np.ones((128, 2), np.float32); sgn[64:, 0] = -1; sgn[:64, 1] = -1
    i128 = np.eye(128, dtype=np.float32)
    jsw = np.zeros((128, 128), np.float32)
    for k in range(128):
        jsw[k, (k + 64) % 128] = 1
    s_idx = np.arange(128) // 16
    tmask = (s_idx[None, :] >= s_idx[:, None]).astype(np.float32)
    st = np.arange(64)
    cm = np.tile((st[:, None] <= st[None, :]).astype(np.float32), (1, 8))
    return dict(jv=jv, sgn=sgn, i128=i128, jsw=jsw, tmask=tmask, cmask=cm)


def run_M1(zT, inp, l):
    nc = build_M1(l)
    cst = _m1_consts()
    maps = []
    for c in range(NCORES):
        b = c // 4
        tsl = slice(b * SEQ, (b + 1) * SEQ)
        m = dict(cst)
        gs = [4 * (c % 4) + gi for gi in range(4)]
        st2 = lambda a: np.concatenate([a, a], axis=0)
        m["lr2"] = np.stack([st2(inp['s5_lambda_re'][l, g]) for g in gs], 1).astype(np.float32)
        m["li2"] = np.stack([st2(inp['s5_lambda_im'][l, g]) for g in gs], 1).astype(np.float32)
        m["ldt"] = np.ascontiguousarray(np.broadcast_to(np.array([inp['s5_log_dt'][l, g] for g in gs], np.float32), (128, 4)))
        m["bst1"] = np.stack([np.concatenate([inp['s5_b_re'][l, g], inp['s5_b_im'][l, g]], 0) for g in gs], 1)
        m["bst2"] = np.stack([np.concatenate([inp['s5_b_im'][l, g], inp['s5_b_re'][l, g]], 0) for g in gs], 1)
        m["cst1"] = np.stack([np.concatenate([inp['s5_c_re'][l, g].T, inp['s5_c_im'][l, g].T], 0) for g in gs], 1)
        m["cst2"] = np.stack([np.concatenate([inp['s5_c_im'][l, g].T, inp['s5_c_re'][l, g].T], 0) for g in gs], 1)
        m["dcol"] = np.stack([np.tile(inp['s5_d'][l, g], 8) for g in gs], 1).astype(np.float32)
        m["uc"] = np.stack([zT[g * 16:(g + 1) * 16, tsl].reshape(16, NCH, 8).transpose(2, 0, 1).reshape(128, NCH) for g in gs], 0)
        h = c % 4
        m["hq"] = zT[256 + 64 * h:256 + 64 * (h + 1), tsl]
        m["hf"] = zT[512 + 64 * h:512 + 64 * (h + 1), tsl]
        v = zT[768 + 64 * h:768 + 64 * (h + 1), tsl].T
        gg = zT[1024 + 64 * h:1024 + 64 * (h + 1), tsl].T
        m["hv64"] = v.reshape(4, 32, 64, 64).transpose(0, 2, 1, 3)
        m["hv128"] = v.reshape(4, 16, 128, 64).transpose(0, 2, 1, 3)
        m["hg64"] = gg.reshape(4, 32, 64, 64).transpose(0, 2, 1, 3)
        m["lbl"] = inp['hgrn_lb_logits'][:, 64 * h:64 * (h + 1)].T
        m["hgain"] = np.broadcast_to(inp['hgrn_norm'][l][None, :], (64, 64))
        maps.append({k: np.ascontiguousarray(v_, dtype=np.float32) for k, v_ in m.items()})
    res = run_bass_kernel_spmd(nc, maps, core_ids=list(range(NCORES)))
    y5T = np.zeros((256, BATCH * SEQ), np.float32)
    yhT = np.zeros((256, BATCH * SEQ), np.float32)
    for c in range(NCORES):
        b = c // 4
        tsl = slice(b * SEQ, (b + 1) * SEQ)
        r = res.results[c]
        for gi in range(4):
            g = 4 * (c % 4) + gi
            y5T[g * 16:(g + 1) * 16, tsl] = r["y5"][gi].reshape(8, 16, NCH).transpose(1, 2, 0).reshape(16, SEQ)
        h = c % 4
        yhT[64 * h:64 * (h + 1), tsl] = r["yh"].transpose(1, 0, 2).reshape(SEQ, 64).T
    return y5T, yhT


OFF_F = 8320
NF_F = OFF_F + 8192 + 128
NEGB = -30000.0
TINY = 1e-30
NQT = 32
SCALE = 0.125


def build_M2():
    P = Prog()
    nc = P.nc
    q65_d = P.dram_in("q65", [4, 65, NQT * 128])
    ksl_d = P.dram_in("kslX", [64, SEQ]); kwn_d = P.dram_in("kwnX", [64, SEQ])
    vsl_d = P.dram_in("vslX", [128, 64 * 64]); vwn_d = P.dram_in("vwnX", [128, 64 * 64])
    kcr_d = P.dram_in("kcrX", [32, 64, 512]); vcr_d = P.dram_in("vcrX", [32, 64, 512])
    w1k_d = P.dram_in("w1k", [64, 32 * 64]); w1v_d = P.dram_in("w1v", [64, 32 * 64])
    w2k_d = P.dram_in("w2k", [64, 64]); w2v_d = P.dram_in("w2v", [64, 64])
    posk_d = P.dram_in("poskT", [64, 32]); posv_d = P.dram_in("posvT", [64, 32])
    fs_d = P.dram_in("fs", [4, NF_F]); fw_d = P.dram_in("fw", [4, NF_F])
    eall_d = P.dram_in("eall", [128, SEQ]); mm_d = P.dram_in("mmat", [128, 4 * 128])
    madd_d = P.dram_in("madd", [NQT, 128, 128]); glog_d = P.dram_in("glog", [128, NQT * 12])
    i128_d = P.dram_in("i128", [128, 128])
    parity_dummy = None
    y_d = P.dram_out("ynsa", [128, NQT, 256])

    T = P.sbuf
    ps = [P.psum(f"ps{i}", [128, 512]) for i in range(8)]
    psA = [(ps[0], "ps0"), (ps[1], "ps1"), (ps[3], "psB0"), (ps[7], "psB1")]
    psL, psO1, psO2, psO3 = ps[2], ps[4], ps[5], ps[6]
    psS = ps[4]
    stgall = T("stgall", [128, 4096], F32)
    stg = [stgall[:, 0:2048], stgall[:, 2048:4096]]
    sctr = [0]

    def stage():
        i = sctr[0] % 2
        sctr[0] += 1
        return stg[i], f"stg{i}"

    i128 = T("i128", [128, 128], F32); P.dma(i128[:], i128_d[:, :], w=["i128"])
    ones = T("ones", [128, 128], BF16); P.op('pool', lambda e: e.memset(ones[:], 1.0), w=["ones"])
    qa = T("qa", [65, 4, NQT * 128], BF16)
    ksl = T("ksl", [65, SEQ], BF16); kwn = T("kwn", [64, SEQ], BF16)
    vsl = T("vsl", [128, 64, 65], BF16); vwn = T("vwn", [128, 64, 65], BF16)
    eall = T("eall", [128, SEQ], BF16); mmt = T("mmt", [128, 4, 128], F32)
    kcmp = T("kcmp", [64, 512], BF16); vcmp = T("vcmp", [128, 4, 65], BF16)
    bs = T("bs", [128, 11, 512], F32)
    glog = T("glog", [128, NQT * 12], F32)

    b31t = T("b31t", [65, 4], F32)

    def load_q(hh):
        for r in range(4):
            P.dma(qa[0:64, r, hh * 2048:(hh + 1) * 2048], q65_d[r, 0:64, hh * 2048:(hh + 1) * 2048], w=[("qa", hh)], q='pool', sem=f"qa{hh}")
            P.op('act', lambda e: e.activation(out=qa[64:65, r, hh * 2048:(hh + 1) * 2048],
                                               in_=b31t[64:65, r:r + 1].to_broadcast([1, 2048]), func=AF.Copy, scale=8.0),
                 r=["b31t"], w=[("qa", hh)])

    def load_piece(p):
        cs = slice(p * 2048, (p + 1) * 2048)
        P.dma(ksl[0:64, cs], ksl_d[:, cs], w=[("ksl", p)], q='pool', sem=f"ksl{p}")
        P.op('pool', lambda e: e.memset(ksl[64:65, cs], 1.0), w=[("ksl", p)])
        P.dma(kwn[0:64, cs], kwn_d[:, cs], w=[("kwn", p)], q='pool', sem=f"kwn{p}")
        for (dst, dn, src) in [(vsl, "vsl", vsl_d), (vwn, "vwn", vwn_d)]:
            P.dma(dst[:, p * 16:(p + 1) * 16, 0:64], src[:, p * 1024:(p + 1) * 1024].rearrange("p (a b) -> p a b", b=64),
                  w=[(dn, p)], q='pool', sem=f"{dn}{p}")
            P.op('pool', lambda e: e.memset(dst[:, p * 16:(p + 1) * 16, 64:65], 1.0), w=[(dn, p)])
        P.dma(eall[:, cs], eall_d[:, cs], w=[("eall", p)], q='pool', sem=f"eall{p}")

    for r in range(4):
        P.dma(b31t[64:65, r:r + 1], q65_d[r, 64:65, 0:1], w=["b31t"])
    load_q(0)
    load_piece(0)
    P.dma(mmt[:].rearrange("p a b -> p (a b)"), mm_d[:, :], w=["mmt"])
    P.dma(glog[:], glog_d[:, :], w=["glog"])
    P.op('act', lambda e: e.activation(out=glog[:], in_=glog[:], func=AF.Sigmoid), r=["glog"], w=["glog"])
    for j in range(11):
        tab = fs_d if j < 9 else fw_d
        dl = 128 * j if j < 9 else 128 * (j - 5)
        src = bass.AP(tab.tensor, OFF_F + dl - 127, [(1, 128), (NF_F, 4), (1, 128)])
        P.dma(bs[:, j, :].rearrange("p (r q) -> p r q", q=128), src, w=[("bs", j)], sem="bs")

    w1 = T("w1", [64, 32 * 64], BF16); w2 = T("w2", [64, 64], BF16); posT = T("posT", [64, 32], BF16)
    pb = T("pb", [64, 1], F32); xj = [T(f"xj{i}", [64, 512], BF16) for i in range(2)]
    ga = T("ga", [64, 512], F32); gb = T("gb", [64, 512], F32); gel = T("gel", [64, 512], BF16)
    for which in range(2):
        w1_d, w2_d, pos_d, x_d = [(w1k_d, w2k_d, posk_d, kcr_d), (w1v_d, w2v_d, posv_d, vcr_d)][which]
        P.dma(w1[:], w1_d[:, :], w=["w1"], q='pool')
        P.dma(w2[:], w2_d[:, :], w=["w2"], q='pool')
        P.dma(posT[:], pos_d[:, :], w=["posT"], q='pool')
        for j in range(32):
            P.op('pe', lambda e: e.matmul(psL[0:64, 0:1], lhsT=w1[:, j * 64:(j + 1) * 64], rhs=posT[:, j:j + 1], start=(j == 0), stop=(j == 31)),
                 r=["w1", "posT"], w=["psL"])
        P.op('act', lambda e: e.activation(out=pb[:], in_=psL[0:64, 0:1], func=AF.Copy), r=["psL"], w=["pb"])
        for j in range(32):
            P.dma(xj[j % 2][:], x_d[j], w=[f"xj{j % 2}"], q='pool')
            P.op('pe', lambda e: e.matmul(ps[3][0:64, :], lhsT=w1[:, j * 64:(j + 1) * 64], rhs=xj[j % 2][:], start=(j == 0), stop=(j == 31)),
                 r=["w1", f"xj{j % 2}"], w=["psB0"])
        emit_gelu(P, gel[:], ps[3][0:64, :], ga[:], gb[:], ("gel", "ga", "gb"), ["psB0"], bias=(pb[:, 0:1], "pb"))
        if which == 0:
            P.op('pe', lambda e: e.matmul(psO1[0:64, :], lhsT=w2[:], rhs=gel[:], start=True, stop=True), r=["w2", "gel"], w=["psO1"])
            P.op('act', lambda e: e.activation(out=kcmp[:], in_=psO1[0:64, :], func=AF.Copy), r=["psO1"], w=["kcmp"])
        else:
            for ci in range(4):
                P.op('pe', lambda e: e.matmul(psO2[:, ci * 64:(ci + 1) * 64], lhsT=gel[:, ci * 128:(ci + 1) * 128], rhs=w2[:], start=True, stop=True),
                     r=["w2", "gel"], w=["psO2"])
            P.op('act', lambda e: e.activation(out=vcmp[:, :, 0:64], in_=psO2[:, 0:256].rearrange("p (a b) -> p a b", b=64), func=AF.Copy),
                 r=["psO2"], w=["vcmp"])
            P.op('pool', lambda e: e.memset(vcmp[:, :, 64:65], 1.0), w=["vcmp"])

    bc = [T(f"bc{i}", [128, 512], F32) for i in range(2)]
    tmpb = [T(f"tmpb{i}", [128, 512], F32) for i in range(2)]
    ebuf = [T(f"ebuf{i}", [128, 512], BF16) for i in range(4)]
    pbuf = [T(f"pbuf{i}", [128, 512], BF16) for i in range(4)]
    maskall = T("maskall", [128, 64, 128], BF16)
    ec = T("ec", [128, 4, 512], BF16)
    rlb = T("rlb", [128, 512], F32); pnb = T("pnb", [128, 512], F32); impT = T("impT", [128, 4, 128], F32)
    madd = [T(f"madd{i}", [128, 128], F32) for i in range(2)]
    score = T("score", [128, 128], F32); sc2 = T("sc2", [128, 128], F32); m8a = T("m8a", [128, 8], F32); m8b = T("m8b", [128, 8], F32)
    self_ = T("self", [128, 128], F32); selT = T("selT", [128, 128], BF16)
    osb = [T(f"osb{i}", [128, 260], F32) for i in range(3)]
    lc = T("lc", [128, 3, 4], F32); coef = T("coef", [128, 3, 4], F32)
    yb = [T(f"yb{i}", [128, 256], F32) for i in range(2)]
    rot = dict(a=0, t=0, e=0, p=0, b=0)

    NROT = dict(a=4, t=2, e=4, p=4, b=2)

    def nxt(k):
        n = NROT[k]
        v = rot[k] % n
        rot[k] += 1
        return v

    def softmax_tile(A_src, An, bias_ap, bias_names, eout, eout_name):
        if bias_ap is None:
            P.op('act', lambda e: e.activation(out=eout, in_=A_src, func=AF.Exp, scale=SCALE), r=[An], w=[eout_name])
        else:
            ti = nxt('t')
            P.op('dve', lambda e: e.scalar_tensor_tensor(out=tmpb[ti][:], in0=A_src, scalar=SCALE, in1=bias_ap, op0=ALU.mult, op1=ALU.add),
                 r=[An] + bias_names, w=[f"tmpb{ti}"])
            P.op('act', lambda e: e.activation(out=eout, in_=tmpb[ti][:], func=AF.Exp), r=[f"tmpb{ti}"], w=[eout_name])

    def pv(psO, psOn, lhs_tile, lhs_name, v_ap, v_name, first, last=False):
        for r in range(4):
            P.op('pe', lambda e: e.matmul(psO[:, r * 65:(r + 1) * 65], lhsT=lhs_tile[:, r * 128:(r + 1) * 128], rhs=v_ap,
                                          start=(first and r == 0), stop=(last and r == 3)), r=[lhs_name, v_name], w=[psOn])

    import os
    NT_RUN = int(os.environ.get("M2_NT", NQT))
    scr = T("scr", [128, 1], F32)
    P.op('pool', lambda e: e.memset(scr[:], 0.0), w=["maskall1", "scr"])
    maskall2 = [maskall, stgall[:, :].bitcast(BF16).rearrange("p (a b) -> p a b", b=128)]
    osb0 = [osb[0], T("osb0b", [128, 260], F32), T("osb0c", [128, 260], F32)]
    osb2 = [osb[2], T("osb2b", [128, 260], F32), T("osb2c", [128, 260], F32)]
    selT3 = [selT, T("selTb", [128, 128], BF16), T("selTc", [128, 128], BF16)]

    def pairs_gen(pairs, depth):
        pend = []
        for (A_, B_) in pairs:
            pend.append((B_, A_()))
            if len(pend) > depth:
                b_, c_ = pend.pop(0)
                b_(c_)
            yield
        for b_, c_ in pend:
            b_(c_)
            yield

    def phaseX(m):
        qi = 2 * m + 1
        t0 = 128 * qi
        q64 = qa[0:64, :, m * 128:(m + 1) * 128]
        nck = min(4, ((t0 + 96) // 16) // 128 + 1)
        selTm = selT3[m % 3]; selTn = f"selT{m % 3}"

        def cmpA(ci):
            def f():
                bi = nxt('b')
                src = bass.AP(fs_d.tensor, OFF_F + t0 - 2048 * ci - 2063, [(16, 128), (NF_F, 4), (1, 128)])
                P.dma(bc[bi][:].rearrange("p (r q) -> p r q", q=128), src, w=[f"bc{bi}"])
                ai = nxt('a'); A, An = psA[ai]
                P.op('pe', lambda e: e.matmul(A[:, :], lhsT=kcmp[:, ci * 128:(ci + 1) * 128], rhs=q64, start=True, stop=True),
                     r=["kcmp", ("qa", m // 16)], w=[An])
                softmax_tile(A[:, :], An, bc[bi][:], [f"bc{bi}"], ec[:, ci, :], ("ec", ci))
                return ci
            return f

        def cmpB(ci):
            P.op('pe', lambda e: e.matmul(psL[:, :], lhsT=ones[:], rhs=ec[:, ci, :], start=(ci == 0), stop=(ci == nck - 1)),
                 r=["ones", ("ec", ci)], w=["psL"])
            pv(psO1, "psO1", ec[:, ci, :], ("ec", ci), vcmp[:, ci, :], "vcmp", ci == 0, ci == nck - 1)
        yield from pairs_gen([(cmpA(ci), cmpB) for ci in range(nck)], 2)
        P.op('dve', lambda e: e.tensor_scalar(out=rlb[:], in0=psL[:, :], scalar1=TINY, scalar2=None, op0=ALU.max), r=["psL"], w=["rlb"])
        P.op('dve', lambda e: e.reciprocal(out=rlb[:], in_=rlb[:]), r=["rlb"], w=["rlb"])
        yield
        for ci in range(nck):
            P.op('dve', lambda e: e.tensor_tensor(out=pnb[:], in0=ec[:, ci, :], in1=rlb[:], op=ALU.mult), r=[("ec", ci), "rlb"], w=["pnb"])
            P.op('dve', lambda e: e.tensor_reduce(out=impT[:, ci, :], in_=pnb[:].rearrange("p (r q) -> p q r", q=128), axis=AX.X, op=ALU.add),
                 r=["pnb"], w=[("impT", ci)])
            yield
        for ci in range(nck):
            P.op('pe', lambda e: e.matmul(psS[:, 260:388], lhsT=impT[:, ci, :], rhs=mmt[:, ci, :], start=False, stop=(ci == nck - 1)),
                 r=[("impT", ci), "mmt"], w=["psO1"])
        P.op('act', lambda e: e.activation(out=osb0[m % 3][:], in_=psO1[:, 0:260], func=AF.Copy), r=["psO1"], w=[f"osb0_{m % 3}"])
        yield
        mi = m % 2
        P.dma(madd[mi][:], madd_d[m], w=[f"madd{mi}"])
        P.op('dve', lambda e: e.tensor_tensor(out=score[:], in0=psS[:, 260:388], in1=madd[mi][:], op=ALU.add), r=["psO1", f"madd{mi}"], w=["score"])
        P.op('dve', lambda e: e.max(out=m8a[:], in_=score[:]), r=["score"], w=["m8a"])
        yield
        P.op('dve', lambda e: e.match_replace(out=sc2[:], in_to_replace=m8a[:], in_values=score[:], imm_value=-3e38), r=["score", "m8a"], w=["sc2"])
        P.op('dve', lambda e: e.max(out=m8b[:], in_=sc2[:]), r=["sc2"], w=["m8b"])
        yield
        P.op('dve', lambda e: e.tensor_scalar(out=self_[:], in0=score[:], scalar1=m8b[:, 7:8], scalar2=None, op0=ALU.is_ge),
             r=["score", "m8b"], w=["self"])
        P.op('pe', lambda e: e.transpose(out=psL[:, 0:128], in_=self_[:], identity=i128[:]), r=["self", "i128"], w=["psL"])
        P.op('act', lambda e: e.activation(out=selTm[:], in_=psL[:, 0:128], func=AF.Copy), r=["psL"], w=[selTn])
        yield
        kts = list(range(max(0, qi - 5), qi + 1))

        def winA(kt):
            def f():
                j = qi - kt
                ai = nxt('a'); A, An = psA[ai]
                P.op('pe', lambda e: e.matmul(A[:, :], lhsT=kwn[0:64, kt * 128:(kt + 1) * 128], rhs=q64, start=True, stop=True),
                     r=[("kwn", kt // 16), ("qa", m // 16)], w=[An])
                ei = nxt('e')
                jj = j if j < 4 else 5 + j
                softmax_tile(A[:, :], An, bs[:, jj, :], [("bs", jj)], ebuf[ei][:], f"ebuf{ei}")
                return (kt, ei)
            return f

        def winB(ctx):
            kt, ei = ctx
            pv(psO3, "psO3", ebuf[ei], f"ebuf{ei}", vwn[:, kt, :], ("vwn", kt // 16), kt == kts[0], kt == kts[-1])
        yield from pairs_gen([(winA(kt), winB) for kt in kts], 2)
        P.op('act', lambda e: e.activation(out=osb2[m % 3][:], in_=psO3[:, 0:260], func=AF.Copy), r=["psO3"], w=[f"osb2_{m % 3}"])
        yield

    def phaseX2(m):
        qi = 2 * m + 1
        mk = maskall2[m % 2]; mkn = f"maskall{m % 2}"
        selTm = selT3[m % 3]; selTn = f"selT{m % 3}"
        for k0 in range(0, qi + 1, 4):
            n4 = min(4, qi + 1 - k0)
            ai = nxt('a'); A, An = psA[ai]
            for u in range(n4):
                P.op('pe', lambda e: e.matmul(A[:, u * 128:(u + 1) * 128], lhsT=eall[:, (k0 + u) * 128:(k0 + u + 1) * 128], rhs=selTm[:],
                                              start=(u == 0), stop=(u == n4 - 1)), r=[("eall", (k0 + u) // 16), selTn], w=[An])
            P.op('act', lambda e: e.activation(out=mk[:, k0:k0 + n4, :].rearrange("p a b -> p (a b)"), in_=A[:, 0:n4 * 128], func=AF.Copy),
                 r=[An], w=[(mkn, k0)])
            yield

    def phaseY(m):
        qi = 2 * m + 1
        q64 = qa[0:64, :, m * 128:(m + 1) * 128]
        q65 = qa[0:65, :, m * 128:(m + 1) * 128]
        mk = maskall2[m % 2]; mkn = f"maskall{m % 2}"

        def selA(kt):
            def f():
                near = (qi - kt) <= 8
                ai = nxt('a'); A, An = psA[ai]
                if near:
                    P.op('pe', lambda e: e.matmul(A[:, :], lhsT=ksl[0:64, kt * 128:(kt + 1) * 128], rhs=q64, start=True, stop=True),
                         r=[("ksl", kt // 16), ("qa", m // 16)], w=[An])
                else:
                    P.op('pe', lambda e: e.matmul(A[:, :], lhsT=ksl[0:65, kt * 128:(kt + 1) * 128], rhs=q65, start=True, stop=True),
                         r=[("ksl", kt // 16), ("qa", m // 16)], w=[An])
                ei = nxt('e')
                if near:
                    softmax_tile(A[:, :], An, bs[:, qi - kt, :], [("bs", qi - kt)], ebuf[ei][:], f"ebuf{ei}")
                else:
                    softmax_tile(A[:, :], An, None, None, ebuf[ei][:], f"ebuf{ei}")
                bi = nxt('p')
                P.op('pool' if (kt % 5) < 2 else 'dve', lambda e: e.tensor_tensor(out=pbuf[bi][:].rearrange("p (r q) -> p r q", q=128),
                                                       in0=ebuf[ei][:].rearrange("p (r q) -> p r q", q=128),
                                                       in1=mk[:, kt:kt + 1, :].to_broadcast([128, 4, 128]), op=ALU.mult),
                     r=[f"ebuf{ei}", (mkn, (kt // 4) * 4)], w=[f"pbuf{bi}"])
                return (kt, bi)
            return f

        def selB(ctx):
            kt, bi = ctx
            pv(psO2, "psO2", pbuf[bi], f"pbuf{bi}", vsl[:, kt, :], ("vsl", kt // 16), kt == 0, kt == qi)
        yield from pairs_gen([(selA(kt), selB) for kt in range(qi + 1)], 3)
        srcs = [(osb0[m % 3], f"osb0_{m % 3}"), (osb[1], "osb1"), (osb2[m % 3], f"osb2_{m % 3}")]
        P.op('act', lambda e: e.activation(out=osb[1][:], in_=psO2[:, 0:260], func=AF.Copy), r=["psO2"], w=["osb1"])
        for b_, (ot, on) in enumerate(srcs):
            P.op('dve', lambda e: e.tensor_copy(out=lc[:, b_, :], in_=ot[:].rearrange("p (r d) -> p r d", d=65)[:, :, 64]),
                 r=[on], w=["lc"])
        P.op('dve', lambda e: e.tensor_scalar(out=lc[:], in0=lc[:], scalar1=TINY, scalar2=None, op0=ALU.max), r=["lc"], w=["lc"])
        P.op('dve', lambda e: e.reciprocal(out=lc[:], in_=lc[:]), r=["lc"], w=["lc"])
        P.op('dve', lambda e: e.tensor_tensor(out=coef[:], in0=lc[:],
                                              in1=glog[:, m * 12:(m + 1) * 12].rearrange("p (r b) -> p b r", b=3), op=ALU.mult),
             r=["lc", "glog"], w=["coef"])
        yield
        y_ = yb[m % 2]; yn = f"yb{m % 2}"
        for r in range(4):
            for b_, (ot, on) in enumerate(srcs):
                src_ = ot[:, r * 65:r * 65 + 64]
                if b_ == 0:
                    P.op('dve', lambda e: e.tensor_scalar(out=y_[:, r * 64:(r + 1) * 64], in0=src_, scalar1=coef[:, b_, r:r + 1], scalar2=None,
                                                          op0=ALU.mult), r=[on, "coef"], w=[(yn, r)])
                else:
                    P.op('dve', lambda e: e.scalar_tensor_tensor(out=y_[:, r * 64:(r + 1) * 64], in0=src_, scalar=coef[:, b_, r:r + 1],
                                                                 in1=y_[:, r * 64:(r + 1) * 64], op0=ALU.mult, op1=ALU.add),
                         r=[on, "coef", (yn, r)], w=[(yn, r)])
            yield
        P.dma(y_d[:, m, :], y_[:], r=[yn], w=[("yd", m)], sem=f"ys{m % 2}")
        yield

    import itertools
    for g_ in [phaseX(0), phaseX2(0)] + ([phaseX(1)] if NT_RUN > 1 else []):
        for _ in g_:
            pass
    for m in range(NT_RUN):
        for p_ in range(1, 4):
            if m == max(0, 8 * p_ - 6):
                load_piece(p_)
        if m == 8:
            load_q(1)
        gy = phaseY(m)
        gens = []
        if m + 1 < NT_RUN:
            gens.append(phaseX2(m + 1))
        if m + 2 < NT_RUN:
            gens.append(phaseX(m + 2))
        gx = itertools.chain(*gens)
        ny = 2 * m + 2 + 6
        kx = max(1, -(-90 // ny))
        for _ in gy:
            for _ in range(kx):
                next(gx, None)
        for _ in gx:
            pass
    return P.finish()


def _t5_bucket_np(d):
    import math
    n = np.maximum(d, 0)
    nf = np.maximum(n, 16).astype(np.float32)
    large = 16 + (np.log(nf / np.float32(16)) / np.float32(math.log(64.0)) * np.float32(16)).astype(np.int32)
    large = np.minimum(large, 31)
    return np.where(n < 16, n, large)


def _m2_consts():
    eall = np.zeros((128, SEQ), np.float32)
    col = np.arange(SEQ)
    kt = col // 128
    p = col % 128
    eall[2 * kt + (127 - p) // 64, col] = 1.0
    mm = np.zeros((128, 4, 128), np.float32)
    wts = {-1: 1.0, 0: 2.0, 1: 2.0, 2: 2.0, 3: 1.0}
    for ci in range(4):
        for pp in range(128):
            c = 128 * ci + 127 - pp
            for dlt, wv in wts.items():
                if (c - dlt) % 4 == 0:
                    j = (c - dlt) // 4
                    if 0 <= j < 128:
                        mm[pp, ci, j] = wv
    return dict(eall=eall, mmat=mm.reshape(128, 512), i128=np.eye(128, dtype=np.float32))


def run_M2(zT, inp, l):
    nc = build_M2()
    cst = _m2_consts()
    rel = inp['rel_bias'].astype(np.float32)
    didx = np.arange(NF_F) - OFF_F
    bk = _t5_bucket_np(didx)
    maps = []
    for c in range(NCORES):
        b = c // 4
        g = (c % 4) // 2
        par = c % 2
        tsl = slice(b * SEQ, (b + 1) * SEQ)
        m = dict(cst)
        sh = 128 * (1 - par)
        fs = np.full((4, NF_F), NEGB, np.float32)
        fw = np.full((4, NF_F), NEGB, np.float32)
        for r in range(4):
            base_s = np.where(didx >= 0, rel[bk, g * 4 + r], NEGB).astype(np.float32)
            base_w = np.where((didx >= 0) & (didx < 512), rel[bk, g * 4 + r], NEGB).astype(np.float32)
            fs[r, sh:] = base_s[:NF_F - sh]
            fw[r, sh:] = base_w[:NF_F - sh]
        m["fs"] = fs
        m["fw"] = fw
        tiles = 2 * np.arange(NQT) + par
        tok = (tiles[:, None] * 128 + np.arange(128)[None, :]).reshape(-1)
        q65 = np.zeros((4, 65, NQT * 128), np.float32)
        for r in range(4):
            q65[r, 0:64] = zT[1280 + g * 256 + r * 64:1280 + g * 256 + (r + 1) * 64, tsl][:, tok]
            q65[r, 64] = rel[31, g * 4 + r]
        m["q65"] = q65

        def kv(j):
            return zT[1792 + j * 128 + g * 64:1792 + j * 128 + (g + 1) * 64, tsl]
        rev = lambda a: a.reshape(64, 64, 128)[:, :, ::-1].reshape(64, SEQ)
        m["kslX"] = rev(kv(2))
        m["kwnX"] = rev(kv(4))
        vrev = lambda a: a.T.reshape(64, 128, 64)[:, ::-1, :].transpose(1, 0, 2).reshape(128, 64 * 64)
        m["vslX"] = vrev(kv(3))
        m["vwnX"] = vrev(kv(5))
        cc = (128 * (np.arange(512) // 128) + 127 - (np.arange(512) % 128))
        for nm, j in [("kcrX", 0), ("vcrX", 1)]:
            src = np.concatenate([kv(j), np.zeros((64, 64), np.float32)], axis=1)
            arr = np.zeros((32, 64, 512), np.float32)
            for jj in range(32):
                arr[jj] = src[:, np.minimum(16 * cc + jj, SEQ + 63)]
            m[nm] = arr
        for sfx, key in [("k", "k"), ("v", "v")]:
            w1 = inp[f'nsa_cmp_w1_{key}'][l]
            m[f"w1{sfx}"] = w1.reshape(32, 64, 64).transpose(1, 0, 2).reshape(64, 2048)
            m[f"w2{sfx}"] = inp[f'nsa_cmp_w2_{key}'][l]
            m[f"pos{sfx}T"] = inp[f'nsa_cmp_pos_{key}'][l].T
        t = tiles[:, None] * 128 + np.arange(128)[None, :]
        blk = np.arange(128)
        cur = t // 64
        ok = (blk[None, None, :] * 64) <= t[:, :, None]
        forced = (blk[None, None, :] == 0) | (blk[None, None, :] == cur[:, :, None]) | (blk[None, None, :] == cur[:, :, None] - 1)
        m["madd"] = np.where(ok, np.where(forced, 1e4, 0.0), -1e30).astype(np.float32)
        gl = zT[2560 + g * 12:2560 + (g + 1) * 12, tsl][:, tok]
        m["glog"] = gl.T.reshape(NQT, 128, 12).transpose(1, 0, 2).reshape(128, NQT * 12)
        maps.append({k: np.ascontiguousarray(v_, dtype=np.float32) for k, v_ in m.items()})
    res = run_bass_kernel_spmd(nc, maps, core_ids=list(range(NCORES)))
    yT = np.zeros((512, BATCH * SEQ), np.float32)
    for c in range(NCORES):
        b = c // 4
        g = (c % 4) // 2
        par = c % 2
        tiles = 2 * np.arange(NQT) + par
        tok = b * SEQ + (tiles[:, None] * 128 + np.arange(128)[None, :]).reshape(-1)
        y = res.results[c]["ynsa"]
        y = y.transpose(1, 0, 2).reshape(NQT * 128, 256)
        yT[g * 256:(g + 1) * 256, tok] = y.T
    return yT


def kernel(**inputs):
    inp = {k: np.asarray(v) for k, v in inputs.items()}
    x = inp['x'].astype(np.float32).reshape(BATCH * SEQ, D_MODEL)
    xT = np.ascontiguousarray(x.T)

    def win_map(l):
        win = np.zeros((D_MODEL, NZC * 128), np.float32)
        win[:, :D_IN] = inp['w_in'][l]
        return {"mix_g": _gcol(inp['mix_norm'][l]), "win": _chunkT(win, 8)}

    def wout_map(l):
        return {"wglu": _chunkT(inp['s5_w_glu'][l], 2), "wout": _chunkT(inp['w_out'][l], 8)}

    def mixers(zT, l):
        y5T, yhT = run_M1(zT, inp, l)
        ynT = run_M2(zT, inp, l)
        return np.concatenate([y5T, yhT, ynT], axis=0)

    common = _ffn_maps("f0", inp['ffn1_norm'][0], inp['ffn1_w_gate'][0], inp['ffn1_w_up'][0], inp['ffn1_w_down'][0])
    common.update(win_map(0))
    o = run_T(xT, common, False, 1, True, False)
    yT = mixers(o["zT"], 0)
    common = _ffn_maps("f0", inp['ffn2_norm'][0], inp['ffn2_w_gate'][0], inp['ffn2_w_up'][0], inp['ffn2_w_down'][0])
    common.update(_ffn_maps("f1", inp['ffn1_norm'][1], inp['ffn1_w_gate'][1], inp['ffn1_w_up'][1], inp['ffn1_w_down'][1]))
    common.update(win_map(1))
    common.update(wout_map(0))
    o = run_T(o["xoT"], common, True, 2, True, False, yT_full=yT)
    yT = mixers(o["zT"], 1)
    common = _ffn_maps("f0", inp['ffn2_norm'][1], inp['ffn2_w_gate'][1], inp['ffn2_w_up'][1], inp['ffn2_w_down'][1])
    common.update(wout_map(1))
    common["fin_g"] = _gcol(inp['final_norm'])
    o = run_T(o["xoT"], common, True, 1, False, True, yT_full=yT)
    out = np.ascontiguousarray(o["xoT"].T).reshape(BATCH, SEQ, D_MODEL).astype(np.float32)
    return out
```

```python
import contextlib
import numpy as np
import concourse.bass as bass
import concourse.mybir as mybir
from concourse.bass_utils import run_bass_kernel_spmd

F32 = mybir.dt.float32
BF16 = mybir.dt.bfloat16
AF = mybir.ActivationFunctionType
ALU = mybir.AluOpType
AX = mybir.AxisListType

D_MODEL = 1024
D_FF = 2816
SEQ = 8192
BATCH = 2
D_IN = 2584
NCORES = 8
EPS = 1e-6


class Prog:
    def __init__(self):
        self.nc = bass.Bass("TRN2", target_bir_lowering=False)
        self.es = contextlib.ExitStack()
        nc = self.nc
        self.eng = {'pe': nc.tensor, 'act': nc.scalar, 'dve': nc.vector, 'pool': nc.gpsimd, 'sp': nc.sync}
        self.sems = {}
        self.cnt = {}
        for k in ['pe', 'act', 'dve', 'pool']:
            self.sems[('e', k)] = self.es.enter_context(nc.semaphore('e_' + k))
            self.cnt[('e', k)] = 0
        self.waited = {k: {} for k in self.eng}
        self.state = {}
        self.nps = 0

    def sbuf(self, name, shape, dt):
        return self.es.enter_context(self.nc.sbuf_tensor("s_" + name, list(shape), dt))

    def psum(self, name, shape, dt=F32):
        return self.es.enter_context(self.nc.psum_tensor("p_" + name, list(shape), dt))

    def dram_in(self, name, shape, dt=F32):
        return self.nc.dram_tensor(name, list(shape), dt, kind="ExternalInput").ap()

    def dram_out(self, name, shape, dt=F32):
        return self.nc.dram_tensor(name, list(shape), dt, kind="ExternalOutput").ap()

    @staticmethod
    def _norm(x):
        return x if isinstance(x, tuple) else (x, None)

    def _deps(self, rs, ws, e):
        need = {}

        def add(sv):
            if sv is None:
                return
            s, v = sv
            if s[0] == 'd':
                v = self.cnt[s]
            if need.get(s, 0) < v:
                need[s] = v

        for (n, k) in rs:
            for kk, ent in self.state.get(n, {}).items():
                if k is None or kk is None or kk == k:
                    add(ent['w'])
        for (n, k) in ws:
            for kk, ent in self.state.get(n, {}).items():
                if k is None or kk is None or kk == k:
                    if ent['w'] is not None and ent['w'][0] != ('e', e):
                        add(ent['w'])
                    for s, v in ent['r'].items():
                        if s != ('e', e):
                            add((s, v))
        if e == 'pe':
            need.pop(('e', 'pe'), None)
        return need

    def _record(self, rs, ws, sv):
        s, v = sv
        for (n, k) in rs:
            ent = self.state.setdefault(n, {}).setdefault(k, {'w': None, 'r': {}})
            ent['r'][s] = max(ent['r'].get(s, 0), v)
        for (n, k) in ws:
            d = self.state.setdefault(n, {})
            if k is None:
                d.clear()
            d[k] = {'w': (s, v), 'r': {}}

    def _emit_waits(self, e, need):
        eng = self.eng[e]
        for s, v in need.items():
            if self.waited[e].get(s, 0) < v:
                eng.wait_ge(self.sems[s], v)
                self.waited[e][s] = v

    def op(self, e, fn, r=(), w=()):
        rs = [self._norm(x) for x in r]
        ws = [self._norm(x) for x in w]
        ws = ws + [(n, None) for (n, k) in rs if n.startswith("ps")]
        ws = [((n, None) if n.startswith("ps") else (n, k)) for (n, k) in ws]
        rs = [x for x in rs if not x[0].startswith("ps")]
        self._emit_waits(e, self._deps(rs, ws, e))
        inst = fn(self.eng[e])
        s = ('e', e)
        self.cnt[s] += 1
        inst.then_inc(self.sems[s], 1)
        self._record(rs, ws, (s, self.cnt[s]))

    def dma(self, out, in_, r=(), w=(), q='sp', sem=None):
        rs = [self._norm(x) for x in r]
        ws = [self._norm(x) for x in w]
        if sem is None:
            sem = ws[0][0]
        s = ('d', sem)
        if s not in self.sems:
            self.sems[s] = self.es.enter_context(self.nc.semaphore('d_' + sem))
            self.cnt[s] = 0
        self._emit_waits(q, self._deps(rs, ws, q))
        self.eng[q].dma_start(out=out, in_=in_).then_inc(self.sems[s], 16)
        self.cnt[s] += 16
        self._record(rs, ws, (s, self.cnt[s]))

    def finish(self):
        sp = self.eng['sp']
        for s, h in self.sems.items():
            if self.cnt[s] > 0 and self.waited['sp'].get(s, 0) < self.cnt[s]:
                sp.wait_ge(h, self.cnt[s])
        self.es.close()
        return self.nc


def mm(ps, lhsT, rhs, start, stop):
    return lambda e: e.matmul(ps, lhsT=lhsT, rhs=rhs, start=start, stop=stop)


NTOK = 2048
TP = 1024
NFC = D_FF // 128
NZC = 21


def build_T(do_wout, n_ffn, do_win, do_final):
    P = Prog()
    nc = P.nc
    xT_d = P.dram_in("xT", [8, 128, NTOK])
    if do_wout:
        yT_d = P.dram_in("yT", [8, 128, NTOK])
        wglu_d = P.dram_in("wglu", [2, 128, 2, 128])
        wout_d = P.dram_in("wout", [8, 128, 8, 128])
    ffn_d = []
    for i in range(n_ffn):
        ffn_d.append(dict(
            g=P.dram_in(f"f{i}_g", [128, 8]),
            wg=P.dram_in(f"f{i}_wg", [NFC, 128, 8, 128]),
            wu=P.dram_in(f"f{i}_wu", [NFC, 128, 8, 128]),
            wd=P.dram_in(f"f{i}_wd", [8, 128, NFC, 128]),
        ))
    if do_win:
        ming_d = P.dram_in("mix_g", [128, 8])
        win_d = P.dram_in("win", [NZC, 128, 8, 128])
        zT_d = P.dram_out("zT", [NZC, 128, NTOK])
    if do_final:
        fing_d = P.dram_in("fin_g", [128, 8])
    xo_d = P.dram_out("xoT", [8, 128, NTOK])

    xT = P.sbuf("xT_s", [128, 8, TP], F32)
    hT = P.sbuf("hT_s", [128, 8, TP], BF16)
    aT = P.sbuf("aT_s", [128, NFC, TP], BF16)
    sq = P.sbuf("sq_s", [128, 2, TP], BF16)
    rstd = P.sbuf("rstd_s", [128, TP], F32)
    ones = P.sbuf("ones_s", [128, 128], BF16)
    gt = P.sbuf("g_s", [128, 8], F32)
    stg = [P.sbuf(f"stg{i}", [128, NFC * 128], F32) for i in range(3)]
    wbf = [P.sbuf(f"wbf{i}", [128, NFC * 128], BF16) for i in range(3)]
    sg = [P.sbuf(f"sg{i}", [128, 512], F32) for i in range(2)]
    ev = [P.sbuf(f"ev{i}", [128, 512], F32) for i in range(2)]
    ps = [P.psum(f"ps{i}", [128, 512]) for i in range(8)]
    if do_wout:
        yf = P.sbuf("yf_s", [128, 8, TP], F32)

    P.op('pool', lambda e: e.memset(ones[:], 1.0), w=["ones"])

    wctr = [0]

    def load_w(src_ap, nk):
        i = wctr[0] % 3
        wctr[0] += 1
        P.dma(stg[i][:, 0:nk * 128], src_ap, w=[f"stg{i}"])
        P.op('pool', lambda e: e.tensor_copy(out=wbf[i][:, 0:nk * 128], in_=stg[i][:, 0:nk * 128]),
             r=[f"stg{i}"], w=[f"wbf{i}"])
        return wbf[i], f"wbf{i}"

    psctr = [0]

    def next_ps():
        i = psctr[0] % 8
        psctr[0] += 1
        return ps[i], f"ps{i}"

    def rmsnorm(g_dram, final=False):
        P.dma(gt[:], g_dram, w=["g"])
        pss = [next_ps(), next_ps()]
        for k in range(8):
            j = k % 2
            P.op('act', lambda e: e.activation(out=sq[:, j, :], in_=xT[:, k, :], func=AF.Square),
                 r=[("xT", k)], w=[("sq", j)])
            for tg in range(2):
                P.op('pe', mm(pss[tg][0][:, :], ones[:], sq[:, j, tg * 512:(tg + 1) * 512], k == 0, k == 7),
                     r=["ones", ("sq", j)], w=[pss[tg][1]])
        for tg in range(2):
            sl = slice(tg * 512, (tg + 1) * 512)
            P.op('act', lambda e: e.activation(out=rstd[:, sl], in_=pss[tg][0][:, :], func=AF.Sqrt,
                                               scale=1.0 / D_MODEL, bias=EPS),
                 r=[pss[tg][1]], w=[("rstd", tg)])
            P.op('dve', lambda e: e.reciprocal(out=rstd[:, sl], in_=rstd[:, sl]),
                 r=[("rstd", tg)], w=[("rstd", tg)])
        for k in range(8):
            if final:
                P.op('dve', lambda e: e.scalar_tensor_tensor(out=xT[:, k, :], in0=xT[:, k, :], scalar=gt[:, k:k + 1],
                                                             in1=rstd[:, :], op0=ALU.mult, op1=ALU.mult),
                     r=[("xT", k), "g", "rstd"], w=[("xT", k)])
            else:
                P.op('dve', lambda e: e.scalar_tensor_tensor(out=hT[:, k, :], in0=xT[:, k, :], scalar=gt[:, k:k + 1],
                                                             in1=rstd[:, :], op0=ALU.mult, op1=ALU.mult),
                     r=[("xT", k), "g", "rstd"], w=[("hT", k)])

    def ffn(fd):
        rmsnorm(fd['g'])
        for fc in range(NFC):
            wg, wgn = load_w(fd['wg'][fc].rearrange("p k j -> p (k j)"), 8)
            wu, wun = load_w(fd['wu'][fc].rearrange("p k j -> p (k j)"), 8)
            pg = [next_ps(), next_ps()]
            pu = [next_ps(), next_ps()]
            for (w_, wn_, pp_) in [(wg, wgn, pg), (wu, wun, pu)]:
                for k in range(8):
                    for tg in range(2):
                        P.op('pe', mm(pp_[tg][0][:, :], w_[:, k * 128:(k + 1) * 128], hT[:, k, tg * 512:(tg + 1) * 512], k == 0, k == 7),
                             r=[wn_, "hT"], w=[pp_[tg][1]])
            for tg in range(2):
                sl = slice(tg * 512, (tg + 1) * 512)
                i = tg
                P.op('act', lambda e: e.activation(out=sg[i][:, :], in_=pg[tg][0][:, :], func=AF.Silu),
                     r=[pg[tg][1]], w=[f"sg{i}"])
                P.op('dve', lambda e: e.tensor_tensor(out=aT[:, fc, sl], in0=sg[i][:, :], in1=pu[tg][0][:, :], op=ALU.mult),
                     r=[f"sg{i}", pu[tg][1]], w=[("aT", fc)])
        for mc in range(8):
            wd, wdn = load_w(fd['wd'][mc].rearrange("p k j -> p (k j)"), NFC)
            pd = [next_ps(), next_ps()]
            for k in range(NFC):
                for tg in range(2):
                    P.op('pe', mm(pd[tg][0][:, :], wd[:, k * 128:(k + 1) * 128], aT[:, k, tg * 512:(tg + 1) * 512], k == 0, k == NFC - 1),
                         r=[wdn, "aT"], w=[pd[tg][1]])
            for tg in range(2):
                sl = slice(tg * 512, (tg + 1) * 512)
                P.op('dve', lambda e: e.scalar_tensor_tensor(out=xT[:, mc, sl], in0=pd[tg][0][:, :], scalar=0.5,
                                                             in1=xT[:, mc, sl], op0=ALU.mult, op1=ALU.add),
                     r=[pd[tg][1], ("xT", mc)], w=[("xT", mc)])

    for ps_i in range(NTOK // TP):
        tsl = slice(ps_i * TP, (ps_i + 1) * TP)
        for k in range(8):
            P.dma(xT[:, k, :], xT_d[k, :, tsl], w=[("xT", k)], sem="xT")
        if do_wout:
            for k in range(8):
                P.dma(yf[:, k, :], yT_d[k, :, tsl], w=[("yf", k)], sem="yf")
            for k in range(2):
                P.op('pool', lambda e: e.tensor_copy(out=hT[:, k, :], in_=yf[:, k, :]), r=[("yf", k)], w=[("hT", k)])
            for mc in range(2):
                wl, wln = load_w(wglu_d[mc].rearrange("p k j -> p (k j)"), 2)
                for tg in range(2):
                    sl = slice(tg * 512, (tg + 1) * 512)
                    pg, pgn = next_ps()
                    for k in range(2):
                        P.op('pe', mm(pg[:, :], wl[:, k * 128:(k + 1) * 128], hT[:, k, sl], k == 0, k == 1),
                             r=[wln, ("hT", 0), ("hT", 1)], w=[pgn])
                    i = tg
                    P.op('act', lambda e: e.activation(out=sg[i][:, :], in_=pg[:, :], func=AF.Sigmoid),
                         r=[pgn], w=[f"sg{i}"])
                    P.op('dve', lambda e: e.tensor_tensor(out=aT[:, mc, sl], in0=sg[i][:, :], in1=yf[:, mc, sl],
                                                          op=ALU.mult),
                         r=[f"sg{i}", ("yf", mc)], w=[("aT", mc)])
            for k in range(2, 8):
                P.op('pool', lambda e: e.tensor_copy(out=aT[:, k, :], in_=yf[:, k, :]), r=[("yf", k)], w=[("aT", k)])
            for mc in range(8):
                wl, wln = load_w(wout_d[mc].rearrange("p k j -> p (k j)"), 8)
                for tg in range(2):
                    sl = slice(tg * 512, (tg + 1) * 512)
                    pd, pdn = next_ps()
                    for k in range(8):
                        P.op('pe', mm(pd[:, :], wl[:, k * 128:(k + 1) * 128], aT[:, k, sl], k == 0, k == 7),
                             r=[wln, "aT"], w=[pdn])
                    P.op('dve', lambda e: e.tensor_tensor(out=xT[:, mc, sl], in0=pd[:, :], in1=xT[:, mc, sl],
                                                          op=ALU.add),
                         r=[pdn, ("xT", mc)], w=[("xT", mc)])
        for i in range(n_ffn):
            ffn(ffn_d[i])
        if do_win:
            rmsnorm(ming_d)
            for cc in range(NZC):
                wl, wln = load_w(win_d[cc].rearrange("p k j -> p (k j)"), 8)
                for tg in range(2):
                    sl = slice(tg * 512, (tg + 1) * 512)
                    pz, pzn = next_ps()
                    for k in range(8):
                        P.op('pe', mm(pz[:, :], wl[:, k * 128:(k + 1) * 128], hT[:, k, sl], k == 0, k == 7),
                             r=[wln, "hT"], w=[pzn])
                    i = (cc * 2 + tg) % 2
                    P.op('act', lambda e: e.activation(out=ev[i][:, :], in_=pz[:, :], func=AF.Copy),
                         r=[pzn], w=[f"ev{i}"])
                    P.dma(zT_d[cc, :, ps_i * TP + tg * 512: ps_i * TP + (tg + 1) * 512], ev[i][:, :],
                          r=[f"ev{i}"], w=[("zT", (ps_i, cc, tg))], sem=f"zst{i}", q='act')
        if do_final:
            rmsnorm(fing_d, final=True)
        for k in range(8):
            P.dma(xo_d[k, :, tsl], xT[:, k, :], r=[("xT", k)], w=[("xo", (ps_i, k))], sem="xo")
    return P.finish()


def _chunkT(w, nk):
    K, M = w.shape
    return np.ascontiguousarray(w.reshape(nk, 128, M // 128, 128).transpose(2, 1, 0, 3))


def _gcol(g):
    return np.ascontiguousarray(g.reshape(8, 128).T)


def _ffn_maps(prefix, g, wg, wu, wd):
    return {f"{prefix}_g": _gcol(g), f"{prefix}_wg": _chunkT(wg, 8), f"{prefix}_wu": _chunkT(wu, 8),
            f"{prefix}_wd": _chunkT(wd, NFC)}


def run_T(xT_full, common, do_wout, n_ffn, do_win, do_final, yT_full=None):
    nc = build_T(do_wout, n_ffn, do_win, do_final)
    maps = []
    for c in range(NCORES):
        m = dict(common)
        m["xT"] = np.ascontiguousarray(xT_full[:, c * NTOK:(c + 1) * NTOK].reshape(8, 128, NTOK))
        if do_wout:
            m["yT"] = np.ascontiguousarray(yT_full[:, c * NTOK:(c + 1) * NTOK].reshape(8, 128, NTOK))
        maps.append(m)
    res = run_bass_kernel_spmd(nc, maps, core_ids=list(range(NCORES)))
    out = {}
    out["xoT"] = np.concatenate([r["xoT"].reshape(1024, NTOK) for r in res.results], axis=1)
    if do_win:
        out["zT"] = np.concatenate([r["zT"].reshape(NZC * 128, NTOK) for r in res.results], axis=1)
    return out


PI = float(np.pi)
NCH = 1024
HB = 2048
GELU_C = 1.5957691216057308


def emit_gelu(P, dst, src_ps, tmp_a, tmp_b, names, src_names, bias=None):
    dn, an, bn = names
    if bias is None:
        P.op('act', lambda e: e.activation(out=tmp_a, in_=src_ps, func=AF.Copy), r=src_names, w=[an])
    else:
        P.op('act', lambda e: e.activation(out=tmp_a, in_=src_ps, func=AF.Identity, bias=bias[0]), r=src_names + [bias[1]], w=[an])
    P.op('pool', lambda e: e.tensor_tensor(out=tmp_b, in0=tmp_a, in1=tmp_a, op=ALU.mult), r=[an], w=[bn])
    P.op('dve', lambda e: e.tensor_scalar(out=tmp_b, in0=tmp_b, scalar1=0.044715, scalar2=1.0, op0=ALU.mult, op1=ALU.add),
         r=[bn], w=[bn])
    P.op('dve', lambda e: e.tensor_tensor(out=tmp_b, in0=tmp_b, in1=tmp_a, op=ALU.mult), r=[bn, an], w=[bn])
    P.op('act', lambda e: e.activation(out=tmp_b, in_=tmp_b, func=AF.Sigmoid, scale=GELU_C), r=[bn], w=[bn])
    P.op('dve', lambda e: e.tensor_tensor(out=dst, in0=tmp_a, in1=tmp_b, op=ALU.mult), r=[an, bn], w=[dn])


def build_M1(layer):
    P = Prog()
    lr2_d = P.dram_in("lr2", [128, 4]); li2_d = P.dram_in("li2", [128, 4]); ldt_d = P.dram_in("ldt", [128, 4])
    b1_d = P.dram_in("bst1", [128, 4, 16]); b2_d = P.dram_in("bst2", [128, 4, 16])
    c1_d = P.dram_in("cst1", [128, 4, 16]); c2_d = P.dram_in("cst2", [128, 4, 16])
    dcol_d = P.dram_in("dcol", [128, 4]); sgn_d = P.dram_in("sgn", [128, 2]); jv_d = P.dram_in("jv", [128, 4, 24])
    i128_d = P.dram_in("i128", [128, 128]); jsw_d = P.dram_in("jsw", [128, 128]); tmask_d = P.dram_in("tmask", [128, 128])
    uc_d = P.dram_in("uc", [4, 128, NCH])
    y5_d = P.dram_out("y5", [4, 128, NCH])
    hq_d = P.dram_in("hq", [64, SEQ]); hf_d = P.dram_in("hf", [64, SEQ])
    hv64_d = P.dram_in("hv64", [4, 64, 32, 64]); hv128_d = P.dram_in("hv128", [4, 128, 16, 64]); hg64_d = P.dram_in("hg64", [4, 64, 32, 64])
    lbl_d = P.dram_in("lbl", [64, 2]); gain_d = P.dram_in("hgain", [64, 64]); cm_d = P.dram_in("cmask", [64, 512])
    yh_d = P.dram_out("yh", [64, 128, 64])

    ps = [P.psum(f"ps{i}", [128, 512]) for i in range(8)]
    i128 = P.sbuf("i128", [128, 128], F32)
    P.dma(i128[:], i128_d[:, :], w=["i128"])

    import os
    PARTS = os.environ.get('M1PARTS', 's5,hg')
    NG5 = 4 if 's5' in PARTS else 0
    def T(name, shape, dt=F32):
        return P.sbuf(name, shape, dt)
    lr2 = T("lr2", [128, 4]); li2 = T("li2", [128, 4]); dt = T("dt", [128, 4]); sgn = T("sgn", [128, 2])
    b1 = T("b1", [128, 4, 16]); b2 = T("b2", [128, 4, 16]); c1 = T("c1", [128, 4, 16]); c2 = T("c2", [128, 4, 16])
    dcol = T("dcol", [128, 4]); jv = T("jv", [128, 4, 24]); jsw = T("jsw", [128, 128]); tmask = T("tmask", [128, 128])
    for t_, d_, n_ in [(lr2, lr2_d, "lr2"), (li2, li2_d, "li2"), (dt, ldt_d, "dt"), (sgn, sgn_d, "sgn"), (dcol, dcol_d, "dcol"),
                       (jsw, jsw_d, "jsw"), (tmask, tmask_d, "tmask")]:
        P.dma(t_[:], d_[:, :], w=[n_])
    for t_, d_, n_ in [(b1, b1_d, "b1"), (b2, b2_d, "b2"), (c1, c1_d, "c1"), (c2, c2_d, "c2"), (jv, jv_d, "jv")]:
        P.dma(t_[:], d_[:, :, :], w=[n_])
    V = lambda e: e

    def dv(fn, r, w):
        P.op('dve', fn, r=r, w=w)

    def ac(fn, r, w):
        P.op('act', fn, r=r, w=w)
    ac(lambda e: e.activation(out=dt[:], in_=dt[:], func=AF.Exp), ["dt"], ["dt"])
    lrdt = T("lrdt", [128, 4]); lidt = T("lidt", [128, 4])
    dv(lambda e: e.tensor_tensor(out=lrdt[:], in0=lr2[:], in1=dt[:], op=ALU.mult), ["lr2", "dt"], ["lrdt"])
    dv(lambda e: e.tensor_tensor(out=lidt[:], in0=li2[:], in1=dt[:], op=ALU.mult), ["li2", "dt"], ["lidt"])
    am = T("am", [128, 4, 24]); aa = T("aa", [128, 4, 24]); pr = T("pr", [128, 4, 24]); pi_ = T("pi", [128, 4, 24])
    tA = T("tA", [128, 4, 24]); tB = T("tB", [128, 4, 24]); tI = T("tI", [128, 4, 24], mybir.dt.int32)
    bc24 = lambda t_: t_[:, :, None].to_broadcast([128, 4, 24])
    dv(lambda e: e.tensor_tensor(out=am[:], in0=jv[:], in1=bc24(lrdt), op=ALU.mult), ["jv", "lrdt"], ["am"])
    ac(lambda e: e.activation(out=am[:], in_=am[:], func=AF.Exp), ["am"], ["am"])
    dv(lambda e: e.tensor_tensor(out=aa[:], in0=jv[:], in1=bc24(lidt), op=ALU.mult), ["jv", "lidt"], ["aa"])

    def sin_of(dst, dn, src, sn, shift):
        dv(lambda e: e.tensor_scalar(out=tA[:], in0=src[:], scalar1=shift, scalar2=None, op0=ALU.add), [sn], ["tA"])
        dv(lambda e: e.tensor_scalar(out=tB[:], in0=tA[:], scalar1=1.0 / (2 * PI), scalar2=64.5, op0=ALU.mult, op1=ALU.add),
           ["tA"], ["tB"])
        dv(lambda e: e.tensor_copy(out=tI[:], in_=tB[:]), ["tB"], ["tI"])
        dv(lambda e: e.tensor_copy(out=tB[:], in_=tI[:]), ["tI"], ["tB"])
        dv(lambda e: e.tensor_scalar(out=tB[:], in0=tB[:], scalar1=-64.0, scalar2=-2 * PI, op0=ALU.add, op1=ALU.mult),
           ["tB"], ["tB"])
        dv(lambda e: e.tensor_tensor(out=tA[:], in0=tA[:], in1=tB[:], op=ALU.add), ["tA", "tB"], ["tA"])
        dv(lambda e: e.tensor_scalar(out=tB[:], in0=tA[:], scalar1=-PI, scalar2=2 * PI, op0=ALU.is_lt, op1=ALU.mult),
           ["tA"], ["tB"])
        dv(lambda e: e.tensor_tensor(out=tA[:], in0=tA[:], in1=tB[:], op=ALU.add), ["tA", "tB"], ["tA"])
        dv(lambda e: e.tensor_scalar(out=tB[:], in0=tA[:], scalar1=PI, scalar2=-2 * PI, op0=ALU.is_gt, op1=ALU.mult),
           ["tA"], ["tB"])
        dv(lambda e: e.tensor_tensor(out=tA[:], in0=tA[:], in1=tB[:], op=ALU.add), ["tA", "tB"], ["tA"])
        dv(lambda e: e.tensor_scalar(out=tA[:], in0=tA[:], scalar1=-PI, scalar2=PI, op0=ALU.max, op1=ALU.min),
           ["tA"], ["tA"])
        ac(lambda e: e.activation(out=dst[:], in_=tA[:], func=AF.Sin), ["tA"], [dn])
    sin_of(pi_, "pi", aa, "aa", 0.0)
    sin_of(pr, "pr", aa, "aa", PI / 2)
    dv(lambda e: e.tensor_tensor(out=pr[:], in0=pr[:], in1=am[:], op=ALU.mult), ["pr", "am"], ["pr"])
    dv(lambda e: e.tensor_tensor(out=pi_[:], in0=pi_[:], in1=am[:], op=ALU.mult), ["pi", "am"], ["pi"])
    den = T("den", [128, 4]); t4a = T("t4a", [128, 4]); t4b = T("t4b", [128, 4]); nr = T("nr", [128, 4])
    gre = T("gre", [128, 4]); gim = T("gim", [128, 4])
    ar1 = pr[:, :, 8]; ai1 = pi_[:, :, 8]
    dv(lambda e: e.tensor_tensor(out=den[:], in0=lr2[:], in1=lr2[:], op=ALU.mult), ["lr2"], ["den"])
    dv(lambda e: e.tensor_tensor(out=t4a[:], in0=li2[:], in1=li2[:], op=ALU.mult), ["li2"], ["t4a"])
    dv(lambda e: e.tensor_tensor(out=den[:], in0=den[:], in1=t4a[:], op=ALU.add), ["den", "t4a"], ["den"])
    dv(lambda e: e.reciprocal(out=den[:], in_=den[:]), ["den"], ["den"])
    dv(lambda e: e.tensor_scalar(out=nr[:], in0=ar1, scalar1=-1.0, scalar2=None, op0=ALU.add), ["pr"], ["nr"])
    dv(lambda e: e.tensor_tensor(out=t4a[:], in0=nr[:], in1=lr2[:], op=ALU.mult), ["nr", "lr2"], ["t4a"])
    dv(lambda e: e.tensor_tensor(out=t4b[:], in0=ai1, in1=li2[:], op=ALU.mult), ["pi", "li2"], ["t4b"])
    dv(lambda e: e.tensor_tensor(out=t4a[:], in0=t4a[:], in1=t4b[:], op=ALU.add), ["t4a", "t4b"], ["t4a"])
    dv(lambda e: e.tensor_tensor(out=gre[:], in0=t4a[:], in1=den[:], op=ALU.mult), ["t4a", "den"], ["gre"])
    dv(lambda e: e.tensor_tensor(out=t4a[:], in0=ai1, in1=lr2[:], op=ALU.mult), ["pi", "lr2"], ["t4a"])
    dv(lambda e: e.tensor_tensor(out=t4b[:], in0=nr[:], in1=li2[:], op=ALU.mult), ["nr", "li2"], ["t4b"])
    dv(lambda e: e.tensor_tensor(out=t4a[:], in0=t4a[:], in1=t4b[:], op=ALU.subtract), ["t4a", "t4b"], ["t4a"])
    dv(lambda e: e.tensor_tensor(out=gim[:], in0=t4a[:], in1=den[:], op=ALU.mult), ["t4a", "den"], ["gim"])
    er = T("er", [128, 4, 8]); ei = T("ei", [128, 4, 8]); t8 = T("t8", [128, 4, 8])
    bc8 = lambda t_: t_[:, :, None].to_broadcast([128, 4, 8])
    dv(lambda e: e.tensor_tensor(out=er[:], in0=pr[:, :, 0:8], in1=bc8(gre), op=ALU.mult), ["pr", "gre"], ["er"])
    dv(lambda e: e.tensor_tensor(out=t8[:], in0=pi_[:, :, 0:8], in1=bc8(gim), op=ALU.mult), ["pi", "gim"], ["t8"])
    dv(lambda e: e.tensor_tensor(out=er[:], in0=er[:], in1=t8[:], op=ALU.subtract), ["er", "t8"], ["er"])
    dv(lambda e: e.tensor_tensor(out=ei[:], in0=pr[:, :, 0:8], in1=bc8(gim), op=ALU.mult), ["pr", "gim"], ["ei"])
    dv(lambda e: e.tensor_tensor(out=t8[:], in0=pi_[:, :, 0:8], in1=bc8(gre), op=ALU.mult), ["pi", "gre"], ["t8"])
    dv(lambda e: e.tensor_tensor(out=ei[:], in0=ei[:], in1=t8[:], op=ALU.add), ["ei", "t8"], ["ei"])
    m2 = T("m2", [128, 4, 8]); n1f = T("n1f", [128, 4, 8]); n2f = T("n2f", [128, 4, 8]); n1h = T("n1h", [128, 4, 8]); n2h = T("n2h", [128, 4, 8])
    dv(lambda e: e.tensor_scalar(out=m2[:], in0=ei[:], scalar1=sgn[:, 1:2], scalar2=None, op0=ALU.mult), ["ei", "sgn"], ["m2"])
    dv(lambda e: e.tensor_scalar(out=n1f[:], in0=pr[:, :, 8:16], scalar1=sgn[:, 0:1], scalar2=None, op0=ALU.mult), ["pr", "sgn"], ["n1f"])
    dv(lambda e: e.tensor_scalar(out=n2f[:], in0=pi_[:, :, 8:16], scalar1=-1.0, scalar2=None, op0=ALU.mult), ["pi"], ["n2f"])
    dv(lambda e: e.tensor_scalar(out=n1h[:], in0=pr[:, :, 16:24], scalar1=sgn[:, 0:1], scalar2=None, op0=ALU.mult), ["pr", "sgn"], ["n1h"])
    dv(lambda e: e.tensor_scalar(out=n2h[:], in0=pi_[:, :, 16:24], scalar1=-1.0, scalar2=None, op0=ALU.mult), ["pi"], ["n2h"])
    bcm = T("bcm", [128, 4, 8, 16]); ccm = T("ccm", [128, 4, 8, 16]); qmm = T("qmm", [128, 4, 8, 16]); t816 = T("t816", [128, 4, 8, 16])
    S4 = [128, 4, 8, 16]

    def outer(dst, dn, st1, s1n, co1, c1n, st2, s2n, co2, c2n):
        dv(lambda e: e.tensor_tensor(out=dst[:], in0=st1[:, :, None, :].to_broadcast(S4), in1=co1[:, :, :, None].to_broadcast(S4),
                                     op=ALU.mult), [s1n, c1n], [dn])
        dv(lambda e: e.tensor_tensor(out=t816[:], in0=st2[:, :, None, :].to_broadcast(S4), in1=co2[:, :, :, None].to_broadcast(S4),
                                     op=ALU.mult), [s2n, c2n], ["t816"])
        dv(lambda e: e.tensor_tensor(out=dst[:], in0=dst[:], in1=t816[:], op=ALU.add), [dn, "t816"], [dn])
    outer(bcm, "bcm", b1, "b1", er, "er", b2, "b2", m2, "m2")
    outer(ccm, "ccm", c1, "c1", n1f, "n1f", c2, "c2", n2f, "n2f")
    outer(qmm, "qmm", c1, "c1", n1h, "n1h", c2, "c2", n2h, "n2h")
    tz = T("tz", [128, 4, 128]); bct = T("bct", [128, 4, 128])
    for g in range(4):
        pg, pgn = ps[g % 2], f"ps{g % 2}"
        P.op('pe', lambda e: e.matmul(pg[:, 0:128], lhsT=bcm[:, g].rearrange("p s h -> p (s h)"),
                                      rhs=qmm[:, g].rearrange("p s h -> p (s h)"), start=True, stop=True),
             r=["bcm", "qmm"], w=[pgn])
        dv(lambda e: e.tensor_tensor(out=tz[:, g, :], in0=pg[:, 0:128], in1=tmask[:], op=ALU.mult), [pgn, "tmask"], [("tz", g)])
        dv(lambda e: e.scalar_tensor_tensor(out=tz[:, g, :], in0=i128[:], scalar=dcol[:, g:g + 1], in1=tz[:, g, :],
                                            op0=ALU.mult, op1=ALU.add), ["i128", "dcol", ("tz", g)], [("tz", g)])
        pt, ptn = ps[2 + g % 2], f"ps{2 + g % 2}"
        P.op('pe', lambda e: e.transpose(out=pt[:, 0:128], in_=bcm[:, g].rearrange("p s h -> p (s h)"), identity=i128[:]),
             r=["bcm", "i128"], w=[ptn])
        ac(lambda e: e.activation(out=bct[:, g, :], in_=pt[:, 0:128], func=AF.Copy), [ptn], [("bct", g)])
    NK = 10
    a8r = T("a8r", [128, NK, 4]); a8i = T("a8i", [128, NK, 4]); rm = T("rm", [128, NK * 4, 128])
    dv(lambda e: e.tensor_copy(out=a8r[:, 0, :], in_=pr[:, :, 15]), ["pr"], ["a8r"])
    dv(lambda e: e.tensor_copy(out=a8i[:, 0, :], in_=pi_[:, :, 15]), ["pi"], ["a8i"])
    for k in range(1, NK):
        dv(lambda e: e.tensor_tensor(out=t4a[:], in0=a8r[:, k - 1, :], in1=a8r[:, k - 1, :], op=ALU.mult), ["a8r"], ["t4a"])
        dv(lambda e: e.tensor_tensor(out=t4b[:], in0=a8i[:, k - 1, :], in1=a8i[:, k - 1, :], op=ALU.mult), ["a8i"], ["t4b"])
        dv(lambda e: e.tensor_tensor(out=a8r[:, k, :], in0=t4a[:], in1=t4b[:], op=ALU.subtract), ["t4a", "t4b"], ["a8r"])
        dv(lambda e: e.tensor_tensor(out=t4a[:], in0=a8r[:, k - 1, :], in1=a8i[:, k - 1, :], op=ALU.mult), ["a8r", "a8i"], ["t4a"])
        dv(lambda e: e.tensor_scalar(out=a8i[:, k, :], in0=t4a[:], scalar1=2.0, scalar2=None, op0=ALU.mult), ["t4a"], ["a8i"])
    a8is = T("a8is", [128, NK, 4])
    dv(lambda e: e.tensor_scalar(out=a8is[:], in0=a8i[:], scalar1=sgn[:, 0:1], scalar2=None, op0=ALU.mult), ["a8i", "sgn"], ["a8is"])
    for k in range(NK):
        for g in range(4):
            i = k * 4 + g
            dv(lambda e: e.tensor_scalar(out=rm[:, i, :], in0=i128[:], scalar1=a8r[:, k, g:g + 1], scalar2=None, op0=ALU.mult),
               ["i128", "a8r"], [("rm", i)])
            dv(lambda e: e.scalar_tensor_tensor(out=rm[:, i, :], in0=jsw[:], scalar=a8is[:, k, g:g + 1], in1=rm[:, i, :],
                                                op0=ALU.mult, op1=ALU.add), ["jsw", "a8is", ("rm", i)], [("rm", i)])
    uc = [T(f"uc{i}", [128, NCH]) for i in range(2)]
    xs = T("xs", [128, NCH + 1]); ga = T("ga", [128, 512]); gb = T("gb", [128, 512]); yo = [T(f"yo{i}", [128, 512]) for i in range(2)]
    P.op('pool', lambda e: e.memset(xs[:, 0:1], 0.0), w=[("xs", "z")])
    def s5_gen():
        for g in range(NG5):
            u = uc[g % 2]; un = f"uc{g % 2}"
            P.dma(u[:], uc_d[g], w=[un])
            for h in range(2):
                pp, ppn = ps[h], f"ps{h}"
                P.op('pe', lambda e: e.matmul(pp[:, :], lhsT=bct[:, g, :], rhs=u[:, h * 512:(h + 1) * 512], start=True, stop=True),
                     r=[("bct", g), un], w=[ppn])
                ac(lambda e: e.activation(out=xs[:, 1 + h * 512:1 + (h + 1) * 512], in_=pp[:, :], func=AF.Copy), [ppn], [("xs", "x")])
            for k in range(NK):
                d = 1 << k
                n = NCH - d
                pieces = [(0, min(512, n))] + ([(512, n)] if n > 512 else [])
                for h, (a, b) in enumerate(pieces):
                    pp, ppn = ps[h], f"ps{h}"
                    P.op('pe', lambda e: e.matmul(pp[:, 0:b - a], lhsT=rm[:, k * 4 + g, :], rhs=xs[:, 1 + a:1 + b], start=True, stop=True),
                         r=[("rm", k * 4 + g), ("xs", "x")], w=[ppn])
                for h, (a, b) in enumerate(pieces):
                    pp, ppn = ps[h], f"ps{h}"
                    dv(lambda e: e.tensor_tensor(out=xs[:, 1 + d + a:1 + d + b], in0=pp[:, 0:b - a], in1=xs[:, 1 + d + a:1 + d + b], op=ALU.add),
                       [ppn, ("xs", "x")], [("xs", "x")])
                yield
            for h in range(2):
                pp, ppn = ps[h], f"ps{h}"
                P.op('pe', lambda e: e.matmul(pp[:, :], lhsT=tz[:, g, :], rhs=u[:, h * 512:(h + 1) * 512], start=True, stop=False),
                     r=[("tz", g), un], w=[ppn])
                P.op('pe', lambda e: e.matmul(pp[:, :], lhsT=ccm[:, g].rearrange("p s h -> p (s h)"), rhs=xs[:, h * 512:(h + 1) * 512],
                                              start=False, stop=True), r=["ccm", ("xs", "x"), ("xs", "z")], w=[ppn])
                emit_gelu(P, yo[h][:], pp[:, :], ga[:], gb[:], (f"yo{h}", "ga", "gb"), [ppn])
                P.dma(y5_d[g, :, h * 512:(h + 1) * 512], yo[h][:], r=[f"yo{h}"], w=[("y5", (g, h))], sem=f"y5s{h}")
                yield


    lbl = T("lbl", [64, 2]); lb = T("lb", [64, 1]); oml = T("oml", [64, 1]); gain = T("gain", [64, 64]); cm = T("cm", [64, 512])
    P.dma(lbl[:], lbl_d[:, :], w=["lbl"]); P.dma(gain[:], gain_d[:, :], w=["gain"]); P.dma(cm[:], cm_d[:, :], w=["cm"])
    if layer == 0:
        P.op('pool', lambda e: e.memset(lb[:], 0.0), w=["lb"])
    else:
        dv(lambda e: e.tensor_tensor(out=lb[:], in0=lbl[:, 1:2], in1=lbl[:, 0:1], op=ALU.subtract), ["lbl"], ["lb"])
        ac(lambda e: e.activation(out=lb[:], in_=lb[:], func=AF.Sigmoid), ["lb"], ["lb"])
    dv(lambda e: e.tensor_scalar(out=oml[:], in0=lb[:], scalar1=-1.0, scalar2=1.0, op0=ALU.mult, op1=ALU.add), ["lb"], ["oml"])
    rmask = T("rmask", [64, HB])
    P.op('pool', lambda e: e.memset(rmask[:], 1.0), w=["rmask"])
    P.op('pool', lambda e: e.memset(rmask[:, 0:HB:64], 0.0), w=["rmask"])
    hq = T("hq", [64, HB]); hf = T("hf", [64, HB]); sig = T("sig", [64, HB]); fbuf = T("fbuf", [64, HB]); bb = T("bb", [64, HB])
    eb = T("eb", [64, HB]); enb = T("enb", [64, HB]); qtil = T("qtil", [64, HB], BF16); ktil = T("ktil", [64, HB])
    ktb = T("ktb", [64, HB], BF16); khat = T("khat", [64, HB]); dec = T("dec", [64, 32])
    khT = T("khT", [128, 16, 64], BF16); v64f = T("v64f", [64, 32, 64]); v64 = T("v64", [64, 32, 64], BF16)
    v128f = T("v128f", [128, 16, 64]); v128 = T("v128", [128, 16, 64], BF16)
    g64 = T("g64", [64, 32, 64]); sall = T("sall", [64, 33, 64]); sbf = T("sbf", [64, 32, 64], BF16)
    attm = T("attm", [64, 512], BF16); osb = T("osb", [64, 8, 64]); osq = T("osq", [64, 8, 64]); ss = T("ss", [64, 8])
    yh = [T(f"yh{i}", [64, 8, 64]) for i in range(2)]
    P.op('pool', lambda e: e.memset(sall[:, 0, :], 0.0), w=[("sall", 0)])
    def hg_gen():
        for blk in range(SEQ // HB):
            tsl = slice(blk * HB, (blk + 1) * HB)
            P.dma(hq[:], hq_d[:, tsl], w=["hq"]); P.dma(hf[:], hf_d[:, tsl], w=["hf"])
            P.dma(v64f[:], hv64_d[blk], w=["v64f"])
            P.dma(v128f[:], hv128_d[blk], w=["v128f"])
            P.dma(g64[:], hg64_d[blk], w=["g64"])
            P.op('pool', lambda e: e.tensor_copy(out=v64[:], in_=v64f[:]), r=["v64f"], w=["v64"])
            P.op('pool', lambda e: e.tensor_copy(out=v128[:], in_=v128f[:]), r=["v128f"], w=["v128"])
            ac(lambda e: e.activation(out=sig[:], in_=hf[:], func=AF.Sigmoid), ["hf"], ["sig"])
            dv(lambda e: e.tensor_scalar(out=fbuf[:], in0=sig[:], scalar1=oml[:, 0:1], scalar2=lb[:, 0:1], op0=ALU.mult, op1=ALU.add),
               ["sig", "oml", "lb"], ["fbuf"])
            ac(lambda e: e.activation(out=fbuf[:], in_=fbuf[:], func=AF.Ln), ["fbuf"], ["fbuf"])
            dv(lambda e: e.tensor_tensor_scan(out=bb[:], data0=rmask[:], data1=fbuf[:], initial=0.0, op0=ALU.mult, op1=ALU.add),
               ["rmask", "fbuf"], ["bb"])
            ac(lambda e: e.activation(out=eb[:], in_=bb[:], func=AF.Exp), ["bb"], ["eb"])
            ac(lambda e: e.activation(out=enb[:], in_=bb[:], func=AF.Exp, scale=-1.0), ["bb"], ["enb"])
            ac(lambda e: e.activation(out=sig[:], in_=hf[:], func=AF.Sigmoid, scale=-1.0), ["hf"], ["sig"])
            dv(lambda e: e.scalar_tensor_tensor(out=ktil[:], in0=sig[:], scalar=oml[:, 0:1], in1=enb[:], op0=ALU.mult, op1=ALU.mult),
               ["sig", "oml", "enb"], ["ktil"])
            P.op('pool', lambda e: e.tensor_copy(out=ktb[:], in_=ktil[:]), r=["ktil"], w=["ktb"])
            ac(lambda e: e.activation(out=hq[:], in_=hq[:], func=AF.Silu), ["hq"], ["hq"])
            dv(lambda e: e.tensor_tensor(out=qtil[:], in0=hq[:], in1=eb[:], op=ALU.mult), ["hq", "eb"], ["qtil"])
            dv(lambda e: e.tensor_copy(out=dec[:], in_=eb[:, 63:HB:64]), ["eb"], ["dec"])
            dv(lambda e: e.tensor_tensor(out=khat[:].rearrange("p (c s) -> p c s", s=64), in0=ktil[:].rearrange("p (c s) -> p c s", s=64),
                                         in1=dec[:, :, None].to_broadcast([64, 32, 64]), op=ALU.mult), ["ktil", "dec"], ["khat"])
            ac(lambda e: e.activation(out=g64[:], in_=g64[:], func=AF.Silu), ["g64"], ["g64"])
            yield
            for hlf in range(2):
                pp, ppn = ps[6 + hlf], f"ps{6 + hlf}"
                for j in range(8):
                    jj = hlf * 8 + j
                    P.op('pe', lambda e: e.transpose(out=pp[:, j * 64:(j + 1) * 64], in_=khat[:, jj * 128:(jj + 1) * 128],
                                                     identity=i128[0:64, 0:64]), r=["khat", "i128"], w=[ppn])
                ac(lambda e: e.activation(out=khT[:, hlf * 8:(hlf + 1) * 8, :].rearrange("p a b -> p (a b)"), in_=pp[:, :], func=AF.Copy),
                   [ppn], ["khT"])
                yield
            for cg in range(4):
                for ci in range(8):
                    c = cg * 8 + ci
                    po = (c % 2) * 64
                    pu, pun = ps[2 + c % 2], f"ps{2 + c % 2}"
                    sl_ = slice((ci // 2) * 64, (ci // 2 + 1) * 64)
                    P.op('pe', lambda e: e.matmul(pu[0:64, sl_], lhsT=khT[po:po + 64, c // 2, :],
                                                  rhs=v128[po:po + 64, c // 2, :], start=True, stop=True),
                         r=["khT", "v128"], w=[pun])
                for ci in range(8):
                    c = cg * 8 + ci
                    pu, pun = ps[2 + c % 2], f"ps{2 + c % 2}"
                    sl_ = slice((ci // 2) * 64, (ci // 2 + 1) * 64)
                    dv(lambda e: e.scalar_tensor_tensor(out=sall[:, c + 1, :], in0=sall[:, c, :], scalar=dec[:, c:c + 1],
                                                        in1=pu[0:64, sl_], op0=ALU.mult, op1=ALU.add),
                       [("sall", c), "dec", pun], [("sall", c + 1)])
                    if ci % 2 == 1:
                        yield
            P.op('pool', lambda e: e.tensor_copy(out=sbf[:], in_=sall[:, 0:32, :]), r=["sall"], w=["sbf"])
            for cg in range(4):
                pa, pan = ps[4], "ps4"
                po_, pon = ps[5], "ps5"
                for ci in range(8):
                    c = cg * 8 + ci
                    cs = slice(c * 64, (c + 1) * 64)
                    P.op('pe', lambda e: e.matmul(pa[0:64, ci * 64:(ci + 1) * 64], lhsT=ktb[:, cs], rhs=qtil[:, cs], start=True, stop=True),
                         r=["ktb", "qtil"], w=[pan])
                dv(lambda e: e.tensor_tensor(out=attm[:], in0=pa[0:64, :], in1=cm[:], op=ALU.mult), [pan, "cm"], ["attm"])
                for ci in range(8):
                    c = cg * 8 + ci
                    cs = slice(c * 64, (c + 1) * 64)
                    P.op('pe', lambda e: e.matmul(po_[0:64, ci * 64:(ci + 1) * 64], lhsT=attm[:, ci * 64:(ci + 1) * 64], rhs=v64[:, c, :],
                                                  start=True, stop=False), r=["attm", "v64"], w=[pon])
                    P.op('pe', lambda e: e.matmul(po_[0:64, ci * 64:(ci + 1) * 64], lhsT=qtil[:, cs], rhs=sbf[:, c, :],
                                                  start=False, stop=True), r=["qtil", "sbf"], w=[pon])
                ac(lambda e: e.activation(out=osb[:].rearrange("p a b -> p (a b)"), in_=po_[0:64, :], func=AF.Copy), [pon], ["osb"])
                P.op('pool', lambda e: e.tensor_tensor(out=osq[:], in0=osb[:], in1=osb[:], op=ALU.mult), r=["osb"], w=["osq"])
                dv(lambda e: e.tensor_reduce(out=ss[:], in_=osq[:], axis=AX.X, op=ALU.add), ["osq"], ["ss"])
                ac(lambda e: e.activation(out=ss[:], in_=ss[:], func=AF.Sqrt, scale=1.0 / 64, bias=EPS), ["ss"], ["ss"])
                dv(lambda e: e.reciprocal(out=ss[:], in_=ss[:]), ["ss"], ["ss"])
                y_ = yh[cg % 2]; yn = f"yh{cg % 2}"
                dv(lambda e: e.tensor_tensor(out=y_[:], in0=osb[:], in1=ss[:, :, None].to_broadcast([64, 8, 64]), op=ALU.mult),
                   ["osb", "ss"], [yn])
                dv(lambda e: e.tensor_tensor(out=y_[:], in0=y_[:], in1=gain[:, None, :].to_broadcast([64, 8, 64]), op=ALU.mult),
                   [yn, "gain"], [yn])
                dv(lambda e: e.tensor_tensor(out=y_[:], in0=y_[:], in1=g64[:, cg * 8:(cg + 1) * 8, :], op=ALU.mult), [yn, "g64"], [yn])
                c0 = blk * 32 + cg * 8
                P.dma(yh_d[:, c0:c0 + 8, :], y_[:], r=[yn], w=[("yhd", c0)], sem=f"yhs{cg % 2}")
                yield
            dv(lambda e: e.tensor_copy(out=sall[:, 0, :], in_=sall[:, 32, :]), [("sall", 32), "sbf"], [("sall", 0)])

    threads = []
    if 's5' in PARTS:
        threads.append(s5_gen())
    if 'hg' in PARTS:
        threads.append(hg_gen())
    while threads:
        for t_ in list(threads):
            try:
                next(t_)
            except StopIteration:
                threads.remove(t_)
    return P.finish()


def _m1_consts():
    jv1 = np.array([7, 6, 5, 4, 3, 2, 1, 0, 1, 2, 3, 4, 5, 6, 7, 8, -7, -6, -5, -4, -3, -2, -1, 0], np.float32)
    jv = np.ascontiguousarray(np.broadcast_to(jv1, (128, 4, 24))).astype(np.float32)
    sgn = np.ones((128, 2), np.float32); sgn[64:, 0] = -1; sgn[:64, 1] = -1
    i128 = np.eye(128, dtype=np.float32)
    jsw = np.zeros((128, 128), np.float32)
    for k in range(128):
        jsw[k, (k + 64) % 128] = 1
    s_idx = np.arange(128) // 16
    tmask = (s_idx[None, :] >= s_idx[:, None]).astype(np.float32)
    st = np.arange(64)
    cm = np.tile((st[:, None] <= st[None, :]).astype(np.float32), (1, 8))
    return dict(jv=jv, sgn=sgn, i128=i128, jsw=jsw, tmask=tmask, cmask=cm)


def run_M1(zT, inp, l):
    nc = build_M1(l)
    cst = _m1_consts()
    maps = []
    for c in range(NCORES):
        b = c // 4
        tsl = slice(b * SEQ, (b + 1) * SEQ)
        m = dict(cst)
        gs = [4 * (c % 4) + gi for gi in range(4)]
        st2 = lambda a: np.concatenate([a, a], axis=0)
        m["lr2"] = np.stack([st2(inp['s5_lambda_re'][l, g]) for g in gs], 1).astype(np.float32)
        m["li2"] = np.stack([st2(inp['s5_lambda_im'][l, g]) for g in gs], 1).astype(np.float32)
        m["ldt"] = np.ascontiguousarray(np.broadcast_to(np.array([inp['s5_log_dt'][l, g] for g in gs], np.float32), (128, 4)))
        m["bst1"] = np.stack([np.concatenate([inp['s5_b_re'][l, g], inp['s5_b_im'][l, g]], 0) for g in gs], 1)
        m["bst2"] = np.stack([np.concatenate([inp['s5_b_im'][l, g], inp['s5_b_re'][l, g]], 0) for g in gs], 1)
        m["cst1"] = np.stack([np.concatenate([inp['s5_c_re'][l, g].T, inp['s5_c_im'][l, g].T], 0) for g in gs], 1)
        m["cst2"] = np.stack([np.concatenate([inp['s5_c_im'][l, g].T, inp['s5_c_re'][l, g].T], 0) for g in gs], 1)
        m["dcol"] = np.stack([np.tile(inp['s5_d'][l, g], 8) for g in gs], 1).astype(np.float32)
        m["uc"] = np.stack([zT[g * 16:(g + 1) * 16, tsl].reshape(16, NCH, 8).transpose(2, 0, 1).reshape(128, NCH) for g in gs], 0)
        h = c % 4
        m["hq"] = zT[256 + 64 * h:256 + 64 * (h + 1), tsl]
        m["hf"] = zT[512 + 64 * h:512 + 64 * (h + 1), tsl]
        v = zT[768 + 64 * h:768 + 64 * (h + 1), tsl].T
        gg = zT[1024 + 64 * h:1024 + 64 * (h + 1), tsl].T
        m["hv64"] = v.reshape(4, 32, 64, 64).transpose(0, 2, 1, 3)
        m["hv128"] = v.reshape(4, 16, 128, 64).transpose(0, 2, 1, 3)
        m["hg64"] = gg.reshape(4, 32, 64, 64).transpose(0, 2, 1, 3)
        m["lbl"] = inp['hgrn_lb_logits'][:, 64 * h:64 * (h + 1)].T
        m["hgain"] = np.broadcast_to(inp['hgrn_norm'][l][None, :], (64, 64))
        maps.append({k: np.ascontiguousarray(v_, dtype=np.float32) for k, v_ in m.items()})
    res = run_bass_kernel_spmd(nc, maps, core_ids=list(range(NCORES)))
    y5T = np.zeros((256, BATCH * SEQ), np.float32)
    yhT = np.zeros((256, BATCH * SEQ), np.float32)
    for c in range(NCORES):
        b = c // 4
        tsl = slice(b * SEQ, (b + 1) * SEQ)
        r = res.results[c]
        for gi in range(4):
            g = 4 * (c % 4) + gi
            y5T[g * 16:(g + 1) * 16, tsl] = r["y5"][gi].reshape(8, 16, NCH).transpose(1, 2, 0).reshape(16, SEQ)
        h = c % 4
        yhT[64 * h:64 * (h + 1), tsl] = r["yh"].transpose(1, 0, 2).reshape(SEQ, 64).T
    return y5T, yhT


OFF_F = 8320
NF_F = OFF_F + 8192 + 128
NEGB = -30000.0
TINY = 1e-30
NQT = 32
SCALE = 0.125


def build_M2():
    P = Prog()
    nc = P.nc
    q65_d = P.dram_in("q65", [4, 65, NQT * 128])
    ksl_d = P.dram_in("kslX", [64, SEQ]); kwn_d = P.dram_in("kwnX", [64, SEQ])
    vsl_d = P.dram_in("vslX", [128, 64 * 64]); vwn_d = P.dram_in("vwnX", [128, 64 * 64])
    kcr_d = P.dram_in("kcrX", [32, 64, 512]); vcr_d = P.dram_in("vcrX", [32, 64, 512])
    w1k_d = P.dram_in("w1k", [64, 32 * 64]); w1v_d = P.dram_in("w1v", [64, 32 * 64])
    w2k_d = P.dram_in("w2k", [64, 64]); w2v_d = P.dram_in("w2v", [64, 64])
    posk_d = P.dram_in("poskT", [64, 32]); posv_d = P.dram_in("posvT", [64, 32])
    fs_d = P.dram_in("fs", [4, NF_F]); fw_d = P.dram_in("fw", [4, NF_F])
    eall_d = P.dram_in("eall", [128, SEQ]); mm_d = P.dram_in("mmat", [128, 4 * 128])
    madd_d = P.dram_in("madd", [NQT, 128, 128]); glog_d = P.dram_in("glog", [128, NQT * 12])
    i128_d = P.dram_in("i128", [128, 128])
    parity_dummy = None
    y_d = P.dram_out("ynsa", [128, NQT, 256])

    T = P.sbuf
    ps = [P.psum(f"ps{i}", [128, 512]) for i in range(8)]
    psA = [(ps[0], "ps0"), (ps[1], "ps1"), (ps[3], "psB0"), (ps[7], "psB1")]
    psL, psO1, psO2, psO3 = ps[2], ps[4], ps[5], ps[6]
    psS = ps[4]
    stgall = T("stgall", [128, 4096], F32)
    stg = [stgall[:, 0:2048], stgall[:, 2048:4096]]
    sctr = [0]

    def stage():
        i = sctr[0] % 2
        sctr[0] += 1
        return stg[i], f"stg{i}"

    i128 = T("i128", [128, 128], F32); P.dma(i128[:], i128_d[:, :], w=["i128"])
    ones = T("ones", [128, 128], BF16); P.op('pool', lambda e: e.memset(ones[:], 1.0), w=["ones"])
    qa = T("qa", [65, 4, NQT * 128], BF16)
    ksl = T("ksl", [65, SEQ], BF16); kwn = T("kwn", [64, SEQ], BF16)
    vsl = T("vsl", [128, 64, 65], BF16); vwn = T("vwn", [128, 64, 65], BF16)
    eall = T("eall", [128, SEQ], BF16); mmt = T("mmt", [128, 4, 128], F32)
    kcmp = T("kcmp", [64, 512], BF16); vcmp = T("vcmp", [128, 4, 65], BF16)
    bs = T("bs", [128, 11, 512], F32)
    glog = T("glog", [128, NQT * 12], F32)

    b31t = T("b31t", [65, 4], F32)

    def load_q(hh):
        for r in range(4):
            P.dma(qa[0:64, r, hh * 2048:(hh + 1) * 2048], q65_d[r, 0:64, hh * 2048:(hh + 1) * 2048], w=[("qa", hh)], q='pool', sem=f"qa{hh}")
            P.op('act', lambda e: e.activation(out=qa[64:65, r, hh * 2048:(hh + 1) * 2048],
                                               in_=b31t[64:65, r:r + 1].to_broadcast([1, 2048]), func=AF.Copy, scale=8.0),
                 r=["b31t"], w=[("qa", hh)])

    def load_piece(p):
        cs = slice(p * 2048, (p + 1) * 2048)
        P.dma(ksl[0:64, cs], ksl_d[:, cs], w=[("ksl", p)], q='pool', sem=f"ksl{p}")
        P.op('pool', lambda e: e.memset(ksl[64:65, cs], 1.0), w=[("ksl", p)])
        P.dma(kwn[0:64, cs], kwn_d[:, cs], w=[("kwn", p)], q='pool', sem=f"kwn{p}")
        for (dst, dn, src) in [(vsl, "vsl", vsl_d), (vwn, "vwn", vwn_d)]:
            P.dma(dst[:, p * 16:(p + 1) * 16, 0:64], src[:, p * 1024:(p + 1) * 1024].rearrange("p (a b) -> p a b", b=64),
                  w=[(dn, p)], q='pool', sem=f"{dn}{p}")
            P.op('pool', lambda e: e.memset(dst[:, p * 16:(p + 1) * 16, 64:65], 1.0), w=[(dn, p)])
        P.dma(eall[:, cs], eall_d[:, cs], w=[("eall", p)], q='pool', sem=f"eall{p}")

    for r in range(4):
        P.dma(b31t[64:65, r:r + 1], q65_d[r, 64:65, 0:1], w=["b31t"])
    load_q(0)
    load_piece(0)
    P.dma(mmt[:].rearrange("p a b -> p (a b)"), mm_d[:, :], w=["mmt"])
    P.dma(glog[:], glog_d[:, :], w=["glog"])
    P.op('act', lambda e: e.activation(out=glog[:], in_=glog[:], func=AF.Sigmoid), r=["glog"], w=["glog"])
    for j in range(11):
        tab = fs_d if j < 9 else fw_d
        dl = 128 * j if j < 9 else 128 * (j - 5)
        src = bass.AP(tab.tensor, OFF_F + dl - 127, [(1, 128), (NF_F, 4), (1, 128)])
        P.dma(bs[:, j, :].rearrange("p (r q) -> p r q", q=128), src, w=[("bs", j)], sem="bs")

    w1 = T("w1", [64, 32 * 64], BF16); w2 = T("w2", [64, 64], BF16); posT = T("posT", [64, 32], BF16)
    pb = T("pb", [64, 1], F32); xj = [T(f"xj{i}", [64, 512], BF16) for i in range(2)]
    ga = T("ga", [64, 512], F32); gb = T("gb", [64, 512], F32); gel = T("gel", [64, 512], BF16)
    for which in range(2):
        w1_d, w2_d, pos_d, x_d = [(w1k_d, w2k_d, posk_d, kcr_d), (w1v_d, w2v_d, posv_d, vcr_d)][which]
        P.dma(w1[:], w1_d[:, :], w=["w1"], q='pool')
        P.dma(w2[:], w2_d[:, :], w=["w2"], q='pool')
        P.dma(posT[:], pos_d[:, :], w=["posT"], q='pool')
        for j in range(32):
            P.op('pe', lambda e: e.matmul(psL[0:64, 0:1], lhsT=w1[:, j * 64:(j + 1) * 64], rhs=posT[:, j:j + 1], start=(j == 0), stop=(j == 31)),
                 r=["w1", "posT"], w=["psL"])
        P.op('act', lambda e: e.activation(out=pb[:], in_=psL[0:64, 0:1], func=AF.Copy), r=["psL"], w=["pb"])
        for j in range(32):
            P.dma(xj[j % 2][:], x_d[j], w=[f"xj{j % 2}"], q='pool')
            P.op('pe', lambda e: e.matmul(ps[3][0:64, :], lhsT=w1[:, j * 64:(j + 1) * 64], rhs=xj[j % 2][:], start=(j == 0), stop=(j == 31)),
                 r=["w1", f"xj{j % 2}"], w=["psB0"])
        emit_gelu(P, gel[:], ps[3][0:64, :], ga[:], gb[:], ("gel", "ga", "gb"), ["psB0"], bias=(pb[:, 0:1], "pb"))
        if which == 0:
            P.op('pe', lambda e: e.matmul(psO1[0:64, :], lhsT=w2[:], rhs=gel[:], start=True, stop=True), r=["w2", "gel"], w=["psO1"])
            P.op('act', lambda e: e.activation(out=kcmp[:], in_=psO1[0:64, :], func=AF.Copy), r=["psO1"], w=["kcmp"])
        else:
            for ci in range(4):
                P.op('pe', lambda e: e.matmul(psO2[:, ci * 64:(ci + 1) * 64], lhsT=gel[:, ci * 128:(ci + 1) * 128], rhs=w2[:], start=True, stop=True),
                     r=["w2", "gel"], w=["psO2"])
            P.op('act', lambda e: e.activation(out=vcmp[:, :, 0:64], in_=psO2[:, 0:256].rearrange("p (a b) -> p a b", b=64), func=AF.Copy),
                 r=["psO2"], w=["vcmp"])
            P.op('pool', lambda e: e.memset(vcmp[:, :, 64:65], 1.0), w=["vcmp"])

    bc = [T(f"bc{i}", [128, 512], F32) for i in range(2)]
    tmpb = [T(f"tmpb{i}", [128, 512], F32) for i in range(2)]
    ebuf = [T(f"ebuf{i}", [128, 512], BF16) for i in range(4)]
    pbuf = [T(f"pbuf{i}", [128, 512], BF16) for i in range(4)]
    maskall = T("maskall", [128, 64, 128], BF16)
    ec = T("ec", [128, 4, 512], BF16)
    rlb = T("rlb", [128, 512], F32); pnb = T("pnb", [128, 512], F32); impT = T("impT", [128, 4, 128], F32)
    madd = [T(f"madd{i}", [128, 128], F32) for i in range(2)]
    score = T("score", [128, 128], F32); sc2 = T("sc2", [128, 128], F32); m8a = T("m8a", [128, 8], F32); m8b = T("m8b", [128, 8], F32)
    self_ = T("self", [128, 128], F32); selT = T("selT", [128, 128], BF16)
    osb = [T(f"osb{i}", [128, 260], F32) for i in range(3)]
    lc = T("lc", [128, 3, 4], F32); coef = T("coef", [128, 3, 4], F32)
    yb = [T(f"yb{i}", [128, 256], F32) for i in range(2)]
    rot = dict(a=0, t=0, e=0, p=0, b=0)

    NROT = dict(a=4, t=2, e=4, p=4, b=2)

    def nxt(k):
        n = NROT[k]
        v = rot[k] % n
        rot[k] += 1
        return v

    def softmax_tile(A_src, An, bias_ap, bias_names, eout, eout_name):
        if bias_ap is None:
            P.op('act', lambda e: e.activation(out=eout, in_=A_src, func=AF.Exp, scale=SCALE), r=[An], w=[eout_name])
        else:
            ti = nxt('t')
            P.op('dve', lambda e: e.scalar_tensor_tensor(out=tmpb[ti][:], in0=A_src, scalar=SCALE, in1=bias_ap, op0=ALU.mult, op1=ALU.add),
                 r=[An] + bias_names, w=[f"tmpb{ti}"])
            P.op('act', lambda e: e.activation(out=eout, in_=tmpb[ti][:], func=AF.Exp), r=[f"tmpb{ti}"], w=[eout_name])

    def pv(psO, psOn, lhs_tile, lhs_name, v_ap, v_name, first, last=False):
        for r in range(4):
            P.op('pe', lambda e: e.matmul(psO[:, r * 65:(r + 1) * 65], lhsT=lhs_tile[:, r * 128:(r + 1) * 128], rhs=v_ap,
                                          start=(first and r == 0), stop=(last and r == 3)), r=[lhs_name, v_name], w=[psOn])

    import os
    NT_RUN = int(os.environ.get("M2_NT", NQT))
    scr = T("scr", [128, 1], F32)
    P.op('pool', lambda e: e.memset(scr[:], 0.0), w=["maskall1", "scr"])
    maskall2 = [maskall, stgall[:, :].bitcast(BF16).rearrange("p (a b) -> p a b", b=128)]
    osb0 = [osb[0], T("osb0b", [128, 260], F32), T("osb0c", [128, 260], F32)]
    osb2 = [osb[2], T("osb2b", [128, 260], F32), T("osb2c", [128, 260], F32)]
    selT3 = [selT, T("selTb", [128, 128], BF16), T("selTc", [128, 128], BF16)]

    def pairs_gen(pairs, depth):
        pend = []
        for (A_, B_) in pairs:
            pend.append((B_, A_()))
            if len(pend) > depth:
                b_, c_ = pend.pop(0)
                b_(c_)
            yield
        for b_, c_ in pend:
            b_(c_)
            yield

    def phaseX(m):
        qi = 2 * m + 1
        t0 = 128 * qi
        q64 = qa[0:64, :, m * 128:(m + 1) * 128]
        nck = min(4, ((t0 + 96) // 16) // 128 + 1)
        selTm = selT3[m % 3]; selTn = f"selT{m % 3}"

        def cmpA(ci):
            def f():
                bi = nxt('b')
                src = bass.AP(fs_d.tensor, OFF_F + t0 - 2048 * ci - 2063, [(16, 128), (NF_F, 4), (1, 128)])
                P.dma(bc[bi][:].rearrange("p (r q) -> p r q", q=128), src, w=[f"bc{bi}"])
                ai = nxt('a'); A, An = psA[ai]
                P.op('pe', lambda e: e.matmul(A[:, :], lhsT=kcmp[:, ci * 128:(ci + 1) * 128], rhs=q64, start=True, stop=True),
                     r=["kcmp", ("qa", m // 16)], w=[An])
                softmax_tile(A[:, :], An, bc[bi][:], [f"bc{bi}"], ec[:, ci, :], ("ec", ci))
                return ci
            return f

        def cmpB(ci):
            P.op('pe', lambda e: e.matmul(psL[:, :], lhsT=ones[:], rhs=ec[:, ci, :], start=(ci == 0), stop=(ci == nck - 1)),
                 r=["ones", ("ec", ci)], w=["psL"])
            pv(psO1, "psO1", ec[:, ci, :], ("ec", ci), vcmp[:, ci, :], "vcmp", ci == 0, ci == nck - 1)
        yield from pairs_gen([(cmpA(ci), cmpB) for ci in range(nck)], 2)
        P.op('dve', lambda e: e.tensor_scalar(out=rlb[:], in0=psL[:, :], scalar1=TINY, scalar2=None, op0=ALU.max), r=["psL"], w=["rlb"])
        P.op('dve', lambda e: e.reciprocal(out=rlb[:], in_=rlb[:]), r=["rlb"], w=["rlb"])
        yield
        for ci in range(nck):
            P.op('dve', lambda e: e.tensor_tensor(out=pnb[:], in0=ec[:, ci, :], in1=rlb[:], op=ALU.mult), r=[("ec", ci), "rlb"], w=["pnb"])
            P.op('dve', lambda e: e.tensor_reduce(out=impT[:, ci, :], in_=pnb[:].rearrange("p (r q) -> p q r", q=128), axis=AX.X, op=ALU.add),
                 r=["pnb"], w=[("impT", ci)])
            yield
        for ci in range(nck):
            P.op('pe', lambda e: e.matmul(psS[:, 260:388], lhsT=impT[:, ci, :], rhs=mmt[:, ci, :], start=False, stop=(ci == nck - 1)),
                 r=[("impT", ci), "mmt"], w=["psO1"])
        P.op('act', lambda e: e.activation(out=osb0[m % 3][:], in_=psO1[:, 0:260], func=AF.Copy), r=["psO1"], w=[f"osb0_{m % 3}"])
        yield
        mi = m % 2
        P.dma(madd[mi][:], madd_d[m], w=[f"madd{mi}"])
        P.op('dve', lambda e: e.tensor_tensor(out=score[:], in0=psS[:, 260:388], in1=madd[mi][:], op=ALU.add), r=["psO1", f"madd{mi}"], w=["score"])
        P.op('dve', lambda e: e.max(out=m8a[:], in_=score[:]), r=["score"], w=["m8a"])
        yield
        P.op('dve', lambda e: e.match_replace(out=sc2[:], in_to_replace=m8a[:], in_values=score[:], imm_value=-3e38), r=["score", "m8a"], w=["sc2"])
        P.op('dve', lambda e: e.max(out=m8b[:], in_=sc2[:]), r=["sc2"], w=["m8b"])
        yield
        P.op('dve', lambda e: e.tensor_scalar(out=self_[:], in0=score[:], scalar1=m8b[:, 7:8], scalar2=None, op0=ALU.is_ge),
             r=["score", "m8b"], w=["self"])
        P.op('pe', lambda e: e.transpose(out=psL[:, 0:128], in_=self_[:], identity=i128[:]), r=["self", "i128"], w=["psL"])
        P.op('act', lambda e: e.activation(out=selTm[:], in_=psL[:, 0:128], func=AF.Copy), r=["psL"], w=[selTn])
        yield
        kts = list(range(max(0, qi - 5), qi + 1))

        def winA(kt):
            def f():
                j = qi - kt
                ai = nxt('a'); A, An = psA[ai]
                P.op('pe', lambda e: e.matmul(A[:, :], lhsT=kwn[0:64, kt * 128:(kt + 1) * 128], rhs=q64, start=True, stop=True),
                     r=[("kwn", kt // 16), ("qa", m // 16)], w=[An])
                ei = nxt('e')
                jj = j if j < 4 else 5 + j
                softmax_tile(A[:, :], An, bs[:, jj, :], [("bs", jj)], ebuf[ei][:], f"ebuf{ei}")
                return (kt, ei)
            return f

        def winB(ctx):
            kt, ei = ctx
            pv(psO3, "psO3", ebuf[ei], f"ebuf{ei}", vwn[:, kt, :], ("vwn", kt // 16), kt == kts[0], kt == kts[-1])
        yield from pairs_gen([(winA(kt), winB) for kt in kts], 2)
        P.op('act', lambda e: e.activation(out=osb2[m % 3][:], in_=psO3[:, 0:260], func=AF.Copy), r=["psO3"], w=[f"osb2_{m % 3}"])
        yield

    def phaseX2(m):
        qi = 2 * m + 1
        mk = maskall2[m % 2]; mkn = f"maskall{m % 2}"
        selTm = selT3[m % 3]; selTn = f"selT{m % 3}"
        for k0 in range(0, qi + 1, 4):
            n4 = min(4, qi + 1 - k0)
            ai = nxt('a'); A, An = psA[ai]
            for u in range(n4):
                P.op('pe', lambda e: e.matmul(A[:, u * 128:(u + 1) * 128], lhsT=eall[:, (k0 + u) * 128:(k0 + u + 1) * 128], rhs=selTm[:],
                                              start=(u == 0), stop=(u == n4 - 1)), r=[("eall", (k0 + u) // 16), selTn], w=[An])
            P.op('act', lambda e: e.activation(out=mk[:, k0:k0 + n4, :].rearrange("p a b -> p (a b)"), in_=A[:, 0:n4 * 128], func=AF.Copy),
                 r=[An], w=[(mkn, k0)])
            yield

    def phaseY(m):
        qi = 2 * m + 1
        q64 = qa[0:64, :, m * 128:(m + 1) * 128]
        q65 = qa[0:65, :, m * 128:(m + 1) * 128]
        mk = maskall2[m % 2]; mkn = f"maskall{m % 2}"

        def selA(kt):
            def f():
                near = (qi - kt) <= 8
                ai = nxt('a'); A, An = psA[ai]
                if near:
                    P.op('pe', lambda e: e.matmul(A[:, :], lhsT=ksl[0:64, kt * 128:(kt + 1) * 128], rhs=q64, start=True, stop=True),
                         r=[("ksl", kt // 16), ("qa", m // 16)], w=[An])
                else:
                    P.op('pe', lambda e: e.matmul(A[:, :], lhsT=ksl[0:65, kt * 128:(kt + 1) * 128], rhs=q65, start=True, stop=True),
                         r=[("ksl", kt // 16), ("qa", m // 16)], w=[An])
                ei = nxt('e')
                if near:
                    softmax_tile(A[:, :], An, bs[:, qi - kt, :], [("bs", qi - kt)], ebuf[ei][:], f"ebuf{ei}")
                else:
                    softmax_tile(A[:, :], An, None, None, ebuf[ei][:], f"ebuf{ei}")
                bi = nxt('p')
                P.op('pool' if (kt % 5) < 2 else 'dve', lambda e: e.tensor_tensor(out=pbuf[bi][:].rearrange("p (r q) -> p r q", q=128),
                                                       in0=ebuf[ei][:].rearrange("p (r q) -> p r q", q=128),
                                                       in1=mk[:, kt:kt + 1, :].to_broadcast([128, 4, 128]), op=ALU.mult),
                     r=[f"ebuf{ei}", (mkn, (kt // 4) * 4)], w=[f"pbuf{bi}"])
                return (kt, bi)
            return f

        def selB(ctx):
            kt, bi = ctx
            pv(psO2, "psO2", pbuf[bi], f"pbuf{bi}", vsl[:, kt, :], ("vsl", kt // 16), kt == 0, kt == qi)
        yield from pairs_gen([(selA(kt), selB) for kt in range(qi + 1)], 3)
        srcs = [(osb0[m % 3], f"osb0_{m % 3}"), (osb[1], "osb1"), (osb2[m % 3], f"osb2_{m % 3}")]
        P.op('act', lambda e: e.activation(out=osb[1][:], in_=psO2[:, 0:260], func=AF.Copy), r=["psO2"], w=["osb1"])
        for b_, (ot, on) in enumerate(srcs):
            P.op('dve', lambda e: e.tensor_copy(out=lc[:, b_, :], in_=ot[:].rearrange("p (r d) -> p r d", d=65)[:, :, 64]),
                 r=[on], w=["lc"])
        P.op('dve', lambda e: e.tensor_scalar(out=lc[:], in0=lc[:], scalar1=TINY, scalar2=None, op0=ALU.max), r=["lc"], w=["lc"])
        P.op('dve', lambda e: e.reciprocal(out=lc[:], in_=lc[:]), r=["lc"], w=["lc"])
        P.op('dve', lambda e: e.tensor_tensor(out=coef[:], in0=lc[:],
                                              in1=glog[:, m * 12:(m + 1) * 12].rearrange("p (r b) -> p b r", b=3), op=ALU.mult),
             r=["lc", "glog"], w=["coef"])
        yield
        y_ = yb[m % 2]; yn = f"yb{m % 2}"
        for r in range(4):
            for b_, (ot, on) in enumerate(srcs):
                src_ = ot[:, r * 65:r * 65 + 64]
                if b_ == 0:
                    P.op('dve', lambda e: e.tensor_scalar(out=y_[:, r * 64:(r + 1) * 64], in0=src_, scalar1=coef[:, b_, r:r + 1], scalar2=None,
                                                          op0=ALU.mult), r=[on, "coef"], w=[(yn, r)])
                else:
                    P.op('dve', lambda e: e.scalar_tensor_tensor(out=y_[:, r * 64:(r + 1) * 64], in0=src_, scalar=coef[:, b_, r:r + 1],
                                                                 in1=y_[:, r * 64:(r + 1) * 64], op0=ALU.mult, op1=ALU.add),
                         r=[on, "coef", (yn, r)], w=[(yn, r)])
            yield
        P.dma(y_d[:, m, :], y_[:], r=[yn], w=[("yd", m)], sem=f"ys{m % 2}")
        yield

    import itertools
    for g_ in [phaseX(0), phaseX2(0)] + ([phaseX(1)] if NT_RUN > 1 else []):
        for _ in g_:
            pass
    for m in range(NT_RUN):
        for p_ in range(1, 4):
            if m == max(0, 8 * p_ - 6):
                load_piece(p_)
        if m == 8:
            load_q(1)
        gy = phaseY(m)
        gens = []
        if m + 1 < NT_RUN:
            gens.append(phaseX2(m + 1))
        if m + 2 < NT_RUN:
            gens.append(phaseX(m + 2))
        gx = itertools.chain(*gens)
        ny = 2 * m + 2 + 6
        kx = max(1, -(-90 // ny))
        for _ in gy:
            for _ in range(kx):
                next(gx, None)
        for _ in gx:
            pass
    return P.finish()


def _t5_bucket_np(d):
    import math
    n = np.maximum(d, 0)
    nf = np.maximum(n, 16).astype(np.float32)
    large = 16 + (np.log(nf / np.float32(16)) / np.float32(math.log(64.0)) * np.float32(16)).astype(np.int32)
    large = np.minimum(large, 31)
    return np.where(n < 16, n, large)


def _m2_consts():
    eall = np.zeros((128, SEQ), np.float32)
    col = np.arange(SEQ)
    kt = col // 128
    p = col % 128
    eall[2 * kt + (127 - p) // 64, col] = 1.0
    mm = np.zeros((128, 4, 128), np.float32)
    wts = {-1: 1.0, 0: 2.0, 1: 2.0, 2: 2.0, 3: 1.0}
    for ci in range(4):
        for pp in range(128):
            c = 128 * ci + 127 - pp
            for dlt, wv in wts.items():
                if (c - dlt) % 4 == 0:
                    j = (c - dlt) // 4
                    if 0 <= j < 128:
                        mm[pp, ci, j] = wv
    return dict(eall=eall, mmat=mm.reshape(128, 512), i128=np.eye(128, dtype=np.float32))


def run_M2(zT, inp, l):
    nc = build_M2()
    cst = _m2_consts()
    rel = inp['rel_bias'].astype(np.float32)
    didx = np.arange(NF_F) - OFF_F
    bk = _t5_bucket_np(didx)
    maps = []
    for c in range(NCORES):
        b = c // 4
        g = (c % 4) // 2
        par = c % 2
        tsl = slice(b * SEQ, (b + 1) * SEQ)
        m = dict(cst)
        sh = 128 * (1 - par)
        fs = np.full((4, NF_F), NEGB, np.float32)
        fw = np.full((4, NF_F), NEGB, np.float32)
        for r in range(4):
            base_s = np.where(didx >= 0, rel[bk, g * 4 + r], NEGB).astype(np.float32)
            base_w = np.where((didx >= 0) & (didx < 512), rel[bk, g * 4 + r], NEGB).astype(np.float32)
            fs[r, sh:] = base_s[:NF_F - sh]
            fw[r, sh:] = base_w[:NF_F - sh]
        m["fs"] = fs
        m["fw"] = fw
        tiles = 2 * np.arange(NQT) + par
        tok = (tiles[:, None] * 128 + np.arange(128)[None, :]).reshape(-1)
        q65 = np.zeros((4, 65, NQT * 128), np.float32)
        for r in range(4):
            q65[r, 0:64] = zT[1280 + g * 256 + r * 64:1280 + g * 256 + (r + 1) * 64, tsl][:, tok]
            q65[r, 64] = rel[31, g * 4 + r]
        m["q65"] = q65

        def kv(j):
            return zT[1792 + j * 128 + g * 64:1792 + j * 128 + (g + 1) * 64, tsl]
        rev = lambda a: a.reshape(64, 64, 128)[:, :, ::-1].reshape(64, SEQ)
        m["kslX"] = rev(kv(2))
        m["kwnX"] = rev(kv(4))
        vrev = lambda a: a.T.reshape(64, 128, 64)[:, ::-1, :].transpose(1, 0, 2).reshape(128, 64 * 64)
        m["vslX"] = vrev(kv(3))
        m["vwnX"] = vrev(kv(5))
        cc = (128 * (np.arange(512) // 128) + 127 - (np.arange(512) % 128))
        for nm, j in [("kcrX", 0), ("vcrX", 1)]:
            src = np.concatenate([kv(j), np.zeros((64, 64), np.float32)], axis=1)
            arr = np.zeros((32, 64, 512), np.float32)
            for jj in range(32):
                arr[jj] = src[:, np.minimum(16 * cc + jj, SEQ + 63)]
            m[nm] = arr
        for sfx, key in [("k", "k"), ("v", "v")]:
            w1 = inp[f'nsa_cmp_w1_{key}'][l]
            m[f"w1{sfx}"] = w1.reshape(32, 64, 64).transpose(1, 0, 2).reshape(64, 2048)
            m[f"w2{sfx}"] = inp[f'nsa_cmp_w2_{key}'][l]
            m[f"pos{sfx}T"] = inp[f'nsa_cmp_pos_{key}'][l].T
        t = tiles[:, None] * 128 + np.arange(128)[None, :]
        blk = np.arange(128)
        cur = t // 64
        ok = (blk[None, None, :] * 64) <= t[:, :, None]
        forced = (blk[None, None, :] == 0) | (blk[None, None, :] == cur[:, :, None]) | (blk[None, None, :] == cur[:, :, None] - 1)
        m["madd"] = np.where(ok, np.where(forced, 1e4, 0.0), -1e30).astype(np.float32)
        gl = zT[2560 + g * 12:2560 + (g + 1) * 12, tsl][:, tok]
        m["glog"] = gl.T.reshape(NQT, 128, 12).transpose(1, 0, 2).reshape(128, NQT * 12)
        maps.append({k: np.ascontiguousarray(v_, dtype=np.float32) for k, v_ in m.items()})
    res = run_bass_kernel_spmd(nc, maps, core_ids=list(range(NCORES)))
    yT = np.zeros((512, BATCH * SEQ), np.float32)
    for c in range(NCORES):
        b = c // 4
        g = (c % 4) // 2
        par = c % 2
        tiles = 2 * np.arange(NQT) + par
        tok = b * SEQ + (tiles[:, None] * 128 + np.arange(128)[None, :]).reshape(-1)
        y = res.results[c]["ynsa"]
        y = y.transpose(1, 0, 2).reshape(NQT * 128, 256)
        yT[g * 256:(g + 1) * 256, tok] = y.T
    return yT


def kernel(**inputs):
    inp = {k: np.asarray(v) for k, v in inputs.items()}
    x = inp['x'].astype(np.float32).reshape(BATCH * SEQ, D_MODEL)
    xT = np.ascontiguousarray(x.T)

    def win_map(l):
        win = np.zeros((D_MODEL, NZC * 128), np.float32)
        win[:, :D_IN] = inp['w_in'][l]
        return {"mix_g": _gcol(inp['mix_norm'][l]), "win": _chunkT(win, 8)}

    def wout_map(l):
        return {"wglu": _chunkT(inp['s5_w_glu'][l], 2), "wout": _chunkT(inp['w_out'][l], 8)}

    def mixers(zT, l):
        y5T, yhT = run_M1(zT, inp, l)
        ynT = run_M2(zT, inp, l)
        return np.concatenate([y5T, yhT, ynT], axis=0)

    common = _ffn_maps("f0", inp['ffn1_norm'][0], inp['ffn1_w_gate'][0], inp['ffn1_w_up'][0], inp['ffn1_w_down'][0])
    common.update(win_map(0))
    o = run_T(xT, common, False, 1, True, False)
    yT = mixers(o["zT"], 0)
    common = _ffn_maps("f0", inp['ffn2_norm'][0], inp['ffn2_w_gate'][0], inp['ffn2_w_up'][0], inp['ffn2_w_down'][0])
    common.update(_ffn_maps("f1", inp['ffn1_norm'][1], inp['ffn1_w_gate'][1], inp['ffn1_w_up'][1], inp['ffn1_w_down'][1]))
    common.update(win_map(1))
    common.update(wout_map(0))
    o = run_T(o["xoT"], common, True, 2, True, False, yT_full=yT)
    yT = mixers(o["zT"], 1)
    common = _ffn_maps("f0", inp['ffn2_norm'][1], inp['ffn2_w_gate'][1], inp['ffn2_w_up'][1], inp['ffn2_w_down'][1])
    common.update(wout_map(1))
    common["fin_g"] = _gcol(inp['final_norm'])
    o = run_T(o["xoT"], common, True, 1, False, True, yT_full=yT)
    out = np.ascontiguousarray(o["xoT"].T).reshape(BATCH, SEQ, D_MODEL).astype(np.float32)
    return out
```
